# Optimizing a Trainium2 kernel written in Bass

```python
import jax, jax.numpy as jnp
from jax import lax
import numpy as np

D_MODEL = 2048
BATCH = 2
SEQ = 4096
DEPTH = 1
DEC_BATCH = 32
DEC_SEQ = 8
PAST_LEN = 8192
PAGE_SIZE = 128

HEAD_DIM = 128
N_HEADS = D_MODEL // HEAD_DIM
H_A = N_HEADS // 2
H_B = N_HEADS - H_A
KV_B = 2
GROUP_B = H_B // KV_B
H_IDX = 16
D_IDX = 64
TOPK_MAX = 256
Q_BLOCK = 128
ROPE_THETA = 10000.0
EPS = 1e-6
FORGET_BIAS_MEAN = 2.0
D_FF = 128 * ((8 * D_MODEL // 3 + 127) // 128)
CONV_W = 3
IN_SPLITS = (H_A * HEAD_DIM, H_A * HEAD_DIM, H_A * HEAD_DIM, H_A,
             H_B * HEAD_DIM, KV_B * HEAD_DIM, KV_B * HEAD_DIM,
             H_IDX * D_IDX, D_IDX, H_IDX)
D_IN = sum(IN_SPLITS)

kernel_name = 'hymba_fox_dsa_convffn_step'


def rms_norm(x, g):
    xf = x.astype(jnp.float32)
    y = xf * lax.rsqrt(jnp.mean(xf * xf, axis=-1, keepdims=True) + EPS)
    return (y * g.astype(jnp.float32)).astype(x.dtype)


def rope(x, pos):
    half = x.shape[-1] // 2
    inv = ROPE_THETA ** (-jnp.arange(half, dtype=jnp.float32) / half)
    ang = pos.astype(jnp.float32)[:, None] * inv[None, :]
    cos = jnp.cos(ang)[:, None, :]
    sin = jnp.sin(ang)[:, None, :]
    xf = x.astype(jnp.float32)
    x1, x2 = xf[..., :half], xf[..., half:]
    return jnp.concatenate([x1 * cos - x2 * sin, x2 * cos + x1 * sin], axis=-1).astype(x.dtype)


def project(xn, pos, w_in, b_f, g_qa, g_ka, g_qb, g_kb):
    B, T, _ = xn.shape
    points = [int(v) for v in np.cumsum(IN_SPLITS)[:-1]]
    qa, ka, va, fa, qb, kb, vb, qi, ki, wi = jnp.split(xn @ w_in, points, axis=-1)
    qa = rms_norm(qa.reshape(B, T, H_A, HEAD_DIM), g_qa)
    ka = rms_norm(ka.reshape(B, T, H_A, HEAD_DIM), g_ka)
    va = va.reshape(B, T, H_A, HEAD_DIM)
    logf = jax.nn.log_sigmoid((fa + b_f).astype(jnp.float32)).astype(xn.dtype)
    qb = rope(rms_norm(qb.reshape(B, T, H_B, HEAD_DIM), g_qb), pos)
    kb = rope(rms_norm(kb.reshape(B, T, KV_B, HEAD_DIM), g_kb), pos)
    vb = vb.reshape(B, T, KV_B, HEAD_DIM)
    qi = rope(qi.reshape(B, T, H_IDX, D_IDX), pos)
    ki = rope(ki[:, :, None, :], pos)[:, :, 0, :]
    return qa, ka, va, logf, qb, kb, vb, qi, ki, wi


def _blocks(n):
    qb = Q_BLOCK if n % Q_BLOCK == 0 else n
    return qb, n // qb


def _to_blocks(a, nb, qb):
    return jnp.swapaxes(a.reshape(a.shape[0], nb, qb, *a.shape[2:]), 0, 1)


def _from_blocks(a):
    a = jnp.swapaxes(a, 0, 1)
    return a.reshape(a.shape[0], a.shape[1] * a.shape[2], *a.shape[3:])


def forgetting_attention(q, k, v, logf, q_pos):
    Tk = k.shape[1]
    qb, nb = _blocks(q.shape[1])
    c = jnp.cumsum(logf.astype(jnp.float32), axis=1)
    c_q = jnp.take(c, q_pos, axis=1)
    c_k = jnp.swapaxes(c, 1, 2)
    k_pos = jnp.arange(Tk)
    scale = HEAD_DIM ** -0.5

    def block(args):
        qblk, cqblk, pblk = args
        s = jnp.einsum('bqhd,bkhd->bhqk', qblk, k, preferred_element_type=jnp.float32) * scale
        s = s + jnp.swapaxes(cqblk, 1, 2)[..., None] - c_k[:, :, None, :]
        s = jnp.where((k_pos[None, :] <= pblk[:, None])[None, None], s, -jnp.inf)
        p = jax.nn.softmax(s, axis=-1).astype(v.dtype)
        return jnp.einsum('bhqk,bkhd->bqhd', p, v)

    out = lax.map(block, (_to_blocks(q, nb, qb), _to_blocks(c_q, nb, qb), q_pos.reshape(nb, qb)))
    return _from_blocks(out)


def indexed_sparse_attention(q, k, v, qi, ki, wi, q_pos):
    B, Tk = k.shape[:2]
    n_sel = min(TOPK_MAX, Tk // 4)
    qb, nb = _blocks(q.shape[1])
    k_pos = jnp.arange(Tk)
    scale = HEAD_DIM ** -0.5
    idx_scale = (H_IDX * D_IDX) ** -0.5
    gather = jax.vmap(lambda rows, ids: rows[ids])

    def block(args):
        qblk, qiblk, wiblk, pblk = args
        causal = k_pos[None, :] <= pblk[:, None]
        dots = jnp.einsum('bqhd,bkd->bqhk', qiblk, ki, preferred_element_type=jnp.float32)
        score = jnp.einsum('bqhk,bqh->bqk', jax.nn.relu(dots), wiblk.astype(jnp.float32)) * idx_scale
        score = jnp.where(causal[None], score, -jnp.inf)
        _, sel = lax.top_k(score, n_sel)
        valid = sel <= pblk[None, :, None]
        ks = gather(k, sel)
        vs = gather(v, sel)
        qg = qblk.reshape(B, qb, KV_B, GROUP_B, HEAD_DIM)
        s = jnp.einsum('bqgrd,bqngd->bqgrn', qg, ks, preferred_element_type=jnp.float32) * scale
        s = jnp.where(valid[:, :, None, None, :], s, -jnp.inf)
        p = jax.nn.softmax(s, axis=-1).astype(vs.dtype)
        o = jnp.einsum('bqgrn,bqngd->bqgrd', p, vs)
        return o.reshape(B, qb, H_B, HEAD_DIM)

    out = lax.map(block, (_to_blocks(q, nb, qb), _to_blocks(qi, nb, qb),
                          _to_blocks(wi, nb, qb), q_pos.reshape(nb, qb)))
    return _from_blocks(out)


def conv_ffn(hn, prev_g, w_gate, w_up, conv_w, conv_b, w_down):
    T = hn.shape[1]
    g = hn @ w_gate
    u = hn @ w_up
    gp = jnp.concatenate([prev_g.astype(g.dtype), g], axis=1)
    gc = conv_b + sum(conv_w[j] * gp[:, j:j + T] for j in range(CONV_W))
    return (jax.nn.silu(gc) * u) @ w_down, gp[:, T:]


def decoder_layer(x, q_pos, past, prev_g, w_in, b_f, g_qa, g_ka, g_qb, g_kb, g_attn, w_out,
                  g_ffn, w_gate, w_up, conv_w, conv_b, w_down):
    B, T, _ = x.shape
    xn = rms_norm(x, g_attn)
    qa, ka, va, logf, qb, kb, vb, qi, ki, wi = project(xn, q_pos, w_in, b_f, g_qa, g_ka, g_qb, g_kb)
    new_rows = (ka, va, logf, kb, vb, ki)
    if past is None:
        ka_all, va_all, logf_all, kb_all, vb_all, ki_all = new_rows
    else:
        ka_all, va_all, logf_all, kb_all, vb_all, ki_all = [
            jnp.concatenate([p_, n_.astype(p_.dtype)], axis=1) for p_, n_ in zip(past, new_rows)]
    oa = forgetting_attention(qa, ka_all, va_all, logf_all, q_pos)
    ob = indexed_sparse_attention(qb, kb_all, vb_all, qi, ki_all, wi, q_pos)
    h = x + jnp.concatenate([oa.reshape(B, T, -1), ob.reshape(B, T, -1)], axis=-1) @ w_out
    f, new_g = conv_ffn(rms_norm(h, g_ffn), prev_g, w_gate, w_up, conv_w, conv_b, w_down)
    return h + f, new_rows + (new_g,)


def gather_pages(pool, layer, page_table):
    rows = pool[layer, page_table]
    return rows.reshape(rows.shape[0], rows.shape[1] * rows.shape[2], *rows.shape[3:])


def setup_inputs(seed: int = 0) -> dict:
    key = jax.random.key(seed)
    k = jax.random.split(key, 24)
    n_pages = PAST_LEN // PAGE_SIZE
    n_used = DEC_BATCH * n_pages
    n_pool = n_used + (n_used + 3) // 4

    def normal(kk, shape, scale=1.0):
        return scale * jax.random.normal(kk, shape, jnp.float32)

    pool = (DEPTH, n_pool, PAGE_SIZE)
    return {
        'x_prompt': normal(k[0], (BATCH, SEQ, D_MODEL)),
        'x_sample': normal(k[1], (DEC_BATCH, DEC_SEQ, D_MODEL)),
        'cache_fox_k': normal(k[2], pool + (H_A, HEAD_DIM)),
        'cache_fox_v': normal(k[3], pool + (H_A, HEAD_DIM)),
        'cache_fox_logf': jax.nn.log_sigmoid(FORGET_BIAS_MEAN + normal(k[4], pool + (H_A,))),
        'cache_dsa_k': normal(k[5], pool + (KV_B, HEAD_DIM)),
        'cache_dsa_v': normal(k[6], pool + (KV_B, HEAD_DIM)),
        'cache_idx_k': normal(k[7], pool + (D_IDX,)),
        'state_ffn_conv': normal(k[8], (DEPTH, DEC_BATCH, CONV_W - 1, D_FF)),
        'page_table': jax.random.permutation(k[9], n_pool)[:n_used].reshape(DEC_BATCH, n_pages).astype(jnp.int32),
        'w_in': normal(k[10], (DEPTH, D_MODEL, D_IN), D_MODEL ** -0.5),
        'b_f': FORGET_BIAS_MEAN + normal(k[11], (DEPTH, H_A), 0.1),
        'g_qa': 1.0 + normal(k[12], (DEPTH, HEAD_DIM), 0.02),
        'g_ka': 1.0 + normal(k[13], (DEPTH, HEAD_DIM), 0.02),
        'g_qb': 1.0 + normal(k[14], (DEPTH, HEAD_DIM), 0.02),
        'g_kb': 1.0 + normal(k[15], (DEPTH, HEAD_DIM), 0.02),
        'g_attn': 1.0 + normal(k[16], (DEPTH, D_MODEL), 0.02),
        'w_out': normal(k[17], (DEPTH, D_MODEL, D_MODEL), D_MODEL ** -0.5),
        'g_ffn': 1.0 + normal(k[18], (DEPTH, D_MODEL), 0.02),
        'w_gate': normal(k[19], (DEPTH, D_MODEL, D_FF), D_MODEL ** -0.5),
        'w_up': normal(k[20], (DEPTH, D_MODEL, D_FF), D_MODEL ** -0.5),
        'conv_w': normal(k[21], (DEPTH, CONV_W, D_FF), CONV_W ** -0.5),
        'conv_b': normal(k[22], (DEPTH, D_FF), 0.02),
        'w_down': normal(k[23], (DEPTH, D_FF, D_MODEL), D_FF ** -0.5),
    }


def reference(x_prompt, x_sample, cache_fox_k, cache_fox_v, cache_fox_logf, cache_dsa_k, cache_dsa_v,
              cache_idx_k, state_ffn_conv, page_table, w_in, b_f, g_qa, g_ka, g_qb, g_kb, g_attn,
              w_out, g_ffn, w_gate, w_up, conv_w, conv_b, w_down):
    past_len = page_table.shape[1] * PAGE_SIZE
    pos_p = jnp.arange(x_prompt.shape[1], dtype=jnp.int32)
    pos_s = past_len + jnp.arange(x_sample.shape[1], dtype=jnp.int32)
    pools = (cache_fox_k, cache_fox_v, cache_fox_logf, cache_dsa_k, cache_dsa_v, cache_idx_k)
    y_p, y_s = x_prompt, x_sample
    rows_p, rows_s = [], []
    for l in range(DEPTH):
        lw = (w_in[l], b_f[l], g_qa[l], g_ka[l], g_qb[l], g_kb[l], g_attn[l], w_out[l], g_ffn[l],
              w_gate[l], w_up[l], conv_w[l], conv_b[l], w_down[l])
        zero_g = jnp.zeros((y_p.shape[0], CONV_W - 1, D_FF), y_p.dtype)
        y_p, new_p = decoder_layer(y_p, pos_p, None, zero_g, *lw)
        past = tuple(gather_pages(pl, l, page_table) for pl in pools)
        y_s, new_s = decoder_layer(y_s, pos_s, past, state_ffn_conv[l], *lw)
        rows_p.append(new_p)
        rows_s.append(new_s)
    fox_k_p, fox_v_p, fox_logf_p, dsa_k_p, dsa_v_p, idx_k_p, conv_p = [jnp.stack(t) for t in zip(*rows_p)]
    fox_k_s, fox_v_s, fox_logf_s, dsa_k_s, dsa_v_s, idx_k_s, conv_s = [jnp.stack(t) for t in zip(*rows_s)]
    return (y_p, y_s, fox_k_p, fox_v_p, fox_logf_p, dsa_k_p, dsa_v_p, idx_k_p, conv_p,
            fox_k_s, fox_v_s, fox_logf_s, dsa_k_s, dsa_v_s, idx_k_s, conv_s)
```

```python
from contextlib import ExitStack
import numpy as np
import concourse.bass as bass
import concourse.mybir as mybir
from concourse.bass_utils import run_bass_kernel_spmd

F32 = mybir.dt.float32
BF16 = mybir.dt.bfloat16
I32 = mybir.dt.int32
U32 = mybir.dt.uint32
AF = mybir.ActivationFunctionType
ALU = mybir.AluOpType
AX = mybir.AxisListType

D = 2048
HD = 128
H_A = 8
H_B = 8
KV_B = 2
H_IDX = 16
D_IDX = 64
D_FF = 5504
NFF = D_FF // 128
SEQ = 4096
PAST = 8192
NPG = 64
EPS = 1e-6
SCALE = HD ** -0.5
IDX_SCALE = (H_IDX * D_IDX) ** -0.5
NEG = -1.0e30
N_KV = 2632
N_Q = 3088
NT_Q = 9
ENGS = ("pe", "dve", "act", "pool", "sp")


class Res:
    __slots__ = ("name", "w", "r", "k_in", "k_out")

    def __init__(self, name):
        self.name = name
        self.w = None
        self.r = []
        self.k_in = None
        self.k_out = None


class Prog:
    def __init__(self, nc):
        self.nc = nc
        self.stack = ExitStack()
        self.q = {e: [] for e in ENGS}
        self.cnt = {}
        self.waited = {e: {} for e in ENGS}
        self.nres = 0
        self.final_waits = {}
        self.n_ops = 0
        self.phys_of = {}
        self.phys_cnt = []
        self.free_phys = []
        self.active_dma = []

    def sbuf(self, name, shape, dtype):
        return self.stack.enter_context(self.nc.sbuf_tensor("sb_" + name, list(shape), dtype))

    def psum(self, name, shape, dtype):
        return self.stack.enter_context(self.nc.psum_tensor("ps_" + name, list(shape), dtype))

    def res(self, name=None):
        self.nres += 1
        return Res(name or f"r{self.nres}")

    def _need(self, eng, reads, writes):
        ev = {}

        def add(e):
            if e is None:
                return
            k, v = e
            if ev.get(k, 0) < v:
                ev[k] = v
        for r in reads:
            add(r.w)
        for r in writes:
            add(r.w)
            for e in r.r:
                add(e)
        out = []
        wd = self.waited[eng]
        for k, v in ev.items():
            if eng == "pe" and k == "E:pe":
                continue
            if wd.get(k, 0) >= v:
                continue
            wd[k] = v
            out.append((k, v))
        return out

    def _commit(self, ev, reads, writes):
        for r in reads:
            r.r.append(ev)
            if len(r.r) > 64:
                best = {}
                for k, v in r.r:
                    if best.get(k, 0) < v:
                        best[k] = v
                r.r = list(best.items())
        for r in writes:
            r.w = ev
            r.r = []

    def op(self, eng, fn, reads=(), writes=()):
        waits = self._need(eng, reads, writes)
        k = "E:" + eng
        self.cnt[k] = self.cnt.get(k, 0) + 1
        ev = (k, self.cnt[k])
        self.q[eng].append((waits, fn, k, 1))
        self._commit(ev, reads, writes)
        self.n_ops += 1
        return ev

    def dma(self, queue, fn, reads=(), writes=(), owner=None, kind="in", final=False):
        waits = self._need(queue, reads, writes)
        if kind == "in":
            if owner.k_in is None:
                owner.k_in = self._new_dma_key(owner, "in")
            k = owner.k_in
        else:
            if owner.k_out is None:
                owner.k_out = self._new_dma_key(owner, "out")
            k = owner.k_out
        self.cnt[k] = self.cnt[k] + 16
        ev = (k, self.cnt[k])
        self.q[queue].append((waits, fn, k, 16))
        self._commit(ev, reads, writes)
        if final:
            self.final_waits[k] = self.cnt[k]
        self.n_ops += 1
        return ev

    def _phys(self, k):
        if k not in self.phys_of:
            self.phys_of[k] = len(self.phys_cnt)
            self.phys_cnt.append(0)
        return self.phys_of[k]

    def _new_dma_key(self, owner, kind):
        self.nres += 1
        k = f"D:{kind}:{owner.name}:{self.nres}"
        if self.free_phys:
            p = self.free_phys.pop()
        else:
            p = len(self.phys_cnt)
            self.phys_cnt.append(0)
        self.phys_of[k] = p
        self.cnt[k] = self.phys_cnt[p]
        self.active_dma.append((owner, kind, k))
        return k

    def retire_dma_keys(self):
        for owner, kind, k in self.active_dma:
            p = self.phys_of[k]
            self.phys_cnt[p] = self.cnt[k]
            self.free_phys.append(p)
            if kind == "in":
                owner.k_in = None
            else:
                owner.k_out = None
        self.active_dma = []
        self.final_waits = {}

    def emit(self):
        nc = self.nc
        for k in self.cnt:
            self._phys(k)
        psems = [self.stack.enter_context(nc.semaphore(f"s{i}")) for i in range(len(self.phys_cnt))]
        sems = {k: psems[self.phys_of[k]] for k in self.cnt}
        block = self.stack.enter_context(nc.Block())
        q = self.q
        final_waits = self.final_waits

        def run(name, e):
            for waits, fn, k, amt in q[name]:
                for (wk, wv) in waits:
                    e.wait_ge(sems[wk], wv)
                if fn is None:
                    continue
                fn(e).then_inc(sems[k], amt)
            if name == "sp":
                for k, v in final_waits.items():
                    e.wait_ge(sems[k], v)

        @block.tensor
        def _(e):
            run("pe", e)

        @block.vector
        def _(e):
            run("dve", e)

        @block.scalar
        def _(e):
            run("act", e)

        @block.gpsimd
        def _(e):
            run("pool", e)

        @block.sync
        def _(e):
            run("sp", e)

    def close(self):
        self.stack.close()


class Ring:
    def __init__(self, P, name, shape, dtype, n, psum=False):
        self.t = []
        self.r = []
        for i in range(n):
            t = P.psum(f"{name}{i}", shape, dtype) if psum else P.sbuf(f"{name}{i}", shape, dtype)
            self.t.append(t)
            self.r.append(P.res(f"{name}{i}"))
        self.i = 0
        self.n = n

    def next(self):
        i = self.i % self.n
        self.i += 1
        return self.t[i], self.r[i]


def bc(ap, shape):
    return ap.to_broadcast(list(shape))


NPOOL_PAGES = 2560
AW = 52992


def KB_OF(i):
    return min(32, 25 + i)


def MASK_KBS(i):
    return [kb for kb in range(KB_OF(i)) if (kb - i) % 8 in (0, 7)]


class Arena:
    def __init__(self, P):
        self.P = P
        self.t = P.sbuf("arena", [128, AW], F32)
        self.top = 0

    def mark(self):
        return self.top

    def release(self, m):
        self.top = m

    def alloc(self, name, free_shape, dtype=F32):
        n = 1
        for d_ in free_shape:
            n *= d_
        w = n if dtype in (F32, I32, U32) else (n + 1) // 2
        w = (w + 7) // 8 * 8
        off = self.top
        self.top += w
        assert self.top <= AW, f"arena overflow at {name}: {self.top}"
        v = self.t[:, off:off + w]
        if dtype != F32:
            v = v.bitcast(dtype)
        v = v[:, 0:n]
        if len(free_shape) == 2:
            v = v.rearrange("p (a b) -> p a b", b=free_shape[1])
        elif len(free_shape) == 3:
            v = v.rearrange("p (a b c) -> p a b c", b=free_shape[1], c=free_shape[2])
        elif len(free_shape) == 4:
            v = v.rearrange("p (a b c d) -> p a b c d", b=free_shape[1], c=free_shape[2], d=free_shape[3])
        return v, self.P.res(name)

    def ring(self, name, free_shape, dtype, n):
        return VRing([self.alloc(f"{name}{i}", free_shape, dtype) for i in range(n)])


class K:
    pass


class VRing:
    def __init__(self, items):
        self.items = items
        self.i = 0

    def next(self):
        it = self.items[self.i % len(self.items)]
        self.i += 1
        return it


def barrier(P):
    waits = []
    for k, v in P.cnt.items():
        if k == "B:bar":
            continue
        if P.waited["sp"].get(k, 0) < v:
            P.waited["sp"][k] = v
            waits.append((k, v))
    P.cnt["B:bar"] = P.cnt.get("B:bar", 0) + 1
    n = P.cnt["B:bar"]
    P.q["sp"].append((waits, lambda e: e.nop(), "B:bar", 1))
    for eng in ENGS:
        if eng == "sp":
            continue
        P.q[eng].append(([("B:bar", n)], None, None, 0))
        P.waited[eng]["B:bar"] = n
    for eng in ENGS:
        for k, v in P.cnt.items():
            if k != "B:bar":
                P.waited[eng][k] = v
    P.retire_dma_keys()


def build_program(dbg=False):
    nc = bass.Bass("TRN2", target_bir_lowering=False)
    P = Prog(nc)

    def din(name, shape, dt=F32):
        return nc.dram_tensor(name, list(shape), dt, kind="ExternalInput").ap()

    def dout(name, shape, dt=F32):
        return nc.dram_tensor(name, list(shape), dt, kind="ExternalOutput").ap()

    def dscr(name, shape, dt):
        return nc.dram_tensor(name, list(shape), dt).ap()

    NTQ = NT_Q + 1
    NTOK = NTQ * 128
    xb = din("xb", [SEQ, D])
    xq = din("xq", [NTOK, D])
    w_kv = din("w_kv", [D, N_KV])
    w_q = din("w_q", [D, N_Q])
    w_out = din("w_out", [D, D])
    w_gate = din("w_gate", [D, D_FF])
    w_up = din("w_up", [D, D_FF])
    w_down = din("w_down", [D_FF, D])
    ident_d = din("ident", [128, 128])
    tri_d = din("tri", [128, 128])
    gA_d = din("gA", [128, D])
    gF_d = din("gF", [128, D])
    g4_d = din("g4", [128, 4, 128])
    bf_d = din("bfr", [128, 8])
    cw_d = din("cw", [128, NFF, 3])
    cb_d = din("cb", [128, NFF])
    tabA = din("tabA", [SEQ, 192])
    tabQ = din("tabQ", [NTOK, 192])
    fmask_d = din("fmask", [NT_Q, 128, 8, 128])
    bbias_d = din("bbias", [128, NT_Q, 32])
    nm_d = din("nm", [NT_Q, 128, SEQ])
    oh_d = din("oh", [128, NT_Q, 32])
    cst_d = din("cst", [8, D_FF])
    NPOOLR = NPOOL_PAGES * 128
    cfk = din("cfk", [NPOOLR, 1024])
    cfv = din("cfv", [NPOOLR, 1024])
    cfl = din("cfl", [NPOOL_PAGES, 1024])
    cdk = din("cdk", [NPOOLR, 256])
    cdv = din("cdv", [NPOOLR, 256])
    cik = din("cik", [NPOOLR, 64])
    pt_d = din("pt", [4, NPG], I32)
    lst_d = din("lst", [128, 128])
    o64_d = din("o64", [128, 128])
    lblk_d = din("lblk", [128, 128])
    esel_d = din("esel", [128, 4, 128])
    smask_d = din("smask", [128, 4, 32])
    nms_d = din("nms", [128, 128])
    o_fk = dout("o_fk", [SEQ, 1024])
    o_fv = dout("o_fv", [SEQ, 1024])
    o_fl = dout("o_fl", [SEQ, 8])
    o_dk = dout("o_dk", [SEQ, 256])
    o_dv = dout("o_dv", [SEQ, 256])
    o_ik = dout("o_ik", [SEQ, 64])
    s_fk = dout("s_fk", [128, 1024])
    s_fv = dout("s_fv", [128, 1024])
    s_fl = dout("s_fl", [128, 8])
    s_dk = dout("s_dk", [128, 256])
    s_dv = dout("s_dv", [128, 256])
    s_ik = dout("s_ik", [128, 64])
    y_o = dout("y_o", [NTOK, D])
    gT_o = dout("gT_o", [NFF, 128, 16])
    if dbg:
        dbg_att = dout("dbg_att", [NTQ, 128, D], BF16)
        dbg_h = dout("dbg_h", [NTOK, D])
    KaT = dscr("KaT", [8, 128, SEQ], BF16)
    VaE = dscr("VaE", [8, 128, 32, 129], BF16)
    KbT = dscr("KbT", [2, 128, SEQ], BF16)
    VbE = dscr("VbE", [2, 128, 32, 129], BF16)
    KiT = dscr("KiT", [64, SEQ], BF16)
    KaTs = dscr("KaTs", [8, 128, 128], BF16)
    VaEs = dscr("VaEs", [8, 128, 1, 129], BF16)
    KbTs = dscr("KbTs", [2, 128, 128], BF16)
    VbEs = dscr("VbEs", [2, 128, 1, 129], BF16)
    KiTs = dscr("KiTs", [64, 128], BF16)
    QaT = dscr("QaT", [NTQ, 128, 8, 128], BF16)
    QbT = dscr("QbT", [NTQ, 128, 8, 128], BF16)
    QiT = dscr("QiT", [NTQ, 64, 16, 128], BF16)
    h_scr = dscr("h_scr", [NTOK, D], F32)
    aT_scr = dscr("aT_scr", [NTQ, 128, NFF, 128], BF16)
    r_KaT, r_VaE, r_KbT, r_VbE, r_KiT = (P.res(n) for n in ("KaT", "VaE", "KbT", "VbE", "KiT"))
    r_s = [P.res(n) for n in ("KaTs", "VaEs", "KbTs", "VbEs", "KiTs")]
    r_QaT, r_QbT, r_QiT = P.res("QaT"), P.res("QbT"), P.res("QiT")
    r_hscr, r_aTscr = P.res("h_scr"), P.res("aT_scr")
    r_out = P.res("outputs")

    A = Arena(P)
    pb = [P.psum(f"bank{i}", [128, 512], F32) for i in range(8)]
    r_pb = [P.res(f"bank{i}") for i in range(8)]

    def bank_bf(i):
        return pb[i][:, :].bitcast(BF16).rearrange("p (a b) -> p a b", b=128)

    class BankRing:
        def __init__(self, ids):
            self.ids = ids
            self.i = 0

        def next(self):
            b = self.ids[self.i % len(self.ids)]
            self.i += 1
            return b

    identf, r_identf = A.alloc("identf", [128], F32)
    identb, r_identb = A.alloc("identb", [128], BF16)
    tri, r_tri = A.alloc("tri", [128], F32)
    ones, r_ones = A.alloc("ones", [128], F32)
    g4, r_g4 = A.alloc("g4", [4, 128], F32)
    bfr, r_bfr = A.alloc("bfr", [8], F32)
    lfall, r_lfall = A.alloc("lfall", [32, 8], F32)
    wi_all, r_wi = A.alloc("wi_all", [NTQ, 16], F32)
    lfs, r_lfs = A.alloc("lfs", [8], F32)
    P.dma("sp", lambda e: e.dma_start(out=identf, in_=ident_d[:, :]), writes=[r_identf], owner=r_identf)
    P.dma("sp", lambda e: e.dma_start(out=tri, in_=tri_d[:, :]), writes=[r_tri], owner=r_tri)
    P.dma("sp", lambda e: e.dma_start(out=g4, in_=g4_d[:, :, :]), writes=[r_g4], owner=r_g4)
    P.dma("sp", lambda e: e.dma_start(out=bfr, in_=bf_d[:, :]), writes=[r_bfr], owner=r_bfr)
    P.op("dve", lambda e: e.tensor_copy(out=identb, in_=identf), reads=[r_identf], writes=[r_identb])
    P.op("pool", lambda e: e.memset(ones, 1.0), writes=[r_ones])
    base_mark = A.mark()

    def make_proj(G, gvec_d, slim=False):
        m = K()
        m.G = G
        m.gvec, m.r_gvec = A.alloc("gvec", [D], F32)
        P.dma("sp", lambda e: e.dma_start(out=m.gvec, in_=gvec_d[:, :]), writes=[m.r_gvec], owner=m.r_gvec)
        m.junk, m.r_junk = A.alloc("junk", [D], BF16)
        m.junk2, m.r_junk2 = A.alloc("junk2", [128], BF16)
        m.xT, _ = A.alloc("xT", [G, 16, 128], BF16)
        m.r_xT = [P.res(f"xT{i}") for i in range(G)]
        m.ss, _ = A.alloc("ss_t", [G], F32)
        m.rstd, _ = A.alloc("rstd_t", [G], F32)
        m.r_ss = [P.res(f"ss{i}") for i in range(G)]
        m.r_rstd = [P.res(f"rstd{i}") for i in range(G)]
        m.pT = BankRing([0, 1])
        m.pM = BankRing([2, 3])
        if slim:
            return m
        m.xf = A.ring("xf", [D], F32, 2)
        m.xbf = A.ring("xbf", [D], BF16, 2)
        m.tab, _ = A.alloc("tab", [G, 192], F32)
        m.r_tab = [P.res(f"tab{i}") for i in range(G)]
        m.w = A.ring("wch", [16, 512], BF16, 2)
        m.kf = A.ring("kf", [4, 128], F32, 2)
        m.kn = A.ring("kn", [4, 128], F32, 2)
        m.ko = A.ring("ko", [4, 128], F32, 2)
        m.kbb = A.ring("kbb", [4, 128], BF16, 2)
        m.st = A.ring("ktst", [8, 128], BF16, 2)
        m.ve = A.ring("ve", [4, 129], BF16, 2)
        m.sm = A.ring("sm", [16], F32, 4)
        m.ra = A.ring("ra", [4, 64], F32, 2)
        m.rb = A.ring("rb", [4, 64], F32, 2)
        m.lf = A.ring("lf", [8], F32, 2)
        m.lg = A.ring("lg", [8], F32, 4)
        m.pT = BankRing([0, 1])
        m.pM = BankRing([2, 3])
        for t_, r_ in m.ve.items:
            P.op("pool", lambda e, t_=t_: e.memset(t_[:, :, 128:129], 1.0), writes=[r_])
        return m

    def load_x_tile(m, src_rows_ap, slot, tab_rows_ap):
        xf, r_xf = m.xf.next()
        xbf, r_xbf = m.xbf.next()
        P.dma("sp", lambda e: e.dma_start(out=xf, in_=src_rows_ap), writes=[r_xf], owner=r_xf)
        if tab_rows_ap is not None:
            P.dma("sp", lambda e: e.dma_start(out=m.tab[:, slot, :], in_=tab_rows_ap), writes=[m.r_tab[slot]],
                  owner=m.r_tab[slot])
        norm_to_xT(m, xf, r_xf, xbf, r_xbf, slot)

    def norm_to_xT(m, xf, r_xf, xbf, r_xbf, slot):
        P.op("act", lambda e: e.activation(out=m.junk, in_=xf, func=AF.Square, accum_out=m.ss[:, slot:slot + 1]),
             reads=[r_xf], writes=[m.r_junk, m.r_ss[slot]])
        P.op("act", lambda e: e.activation(out=m.ss[:, slot:slot + 1], in_=m.ss[:, slot:slot + 1], func=AF.Ln,
                                           scale=1.0 / D, bias=EPS), reads=[m.r_ss[slot]], writes=[m.r_ss[slot]])
        P.op("act", lambda e: e.activation(out=m.rstd[:, slot:slot + 1], in_=m.ss[:, slot:slot + 1], func=AF.Exp,
                                           scale=-0.5), reads=[m.r_ss[slot]], writes=[m.r_rstd[slot]])
        P.op("dve", lambda e: e.tensor_tensor(out=xbf, in0=xf, in1=m.gvec, op=ALU.mult),
             reads=[r_xf, m.r_gvec], writes=[r_xbf])
        for g in range(2):
            b = m.pT.next()
            pt = bank_bf(b)
            for jj in range(8):
                kc = g * 8 + jj
                P.op("pe", lambda e, kc=kc, jj=jj, pt=pt: e.transpose(out=pt[:, jj, :], in_=xbf[:, kc * 128:(kc + 1) * 128],
                                                                      identity=identb),
                     reads=[r_xbf, r_identb], writes=[r_pb[b]])
            if g == 0:
                P.op("dve", lambda e, g=g, pt=pt: e.tensor_copy(out=m.xT[:, slot, g * 8:(g + 1) * 8, :], in_=pt),
                     reads=[r_pb[b]], writes=[m.r_xT[slot]])
            else:
                P.op("act", lambda e, g=g, pt=pt: e.activation(out=m.xT[:, slot, g * 8:(g + 1) * 8, :], in_=pt, func=AF.Copy),
                     reads=[r_pb[b]], writes=[m.r_xT[slot]])

    def load_w_chunk(m, wd, col0, ncols, nk=16):
        wt, r_w = m.w.next()
        wv = wd.rearrange("(kc p) n -> p kc n", p=128)
        for q4 in range(0, nk, 4):
            P.dma("pool", lambda e, q4=q4: e.dma_start(out=wt[:, q4:q4 + 4, 0:ncols],
                                                        in_=wv[:, q4:q4 + 4, col0:col0 + ncols]),
                  writes=[r_w], owner=r_w)
        return wt, r_w

    def project(m, slot, wt, r_w, ncols):
        b = m.pM.next()
        pm = pb[b]
        for kc in range(16):
            P.op("pe", lambda e, kc=kc: e.matmul(pm[:, 0:ncols], lhsT=m.xT[:, slot, kc, :], rhs=wt[:, kc, 0:ncols],
                                                 start=(kc == 0), stop=(kc == 15)),
                 reads=[m.r_xT[slot], r_w], writes=[r_pb[b]])
        return pm, r_pb[b]

    def rstd_of(ssq_ap, r_in, inv_n):
        P.op("act", lambda e: e.activation(out=ssq_ap, in_=ssq_ap, func=AF.Ln, scale=inv_n, bias=EPS),
             reads=[r_in], writes=[r_in])
        P.op("act", lambda e: e.activation(out=ssq_ap, in_=ssq_ap, func=AF.Exp, scale=-0.5),
             reads=[r_in], writes=[r_in])

    def head_norm(m, pm_ap, r_pm, slot, nh, g_idx):
        kf, r_kf = m.kf.next()
        kn, r_kn = m.kn.next()
        sm, r_sm = m.sm.next()
        P.op("act", lambda e: e.activation(out=kf[:, 0:nh, :], in_=pm_ap, func=AF.Copy, scale=m.rstd[:, slot:slot + 1]),
             reads=[r_pm, m.r_rstd[slot]], writes=[r_kf])
        for h in range(nh):
            P.op("act", lambda e, h=h: e.activation(out=m.junk2, in_=kf[:, h, :], func=AF.Square,
                                                    accum_out=sm[:, h:h + 1]),
                 reads=[r_kf], writes=[m.r_junk2, r_sm])
        rstd_of(sm[:, 0:nh], r_sm, 1.0 / HD)
        P.op("dve", lambda e: e.tensor_tensor(out=kn[:, 0:nh, :], in0=kf[:, 0:nh, :],
                                              in1=bc(sm[:, 0:nh].unsqueeze(2), [128, nh, 128]), op=ALU.mult),
             reads=[r_kf, r_sm], writes=[r_kn])
        P.op("pool", lambda e: e.tensor_tensor(out=kn[:, 0:nh, :], in0=kn[:, 0:nh, :],
                                               in1=bc(g4[:, g_idx, :].unsqueeze(1), [128, nh, 128]), op=ALU.mult),
             reads=[r_kn, r_g4], writes=[r_kn])
        return kn, r_kn

    def rope(m, src, r_src, nh, hd, slot, tab_off, dst, r_dst):
        half = hd // 2
        ra, r_ra = m.ra.next()
        rb, r_rb = m.rb.next()
        ra = ra.rearrange("p a b -> p (a b)")[:, 0:nh * half].rearrange("p (a b) -> p a b", b=half)
        rb = rb.rearrange("p a b -> p (a b)")[:, 0:nh * half].rearrange("p (a b) -> p a b", b=half)
        cos = bc(m.tab[:, slot, tab_off:tab_off + half].unsqueeze(1), [128, nh, half])
        sin = bc(m.tab[:, slot, tab_off + half:tab_off + 2 * half].unsqueeze(1), [128, nh, half])
        x1 = src[:, 0:nh, 0:half]
        x2 = src[:, 0:nh, half:hd]
        rt = m.r_tab[slot]
        P.op("dve", lambda e: e.tensor_tensor(out=ra, in0=x1, in1=cos, op=ALU.mult), reads=[r_src, rt], writes=[r_ra])
        P.op("pool", lambda e: e.tensor_tensor(out=rb, in0=x2, in1=sin, op=ALU.mult), reads=[r_src, rt], writes=[r_rb])
        P.op("dve", lambda e: e.tensor_tensor(out=dst[:, 0:nh, 0:half], in0=ra, in1=rb, op=ALU.subtract),
             reads=[r_ra, r_rb], writes=[r_dst])
        P.op("dve", lambda e: e.tensor_tensor(out=ra, in0=x2, in1=cos, op=ALU.mult), reads=[r_src, rt, r_dst], writes=[r_ra])
        P.op("pool", lambda e: e.tensor_tensor(out=rb, in0=x1, in1=sin, op=ALU.mult), reads=[r_src, rt, r_dst], writes=[r_rb])
        P.op("dve", lambda e: e.tensor_tensor(out=dst[:, 0:nh, half:hd], in0=ra, in1=rb, op=ALU.add),
             reads=[r_ra, r_rb], writes=[r_dst])

    def transposes_bf(m, src_fn, r_srcb, nh, rows_d, dst, r_dst):
        b = m.pT.next()
        pt = bank_bf(b)
        for h in range(nh):
            P.op("pe", lambda e, h=h: e.transpose(out=pt[0:rows_d, h, :], in_=src_fn(h), identity=identb),
                 reads=[r_srcb, r_identb], writes=[r_pb[b]])
        P.op("act", lambda e: e.activation(out=dst, in_=pt[0:rows_d, 0:nh, :], func=AF.Copy), reads=[r_pb[b]], writes=[r_dst])

    def flat(t):
        return t.rearrange("p h d -> p (h d)")

    def kv_pass(m, n_tiles, x_rows, tab_rows, dst):
        G = m.G
        for g0 in range(0, n_tiles, G):
            gt = min(G, n_tiles - g0)
            for s in range(gt):
                load_x_tile(m, x_rows(g0 + s), s, tab_rows(g0 + s))
            for c in range(6):
                col0 = c * 512
                ncols = 512 if c < 5 else 72
                wt, r_w = load_w_chunk(m, w_kv, col0, ncols)
                for s in range(gt):
                    t = g0 + s
                    pm, r_pm = project(m, s, wt, r_w, ncols)
                    if c in (0, 1):
                        kn, r_kn = head_norm(m, pm[:, 0:512], r_pm, s, 4, 1)
                        P.dma("sp", lambda e, kn=kn, t=t, c=c: e.dma_start(out=dst["fk"](t)[:, c * 512:(c + 1) * 512], in_=flat(kn)),
                              reads=[r_kn], writes=[r_out], owner=r_kn, kind="out", final=True)
                        kbb, r_kbb = m.kbb.next()
                        P.op("pool", lambda e, kn=kn, kbb=kbb: e.tensor_copy(out=kbb, in_=kn), reads=[r_kn], writes=[r_kbb])
                        st, r_st = m.st.next()
                        transposes_bf(m, lambda h, kbb=kbb: kbb[:, h, :], r_kbb, 4, 128, st[:, 0:4, :], r_st)
                        P.dma("sp", lambda e, st=st, t=t, c=c: e.dma_start(out=dst["KaT"](t, c), in_=st[:, 0:4, :]),
                              reads=[r_st], writes=[dst["r_KaT"]], owner=r_st, kind="out")
                    elif c in (2, 3):
                        kf, r_kf = m.kf.next()
                        P.op("act", lambda e, kf=kf, pm=pm, s=s: e.activation(out=flat(kf), in_=pm[:, 0:512],
                                                                              func=AF.Copy, scale=m.rstd[:, s:s + 1]),
                             reads=[r_pm, m.r_rstd[s]], writes=[r_kf])
                        P.dma("sp", lambda e, kf=kf, t=t, c=c: e.dma_start(out=dst["fv"](t)[:, (c - 2) * 512:(c - 1) * 512], in_=flat(kf)),
                              reads=[r_kf], writes=[r_out], owner=r_kf, kind="out", final=True)
                        ve, r_ve = m.ve.next()
                        P.op("dve", lambda e, kf=kf, ve=ve: e.tensor_copy(out=ve[:, :, 0:128], in_=kf), reads=[r_kf], writes=[r_ve])
                        P.dma("sp", lambda e, ve=ve, t=t, c=c: e.dma_start(out=dst["VaE"](t, c - 2), in_=ve),
                              reads=[r_ve], writes=[dst["r_VaE"]], owner=r_ve, kind="out")
                    elif c == 4:
                        kn, r_kn = head_norm(m, pm[:, 0:256], r_pm, s, 2, 3)
                        ko, r_ko = m.ko.next()
                        rope(m, kn, r_kn, 2, 128, s, 0, ko, r_ko)
                        P.dma("sp", lambda e, ko=ko, t=t: e.dma_start(out=dst["dk"](t), in_=flat(ko[:, 0:2, :])),
                              reads=[r_ko], writes=[r_out], owner=r_ko, kind="out", final=True)
                        kbb, r_kbb = m.kbb.next()
                        P.op("pool", lambda e, ko=ko, kbb=kbb: e.tensor_copy(out=kbb[:, 0:2, :], in_=ko[:, 0:2, :]), reads=[r_ko], writes=[r_kbb])
                        st, r_st = m.st.next()
                        transposes_bf(m, lambda h, kbb=kbb: kbb[:, h, :], r_kbb, 2, 128, st[:, 0:2, :], r_st)
                        P.dma("sp", lambda e, st=st, t=t: e.dma_start(out=dst["KbT"](t), in_=st[:, 0:2, :]),
                              reads=[r_st], writes=[dst["r_KbT"]], owner=r_st, kind="out")
                        kf, r_kf = m.kf.next()
                        P.op("act", lambda e, kf=kf, pm=pm, s=s: e.activation(out=flat(kf[:, 0:2, :]), in_=pm[:, 256:512],
                                                                              func=AF.Copy, scale=m.rstd[:, s:s + 1]),
                             reads=[r_pm, m.r_rstd[s]], writes=[r_kf])
                        P.dma("sp", lambda e, kf=kf, t=t: e.dma_start(out=dst["dv"](t), in_=flat(kf[:, 0:2, :])),
                              reads=[r_kf], writes=[r_out], owner=r_kf, kind="out", final=True)
                        ve, r_ve = m.ve.next()
                        P.op("dve", lambda e, kf=kf, ve=ve: e.tensor_copy(out=ve[:, 0:2, 0:128], in_=kf[:, 0:2, :]), reads=[r_kf], writes=[r_ve])
                        P.dma("sp", lambda e, ve=ve, t=t: e.dma_start(out=dst["VbE"](t), in_=ve[:, 0:2, :]),
                              reads=[r_ve], writes=[dst["r_VbE"]], owner=r_ve, kind="out")
                    else:
                        kf, r_kf = m.kf.next()
                        P.op("act", lambda e, kf=kf, pm=pm, s=s: e.activation(out=kf[:, 0, 0:72], in_=pm[:, 0:72],
                                                                              func=AF.Copy, scale=m.rstd[:, s:s + 1]),
                             reads=[r_pm, m.r_rstd[s]], writes=[r_kf])
                        ko, r_ko = m.ko.next()
                        rope(m, kf, r_kf, 1, 64, s, 128, ko, r_ko)
                        P.dma("sp", lambda e, ko=ko, t=t: e.dma_start(out=dst["ik"](t), in_=ko[:, 0, 0:64]),
                              reads=[r_ko], writes=[r_out], owner=r_ko, kind="out", final=True)
                        kbb, r_kbb = m.kbb.next()
                        P.op("pool", lambda e, ko=ko, kbb=kbb: e.tensor_copy(out=kbb[:, 0, 0:64], in_=ko[:, 0, 0:64]), reads=[r_ko], writes=[r_kbb])
                        st, r_st = m.st.next()
                        transposes_bf(m, lambda h, kbb=kbb: kbb[:, 0, 0:64], r_kbb, 1, 64, st[0:64, 0:1, :], r_st)
                        P.dma("sp", lambda e, st=st, t=t: e.dma_start(out=dst["KiT"](t), in_=st[0:64, 0, :]),
                              reads=[r_st], writes=[dst["r_KiT"]], owner=r_st, kind="out")
                        lf, r_lf = m.lf.next()
                        l1, r_l1 = m.lg.next()
                        l2, r_l2 = m.lg.next()
                        P.op("dve", lambda e, kf=kf, lf=lf: e.tensor_tensor(out=lf, in0=kf[:, 0, 64:72], in1=bfr, op=ALU.add),
                             reads=[r_kf, r_bfr], writes=[r_lf])
                        P.op("dve", lambda e, lf=lf, l1=l1: e.scalar_tensor_tensor(out=l1, in0=lf, scalar=-1.0, in1=lf,
                                                                                   op0=ALU.mult, op1=ALU.max),
                             reads=[r_lf], writes=[r_l1])
                        P.op("act", lambda e, l1=l1: e.activation(out=l1, in_=l1, func=AF.Exp, scale=-1.0), reads=[r_l1], writes=[r_l1])
                        P.op("act", lambda e, l1=l1: e.activation(out=l1, in_=l1, func=AF.Ln, scale=1.0, bias=1.0), reads=[r_l1], writes=[r_l1])
                        P.op("dve", lambda e, lf=lf, l2=l2: e.tensor_single_scalar(out=l2, in_=lf, scalar=0.0, op=ALU.min),
                             reads=[r_lf], writes=[r_l2])
                        P.op("dve", lambda e, l1=l1, l2=l2, lf=lf: e.tensor_tensor(out=lf, in0=l2, in1=l1, op=ALU.subtract),
                             reads=[r_l1, r_l2], writes=[r_lf])
                        P.dma("sp", lambda e, lf=lf, t=t: e.dma_start(out=dst["fl"](t), in_=lf),
                              reads=[r_lf], writes=[r_out], owner=r_lf, kind="out", final=True)
                        if dst.get("logf_keep") is not None:
                            dst["logf_keep"](t, lf, r_lf)

    def rows(ap, t):
        return ap[t * 128:(t + 1) * 128, :]

    mA = make_proj(8, gA_d)

    def keep_logf(t, lf, r_lf):
        P.op("pool", lambda e: e.tensor_copy(out=lfall[:, t, :], in_=lf), reads=[r_lf], writes=[r_lfall])

    dstP = dict(
        fk=lambda t: rows(o_fk, t), fv=lambda t: rows(o_fv, t), fl=lambda t: rows(o_fl, t),
        dk=lambda t: rows(o_dk, t), dv=lambda t: rows(o_dv, t), ik=lambda t: rows(o_ik, t),
        KaT=lambda t, c: KaT[4 * c:4 * c + 4, :, t * 128:(t + 1) * 128].rearrange("h d t -> d h t"),
        VaE=lambda t, c: VaE[4 * c:4 * c + 4, :, t, :].rearrange("h p e -> p h e"),
        KbT=lambda t: KbT[:, :, t * 128:(t + 1) * 128].rearrange("h d t -> d h t"),
        VbE=lambda t: VbE[:, :, t, :].rearrange("h p e -> p h e"),
        KiT=lambda t: KiT[:, t * 128:(t + 1) * 128],
        r_KaT=r_KaT, r_VaE=r_VaE, r_KbT=r_KbT, r_VbE=r_VbE, r_KiT=r_KiT, logf_keep=keep_logf,
    )
    kv_pass(mA, 32, lambda t: rows(xb, t), lambda t: rows(tabA, t), dstP)
    dstS = dict(
        fk=lambda t: s_fk[:, :], fv=lambda t: s_fv[:, :], fl=lambda t: s_fl[:, :],
        dk=lambda t: s_dk[:, :], dv=lambda t: s_dv[:, :], ik=lambda t: s_ik[:, :],
        KaT=lambda t, c: KaTs[4 * c:4 * c + 4, :, :].rearrange("h d t -> d h t"),
        VaE=lambda t, c: VaEs[4 * c:4 * c + 4, :, 0, :].rearrange("h p e -> p h e"),
        KbT=lambda t: KbTs[:, :, :].rearrange("h d t -> d h t"),
        VbE=lambda t: VbEs[:, :, 0, :].rearrange("h p e -> p h e"),
        KiT=lambda t: KiTs[:, :],
        r_KaT=r_s[0], r_VaE=r_s[1], r_KbT=r_s[2], r_VbE=r_s[3], r_KiT=r_s[4],
        logf_keep=lambda t, lf, r_lf: P.op("pool", lambda e: e.tensor_copy(out=lfs, in_=lf), reads=[r_lf], writes=[r_lfs]),
    )
    kv_pass(mA, 1, lambda t: rows(xq, 9), lambda t: rows(tabQ, 9), dstS)
    barrier(P)
    A.release(base_mark)

    mB = make_proj(NTQ, gA_d)
    for s in range(NTQ):
        load_x_tile(mB, rows(xq, s), s, rows(tabQ, s))
    for c in range(7):
        col0 = c * 512
        ncols = 512 if c < 6 else 16
        wt, r_w = load_w_chunk(mB, w_q, col0, ncols)
        for s in range(NTQ):
            pm, r_pm = project(mB, s, wt, r_w, ncols)
            if c in (0, 1, 2, 3):
                kn, r_kn = head_norm(mB, pm[:, 0:512], r_pm, s, 4, 0 if c < 2 else 2)
                if c >= 2:
                    ko, r_ko = mB.ko.next()
                    rope(mB, kn, r_kn, 4, 128, s, 0, ko, r_ko)
                    kn, r_kn = ko, r_ko
                kbb, r_kbb = mB.kbb.next()
                P.op("pool", lambda e, kn=kn, kbb=kbb: e.tensor_copy(out=kbb, in_=kn), reads=[r_kn], writes=[r_kbb])
                st, r_st = mB.st.next()
                transposes_bf(mB, lambda h, kbb=kbb: kbb[:, h, :], r_kbb, 4, 128, st[:, 0:4, :], r_st)
                dq, r_dq = (QaT, r_QaT) if c < 2 else (QbT, r_QbT)
                hc = (c % 2) * 4
                P.dma("sp", lambda e, st=st, s=s, dq=dq, hc=hc: e.dma_start(out=dq[s, :, hc:hc + 4, :], in_=st[:, 0:4, :]),
                      reads=[r_st], writes=[r_dq], owner=r_st, kind="out")
            elif c in (4, 5):
                kf, r_kf = mB.kf.next()
                P.op("act", lambda e, kf=kf, pm=pm, s=s: e.activation(out=flat(kf), in_=pm[:, 0:512],
                                                                      func=AF.Copy, scale=mB.rstd[:, s:s + 1]),
                     reads=[r_pm, mB.r_rstd[s]], writes=[r_kf])
                ko, r_ko = mB.ko.next()
                kf8 = flat(kf).rearrange("p (h d) -> p h d", d=64)
                ko8 = flat(ko).rearrange("p (h d) -> p h d", d=64)
                rope(mB, kf8, r_kf, 8, 64, s, 128, ko8, r_ko)
                kbb, r_kbb = mB.kbb.next()
                kbb8 = flat(kbb).rearrange("p (h d) -> p h d", d=64)
                P.op("pool", lambda e, ko8=ko8, kbb8=kbb8: e.tensor_copy(out=kbb8, in_=ko8), reads=[r_ko], writes=[r_kbb])
                st, r_st = mB.st.next()
                transposes_bf(mB, lambda h, kbb8=kbb8: kbb8[:, h, :], r_kbb, 8, 64, st[0:64, :, :], r_st)
                hc = (c - 4) * 8
                P.dma("sp", lambda e, st=st, s=s, hc=hc: e.dma_start(out=QiT[s, :, hc:hc + 8, :], in_=st[0:64, :, :]),
                      reads=[r_st], writes=[r_QiT], owner=r_st, kind="out")
            else:
                P.op("dve", lambda e, pm=pm, s=s: e.tensor_scalar(out=wi_all[:, s, :], in0=pm[:, 0:16], scalar1=mB.rstd[:, s:s + 1],
                                                                 scalar2=IDX_SCALE, op0=ALU.mult, op1=ALU.mult),
                     reads=[r_pm, mB.r_rstd[s]], writes=[r_wi])
    barrier(P)
    A.release(base_mark)

    att, _ = A.alloc("att", [NTQ, D], BF16)
    r_att = [P.res(f"att{i}") for i in range(NTQ)]
    att_mark = A.mark()

    pinc, r_pinc = A.alloc("pinc", [32, 8], F32)
    tot, r_tot = A.alloc("tot", [32, 8], F32)
    cP, r_cP = A.alloc("cP", [32, 8], F32)
    cref, r_cref = A.alloc("cref", [NT_Q, 8], F32)
    biasF, r_biasF = A.alloc("biasF", [NT_Q, 8, 32], F32)
    bbias, r_bbias = A.alloc("bbias", [NT_Q, 32], F32)
    oh, r_oh = A.alloc("oh", [NT_Q, 32], F32)
    tmp4, r_tmp4 = A.alloc("tmp4", [NT_Q, 8, 32], F32)
    fmask, r_fmask = A.alloc("fmask", [NT_Q, 8, 128], BF16)
    P.dma("sp", lambda e: e.dma_start(out=bbias, in_=bbias_d[:, :, :]), writes=[r_bbias], owner=r_bbias)
    P.dma("sp", lambda e: e.dma_start(out=oh, in_=oh_d[:, :, :]), writes=[r_oh], owner=r_oh)
    for i in range(NT_Q):
        P.dma("pool", lambda e, i=i: e.dma_start(out=fmask[:, i, :, :], in_=fmask_d[i]), writes=[r_fmask], owner=r_fmask)
    lf2 = lfall.rearrange("p b h -> p (b h)")
    P.op("pe", lambda e: e.matmul(pb[6][:, 0:256], lhsT=tri, rhs=lf2, start=True, stop=True),
         reads=[r_tri, r_lfall], writes=[r_pb[6]])
    P.op("pe", lambda e: e.matmul(pb[7][:, 0:256], lhsT=ones, rhs=lf2, start=True, stop=True),
         reads=[r_ones, r_lfall], writes=[r_pb[7]])
    P.op("act", lambda e: e.activation(out=tot.rearrange("p b h -> p (b h)"), in_=pb[7][:, 0:256], func=AF.Copy),
         reads=[r_pb[7]], writes=[r_tot])
    for h in range(8):
        P.op("dve", lambda e, h=h: e.tensor_tensor_scan(out=pinc[:, :, h], data0=ones[:, 0:32], data1=tot[:, :, h],
                                                        initial=0.0, op0=ALU.mult, op1=ALU.add),
             reads=[r_tot, r_ones], writes=[r_pinc])
    P.op("dve", lambda e: e.tensor_tensor(out=cP.rearrange("p b h -> p (b h)"), in0=pb[6][:, 0:256],
                                          in1=pinc.rearrange("p b h -> p (b h)"), op=ALU.add),
         reads=[r_pb[6], r_pinc], writes=[r_cP])
    P.op("dve", lambda e: e.tensor_tensor(out=cP, in0=cP, in1=tot, op=ALU.subtract), reads=[r_cP, r_tot], writes=[r_cP])
    pinc_hb = pinc.rearrange("p b h -> p h b")
    cP_hb = cP.rearrange("p b h -> p h b")
    P.op("dve", lambda e: e.tensor_tensor(out=tmp4, in0=bc(pinc_hb.unsqueeze(1), [128, NT_Q, 8, 32]),
                                          in1=bc(oh.unsqueeze(2), [128, NT_Q, 8, 32]), op=ALU.mult),
         reads=[r_pinc, r_oh], writes=[r_tmp4])
    P.op("dve", lambda e: e.tensor_reduce(out=cref.rearrange("p i h -> p (i h)"), in_=tmp4.rearrange("p i h b -> p (i h) b"),
                                          axis=AX.X, op=ALU.add),
         reads=[r_tmp4], writes=[r_cref])
    P.op("dve", lambda e: e.tensor_tensor(out=biasF, in0=bc(cref.unsqueeze(3), [128, NT_Q, 8, 32]),
                                          in1=bc(cP_hb.unsqueeze(1), [128, NT_Q, 8, 32]), op=ALU.subtract),
         reads=[r_cref, r_cP], writes=[r_biasF])
    P.op("dve", lambda e: e.tensor_tensor(out=biasF, in0=biasF, in1=bc(bbias.unsqueeze(2), [128, NT_Q, 8, 32]), op=ALU.add),
         reads=[r_biasF, r_bbias], writes=[r_biasF])

    kt_ring = A.ring("kt", [SEQ], BF16, 2)
    vt_ring = A.ring("vt", [32, 129], BF16, 2)
    qh_ring = A.ring("qh", [NT_Q, 128], BF16, 2)
    pt_ring = A.ring("ptile", [4, 128], BF16, 3)
    rd_ring = A.ring("rd", [2], F32, 4)
    pS = BankRing([2, 3])
    pO = BankRing([4, 5])
    pend = [None]

    def flush():
        if pend[0] is not None:
            pend[0]()
            pend[0] = None

    for h in range(8):
        kt, r_kt = kt_ring.next()
        vt, r_vt = vt_ring.next()
        qh, r_qh = qh_ring.next()
        P.dma("sp", lambda e, kt=kt, h=h: e.dma_start(out=kt, in_=KaT[h]), reads=[r_KaT], writes=[r_kt], owner=r_kt)
        P.dma("sp", lambda e, vt=vt, h=h: e.dma_start(out=vt, in_=VaE[h]), reads=[r_VaE], writes=[r_vt], owner=r_vt)
        P.dma("sp", lambda e, qh=qh, h=h: e.dma_start(out=qh, in_=QaT[0:NT_Q, :, h, :].rearrange("t d k -> d t k")),
              reads=[r_QaT], writes=[r_qh], owner=r_qh)
        for i in range(NT_Q):
            KBi = KB_OF(i)
            mk = MASK_KBS(i)
            bo = pO.next()
            po = pb[bo][:, 0:129]
            for k0 in range(0, KBi, 4):
                nk = min(4, KBi - k0)
                bs = pS.next()
                ps = pb[bs].rearrange("p (a b) -> p a b", b=128)
                ptile, r_ptile = pt_ring.next()
                for kk in range(nk):
                    kb = k0 + kk
                    P.op("pe", lambda e, ps=ps, kk=kk, kb=kb, kt=kt, qh=qh, i=i: e.matmul(
                        ps[:, kk, :], lhsT=kt[:, kb * 128:(kb + 1) * 128], rhs=qh[:, i, :], start=True, stop=True),
                        reads=[r_kt, r_qh], writes=[r_pb[bs]])
                for kk in range(nk):
                    kb = k0 + kk
                    P.op("act", lambda e, ps=ps, kk=kk, kb=kb, ptile=ptile, i=i, h=h: e.activation(
                        out=ptile[:, kk, :], in_=ps[:, kk, :], func=AF.Exp, scale=SCALE, bias=biasF[:, i, h, kb:kb + 1]),
                        reads=[r_pb[bs], r_biasF], writes=[r_ptile])
                    if kb in mk:
                        sl = mk.index(kb)
                        P.op("pool", lambda e, ptile=ptile, kk=kk, i=i, sl=sl: e.tensor_tensor(
                            out=ptile[:, kk, :], in0=ptile[:, kk, :], in1=fmask[:, i, sl, :], op=ALU.mult),
                            reads=[r_ptile, r_fmask], writes=[r_ptile])
                flush()

                def pv(k0=k0, nk=nk, ptile=ptile, r_ptile=r_ptile, vt=vt, r_vt=r_vt, po=po, bo=bo, KBi=KBi, i=i, h=h):
                    for kk in range(nk):
                        kb = k0 + kk
                        P.op("pe", lambda e, kk=kk, kb=kb: e.matmul(po, lhsT=ptile[:, kk, :], rhs=vt[:, kb, :],
                                                                    start=(kb == 0), stop=(kb == KBi - 1)),
                             reads=[r_ptile, r_vt], writes=[r_pb[bo]])
                    if k0 + nk == KBi:
                        rd, r_rd = rd_ring.next()
                        P.op("dve", lambda e: e.tensor_scalar(out=rd[:, 0:1], in0=po[:, 128:129], scalar1=1e-30, scalar2=None,
                                                              op0=ALU.max), reads=[r_pb[bo]], writes=[r_rd])
                        P.op("dve", lambda e: e.reciprocal(out=rd[:, 1:2], in_=rd[:, 0:1]), reads=[r_rd], writes=[r_rd])
                        P.op("act", lambda e: e.activation(out=att[:, i, h * 128:(h + 1) * 128], in_=po[:, 0:128], func=AF.Copy,
                                                           scale=rd[:, 1:2]), reads=[r_pb[bo], r_rd], writes=[r_att[i]])
                pend[0] = pv
    flush()
    barrier(P)
    A.release(att_mark)

    kbt, r_kbt = A.alloc("kbt", [2, SEQ], BF16)
    vbt, r_vbt = A.alloc("vbt", [2, 32, 129], BF16)
    kit, r_kit = A.alloc("kit", [SEQ], BF16)
    P.dma("sp", lambda e: e.dma_start(out=kbt, in_=KbT.rearrange("h d t -> d h t")), reads=[r_KbT], writes=[r_kbt], owner=r_kbt)
    P.dma("sp", lambda e: e.dma_start(out=vbt, in_=VbE.rearrange("h p b e -> p h b e")), reads=[r_VbE], writes=[r_vbt], owner=r_vbt)
    P.dma("sp", lambda e: e.dma_start(out=kit[0:64, :], in_=KiT[:, :]), reads=[r_KiT], writes=[r_kit], owner=r_kit)
    qi_ring = A.ring("qi_t", [16, 128], BF16, 2)
    qb_ring = A.ring("qb_t", [8, 128], BF16, 2)
    sc, r_sc = A.alloc("sc", [SEQ], F32)
    wk, r_wk = A.alloc("wk", [SEQ], F32)
    nm_ring = A.ring("nm", [SEQ], BF16, 1)
    sel, r_sel = A.alloc("sel", [SEQ], BF16)
    selT, r_selT = A.alloc("selT", [32, 128], BF16)
    rl_ring = A.ring("rl", [512], F32, 3)
    m8_ring = A.ring("m8", [8], F32, 2)
    thr_ring = A.ring("thr", [1], F32, 2)
    pt4_ring = A.ring("pt4", [4, 128], BF16, 3)
    rd2_ring = A.ring("rd2", [2], F32, 4)
    pI = BankRing([2, 3])
    pT2 = BankRing([0, 1])
    pS2 = BankRing([4, 5])

    for i in range(NT_Q):
        KBi = KB_OF(i)
        L = KBi * 128
        qi_t, r_qi = qi_ring.next()
        qb_t, r_qb = qb_ring.next()
        nm_t, r_nm = nm_ring.next()
        P.dma("sp", lambda e, qi_t=qi_t, i=i: e.dma_start(out=qi_t[0:64, :, :], in_=QiT[i]), reads=[r_QiT], writes=[r_qi], owner=r_qi)
        P.dma("sp", lambda e, qb_t=qb_t, i=i: e.dma_start(out=qb_t, in_=QbT[i]), reads=[r_QbT], writes=[r_qb], owner=r_qb)
        for q2 in range(0, L, 2048):
            n2 = min(2048, L - q2)
            P.dma("pool", lambda e, nm_t=nm_t, i=i, q2=q2, n2=n2: e.dma_start(out=nm_t[:, q2:q2 + n2], in_=nm_d[i, :, q2:q2 + n2]),
                  writes=[r_nm], owner=r_nm)
        for g0 in range(0, L, 512):
            ncol = min(512, L - g0)
            for hh in range(H_IDX):
                bi = pI.next()
                rl, r_rl = rl_ring.next()
                P.op("pe", lambda e, bi=bi, hh=hh, g0=g0, ncol=ncol, qi_t=qi_t: e.matmul(
                    pb[bi][:, 0:ncol], lhsT=qi_t[0:64, hh, :], rhs=kit[0:64, g0:g0 + ncol], start=True, stop=True),
                    reads=[r_qi, r_kit], writes=[r_pb[bi]])
                P.op("act", lambda e, bi=bi, rl=rl, ncol=ncol: e.activation(out=rl[:, 0:ncol], in_=pb[bi][:, 0:ncol], func=AF.Relu),
                     reads=[r_pb[bi]], writes=[r_rl])
                if hh == 0:
                    P.op("dve", lambda e, rl=rl, g0=g0, ncol=ncol, i=i, nm_t=nm_t: e.scalar_tensor_tensor(
                        out=sc[:, g0:g0 + ncol], in0=rl[:, 0:ncol], scalar=wi_all[:, i, 0:1], in1=nm_t[:, g0:g0 + ncol],
                        op0=ALU.mult, op1=ALU.add), reads=[r_rl, r_wi, r_nm], writes=[r_sc])
                else:
                    P.op("dve", lambda e, rl=rl, g0=g0, ncol=ncol, i=i, hh=hh: e.scalar_tensor_tensor(
                        out=sc[:, g0:g0 + ncol], in0=rl[:, 0:ncol], scalar=wi_all[:, i, hh:hh + 1], in1=sc[:, g0:g0 + ncol],
                        op0=ALU.mult, op1=ALU.add), reads=[r_rl, r_wi, r_sc], writes=[r_sc])
        m8, r_m8 = m8_ring.next()
        thr, r_thr = thr_ring.next()
        for r in range(32):
            src = sc if r == 0 else wk
            r_src = r_sc if r == 0 else r_wk
            P.op("dve", lambda e, src=src, m8=m8, L=L: e.max(out=m8, in_=src[:, 0:L]), reads=[r_src], writes=[r_m8])
            if r < 31:
                P.op("dve", lambda e, src=src, m8=m8, L=L: e.match_replace(out=wk[:, 0:L], in_to_replace=m8, in_values=src[:, 0:L],
                                                                          imm_value=NEG),
                     reads=[r_src, r_m8], writes=[r_wk])
        P.op("dve", lambda e, m8=m8, thr=thr: e.tensor_scalar(out=thr, in0=m8[:, 7:8], scalar1=-1.0e29, scalar2=None, op0=ALU.max),
             reads=[r_m8], writes=[r_thr])
        P.op("dve", lambda e, thr=thr, L=L: e.tensor_scalar(out=sel[:, 0:L], in0=sc[:, 0:L], scalar1=thr[:, 0:1], scalar2=None,
                                                          op0=ALU.is_ge), reads=[r_sc, r_thr], writes=[r_sel])
        for k0 in range(0, KBi, 8):
            nk = min(8, KBi - k0)
            bt = pT2.next()
            ptb = bank_bf(bt)
            for kk in range(nk):
                kb = k0 + kk
                P.op("pe", lambda e, ptb=ptb, kk=kk, kb=kb: e.transpose(out=ptb[:, kk, :], in_=sel[:, kb * 128:(kb + 1) * 128],
                                                                        identity=identb),
                     reads=[r_sel, r_identb], writes=[r_pb[bt]])
            P.op("act", lambda e, ptb=ptb, k0=k0, nk=nk: e.activation(out=selT[:, k0:k0 + nk, :], in_=ptb[:, 0:nk, :], func=AF.Copy),
                 reads=[r_pb[bt]], writes=[r_selT])
        for g2 in range(KV_B):
            obank = [6, 7]
            ov = [pb[b_][:, 0:258].rearrange("p (a b) -> p a b", b=129) for b_ in obank]
            for kb in range(KBi):
                bs = pS2.next()
                ps = pb[bs].rearrange("p (a b) -> p a b", b=128)
                pt4, r_pt4 = pt4_ring.next()
                P.op("pe", lambda e, ps=ps, kb=kb, g2=g2, qb_t=qb_t: e.matmul(
                    ps, lhsT=kbt[:, g2, kb * 128:(kb + 1) * 128], rhs=qb_t[:, 4 * g2:4 * g2 + 4, :], start=True, stop=True),
                    reads=[r_kbt, r_qb], writes=[r_pb[bs]])
                P.op("act", lambda e, ps=ps, pt4=pt4: e.activation(out=pt4, in_=ps, func=AF.Exp, scale=SCALE),
                     reads=[r_pb[bs]], writes=[r_pt4])
                P.op("dve", lambda e, pt4=pt4, kb=kb: e.tensor_tensor(out=pt4, in0=pt4, in1=bc(selT[:, kb, :].unsqueeze(1), [128, 4, 128]),
                                                                      op=ALU.mult), reads=[r_pt4, r_selT], writes=[r_pt4])
                flush()

                def pv2(kb=kb, pt4=pt4, r_pt4=r_pt4, g2=g2, KBi=KBi, i=i, ov=ov, obank=obank):
                    for hq in range(4):
                        o_ap = ov[hq // 2][:, hq % 2, :]
                        P.op("pe", lambda e, hq=hq, o_ap=o_ap: e.matmul(o_ap, lhsT=pt4[:, hq, :], rhs=vbt[:, g2, kb, :],
                                                                        start=(kb == 0 and hq % 2 == 0), stop=(kb == KBi - 1),
                                                                        skip_group_check=True),
                             reads=[r_pt4, r_vbt], writes=[r_pb[obank[hq // 2]]])
                    if kb == KBi - 1:
                        for hq in range(4):
                            o_ap = ov[hq // 2][:, hq % 2, :]
                            rb_ = r_pb[obank[hq // 2]]
                            rd, r_rd = rd2_ring.next()
                            c0 = 1024 + (4 * g2 + hq) * 128
                            P.op("dve", lambda e, o_ap=o_ap, rd=rd: e.tensor_scalar(out=rd[:, 0:1], in0=o_ap[:, 128:129], scalar1=1e-30,
                                                                                   scalar2=None, op0=ALU.max), reads=[rb_], writes=[r_rd])
                            P.op("dve", lambda e, rd=rd: e.reciprocal(out=rd[:, 1:2], in_=rd[:, 0:1]), reads=[r_rd], writes=[r_rd])
                            P.op("act", lambda e, o_ap=o_ap, rd=rd, c0=c0: e.activation(out=att[:, i, c0:c0 + 128], in_=o_ap[:, 0:128],
                                                                                       func=AF.Copy, scale=rd[:, 1:2]),
                                 reads=[rb_, r_rd], writes=[r_att[i]])
                pend[0] = pv2
            flush()
    barrier(P)
    A.release(att_mark)

    NPOOL = cfk.shape[0] // 128
    SK = PAST + 128
    ptb, r_ptb = A.alloc("ptb", [256], I32)
    idx_tok, r_idx = A.alloc("idx_tok", [256], I32)
    idx_pg, r_idxpg = A.alloc("idx_pg", [4], I32)
    iota_i, r_iota = A.alloc("iota_i", [1], I32)
    iota_f, r_iotaf = A.alloc("iota_f", [1], F32)
    lst, r_lst = A.alloc("lst", [128], F32)
    o64, r_o64 = A.alloc("o64", [128], F32)
    lblk, r_lblk = A.alloc("lblk", [128], F32)
    esel, r_esel = A.alloc("esel", [4, 128], F32)
    smask, r_smask = A.alloc("smask", [4, 32], BF16)
    P.dma("sp", lambda e: e.dma_start(out=ptb, in_=pt_d.rearrange("s l -> (s l)").partition_broadcast(128)),
          writes=[r_ptb], owner=r_ptb)
    P.op("pool", lambda e: e.memset(idx_pg, 0), writes=[r_idxpg])
    P.dma("sp", lambda e: e.dma_start(out=idx_pg[0:64, :], in_=pt_d.rearrange("s l -> l s"), allow_slow_non_contiguous=True),
          writes=[r_idxpg], owner=r_idxpg)
    P.dma("sp", lambda e: e.dma_start(out=lst, in_=lst_d[:, :]), writes=[r_lst], owner=r_lst)
    P.dma("sp", lambda e: e.dma_start(out=o64, in_=o64_d[:, :]), writes=[r_o64], owner=r_o64)
    P.dma("sp", lambda e: e.dma_start(out=lblk, in_=lblk_d[:, :]), writes=[r_lblk], owner=r_lblk)
    P.dma("sp", lambda e: e.dma_start(out=esel, in_=esel_d[:, :, :]), writes=[r_esel], owner=r_esel)
    P.dma("pool", lambda e: e.dma_start(out=smask, in_=smask_d[:, :, :]), writes=[r_smask], owner=r_smask)
    P.op("pool", lambda e: e.iota(iota_i, [[0, 1]], base=0, channel_multiplier=1), writes=[r_iota])
    P.op("dve", lambda e: e.tensor_copy(out=iota_f, in_=iota_i), reads=[r_iota], writes=[r_iotaf])
    P.op("dve", lambda e: e.tensor_scalar(out=idx_tok, in0=ptb, scalar1=128.0, scalar2=iota_f[:, 0:1], op0=ALU.mult, op1=ALU.add),
         reads=[r_ptb, r_iotaf], writes=[r_idx])

    def gather(dst_ap, r_dst, src2d, idx_col_ap, r_ix):
        P.dma("pool", lambda e: e.indirect_dma_start(out=dst_ap, out_offset=None, in_=src2d,
                                                     in_offset=bass.IndirectOffsetOnAxis(ap=idx_col_ap, axis=0)),
              reads=[r_ix], writes=[r_dst], owner=r_dst)

    kp_ring = A.ring("kp", [8, 128], BF16, 2)
    vp_ring = A.ring("vp", [8, 129], BF16, 2)
    ktp_ring = A.ring("ktp", [8, 128], BF16, 2)
    vs_ring = A.ring("vstage", [8, 128], BF16, 2)
    z_ring = A.ring("z_s", [8, 32], F32, 2)
    ps_rings = [A.ring("pt_s0", [8, 32], BF16, 2), A.ring("pt_s1", [8, 32], BF16, 2),
                A.ring("pt_s2", [8, 64], BF16, 2), A.ring("pt_s3", [8, 64], BF16, 2)]
    for t_, r_ in ps_rings[2].items:
        P.op("pool", lambda e, t_=t_: e.memset(t_[:, :, 32:64], 0.0), writes=[r_])
    for t_, r_ in ps_rings[3].items:
        P.op("pool", lambda e, t_=t_: e.memset(t_[:, :, 0:32], 0.0), writes=[r_])
    qs_a, r_qsa = A.alloc("qs_a", [8, 128], BF16)
    qs_b, r_qsb = A.alloc("qs_b", [8, 128], BF16)
    P.dma("sp", lambda e: e.dma_start(out=qs_a, in_=QaT[NT_Q]), reads=[r_QaT], writes=[r_qsa], owner=r_qsa)
    P.dma("sp", lambda e: e.dma_start(out=qs_b, in_=QbT[NT_Q]), reads=[r_QbT], writes=[r_qsb], owner=r_qsb)
    for t_, r_ in vp_ring.items:
        P.op("pool", lambda e, t_=t_: e.memset(t_[:, :, 128:129], 1.0), writes=[r_])
    pTs = BankRing([0, 1])
    pSs = BankRing([2, 3])
    OB = [4, 5, 6]

    def o_view(h, s):
        b = OB[h // 3]
        r0, r1 = (32 * s, 32 * s + 32) if s < 2 else (64, 128)
        return pb[b][r0:r1, (h % 3) * 129:(h % 3) * 129 + 129], b

    def sample_attn(kind):
        nkv = 8 if kind == "fox" else 2
        per = 1 if kind == "fox" else 4
        kcache, vcache = (cfk, cfv) if kind == "fox" else (cdk, cdv)
        qs, r_qs = (qs_a, r_qsa) if kind == "fox" else (qs_b, r_qsb)
        KTs, VEs, rKTs, rVEs = (KaTs, VaEs, r_s[0], r_s[1]) if kind == "fox" else (KbTs, VbEs, r_s[2], r_s[3])
        col0 = 0 if kind == "fox" else 1024
        for s in range(4):
            for lp in range(NPG + 1):
                new = (lp == NPG)
                ktp, r_ktp = ktp_ring.next()
                vp, r_vp = vp_ring.next()
                if not new:
                    kp, r_kp = kp_ring.next()
                    ic = idx_tok[:, s * 64 + lp:s * 64 + lp + 1]
                    vs, r_vs = vs_ring.next()
                    gather(kp.rearrange("p h d -> p (h d)")[:, 0:nkv * 128], r_kp, kcache, ic, r_idx)
                    gather(vs.rearrange("p h d -> p (h d)")[:, 0:nkv * 128], r_vs, vcache, ic, r_idx)
                    P.op("pool", lambda e, vs=vs, vp=vp: e.tensor_copy(out=vp[:, 0:nkv, 0:128], in_=vs[:, 0:nkv, :]),
                         reads=[r_vs], writes=[r_vp])
                    bt = pTs.next()
                    ptb_ = bank_bf(bt)
                    for h in range(nkv):
                        P.op("pe", lambda e, h=h, kp=kp, ptb_=ptb_: e.transpose(out=ptb_[:, h, :], in_=kp[:, h, :], identity=identb),
                             reads=[r_kp, r_identb], writes=[r_pb[bt]])
                    P.op("act", lambda e, ktp=ktp, ptb_=ptb_: e.activation(out=ktp[:, 0:nkv, :], in_=ptb_[:, 0:nkv, :], func=AF.Copy),
                         reads=[r_pb[bt]], writes=[r_ktp])
                else:
                    P.dma("sp", lambda e, ktp=ktp: e.dma_start(out=ktp[:, 0:nkv, :], in_=KTs.rearrange("h d t -> d h t")),
                          reads=[rKTs], writes=[r_ktp], owner=r_ktp)
                    P.dma("sp", lambda e, vp=vp: e.dma_start(out=vp[:, 0:nkv, :], in_=VEs[:, :, 0, :].rearrange("h p e -> p h e")),
                          reads=[rVEs], writes=[r_vp], owner=r_vp)
                bs = pSs.next()
                psv = pb[bs][:, 0:256].rearrange("p (a b) -> p a b", b=32)
                for g in range(nkv):
                    if per == 1:
                        P.op("pe", lambda e, g=g, ktp=ktp, psv=psv, s=s: e.matmul(psv[:, g, :], lhsT=ktp[:, g, :], rhs=qs[:, g, 32 * s:32 * s + 32],
                                                                            start=True, stop=True),
                             reads=[r_ktp, r_qs], writes=[r_pb[bs]])
                    else:
                        P.op("pe", lambda e, g=g, ktp=ktp, psv=psv, s=s: e.matmul(psv[:, 4 * g:4 * g + 4, :], lhsT=ktp[:, g, :],
                                                                            rhs=qs[:, 4 * g:4 * g + 4, 32 * s:32 * s + 32],
                                                                            start=True, stop=True),
                             reads=[r_ktp, r_qs], writes=[r_pb[bs]])
                pt_full, r_pts = ps_rings[s].next()
                pt_s = pt_full if s < 2 else pt_full[:, :, (s - 2) * 32:(s - 2) * 32 + 32]
                if kind == "fox":
                    z, r_z = z_ring.next()
                    bias_ap = bN[:, s, :] if new else bP[:, s, :, lp]
                    r_bias = r_bN if new else r_bP[s]
                    P.op("dve", lambda e, z=z, psv=psv, bias_ap=bias_ap: e.scalar_tensor_tensor(
                        out=z, in0=psv, scalar=SCALE, in1=bc(bias_ap.unsqueeze(2), [128, 8, 32]), op0=ALU.mult, op1=ALU.add),
                        reads=[r_pb[bs], r_bias], writes=[r_z])
                    P.op("act", lambda e, z=z, pt_s=pt_s: e.activation(out=pt_s, in_=z, func=AF.Exp), reads=[r_z], writes=[r_pts])
                    if new:
                        P.op("dve", lambda e, pt_s=pt_s, s=s: e.tensor_tensor(out=pt_s, in0=pt_s, in1=bc(smask[:, s, :].unsqueeze(1), [128, 8, 32]),
                                                                         op=ALU.mult), reads=[r_pts, r_smask], writes=[r_pts])
                else:
                    P.op("act", lambda e, psv=psv, pt_s=pt_s: e.activation(out=pt_s, in_=psv, func=AF.Exp, scale=SCALE),
                         reads=[r_pb[bs]], writes=[r_pts])
                    P.op("dve", lambda e, pt_s=pt_s, lp=lp, s=s: e.tensor_tensor(
                        out=pt_s, in0=pt_s, in1=bc(selTs[:, lp, 32 * s:32 * s + 32].unsqueeze(1), [128, 8, 32]), op=ALU.mult),
                        reads=[r_pts, r_selTs], writes=[r_pts])
                flush()

                def pv3(pt_full=pt_full, r_pts=r_pts, vp=vp, r_vp=r_vp, lp=lp, s=s, new=new):
                    for h in range(8):
                        o_ap, b = o_view(h, s)
                        P.op("pe", lambda e, h=h, o_ap=o_ap: e.matmul(o_ap, lhsT=pt_full[:, h, :], rhs=vp[:, h // per, :],
                                                                      start=(lp == 0 and h % 3 == 0 and s != 3), stop=new,
                                                                      skip_group_check=True),
                             reads=[r_pts, r_vp], writes=[r_pb[b]])
                pend[0] = pv3
            flush()
        for h in range(8):
            b = OB[h // 3]
            o_full = pb[b][:, (h % 3) * 129:(h % 3) * 129 + 129]
            rd, r_rd = rd2_ring.next()
            P.op("dve", lambda e, o_full=o_full, rd=rd: e.tensor_scalar(out=rd[:, 0:1], in0=o_full[:, 128:129], scalar1=1e-30, scalar2=None,
                                                                         op0=ALU.max), reads=[r_pb[b]], writes=[r_rd])
            P.op("dve", lambda e, rd=rd: e.reciprocal(out=rd[:, 1:2], in_=rd[:, 0:1]), reads=[r_rd], writes=[r_rd])
            P.op("act", lambda e, o_full=o_full, rd=rd, h=h: e.activation(out=att[:, NT_Q, col0 + h * 128:col0 + (h + 1) * 128],
                                                                          in_=o_full[:, 0:128], func=AF.Copy, scale=rd[:, 1:2]),
                 reads=[r_pb[b], r_rd], writes=[r_att[NT_Q]])

    rl_ring = A.ring("rl_s", [512], F32, 3)
    m8_ring = A.ring("m8_s", [8], F32, 2)
    thr_ring = A.ring("thr_s", [1], F32, 2)
    rd2_ring = A.ring("rd2_s", [2], F32, 4)
    smp_mark = A.mark()
    cs_s, r_cs = A.alloc("cs_s", [8], F32)
    csref, r_csref = A.alloc("csref", [4, 8], F32)
    bP, r_bP_base = A.alloc("bP", [4, 8, 64], F32)
    r_bP = [P.res(f"bP{s}") for s in range(4)]
    bN, r_bN = A.alloc("bN", [4, 8], F32)
    P.op("pe", lambda e: e.matmul(pb[7][:, 0:8], lhsT=lblk, rhs=lfs, start=True, stop=True), reads=[r_lblk, r_lfs], writes=[r_pb[7]])
    P.op("act", lambda e: e.activation(out=cs_s, in_=pb[7][:, 0:8], func=AF.Copy), reads=[r_pb[7]], writes=[r_cs])
    for s in range(4):
        P.op("pe", lambda e, s=s: e.matmul(pb[7][:, 8 + 8 * s:16 + 8 * s], lhsT=esel[:, s, :], rhs=cs_s, start=True, stop=True),
             reads=[r_esel, r_cs], writes=[r_pb[7]])
    P.op("act", lambda e: e.activation(out=csref.rearrange("p s h -> p (s h)"), in_=pb[7][:, 8:40], func=AF.Copy),
         reads=[r_pb[7]], writes=[r_csref])
    P.op("dve", lambda e: e.tensor_tensor(out=bN, in0=csref, in1=bc(cs_s.unsqueeze(1), [128, 4, 8]), op=ALU.subtract),
         reads=[r_csref, r_cs], writes=[r_bN])
    lp_ring = A.ring("lp_t", [128, 8], F32, 2)
    cw_ring = A.ring("cw", [8, 128], F32, 2)
    sm2_ring = A.ring("sm2", [3, 8], F32, 2)
    for s in range(4):
        lp_t, r_lp = lp_ring.next()
        cw, r_cw = cw_ring.next()
        sm2, r_sm2 = sm2_ring.next()
        gather(lp_t.rearrange("p t h -> p (t h)"), r_lp, cfl, idx_pg[:, s:s + 1], r_idxpg)
        for h in range(8):
            P.op("dve", lambda e, h=h, cw=cw, lp_t=lp_t: e.tensor_tensor_scan(out=cw[:, h, :], data0=ones[:, 0:128], data1=lp_t[:, :, h],
                                                                              initial=0.0, op0=ALU.mult, op1=ALU.add),
                 reads=[r_lp, r_ones], writes=[r_cw])
        P.op("dve", lambda e, cw=cw, sm2=sm2: e.tensor_copy(out=sm2[:, 0, :], in_=cw[:, :, 127]), reads=[r_cw], writes=[r_sm2])
        P.op("pe", lambda e, sm2=sm2: e.matmul(pb[6][:, 0:8], lhsT=lst, rhs=sm2[:, 0, :], start=True, stop=True),
             reads=[r_lst, r_sm2], writes=[r_pb[6]])
        P.op("pe", lambda e, sm2=sm2: e.matmul(pb[6][:, 8:16], lhsT=o64, rhs=sm2[:, 0, :], start=True, stop=True),
             reads=[r_o64, r_sm2], writes=[r_pb[6]])
        P.op("act", lambda e, sm2=sm2: e.activation(out=sm2[:, 1:3, :].rearrange("p a h -> p (a h)"), in_=pb[6][:, 0:16], func=AF.Copy),
             reads=[r_pb[6]], writes=[r_sm2])
        P.op("dve", lambda e, cw=cw, sm2=sm2: e.tensor_tensor(out=cw, in0=cw, in1=bc(sm2[:, 1, :].unsqueeze(2), [128, 8, 128]), op=ALU.add),
             reads=[r_cw, r_sm2], writes=[r_cw])
        P.op("dve", lambda e, sm2=sm2, s=s: e.tensor_tensor(out=sm2[:, 2, :], in0=sm2[:, 2, :], in1=csref[:, s, :], op=ALU.add),
             reads=[r_sm2, r_csref], writes=[r_sm2])
        for h0 in range(0, 8, 4):
            pcT = pb[7][:, :].rearrange("p (a b) -> p a b", b=128)
            for hh in range(4):
                P.op("pe", lambda e, hh=hh, h0=h0, cw=cw, pcT=pcT: e.transpose(out=pcT[:, hh, :], in_=cw[:, h0 + hh, :], identity=identf),
                     reads=[r_cw, r_identf], writes=[r_pb[7]])
            P.op("dve", lambda e, h0=h0, s=s, sm2=sm2, pcT=pcT: e.scalar_tensor_tensor(
                out=bP[:, s, h0:h0 + 4, :], in0=pcT[:, :, 0:64], scalar=-1.0,
                in1=bc(sm2[:, 2, h0:h0 + 4].unsqueeze(2), [128, 4, 64]), op0=ALU.mult, op1=ALU.add),
                reads=[r_pb[7], r_sm2], writes=[r_bP[s]])

    sample_attn("fox")
    barrier(P)
    A.release(smp_mark)

    kiTall = dscr("kiTall", [4, 64, SK], BF16)
    r_kiTall = P.res("kiTall")
    kip_ring = A.ring("kip", [64], BF16, 3)
    kst_ring = A.ring("kist", [8, 128], BF16, 2)
    for s in range(4):
        for l0 in range(0, NPG, 8):
            bt = pTs.next()
            ptb_ = bank_bf(bt)
            kst, r_kst = kst_ring.next()
            for ll in range(8):
                lp = l0 + ll
                kip, r_kip = kip_ring.next()
                gather(kip, r_kip, cik, idx_tok[:, s * 64 + lp:s * 64 + lp + 1], r_idx)
                P.op("pe", lambda e, ll=ll, kip=kip, ptb_=ptb_: e.transpose(out=ptb_[0:64, ll, :], in_=kip, identity=identb),
                     reads=[r_kip, r_identb], writes=[r_pb[bt]])
            P.op("act", lambda e, kst=kst, ptb_=ptb_: e.activation(out=kst[0:64, :, :], in_=ptb_[0:64, :, :], func=AF.Copy),
                 reads=[r_pb[bt]], writes=[r_kst])
            P.dma("sp", lambda e, kst=kst, s=s, l0=l0: e.dma_start(out=kiTall[s, :, l0 * 128:(l0 + 8) * 128],
                                                                   in_=kst[0:64, :, :].rearrange("p a b -> p (a b)")),
                  reads=[r_kst], writes=[r_kiTall], owner=r_kst, kind="out")
        kst, r_kst = kst_ring.next()
        P.dma("sp", lambda e, kst=kst: e.dma_start(out=kst[0:64, 0, :], in_=KiTs[:, :]), reads=[r_s[4]], writes=[r_kst], owner=r_kst)
        P.dma("sp", lambda e, kst=kst, s=s: e.dma_start(out=kiTall[s, :, PAST:SK], in_=kst[0:64, 0, :]),
              reads=[r_kst], writes=[r_kiTall], owner=r_kst, kind="out")
    qis, r_qis = A.alloc("qis", [16, 128], BF16)
    P.dma("sp", lambda e: e.dma_start(out=qis[0:64, :, :], in_=QiT[NT_Q]), reads=[r_QiT], writes=[r_qis], owner=r_qis)
    qis23, r_qis23 = A.alloc("qis23", [2, 16, 64], BF16)
    P.op("pool", lambda e: e.memset(qis23[0:64], 0.0), writes=[r_qis23])
    P.op("pool", lambda e: e.tensor_copy(out=qis23[0:64, 0, :, 0:32], in_=qis[0:64, :, 64:96]), reads=[r_qis], writes=[r_qis23])
    P.op("pool", lambda e: e.tensor_copy(out=qis23[0:64, 1, :, 32:64], in_=qis[0:64, :, 96:128]), reads=[r_qis], writes=[r_qis23])
    scs, r_scs = A.alloc("scs", [SK], F32)
    wks, r_wks = A.alloc("wks", [SK], F32)
    sels_ring = A.ring("sels", [1024], BF16, 2)
    selTs, r_selTs = A.alloc("selTs", [NPG + 1, 128], BF16)
    nms, r_nms = A.alloc("nms", [128], F32)
    P.dma("sp", lambda e: e.dma_start(out=nms, in_=nms_d[:, :]), writes=[r_nms], owner=r_nms)
    kig_ring = A.ring("kig", [4, 512], BF16, 2)
    for g0 in range(0, SK, 512):
        ncol = min(512, SK - g0)
        kig, r_kig = kig_ring.next()
        P.dma("sp", lambda e, kig=kig, g0=g0, ncol=ncol: e.dma_start(out=kig[0:64, :, 0:ncol],
                                                                     in_=kiTall[:, :, g0:g0 + ncol].rearrange("s d k -> d s k")),
              reads=[r_kiTall], writes=[r_kig], owner=r_kig)
        for hh in range(H_IDX):
            bi = pSs.next()
            rl, r_rl = rl_ring.next()
            for s in range(4):
                if s < 2:
                    P.op("pe", lambda e, bi=bi, hh=hh, s=s, ncol=ncol, kig=kig: e.matmul(
                        pb[bi][32 * s:32 * s + 32, 0:ncol], lhsT=qis[0:64, hh, 32 * s:32 * s + 32], rhs=kig[0:64, s, 0:ncol],
                        start=True, stop=True), reads=[r_qis, r_kig], writes=[r_pb[bi]])
                else:
                    P.op("pe", lambda e, bi=bi, hh=hh, s=s, ncol=ncol, kig=kig: e.matmul(
                        pb[bi][64:128, 0:ncol], lhsT=qis23[0:64, s - 2, hh, :], rhs=kig[0:64, s, 0:ncol],
                        start=(s == 2), stop=(s == 3), skip_group_check=True), reads=[r_qis23, r_kig], writes=[r_pb[bi]])
            P.op("act", lambda e, bi=bi, rl=rl, ncol=ncol: e.activation(out=rl[:, 0:ncol], in_=pb[bi][:, 0:ncol], func=AF.Relu),
                 reads=[r_pb[bi]], writes=[r_rl])
            if hh == 0:
                if g0 >= PAST:
                    P.op("dve", lambda e, rl=rl, g0=g0, ncol=ncol: e.scalar_tensor_tensor(
                        out=scs[:, g0:g0 + ncol], in0=rl[:, 0:ncol], scalar=wi_all[:, NT_Q, 0:1], in1=nms[:, 0:ncol],
                        op0=ALU.mult, op1=ALU.add), reads=[r_rl, r_wi, r_nms], writes=[r_scs])
                else:
                    P.op("dve", lambda e, rl=rl, g0=g0, ncol=ncol: e.tensor_scalar(
                        out=scs[:, g0:g0 + ncol], in0=rl[:, 0:ncol], scalar1=wi_all[:, NT_Q, 0:1], scalar2=None, op0=ALU.mult),
                        reads=[r_rl, r_wi], writes=[r_scs])
            else:
                P.op("dve", lambda e, rl=rl, g0=g0, ncol=ncol, hh=hh: e.scalar_tensor_tensor(
                    out=scs[:, g0:g0 + ncol], in0=rl[:, 0:ncol], scalar=wi_all[:, NT_Q, hh:hh + 1], in1=scs[:, g0:g0 + ncol],
                    op0=ALU.mult, op1=ALU.add), reads=[r_rl, r_wi, r_scs], writes=[r_scs])
    m8, r_m8 = m8_ring.next()
    thr, r_thr = thr_ring.next()
    for r in range(32):
        src = scs if r == 0 else wks
        r_src = r_scs if r == 0 else r_wks
        P.op("dve", lambda e, src=src: e.max(out=m8, in_=src), reads=[r_src], writes=[r_m8])
        if r < 31:
            P.op("dve", lambda e, src=src: e.match_replace(out=wks, in_to_replace=m8, in_values=src, imm_value=NEG),
                 reads=[r_src, r_m8], writes=[r_wks])
    P.op("dve", lambda e: e.tensor_scalar(out=thr, in0=m8[:, 7:8], scalar1=-1.0e29, scalar2=None, op0=ALU.max),
         reads=[r_m8], writes=[r_thr])
    for k0 in range(0, NPG + 1, 8):
        nk = min(8, NPG + 1 - k0)
        bt = pTs.next()
        ptb_ = bank_bf(bt)
        sels, r_sels = sels_ring.next()
        P.op("dve", lambda e, sels=sels, k0=k0, nk=nk: e.tensor_scalar(out=sels[:, 0:nk * 128], in0=scs[:, k0 * 128:(k0 + nk) * 128],
                                                                      scalar1=thr[:, 0:1], scalar2=None, op0=ALU.is_ge),
             reads=[r_scs, r_thr], writes=[r_sels])
        for kk in range(nk):
            kb = k0 + kk
            P.op("pe", lambda e, ptb_=ptb_, kk=kk, sels=sels: e.transpose(out=ptb_[:, kk, :], in_=sels[:, kk * 128:(kk + 1) * 128], identity=identb),
                 reads=[r_sels, r_identb], writes=[r_pb[bt]])
        P.op("act", lambda e, ptb_=ptb_, k0=k0, nk=nk: e.activation(out=selTs[:, k0:k0 + nk, :], in_=ptb_[:, 0:nk, :], func=AF.Copy),
             reads=[r_pb[bt]], writes=[r_selTs])
    sample_attn("dsa")
    barrier(P)
    A.release(att_mark)
    if dbg:
        P.dma("sp", lambda e: e.dma_start(out=dbg_att.rearrange("t p d -> p t d"), in_=att), reads=r_att, writes=[r_out],
              owner=r_att[0], kind="out", final=True)

    hnT_scr = dscr("hnT_scr", [128, NTQ, 16, 128], BF16)
    r_hnT = P.res("hnT_scr")
    aTall, _ = A.alloc("aTall", [NTQ, 16, 128], BF16)
    r_aT = [P.res(f"aT{i}") for i in range(NTQ)]
    pTe = BankRing([0, 1])
    for s in range(NTQ):
        for g in range(2):
            b = pTe.next()
            pt = bank_bf(b)
            for jj in range(8):
                kc = g * 8 + jj
                P.op("pe", lambda e, kc=kc, jj=jj, pt=pt, s=s: e.transpose(out=pt[:, jj, :], in_=att[:, s, kc * 128:(kc + 1) * 128],
                                                                          identity=identb),
                     reads=[r_att[s], r_identb], writes=[r_pb[b]])
            P.op("act", lambda e, g=g, pt=pt, s=s: e.activation(out=aTall[:, s, g * 8:(g + 1) * 8, :], in_=pt, func=AF.Copy),
                 reads=[r_pb[b]], writes=[r_aT[s]])
    wo_ring = A.ring("wo", [16, 512], BF16, 2)
    xo_ring = A.ring("xo", [512], F32, 3)
    hc1_ring = A.ring("hc1", [512], F32, 3)
    pH = BankRing([2, 3, 4, 5])
    wov = w_out.rearrange("(kc p) n -> p kc n", p=128)
    for c4 in range(4):
        wo, r_wo = wo_ring.next()
        for q4 in range(0, 16, 4):
            P.dma("pool", lambda e, c4=c4, q4=q4, wo=wo: e.dma_start(out=wo[:, q4:q4 + 4, :],
                                                                     in_=wov[:, q4:q4 + 4, c4 * 512:(c4 + 1) * 512]),
                  writes=[r_wo], owner=r_wo)
        for s in range(NTQ):
            xo, r_xo = xo_ring.next()
            hc1, r_hc1 = hc1_ring.next()
            P.dma("sp", lambda e, xo=xo, s=s, c4=c4: e.dma_start(out=xo, in_=rows(xq, s)[:, c4 * 512:(c4 + 1) * 512]),
                  writes=[r_xo], owner=r_xo)
            b = pH.next()
            for kc in range(16):
                P.op("pe", lambda e, kc=kc, s=s, b=b, wo=wo: e.matmul(pb[b][:, :], lhsT=aTall[:, s, kc, :], rhs=wo[:, kc, :],
                                                                     start=(kc == 0), stop=(kc == 15)),
                     reads=[r_aT[s], r_wo], writes=[r_pb[b]])
            P.op("dve", lambda e, b=b, hc1=hc1, xo=xo: e.tensor_tensor(out=hc1, in0=pb[b][:, :], in1=xo, op=ALU.add),
                 reads=[r_pb[b], r_xo], writes=[r_hc1])
            P.dma("sp", lambda e, hc1=hc1, s=s, c4=c4: e.dma_start(out=rows(h_scr, s)[:, c4 * 512:(c4 + 1) * 512], in_=hc1),
                  reads=[r_hc1], writes=[r_hscr], owner=r_hc1, kind="out")
    barrier(P)
    A.release(base_mark)
    mE = make_proj(NTQ, gF_d, slim=True)
    hf_ring = A.ring("hf", [D], F32, 2)
    hb_ring = A.ring("hb", [D], BF16, 2)
    for s in range(NTQ):
        hf, r_hf = hf_ring.next()
        hb, r_hb = hb_ring.next()
        P.dma("sp", lambda e, hf=hf, s=s: e.dma_start(out=hf, in_=rows(h_scr, s)), reads=[r_hscr], writes=[r_hf], owner=r_hf)
        if dbg:
            P.dma("sp", lambda e, hf=hf, s=s: e.dma_start(out=rows(dbg_h, s), in_=hf), reads=[r_hf], writes=[r_out], owner=r_hf,
                  kind="out", final=True)
        P.op("act", lambda e, hf=hf, s=s: e.activation(out=mE.junk, in_=hf, func=AF.Square, accum_out=mE.ss[:, s:s + 1]),
             reads=[r_hf], writes=[mE.r_junk, mE.r_ss[s]])
        rstd_col = mE.ss[:, s:s + 1]
        P.op("act", lambda e, rstd_col=rstd_col: e.activation(out=rstd_col, in_=rstd_col, func=AF.Ln, scale=1.0 / D, bias=EPS),
             reads=[mE.r_ss[s]], writes=[mE.r_ss[s]])
        P.op("act", lambda e, rstd_col=rstd_col: e.activation(out=rstd_col, in_=rstd_col, func=AF.Exp, scale=-0.5),
             reads=[mE.r_ss[s]], writes=[mE.r_ss[s]])
        P.op("dve", lambda e, hf=hf, hb=hb, rstd_col=rstd_col: e.scalar_tensor_tensor(out=hb, in0=hf, scalar=rstd_col, in1=mE.gvec,
                                                                                    op0=ALU.mult, op1=ALU.mult),
             reads=[r_hf, mE.r_ss[s], mE.r_gvec], writes=[r_hb])
        for g in range(2):
            b = mE.pT.next()
            pt = bank_bf(b)
            for jj in range(8):
                kc = g * 8 + jj
                P.op("pe", lambda e, kc=kc, jj=jj, pt=pt, hb=hb: e.transpose(out=pt[:, jj, :], in_=hb[:, kc * 128:(kc + 1) * 128],
                                                                            identity=identb),
                     reads=[r_hb, r_identb], writes=[r_pb[b]])
            P.op("act", lambda e, g=g, pt=pt, s=s: e.activation(out=mE.xT[:, s, g * 8:(g + 1) * 8, :], in_=pt, func=AF.Copy),
                 reads=[r_pb[b]], writes=[mE.r_xT[s]])
        P.dma("sp", lambda e, s=s: e.dma_start(out=hnT_scr[:, s, :, :], in_=mE.xT[:, s, :, :]), reads=[mE.r_xT[s]], writes=[r_hnT],
              owner=mE.r_xT[s], kind="out")
    barrier(P)
    A.release(base_mark)

    hnT, r_hn = A.alloc("hnT", [NTQ, 16, 128], BF16)
    P.dma("sp", lambda e: e.dma_start(out=hnT, in_=hnT_scr[:, :, :, :]), reads=[r_hnT], writes=[r_hn], owner=r_hn)
    cw, r_cw = A.alloc("cw", [NFF, 3], F32)
    cb, r_cb = A.alloc("cb", [NFF], F32)
    P.dma("sp", lambda e: e.dma_start(out=cw, in_=cw_d[:, :, :]), writes=[r_cw], owner=r_cw)
    P.dma("sp", lambda e: e.dma_start(out=cb, in_=cb_d[:, :]), writes=[r_cb], owner=r_cb)
    cst8, r_cst8 = A.alloc("cst8", [D_FF], F32)
    cstT, r_cstT = A.alloc("cstT", [NFF, 8], F32)
    P.dma("sp", lambda e: e.dma_start(out=cst8[0:8, :], in_=cst_d[:, :]), writes=[r_cst8], owner=r_cst8)
    pc = pb[0][:, 0:NFF * 8].rearrange("p (a b) -> p a b", b=8)
    for f in range(NFF):
        P.op("pe", lambda e, f=f: e.transpose(out=pc[:, f, :], in_=cst8[0:8, f * 128:(f + 1) * 128], identity=identf[0:8, 0:8]),
             reads=[r_cst8, r_identf], writes=[r_pb[0]])
    P.op("act", lambda e: e.activation(out=cstT, in_=pc, func=AF.Copy), reads=[r_pb[0]], writes=[r_cstT])
    wg_ring = A.ring("wg", [16, 512], BF16, 2)
    wu_ring = A.ring("wu", [16, 512], BF16, 2)
    NP = NTOK + 2
    g_ring = A.ring("g_sb", [NP], F32, 2)
    u_ring = A.ring("u_sb", [NTOK], F32, 2)
    ac_ring = A.ring("acc", [NTOK], F32, 2)
    a_ring = A.ring("a_sb", [NTQ, 128], BF16, 2)
    gs_ring = A.ring("gsel", [16], F32, 2)
    for t_, r_ in g_ring.items:
        P.op("pool", lambda e, t_=t_: e.memset(t_[:, 0:2], 0.0), writes=[r_])
    pG = BankRing([0, 1, 2, 3, 4, 5, 6, 7])
    groups = [(0, 4), (4, 4), (8, NTQ - 8)]
    wgv = w_gate.rearrange("(kc p) n -> p kc n", p=128)
    wuv = w_up.rearrange("(kc p) n -> p kc n", p=128)
    for c0 in range(0, NFF, 4):
        nf = min(4, NFF - c0)
        wg, r_wg = wg_ring.next()
        wu, r_wu = wu_ring.next()
        for q4 in range(0, 16, 4):
            P.dma("pool", lambda e, q4=q4, wg=wg, c0=c0, nf=nf: e.dma_start(out=wg[:, q4:q4 + 4, 0:nf * 128],
                                                                          in_=wgv[:, q4:q4 + 4, c0 * 128:(c0 + nf) * 128]),
                  writes=[r_wg], owner=r_wg)
            P.dma("pool", lambda e, q4=q4, wu=wu, c0=c0, nf=nf: e.dma_start(out=wu[:, q4:q4 + 4, 0:nf * 128],
                                                                          in_=wuv[:, q4:q4 + 4, c0 * 128:(c0 + nf) * 128]),
                  writes=[r_wu], owner=r_wu)
        for fi in range(nf):
            f = c0 + fi
            g_sb, r_g = g_ring.next()
            u_sb, r_u = u_ring.next()
            acc, r_acc = ac_ring.next()
            a_sb, r_a = a_ring.next()
            gsel, r_gsel = gs_ring.next()
            for (t0, nt) in groups:
                bg = pG.next()
                bu = pG.next()
                n = nt * 128
                for kc in range(16):
                    P.op("pe", lambda e, kc=kc, bg=bg, wg=wg, fi=fi, t0=t0, nt=nt, n=n: e.matmul(
                        pb[bg][:, 0:n], lhsT=wg[:, kc, fi * 128:(fi + 1) * 128], rhs=hnT[:, t0:t0 + nt, kc, :],
                        start=(kc == 0), stop=(kc == 15)), reads=[r_wg, r_hn], writes=[r_pb[bg]])
                for kc in range(16):
                    P.op("pe", lambda e, kc=kc, bu=bu, wu=wu, fi=fi, t0=t0, nt=nt, n=n: e.matmul(
                        pb[bu][:, 0:n], lhsT=wu[:, kc, fi * 128:(fi + 1) * 128], rhs=hnT[:, t0:t0 + nt, kc, :],
                        start=(kc == 0), stop=(kc == 15)), reads=[r_wu, r_hn], writes=[r_pb[bu]])
                P.op("act", lambda e, bg=bg, g_sb=g_sb, t0=t0, n=n: e.activation(out=g_sb[:, 2 + t0 * 128:2 + t0 * 128 + n],
                                                                              in_=pb[bg][:, 0:n], func=AF.Copy),
                     reads=[r_pb[bg]], writes=[r_g])
                P.op("act", lambda e, bu=bu, u_sb=u_sb, t0=t0, n=n: e.activation(out=u_sb[:, t0 * 128:t0 * 128 + n],
                                                                              in_=pb[bu][:, 0:n], func=AF.Copy),
                     reads=[r_pb[bu]], writes=[r_u])
            P.op("pool", lambda e, g_sb=g_sb, gsel=gsel: e.tensor_copy(out=gsel[:, 0:2], in_=g_sb[:, 2 + 1024:2 + 1026]),
                 reads=[r_g], writes=[r_gsel])
            sv = g_sb[:, 2 + 1152:2 + 1280].rearrange("p (s j) -> p s j", j=32)
            P.op("pool", lambda e, sv=sv, gsel=gsel: e.tensor_copy(out=gsel[:, 2:10].rearrange("p (s r) -> p s r", r=2), in_=sv[:, :, 8:10]),
                 reads=[r_g], writes=[r_gsel])
            P.dma("sp", lambda e, gsel=gsel, f=f: e.dma_start(out=gT_o[f], in_=gsel), reads=[r_gsel], writes=[r_out], owner=r_gsel,
                  kind="out", final=True)
            P.op("pool", lambda e, sv=sv, f=f: e.tensor_copy(out=sv[:, :, 0:2], in_=cstT[:, f, :].rearrange("p (s r) -> p s r", r=2)),
                 reads=[r_cstT, r_gsel], writes=[r_g])
            P.op("dve", lambda e, g_sb=g_sb, acc=acc, f=f: e.tensor_scalar(out=acc, in0=g_sb[:, 2:NP], scalar1=cw[:, f, 2:3],
                                                                         scalar2=cb[:, f:f + 1], op0=ALU.mult, op1=ALU.add),
                 reads=[r_g, r_cw, r_cb], writes=[r_acc])
            P.op("dve", lambda e, g_sb=g_sb, acc=acc, f=f: e.scalar_tensor_tensor(out=acc, in0=g_sb[:, 1:NP - 1], scalar=cw[:, f, 1:2],
                                                                                in1=acc, op0=ALU.mult, op1=ALU.add),
                 reads=[r_g, r_cw, r_acc], writes=[r_acc])
            P.op("dve", lambda e, g_sb=g_sb, acc=acc, f=f: e.scalar_tensor_tensor(out=acc, in0=g_sb[:, 0:NP - 2], scalar=cw[:, f, 0:1],
                                                                                in1=acc, op0=ALU.mult, op1=ALU.add),
                 reads=[r_g, r_cw, r_acc], writes=[r_acc])
            P.op("act", lambda e, acc=acc: e.activation(out=acc, in_=acc, func=AF.Silu), reads=[r_acc], writes=[r_acc])
            P.op("dve", lambda e, acc=acc, u_sb=u_sb, a_sb=a_sb: e.tensor_tensor(out=a_sb.rearrange("p t k -> p (t k)"), in0=acc, in1=u_sb,
                                                                               op=ALU.mult), reads=[r_acc, r_u], writes=[r_a])
            P.dma("sp", lambda e, a_sb=a_sb, f=f: e.dma_start(out=aT_scr[:, :, f, :].rearrange("t p k -> p t k"), in_=a_sb),
                  reads=[r_a], writes=[r_aTscr], owner=r_a, kind="out")
    barrier(P)
    A.release(base_mark)

    wd_ring = A.ring("wd", [NFF, 512], BF16, 2)
    at_ring = A.ring("aTt", [NFF, 128], BF16, 2)
    hc_ring = A.ring("hch", [512], F32, 3)
    yc_ring = A.ring("ych", [512], F32, 3)
    wdv = w_down.rearrange("(f p) n -> p f n", p=128)
    for c4 in range(4):
        wd, r_wd = wd_ring.next()
        for f0 in range(0, NFF, 4):
            nf = min(4, NFF - f0)
            P.dma("pool", lambda e, wd=wd, f0=f0, nf=nf, c4=c4: e.dma_start(out=wd[:, f0:f0 + nf, :],
                                                                          in_=wdv[:, f0:f0 + nf, c4 * 512:(c4 + 1) * 512]),
                  writes=[r_wd], owner=r_wd)
        for t in range(NTQ):
            aTt, r_at = at_ring.next()
            hch, r_hc = hc_ring.next()
            ych, r_yc = yc_ring.next()
            P.dma("sp", lambda e, aTt=aTt, t=t: e.dma_start(out=aTt, in_=aT_scr[t]), reads=[r_aTscr], writes=[r_at], owner=r_at)
            P.dma("sp", lambda e, hch=hch, t=t, c4=c4: e.dma_start(out=hch, in_=rows(h_scr, t)[:, c4 * 512:(c4 + 1) * 512]),
                  reads=[r_hscr], writes=[r_hc], owner=r_hc)
            b = pG.next()
            for f in range(NFF):
                P.op("pe", lambda e, f=f, b=b, aTt=aTt, wd=wd: e.matmul(pb[b][:, :], lhsT=aTt[:, f, :], rhs=wd[:, f, :],
                                                                       start=(f == 0), stop=(f == NFF - 1)),
                     reads=[r_at, r_wd], writes=[r_pb[b]])
            P.op("dve", lambda e, b=b, hch=hch, ych=ych: e.tensor_tensor(out=ych, in0=pb[b][:, :], in1=hch, op=ALU.add),
                 reads=[r_pb[b], r_hc], writes=[r_yc])
            P.dma("sp", lambda e, ych=ych, t=t, c4=c4: e.dma_start(out=rows(y_o, t)[:, c4 * 512:(c4 + 1) * 512], in_=ych),
                  reads=[r_yc], writes=[r_out], owner=r_yc, kind="out", final=True)

    P.emit()
    P.close()
    return nc


def _rope_tab(pos):
    pos = np.asarray(pos, dtype=np.float32)
    out = np.zeros((pos.shape[0], 192), np.float32)
    inv128 = (10000.0 ** (-np.arange(64, dtype=np.float32) / 64)).astype(np.float32)
    inv64 = (10000.0 ** (-np.arange(32, dtype=np.float32) / 32)).astype(np.float32)
    a = pos[:, None] * inv128[None, :]
    out[:, 0:64] = np.cos(a)
    out[:, 64:128] = np.sin(a)
    a = pos[:, None] * inv64[None, :]
    out[:, 128:160] = np.cos(a)
    out[:, 160:192] = np.sin(a)
    return out


def _core_consts(j):
    f32 = np.float32
    p0 = 1024 * j - 2
    fmask = np.zeros((NT_Q, 128, 8, 128), f32)
    bbias = np.zeros((128, NT_Q, 32), f32)
    nm = np.full((NT_Q, 128, SEQ), NEG, f32)
    oh = np.zeros((128, NT_Q, 32), f32)
    kk = np.arange(128)
    for i in range(NT_Q):
        t = p0 + 128 * i + np.arange(128)
        for sl, kb in enumerate(MASK_KBS(i)):
            s = 128 * kb + kk
            fmask[i, :, sl, :] = ((s[:, None] <= t[None, :]) & (t[None, :] >= 0)).astype(f32)
        for kb in range(32):
            if 128 * kb > t[-1]:
                bbias[:, i, kb] = NEG
        s_all = np.arange(SEQ)
        vis = (s_all[None, :] <= t[:, None]) & (t[:, None] >= 0)
        nm[i][vis] = 0.0
        oh[:, i, min(31, 8 * j + i)] = 1.0
    return fmask, bbias, nm, oh


_NC_CACHE = {}


def kernel(x_prompt, x_sample, cache_fox_k, cache_fox_v, cache_fox_logf, cache_dsa_k, cache_dsa_v,
           cache_idx_k, state_ffn_conv, page_table, w_in, b_f, g_qa, g_ka, g_qb, g_kb, g_attn,
           w_out, g_ffn, w_gate, w_up, conv_w, conv_b, w_down, _dbg=False):
    f32 = np.float32
    x_prompt = np.asarray(x_prompt, f32)
    x_sample = np.asarray(x_sample, f32)
    w_in0 = np.asarray(w_in, f32)[0]
    o = np.cumsum([0, 1024, 1024, 1024, 8, 1024, 256, 256, 1024, 64, 16])
    qa, ka, va, fa, qb, kb, vb, qi, ki, wi = [slice(o[i], o[i + 1]) for i in range(10)]
    w_kv = np.ascontiguousarray(np.concatenate([w_in0[:, ka], w_in0[:, va], w_in0[:, kb], w_in0[:, vb],
                                                w_in0[:, ki], w_in0[:, fa]], axis=1))
    w_q = np.ascontiguousarray(np.concatenate([w_in0[:, qa], w_in0[:, qb], w_in0[:, qi], w_in0[:, wi]], axis=1))
    ident = np.eye(128, dtype=f32)
    tri = np.triu(np.ones((128, 128), f32))
    gA = np.ascontiguousarray(np.broadcast_to(np.asarray(g_attn, f32)[0][None, :], (128, D)))
    gF = np.ascontiguousarray(np.broadcast_to(np.asarray(g_ffn, f32)[0][None, :], (128, D)))
    g4 = np.ascontiguousarray(np.broadcast_to(
        np.stack([np.asarray(g_qa, f32)[0], np.asarray(g_ka, f32)[0], np.asarray(g_qb, f32)[0],
                  np.asarray(g_kb, f32)[0]])[None], (128, 4, 128)))
    bfr = np.ascontiguousarray(np.broadcast_to(np.asarray(b_f, f32)[0][None, :], (128, 8)))
    cw = np.ascontiguousarray(np.asarray(conv_w, f32)[0].reshape(3, NFF, 128).transpose(2, 1, 0))
    cb = np.ascontiguousarray(np.asarray(conv_b, f32)[0].reshape(NFF, 128).T)
    tabA = _rope_tab(np.arange(SEQ))
    spos = np.zeros(128, f32)
    for s in range(4):
        spos[32 * s + 2:32 * s + 10] = PAST + np.arange(8)
    shared = dict(w_kv=w_kv, w_q=w_q, w_out=np.ascontiguousarray(np.asarray(w_out, f32)[0]),
                  w_gate=np.ascontiguousarray(np.asarray(w_gate, f32)[0]), w_up=np.ascontiguousarray(np.asarray(w_up, f32)[0]),
                  w_down=np.ascontiguousarray(np.asarray(w_down, f32)[0]), ident=ident, tri=tri, gA=gA, gF=gF, g4=g4, bfr=bfr,
                  cw=cw, cb=cb, tabA=tabA)
    consts = [_core_consts(j) for j in range(4)]
    lst = np.zeros((128, 128), f32)
    o64 = np.zeros((128, 128), f32)
    for p_ in range(64):
        lst[p_, p_ + 1:] = 1.0
        o64[p_, :] = 1.0
    lblk = np.zeros((128, 128), f32)
    esel = np.zeros((128, 4, 128), f32)
    smask = np.zeros((128, 4, 32), f32)
    nms = np.full((128, 128), NEG, f32)
    for s in range(4):
        esel[32 * s + 9, s, :] = 1.0
        for i in range(8):
            for i2 in range(i + 1):
                lblk[32 * s + 2 + i2, 32 * s + 2 + i] = 1.0
                smask[32 * s + 2 + i2, s, 2 + i] = 1.0
                nms[32 * s + 2 + i, 32 * s + 2 + i2] = 0.0
    npool = np.asarray(cache_fox_k).shape[1]
    assert npool == NPOOL_PAGES
    shared.update(
        cfk=np.asarray(cache_fox_k, f32).reshape(npool * 128, 1024), cfv=np.asarray(cache_fox_v, f32).reshape(npool * 128, 1024),
        cfl=np.asarray(cache_fox_logf, f32).reshape(npool, 1024), cdk=np.asarray(cache_dsa_k, f32).reshape(npool * 128, 256),
        cdv=np.asarray(cache_dsa_v, f32).reshape(npool * 128, 256), cik=np.asarray(cache_idx_k, f32).reshape(npool * 128, 64),
        lst=lst, o64=o64, lblk=lblk, esel=esel, smask=smask, nms=nms)
    page_table = np.asarray(page_table, np.int32)
    NTQ = NT_Q + 1
    in_maps = []
    for c in range(8):
        b, j = c // 4, c % 4
        p0 = 1024 * j - 2
        xq = np.zeros((NTQ * 128, D), f32)
        lo, hi = max(p0, 0), min(p0 + NT_Q * 128, SEQ)
        xq[lo - p0:hi - p0] = x_prompt[b, lo:hi]
        for s in range(4):
            xq[NT_Q * 128 + 32 * s + 2:NT_Q * 128 + 32 * s + 10] = x_sample[4 * c + s]
        tabQ = _rope_tab(np.concatenate([np.clip(p0 + np.arange(NT_Q * 128), 0, SEQ - 1), spos]))
        fmask, bbias, nm, oh = consts[j]
        cst = np.ascontiguousarray(np.asarray(state_ffn_conv, f32)[0, 4 * c:4 * c + 4].reshape(8, D_FF))
        m = dict(shared)
        m.update(xb=np.ascontiguousarray(x_prompt[b]), xq=xq, tabQ=tabQ, fmask=fmask, bbias=bbias, nm=nm, oh=oh, cst=cst,
                 pt=np.ascontiguousarray(page_table[4 * c:4 * c + 4]))
        in_maps.append(m)

    key = bool(_dbg)
    if key not in _NC_CACHE:
        _NC_CACHE[key] = build_program(dbg=key)
    nc = _NC_CACHE[key]
    res = run_bass_kernel_spmd(nc, in_maps, core_ids=list(range(8)))
    R = res.results

    def prow(name, shape):
        return np.stack([R[0][name], R[4][name]]).reshape((1, 2, SEQ) + shape)

    def srow(name, shape):
        out = np.zeros((32, 8) + shape, f32)
        for c in range(8):
            a = R[c][name]
            for s in range(4):
                out[4 * c + s] = a[32 * s + 2:32 * s + 10].reshape((8,) + shape)
        return out[None]

    y_p = np.zeros((2, SEQ, D), f32)
    y_s = np.zeros((32, 8, D), f32)
    conv_p = np.zeros((1, 2, 2, D_FF), f32)
    conv_s = np.zeros((1, 32, 2, D_FF), f32)
    for c in range(8):
        b, j = c // 4, c % 4
        yo = R[c]["y_o"]
        y_p[b, 1024 * j:1024 * (j + 1)] = yo[2:1026]
        gt = R[c]["gT_o"]
        for s in range(4):
            y_s[4 * c + s] = yo[NT_Q * 128 + 32 * s + 2:NT_Q * 128 + 32 * s + 10]
            for r in range(2):
                conv_s[0, 4 * c + s, r] = gt[:, :, 2 + 2 * s + r].reshape(D_FF)
        if j == 3:
            for r in range(2):
                conv_p[0, b, r] = gt[:, :, r].reshape(D_FF)
    outs = (y_p, y_s,
            prow("o_fk", (8, 128)), prow("o_fv", (8, 128)), prow("o_fl", (8,)),
            prow("o_dk", (2, 128)), prow("o_dv", (2, 128)), prow("o_ik", (64,)), conv_p,
            srow("s_fk", (8, 128)), srow("s_fv", (8, 128)), srow("s_fl", (8,)),
            srow("s_dk", (2, 128)), srow("s_dv", (2, 128)), srow("s_ik", (64,)), conv_s)
    if _dbg:
        return outs, R
    return outs
```

```python
from contextlib import ExitStack
import numpy as np
import concourse.bass as bass
import concourse.mybir as mybir
from concourse.bass_utils import run_bass_kernel_spmd

F32 = mybir.dt.float32
BF16 = mybir.dt.bfloat16
I32 = mybir.dt.int32
U32 = mybir.dt.uint32
AF = mybir.ActivationFunctionType
ALU = mybir.AluOpType
AX = mybir.AxisListType

D = 2048
HD = 128
H_A = 8
H_B = 8
KV_B = 2
H_IDX = 16
D_IDX = 64
D_FF = 5504
NFF = D_FF // 128
SEQ = 4096
PAST = 8192
NPG = 64
EPS = 1e-6
SCALE = HD ** -0.5
IDX_SCALE = (H_IDX * D_IDX) ** -0.5
NEG = -1.0e30
N_KV = 2632
N_Q = 3088
NT_Q = 9
ENGS = ("pe", "dve", "act", "pool", "sp")


class Res:
    __slots__ = ("name", "w", "r", "k_in", "k_out")

    def __init__(self, name):
        self.name = name
        self.w = None
        self.r = []
        self.k_in = None
        self.k_out = None


class Prog:
    def __init__(self, nc):
        self.nc = nc
        self.stack = ExitStack()
        self.q = {e: [] for e in ENGS}
        self.cnt = {}
        self.waited = {e: {} for e in ENGS}
        self.nres = 0
        self.final_waits = {}
        self.n_ops = 0
        self.phys_of = {}
        self.phys_cnt = []
        self.free_phys = []
        self.active_dma = []

    def sbuf(self, name, shape, dtype):
        return self.stack.enter_context(self.nc.sbuf_tensor("sb_" + name, list(shape), dtype))

    def psum(self, name, shape, dtype):
        return self.stack.enter_context(self.nc.psum_tensor("ps_" + name, list(shape), dtype))

    def res(self, name=None):
        self.nres += 1
        return Res(name or f"r{self.nres}")

    def _need(self, eng, reads, writes):
        ev = {}

        def add(e):
            if e is None:
                return
            k, v = e
            if ev.get(k, 0) < v:
                ev[k] = v
        for r in reads:
            add(r.w)
        for r in writes:
            add(r.w)
            for e in r.r:
                add(e)
        out = []
        wd = self.waited[eng]
        for k, v in ev.items():
            if eng == "pe" and k == "E:pe":
                continue
            if wd.get(k, 0) >= v:
                continue
            wd[k] = v
            out.append((k, v))
        return out

    def _commit(self, ev, reads, writes):
        for r in reads:
            r.r.append(ev)
            if len(r.r) > 64:
                best = {}
                for k, v in r.r:
                    if best.get(k, 0) < v:
                        best[k] = v
                r.r = list(best.items())
        for r in writes:
            r.w = ev
            r.r = []

    def op(self, eng, fn, reads=(), writes=()):
        waits = self._need(eng, reads, writes)
        k = "E:" + eng
        self.cnt[k] = self.cnt.get(k, 0) + 1
        ev = (k, self.cnt[k])
        self.q[eng].append((waits, fn, k, 1))
        self._commit(ev, reads, writes)
        self.n_ops += 1
        return ev

    def dma(self, queue, fn, reads=(), writes=(), owner=None, kind="in", final=False):
        waits = self._need(queue, reads, writes)
        if kind == "in":
            if owner.k_in is None:
                owner.k_in = self._new_dma_key(owner, "in")
            k = owner.k_in
        else:
            if owner.k_out is None:
                owner.k_out = self._new_dma_key(owner, "out")
            k = owner.k_out
        self.cnt[k] = self.cnt[k] + 16
        ev = (k, self.cnt[k])
        self.q[queue].append((waits, fn, k, 16))
        self._commit(ev, reads, writes)
        if final:
            self.final_waits[k] = self.cnt[k]
        self.n_ops += 1
        return ev

    def _phys(self, k):
        if k not in self.phys_of:
            self.phys_of[k] = len(self.phys_cnt)
            self.phys_cnt.append(0)
        return self.phys_of[k]

    def _new_dma_key(self, owner, kind):
        self.nres += 1
        k = f"D:{kind}:{owner.name}:{self.nres}"
        if self.free_phys:
            p = self.free_phys.pop()
        else:
            p = len(self.phys_cnt)
            self.phys_cnt.append(0)
        self.phys_of[k] = p
        self.cnt[k] = self.phys_cnt[p]
        self.active_dma.append((owner, kind, k))
        return k

    def retire_dma_keys(self):
        for owner, kind, k in self.active_dma:
            p = self.phys_of[k]
            self.phys_cnt[p] = self.cnt[k]
            self.free_phys.append(p)
            if kind == "in":
                owner.k_in = None
            else:
                owner.k_out = None
        self.active_dma = []
        self.final_waits = {}

    def emit(self):
        nc = self.nc
        for k in self.cnt:
            self._phys(k)
        psems = [self.stack.enter_context(nc.semaphore(f"s{i}")) for i in range(len(self.phys_cnt))]
        sems = {k: psems[self.phys_of[k]] for k in self.cnt}
        block = self.stack.enter_context(nc.Block())
        q = self.q
        final_waits = self.final_waits

        def run(name, e):
            for waits, fn, k, amt in q[name]:
                for (wk, wv) in waits:
                    e.wait_ge(sems[wk], wv)
                if fn is None:
                    continue
                fn(e).then_inc(sems[k], amt)
            if name == "sp":
                for k, v in final_waits.items():
                    e.wait_ge(sems[k], v)

        @block.tensor
        def _(e):
            run("pe", e)

        @block.vector
        def _(e):
            run("dve", e)

        @block.scalar
        def _(e):
            run("act", e)

        @block.gpsimd
        def _(e):
            run("pool", e)

        @block.sync
        def _(e):
            run("sp", e)

    def close(self):
        self.stack.close()


class Ring:
    def __init__(self, P, name, shape, dtype, n, psum=False):
        self.t = []
        self.r = []
        for i in range(n):
            t = P.psum(f"{name}{i}", shape, dtype) if psum else P.sbuf(f"{name}{i}", shape, dtype)
            self.t.append(t)
            self.r.append(P.res(f"{name}{i}"))
        self.i = 0
        self.n = n

    def next(self):
        i = self.i % self.n
        self.i += 1
        return self.t[i], self.r[i]


def bc(ap, shape):
    return ap.to_broadcast(list(shape))


NPOOL_PAGES = 2560
AW = 52992


def KB_OF(i):
    return min(32, 25 + i)


def MASK_KBS(i):
    return [kb for kb in range(KB_OF(i)) if (kb - i) % 8 in (0, 7)]


class Arena:
    def __init__(self, P):
        self.P = P
        self.t = P.sbuf("arena", [128, AW], F32)
        self.top = 0

    def mark(self):
        return self.top

    def release(self, m):
        self.top = m

    def alloc(self, name, free_shape, dtype=F32):
        n = 1
        for d_ in free_shape:
            n *= d_
        w = n if dtype in (F32, I32, U32) else (n + 1) // 2
        w = (w + 7) // 8 * 8
        off = self.top
        self.top += w
        assert self.top <= AW, f"arena overflow at {name}: {self.top}"
        v = self.t[:, off:off + w]
        if dtype != F32:
            v = v.bitcast(dtype)
        v = v[:, 0:n]
        if len(free_shape) == 2:
            v = v.rearrange("p (a b) -> p a b", b=free_shape[1])
        elif len(free_shape) == 3:
            v = v.rearrange("p (a b c) -> p a b c", b=free_shape[1], c=free_shape[2])
        elif len(free_shape) == 4:
            v = v.rearrange("p (a b c d) -> p a b c d", b=free_shape[1], c=free_shape[2], d=free_shape[3])
        return v, self.P.res(name)

    def ring(self, name, free_shape, dtype, n):
        return VRing([self.alloc(f"{name}{i}", free_shape, dtype) for i in range(n)])


class K:
    pass


class VRing:
    def __init__(self, items):
        self.items = items
        self.i = 0

    def next(self):
        it = self.items[self.i % len(self.items)]
        self.i += 1
        return it


def barrier(P):
    waits = []
    for k, v in P.cnt.items():
        if k == "B:bar":
            continue
        if P.waited["sp"].get(k, 0) < v:
            P.waited["sp"][k] = v
            waits.append((k, v))
    P.cnt["B:bar"] = P.cnt.get("B:bar", 0) + 1
    n = P.cnt["B:bar"]
    P.q["sp"].append((waits, lambda e: e.nop(), "B:bar", 1))
    for eng in ENGS:
        if eng == "sp":
            continue
        P.q[eng].append(([("B:bar", n)], None, None, 0))
        P.waited[eng]["B:bar"] = n
    for eng in ENGS:
        for k, v in P.cnt.items():
            if k != "B:bar":
                P.waited[eng][k] = v
    P.retire_dma_keys()


def build_program(dbg=False):
    nc = bass.Bass("TRN2", target_bir_lowering=False)
    P = Prog(nc)

    def din(name, shape, dt=F32):
        return nc.dram_tensor(name, list(shape), dt, kind="ExternalInput").ap()

    def dout(name, shape, dt=F32):
        return nc.dram_tensor(name, list(shape), dt, kind="ExternalOutput").ap()

    def dscr(name, shape, dt):
        return nc.dram_tensor(name, list(shape), dt).ap()

    NTQ = NT_Q + 1
    NTOK = NTQ * 128
    xb = din("xb", [SEQ, D])
    xq = din("xq", [NTOK, D])
    w_kv = din("w_kv", [D, N_KV])
    w_q = din("w_q", [D, N_Q])
    w_out = din("w_out", [D, D])
    w_gate = din("w_gate", [D, D_FF])
    w_up = din("w_up", [D, D_FF])
    w_down = din("w_down", [D_FF, D])
    ident_d = din("ident", [128, 128])
    tri_d = din("tri", [128, 128])
    gA_d = din("gA", [128, D])
    gF_d = din("gF", [128, D])
    g4_d = din("g4", [128, 4, 128])
    bf_d = din("bfr", [128, 8])
    cw_d = din("cw", [128, NFF, 3])
    cb_d = din("cb", [128, NFF])
    tabA = din("tabA", [SEQ, 192])
    tabQ = din("tabQ", [NTOK, 192])
    fmask_d = din("fmask", [NT_Q, 128, 8, 128])
    bbias_d = din("bbias", [128, NT_Q, 32])
    nm_d = din("nm", [NT_Q, 128, SEQ])
    oh_d = din("oh", [128, NT_Q, 32])
    cst_d = din("cst", [8, D_FF])
    NPOOLR = NPOOL_PAGES * 128
    cfk = din("cfk", [NPOOLR, 1024])
    cfv = din("cfv", [NPOOLR, 1024])
    cfl = din("cfl", [NPOOL_PAGES, 1024])
    cdk = din("cdk", [NPOOLR, 256])
    cdv = din("cdv", [NPOOLR, 256])
    cik = din("cik", [NPOOLR, 64])
    pt_d = din("pt", [4, NPG], I32)
    lst_d = din("lst", [128, 128])
    o64_d = din("o64", [128, 128])
    lblk_d = din("lblk", [128, 128])
    esel_d = din("esel", [128, 4, 128])
    smask_d = din("smask", [128, 4, 32])
    nms_d = din("nms", [128, 128])
    o_fk = dout("o_fk", [SEQ, 1024])
    o_fv = dout("o_fv", [SEQ, 1024])
    o_fl = dout("o_fl", [SEQ, 8])
    o_dk = dout("o_dk", [SEQ, 256])
    o_dv = dout("o_dv", [SEQ, 256])
    o_ik = dout("o_ik", [SEQ, 64])
    s_fk = dout("s_fk", [128, 1024])
    s_fv = dout("s_fv", [128, 1024])
    s_fl = dout("s_fl", [128, 8])
    s_dk = dout("s_dk", [128, 256])
    s_dv = dout("s_dv", [128, 256])
    s_ik = dout("s_ik", [128, 64])
    y_o = dout("y_o", [NTOK, D])
    gT_o = dout("gT_o", [NFF, 128, 16])
    if dbg:
        dbg_att = dout("dbg_att", [NTQ, 128, D], BF16)
        dbg_h = dout("dbg_h", [NTOK, D])
    KaT = dscr("KaT", [8, 128, SEQ], BF16)
    VaE = dscr("VaE", [8, 128, 32, 129], BF16)
    KbT = dscr("KbT", [2, 128, SEQ], BF16)
    VbE = dscr("VbE", [2, 128, 32, 129], BF16)
    KiT = dscr("KiT", [64, SEQ], BF16)
    KaTs = dscr("KaTs", [8, 128, 128], BF16)
    VaEs = dscr("VaEs", [8, 128, 1, 129], BF16)
    KbTs = dscr("KbTs", [2, 128, 128], BF16)
    VbEs = dscr("VbEs", [2, 128, 1, 129], BF16)
    KiTs = dscr("KiTs", [64, 128], BF16)
    QaT = dscr("QaT", [NTQ, 128, 8, 128], BF16)
    QbT = dscr("QbT", [NTQ, 128, 8, 128], BF16)
    QiT = dscr("QiT", [NTQ, 64, 16, 128], BF16)
    h_scr = dscr("h_scr", [NTOK, D], F32)
    aT_scr = dscr("aT_scr", [NTQ, 128, NFF, 128], BF16)
    r_KaT, r_VaE, r_KbT, r_VbE, r_KiT = (P.res(n) for n in ("KaT", "VaE", "KbT", "VbE", "KiT"))
    r_s = [P.res(n) for n in ("KaTs", "VaEs", "KbTs", "VbEs", "KiTs")]
    r_QaT, r_QbT, r_QiT = P.res("QaT"), P.res("QbT"), P.res("QiT")
    r_hscr, r_aTscr = P.res("h_scr"), P.res("aT_scr")
    r_out = P.res("outputs")

    A = Arena(P)
    pb = [P.psum(f"bank{i}", [128, 512], F32) for i in range(8)]
    r_pb = [P.res(f"bank{i}") for i in range(8)]

    def bank_bf(i):
        return pb[i][:, :].bitcast(BF16).rearrange("p (a b) -> p a b", b=128)

    class BankRing:
        def __init__(self, ids):
            self.ids = ids
            self.i = 0

        def next(self):
            b = self.ids[self.i % len(self.ids)]
            self.i += 1
            return b

    identf, r_identf = A.alloc("identf", [128], F32)
    identb, r_identb = A.alloc("identb", [128], BF16)
    tri, r_tri = A.alloc("tri", [128], F32)
    ones, r_ones = A.alloc("ones", [128], F32)
    g4, r_g4 = A.alloc("g4", [4, 128], F32)
    bfr, r_bfr = A.alloc("bfr", [8], F32)
    lfall, r_lfall = A.alloc("lfall", [32, 8], F32)
    wi_all, r_wi = A.alloc("wi_all", [NTQ, 16], F32)
    lfs, r_lfs = A.alloc("lfs", [8], F32)
    P.dma("sp", lambda e: e.dma_start(out=identf, in_=ident_d[:, :]), writes=[r_identf], owner=r_identf)
    P.dma("sp", lambda e: e.dma_start(out=tri, in_=tri_d[:, :]), writes=[r_tri], owner=r_tri)
    P.dma("sp", lambda e: e.dma_start(out=g4, in_=g4_d[:, :, :]), writes=[r_g4], owner=r_g4)
    P.dma("sp", lambda e: e.dma_start(out=bfr, in_=bf_d[:, :]), writes=[r_bfr], owner=r_bfr)
    P.op("dve", lambda e: e.tensor_copy(out=identb, in_=identf), reads=[r_identf], writes=[r_identb])
    P.op("pool", lambda e: e.memset(ones, 1.0), writes=[r_ones])
    base_mark = A.mark()

    def make_proj(G, gvec_d, slim=False):
        m = K()
        m.G = G
        m.gvec, m.r_gvec = A.alloc("gvec", [D], F32)
        P.dma("sp", lambda e: e.dma_start(out=m.gvec, in_=gvec_d[:, :]), writes=[m.r_gvec], owner=m.r_gvec)
        m.junk, m.r_junk = A.alloc("junk", [D], BF16)
        m.junk2, m.r_junk2 = A.alloc("junk2", [128], BF16)
        m.xT, _ = A.alloc("xT", [G, 16, 128], BF16)
        m.r_xT = [P.res(f"xT{i}") for i in range(G)]
        m.ss, _ = A.alloc("ss_t", [G], F32)
        m.rstd, _ = A.alloc("rstd_t", [G], F32)
        m.r_ss = [P.res(f"ss{i}") for i in range(G)]
        m.r_rstd = [P.res(f"rstd{i}") for i in range(G)]
        m.pT = BankRing([0, 1])
        m.pM = BankRing([2, 3])
        if slim:
            return m
        m.xf = A.ring("xf", [D], F32, 2)
        m.xbf = A.ring("xbf", [D], BF16, 2)
        m.tab, _ = A.alloc("tab", [G, 192], F32)
        m.r_tab = [P.res(f"tab{i}") for i in range(G)]
        m.w = A.ring("wch", [16, 512], BF16, 2)
        m.kf = A.ring("kf", [4, 128], F32, 4)
        m.kn = A.ring("kn", [4, 128], F32, 3)
        m.ko = A.ring("ko", [4, 128], F32, 3)
        m.kbb = A.ring("kbb", [4, 128], BF16, 3)
        m.st = A.ring("ktst", [8, 128], BF16, 2)
        m.ve = A.ring("ve", [4, 129], BF16, 2)
        m.sm = A.ring("sm", [16], F32, 6)
        m.ra = A.ring("ra", [4, 64], F32, 2)
        m.rb = A.ring("rb", [4, 64], F32, 2)
        m.lf = A.ring("lf", [8], F32, 2)
        m.lg = A.ring("lg", [8], F32, 4)
        m.pT = BankRing([0, 1])
        m.pM = BankRing([2, 3])
        for t_, r_ in m.ve.items:
            P.op("pool", lambda e, t_=t_: e.memset(t_[:, :, 128:129], 1.0), writes=[r_])
        return m

    def load_x_tile(m, src_rows_ap, slot, tab_rows_ap):
        xf, r_xf = m.xf.next()
        xbf, r_xbf = m.xbf.next()
        P.dma("sp", lambda e: e.dma_start(out=xf, in_=src_rows_ap), writes=[r_xf], owner=r_xf)
        if tab_rows_ap is not None:
            P.dma("sp", lambda e: e.dma_start(out=m.tab[:, slot, :], in_=tab_rows_ap), writes=[m.r_tab[slot]],
                  owner=m.r_tab[slot])
        norm_to_xT(m, xf, r_xf, xbf, r_xbf, slot)

    def norm_to_xT(m, xf, r_xf, xbf, r_xbf, slot):
        P.op("act", lambda e: e.activation(out=m.junk, in_=xf, func=AF.Square, accum_out=m.ss[:, slot:slot + 1]),
             reads=[r_xf], writes=[m.r_junk, m.r_ss[slot]])
        P.op("act", lambda e: e.activation(out=m.ss[:, slot:slot + 1], in_=m.ss[:, slot:slot + 1], func=AF.Ln,
                                           scale=1.0 / D, bias=EPS), reads=[m.r_ss[slot]], writes=[m.r_ss[slot]])
        P.op("act", lambda e: e.activation(out=m.rstd[:, slot:slot + 1], in_=m.ss[:, slot:slot + 1], func=AF.Exp,
                                           scale=-0.5), reads=[m.r_ss[slot]], writes=[m.r_rstd[slot]])
        P.op("dve", lambda e: e.tensor_tensor(out=xbf, in0=xf, in1=m.gvec, op=ALU.mult),
             reads=[r_xf, m.r_gvec], writes=[r_xbf])
        for g in range(2):
            b = m.pT.next()
            pt = bank_bf(b)
            for jj in range(8):
                kc = g * 8 + jj
                P.op("pe", lambda e, kc=kc, jj=jj, pt=pt: e.transpose(out=pt[:, jj, :], in_=xbf[:, kc * 128:(kc + 1) * 128],
                                                                      identity=identb),
                     reads=[r_xbf, r_identb], writes=[r_pb[b]])
            if g == 0:
                P.op("dve", lambda e, g=g, pt=pt: e.tensor_copy(out=m.xT[:, slot, g * 8:(g + 1) * 8, :], in_=pt),
                     reads=[r_pb[b]], writes=[m.r_xT[slot]])
            else:
                P.op("act", lambda e, g=g, pt=pt: e.activation(out=m.xT[:, slot, g * 8:(g + 1) * 8, :], in_=pt, func=AF.Copy),
                     reads=[r_pb[b]], writes=[m.r_xT[slot]])

    def load_w_chunk(m, wd, col0, ncols, nk=16):
        wt, r_w = m.w.next()
        wv = wd.rearrange("(kc p) n -> p kc n", p=128)
        for q4 in range(0, nk, 4):
            P.dma("pool", lambda e, q4=q4: e.dma_start(out=wt[:, q4:q4 + 4, 0:ncols],
                                                        in_=wv[:, q4:q4 + 4, col0:col0 + ncols]),
                  writes=[r_w], owner=r_w)
        return wt, r_w

    def project(m, slot, wt, r_w, ncols):
        b = m.pM.next()
        pm = pb[b]
        for kc in range(16):
            P.op("pe", lambda e, kc=kc: e.matmul(pm[:, 0:ncols], lhsT=m.xT[:, slot, kc, :], rhs=wt[:, kc, 0:ncols],
                                                 start=(kc == 0), stop=(kc == 15)),
                 reads=[m.r_xT[slot], r_w], writes=[r_pb[b]])
        return pm, r_pb[b]

    def rstd_of(ssq_ap, r_in, inv_n):
        P.op("act", lambda e: e.activation(out=ssq_ap, in_=ssq_ap, func=AF.Ln, scale=inv_n, bias=EPS),
             reads=[r_in], writes=[r_in])
        P.op("act", lambda e: e.activation(out=ssq_ap, in_=ssq_ap, func=AF.Exp, scale=-0.5),
             reads=[r_in], writes=[r_in])

    def head_norm_early(m, pm_ap, r_pm, slot, nh):
        kf, r_kf = m.kf.next()
        sm, r_sm = m.sm.next()
        P.op("act", lambda e: e.activation(out=kf[:, 0:nh, :], in_=pm_ap, func=AF.Copy, scale=m.rstd[:, slot:slot + 1]),
             reads=[r_pm, m.r_rstd[slot]], writes=[r_kf])
        for h in range(nh):
            P.op("act", lambda e, h=h: e.activation(out=m.junk2, in_=kf[:, h, :], func=AF.Square,
                                                    accum_out=sm[:, h:h + 1]),
                 reads=[r_kf], writes=[m.r_junk2, r_sm])
        rstd_of(sm[:, 0:nh], r_sm, 1.0 / HD)
        return (kf, r_kf, sm, r_sm)

    def head_norm_late(m, ctx, nh, g_idx):
        kf, r_kf, sm, r_sm = ctx
        kn, r_kn = m.kn.next()
        P.op("dve", lambda e: e.tensor_tensor(out=kn[:, 0:nh, :], in0=kf[:, 0:nh, :],
                                              in1=bc(sm[:, 0:nh].unsqueeze(2), [128, nh, 128]), op=ALU.mult),
             reads=[r_kf, r_sm], writes=[r_kn])
        P.op("pool", lambda e: e.tensor_tensor(out=kn[:, 0:nh, :], in0=kn[:, 0:nh, :],
                                               in1=bc(g4[:, g_idx, :].unsqueeze(1), [128, nh, 128]), op=ALU.mult),
             reads=[r_kn, r_g4], writes=[r_kn])
        return kn, r_kn

    class Defer:
        def __init__(self):
            self.q = []

        def late(self, fn):
            self.q.append(fn)

        def run(self):
            q, self.q = self.q, []
            for fn in q:
                fn()

    def rope(m, src, r_src, nh, hd, slot, tab_off, dst, r_dst):
        half = hd // 2
        ra, r_ra = m.ra.next()
        rb, r_rb = m.rb.next()
        ra = ra.rearrange("p a b -> p (a b)")[:, 0:nh * half].rearrange("p (a b) -> p a b", b=half)
        rb = rb.rearrange("p a b -> p (a b)")[:, 0:nh * half].rearrange("p (a b) -> p a b", b=half)
        cos = bc(m.tab[:, slot, tab_off:tab_off + half].unsqueeze(1), [128, nh, half])
        sin = bc(m.tab[:, slot, tab_off + half:tab_off + 2 * half].unsqueeze(1), [128, nh, half])
        x1 = src[:, 0:nh, 0:half]
        x2 = src[:, 0:nh, half:hd]
        rt = m.r_tab[slot]
        P.op("dve", lambda e: e.tensor_tensor(out=ra, in0=x1, in1=cos, op=ALU.mult), reads=[r_src, rt], writes=[r_ra])
        P.op("pool", lambda e: e.tensor_tensor(out=rb, in0=x2, in1=sin, op=ALU.mult), reads=[r_src, rt], writes=[r_rb])
        P.op("dve", lambda e: e.tensor_tensor(out=dst[:, 0:nh, 0:half], in0=ra, in1=rb, op=ALU.subtract),
             reads=[r_ra, r_rb], writes=[r_dst])
        P.op("dve", lambda e: e.tensor_tensor(out=ra, in0=x2, in1=cos, op=ALU.mult), reads=[r_src, rt, r_dst], writes=[r_ra])
        P.op("pool", lambda e: e.tensor_tensor(out=rb, in0=x1, in1=sin, op=ALU.mult), reads=[r_src, rt, r_dst], writes=[r_rb])
        P.op("dve", lambda e: e.tensor_tensor(out=dst[:, 0:nh, half:hd], in0=ra, in1=rb, op=ALU.add),
             reads=[r_ra, r_rb], writes=[r_dst])

    def transposes_bf(m, src_fn, r_srcb, nh, rows_d, dst, r_dst):
        b = m.pT.next()
        pt = bank_bf(b)
        for h in range(nh):
            P.op("pe", lambda e, h=h: e.transpose(out=pt[0:rows_d, h, :], in_=src_fn(h), identity=identb),
                 reads=[r_srcb, r_identb], writes=[r_pb[b]])
        P.op("act", lambda e: e.activation(out=dst, in_=pt[0:rows_d, 0:nh, :], func=AF.Copy), reads=[r_pb[b]], writes=[r_dst])

    def flat(t):
        return t.rearrange("p h d -> p (h d)")

    def kv_pass(m, n_tiles, x_rows, tab_rows, dst):
        G = m.G
        DF = Defer()
        for g0 in range(0, n_tiles, G):
            gt = min(G, n_tiles - g0)
            for s in range(gt):
                load_x_tile(m, x_rows(g0 + s), s, tab_rows(g0 + s))
            for c in range(6):
                col0 = c * 512
                ncols = 512 if c < 5 else 72
                wt, r_w = load_w_chunk(m, w_kv, col0, ncols)
                for s in range(gt):
                    t = g0 + s
                    pm, r_pm = project(m, s, wt, r_w, ncols)
                    if c in (0, 1):
                        ctx = head_norm_early(m, pm[:, 0:512], r_pm, s, 4)

                        def late(ctx=ctx, t=t, c=c):
                            kn, r_kn = head_norm_late(m, ctx, 4, 1)
                            P.dma("sp", lambda e: e.dma_start(out=dst["fk"](t)[:, c * 512:(c + 1) * 512], in_=flat(kn)),
                                  reads=[r_kn], writes=[r_out], owner=r_kn, kind="out", final=True)
                            kbb, r_kbb = m.kbb.next()
                            P.op("pool", lambda e: e.tensor_copy(out=kbb, in_=kn), reads=[r_kn], writes=[r_kbb])
                            st, r_st = m.st.next()
                            transposes_bf(m, lambda h: kbb[:, h, :], r_kbb, 4, 128, st[:, 0:4, :], r_st)
                            P.dma("sp", lambda e: e.dma_start(out=dst["KaT"](t, c), in_=st[:, 0:4, :]),
                                  reads=[r_st], writes=[dst["r_KaT"]], owner=r_st, kind="out")
                    elif c in (2, 3):
                        kf, r_kf = m.kf.next()
                        P.op("act", lambda e, kf=kf, pm=pm, s=s: e.activation(out=flat(kf), in_=pm[:, 0:512],
                                                                              func=AF.Copy, scale=m.rstd[:, s:s + 1]),
                             reads=[r_pm, m.r_rstd[s]], writes=[r_kf])

                        def late(kf=kf, r_kf=r_kf, t=t, c=c):
                            P.dma("sp", lambda e: e.dma_start(out=dst["fv"](t)[:, (c - 2) * 512:(c - 1) * 512], in_=flat(kf)),
                                  reads=[r_kf], writes=[r_out], owner=r_kf, kind="out", final=True)
                            ve, r_ve = m.ve.next()
                            P.op("dve", lambda e: e.tensor_copy(out=ve[:, :, 0:128], in_=kf), reads=[r_kf], writes=[r_ve])
                            P.dma("sp", lambda e: e.dma_start(out=dst["VaE"](t, c - 2), in_=ve),
                                  reads=[r_ve], writes=[dst["r_VaE"]], owner=r_ve, kind="out")
                    elif c == 4:
                        ctx = head_norm_early(m, pm[:, 0:256], r_pm, s, 2)
                        kf, r_kf = m.kf.next()
                        P.op("act", lambda e, kf=kf, pm=pm, s=s: e.activation(out=flat(kf[:, 0:2, :]), in_=pm[:, 256:512],
                                                                              func=AF.Copy, scale=m.rstd[:, s:s + 1]),
                             reads=[r_pm, m.r_rstd[s]], writes=[r_kf])

                        def late(ctx=ctx, kf=kf, r_kf=r_kf, t=t, s=s):
                            kn, r_kn = head_norm_late(m, ctx, 2, 3)
                            ko, r_ko = m.ko.next()
                            rope(m, kn, r_kn, 2, 128, s, 0, ko, r_ko)
                            P.dma("sp", lambda e: e.dma_start(out=dst["dk"](t), in_=flat(ko[:, 0:2, :])),
                                  reads=[r_ko], writes=[r_out], owner=r_ko, kind="out", final=True)
                            kbb, r_kbb = m.kbb.next()
                            P.op("pool", lambda e: e.tensor_copy(out=kbb[:, 0:2, :], in_=ko[:, 0:2, :]), reads=[r_ko], writes=[r_kbb])
                            st, r_st = m.st.next()
                            transposes_bf(m, lambda h: kbb[:, h, :], r_kbb, 2, 128, st[:, 0:2, :], r_st)
                            P.dma("sp", lambda e: e.dma_start(out=dst["KbT"](t), in_=st[:, 0:2, :]),
                                  reads=[r_st], writes=[dst["r_KbT"]], owner=r_st, kind="out")
                            P.dma("sp", lambda e: e.dma_start(out=dst["dv"](t), in_=flat(kf[:, 0:2, :])),
                                  reads=[r_kf], writes=[r_out], owner=r_kf, kind="out", final=True)
                            ve, r_ve = m.ve.next()
                            P.op("dve", lambda e: e.tensor_copy(out=ve[:, 0:2, 0:128], in_=kf[:, 0:2, :]), reads=[r_kf], writes=[r_ve])
                            P.dma("sp", lambda e: e.dma_start(out=dst["VbE"](t), in_=ve[:, 0:2, :]),
                                  reads=[r_ve], writes=[dst["r_VbE"]], owner=r_ve, kind="out")
                    else:
                        kf, r_kf = m.kf.next()
                        P.op("act", lambda e, kf=kf, pm=pm, s=s: e.activation(out=kf[:, 0, 0:72], in_=pm[:, 0:72],
                                                                              func=AF.Copy, scale=m.rstd[:, s:s + 1]),
                             reads=[r_pm, m.r_rstd[s]], writes=[r_kf])

                        def late(kf=kf, r_kf=r_kf, t=t, s=s):
                            ko, r_ko = m.ko.next()
                            rope(m, kf, r_kf, 1, 64, s, 128, ko, r_ko)
                            P.dma("sp", lambda e: e.dma_start(out=dst["ik"](t), in_=ko[:, 0, 0:64]),
                                  reads=[r_ko], writes=[r_out], owner=r_ko, kind="out", final=True)
                            kbb, r_kbb = m.kbb.next()
                            P.op("pool", lambda e: e.tensor_copy(out=kbb[:, 0, 0:64], in_=ko[:, 0, 0:64]), reads=[r_ko], writes=[r_kbb])
                            st, r_st = m.st.next()
                            transposes_bf(m, lambda h: kbb[:, 0, 0:64], r_kbb, 1, 64, st[0:64, 0:1, :], r_st)
                            P.dma("sp", lambda e: e.dma_start(out=dst["KiT"](t), in_=st[0:64, 0, :]),
                                  reads=[r_st], writes=[dst["r_KiT"]], owner=r_st, kind="out")
                            lf, r_lf = m.lf.next()
                            l1, r_l1 = m.lg.next()
                            l2, r_l2 = m.lg.next()
                            P.op("dve", lambda e: e.tensor_tensor(out=lf, in0=kf[:, 0, 64:72], in1=bfr, op=ALU.add),
                                 reads=[r_kf, r_bfr], writes=[r_lf])
                            P.op("dve", lambda e: e.scalar_tensor_tensor(out=l1, in0=lf, scalar=-1.0, in1=lf, op0=ALU.mult, op1=ALU.max),
                                 reads=[r_lf], writes=[r_l1])
                            P.op("act", lambda e: e.activation(out=l1, in_=l1, func=AF.Exp, scale=-1.0), reads=[r_l1], writes=[r_l1])
                            P.op("act", lambda e: e.activation(out=l1, in_=l1, func=AF.Ln, scale=1.0, bias=1.0), reads=[r_l1], writes=[r_l1])
                            P.op("dve", lambda e: e.tensor_single_scalar(out=l2, in_=lf, scalar=0.0, op=ALU.min),
                                 reads=[r_lf], writes=[r_l2])
                            P.op("dve", lambda e: e.tensor_tensor(out=lf, in0=l2, in1=l1, op=ALU.subtract),
                                 reads=[r_l1, r_l2], writes=[r_lf])
                            P.dma("sp", lambda e: e.dma_start(out=dst["fl"](t), in_=lf),
                                  reads=[r_lf], writes=[r_out], owner=r_lf, kind="out", final=True)
                            if dst.get("logf_keep") is not None:
                                dst["logf_keep"](t, lf, r_lf)
                    DF.run()
                    DF.late(late)
            DF.run()

    def rows(ap, t):
        return ap[t * 128:(t + 1) * 128, :]

    mA = make_proj(8, gA_d)

    def keep_logf(t, lf, r_lf):
        P.op("pool", lambda e: e.tensor_copy(out=lfall[:, t, :], in_=lf), reads=[r_lf], writes=[r_lfall])

    dstP = dict(
        fk=lambda t: rows(o_fk, t), fv=lambda t: rows(o_fv, t), fl=lambda t: rows(o_fl, t),
        dk=lambda t: rows(o_dk, t), dv=lambda t: rows(o_dv, t), ik=lambda t: rows(o_ik, t),
        KaT=lambda t, c: KaT[4 * c:4 * c + 4, :, t * 128:(t + 1) * 128].rearrange("h d t -> d h t"),
        VaE=lambda t, c: VaE[4 * c:4 * c + 4, :, t, :].rearrange("h p e -> p h e"),
        KbT=lambda t: KbT[:, :, t * 128:(t + 1) * 128].rearrange("h d t -> d h t"),
        VbE=lambda t: VbE[:, :, t, :].rearrange("h p e -> p h e"),
        KiT=lambda t: KiT[:, t * 128:(t + 1) * 128],
        r_KaT=r_KaT, r_VaE=r_VaE, r_KbT=r_KbT, r_VbE=r_VbE, r_KiT=r_KiT, logf_keep=keep_logf,
    )
    kv_pass(mA, 32, lambda t: rows(xb, t), lambda t: rows(tabA, t), dstP)
    dstS = dict(
        fk=lambda t: s_fk[:, :], fv=lambda t: s_fv[:, :], fl=lambda t: s_fl[:, :],
        dk=lambda t: s_dk[:, :], dv=lambda t: s_dv[:, :], ik=lambda t: s_ik[:, :],
        KaT=lambda t, c: KaTs[4 * c:4 * c + 4, :, :].rearrange("h d t -> d h t"),
        VaE=lambda t, c: VaEs[4 * c:4 * c + 4, :, 0, :].rearrange("h p e -> p h e"),
        KbT=lambda t: KbTs[:, :, :].rearrange("h d t -> d h t"),
        VbE=lambda t: VbEs[:, :, 0, :].rearrange("h p e -> p h e"),
        KiT=lambda t: KiTs[:, :],
        r_KaT=r_s[0], r_VaE=r_s[1], r_KbT=r_s[2], r_VbE=r_s[3], r_KiT=r_s[4],
        logf_keep=lambda t, lf, r_lf: P.op("pool", lambda e: e.tensor_copy(out=lfs, in_=lf), reads=[r_lf], writes=[r_lfs]),
    )
    kv_pass(mA, 1, lambda t: rows(xq, 9), lambda t: rows(tabQ, 9), dstS)
    barrier(P)
    A.release(base_mark)

    mB = make_proj(NTQ, gA_d)
    DFB = Defer()
    for s in range(NTQ):
        load_x_tile(mB, rows(xq, s), s, rows(tabQ, s))
    for c in range(7):
        col0 = c * 512
        ncols = 512 if c < 6 else 16
        wt, r_w = load_w_chunk(mB, w_q, col0, ncols)
        for s in range(NTQ):
            pm, r_pm = project(mB, s, wt, r_w, ncols)
            if c in (0, 1, 2, 3):
                ctx = head_norm_early(mB, pm[:, 0:512], r_pm, s, 4)

                def late(ctx=ctx, s=s, c=c):
                    kn, r_kn = head_norm_late(mB, ctx, 4, 0 if c < 2 else 2)
                    if c >= 2:
                        ko, r_ko = mB.ko.next()
                        rope(mB, kn, r_kn, 4, 128, s, 0, ko, r_ko)
                        kn, r_kn = ko, r_ko
                    kbb, r_kbb = mB.kbb.next()
                    P.op("pool", lambda e: e.tensor_copy(out=kbb, in_=kn), reads=[r_kn], writes=[r_kbb])
                    st, r_st = mB.st.next()
                    transposes_bf(mB, lambda h: kbb[:, h, :], r_kbb, 4, 128, st[:, 0:4, :], r_st)
                    dq, r_dq = (QaT, r_QaT) if c < 2 else (QbT, r_QbT)
                    hc = (c % 2) * 4
                    P.dma("sp", lambda e: e.dma_start(out=dq[s, :, hc:hc + 4, :], in_=st[:, 0:4, :]),
                          reads=[r_st], writes=[r_dq], owner=r_st, kind="out")
            elif c in (4, 5):
                kf, r_kf = mB.kf.next()
                P.op("act", lambda e, kf=kf, pm=pm, s=s: e.activation(out=flat(kf), in_=pm[:, 0:512],
                                                                      func=AF.Copy, scale=mB.rstd[:, s:s + 1]),
                     reads=[r_pm, mB.r_rstd[s]], writes=[r_kf])

                def late(kf=kf, r_kf=r_kf, s=s, c=c):
                    ko, r_ko = mB.ko.next()
                    kf8 = flat(kf).rearrange("p (h d) -> p h d", d=64)
                    ko8 = flat(ko).rearrange("p (h d) -> p h d", d=64)
                    rope(mB, kf8, r_kf, 8, 64, s, 128, ko8, r_ko)
                    kbb, r_kbb = mB.kbb.next()
                    kbb8 = flat(kbb).rearrange("p (h d) -> p h d", d=64)
                    P.op("pool", lambda e: e.tensor_copy(out=kbb8, in_=ko8), reads=[r_ko], writes=[r_kbb])
                    st, r_st = mB.st.next()
                    transposes_bf(mB, lambda h: kbb8[:, h, :], r_kbb, 8, 64, st[0:64, :, :], r_st)
                    hc = (c - 4) * 8
                    P.dma("sp", lambda e: e.dma_start(out=QiT[s, :, hc:hc + 8, :], in_=st[0:64, :, :]),
                          reads=[r_st], writes=[r_QiT], owner=r_st, kind="out")
            else:
                P.op("dve", lambda e, pm=pm, s=s: e.tensor_scalar(out=wi_all[:, s, :], in0=pm[:, 0:16], scalar1=mB.rstd[:, s:s + 1],
                                                                 scalar2=IDX_SCALE, op0=ALU.mult, op1=ALU.mult),
                     reads=[r_pm, mB.r_rstd[s]], writes=[r_wi])

                def late():
                    pass
            DFB.run()
            DFB.late(late)
    DFB.run()
    barrier(P)
    A.release(base_mark)

    att, _ = A.alloc("att", [NTQ, D], BF16)
    r_att = [P.res(f"att{i}") for i in range(NTQ)]
    att_mark = A.mark()

    pinc, r_pinc = A.alloc("pinc", [32, 8], F32)
    tot, r_tot = A.alloc("tot", [32, 8], F32)
    cP, r_cP = A.alloc("cP", [32, 8], F32)
    cref, r_cref = A.alloc("cref", [NT_Q, 8], F32)
    biasF, r_biasF = A.alloc("biasF", [NT_Q, 8, 32], F32)
    bbias, r_bbias = A.alloc("bbias", [NT_Q, 32], F32)
    oh, r_oh = A.alloc("oh", [NT_Q, 32], F32)
    tmp4, r_tmp4 = A.alloc("tmp4", [NT_Q, 8, 32], F32)
    fmask, r_fmask = A.alloc("fmask", [NT_Q, 8, 128], BF16)
    P.dma("sp", lambda e: e.dma_start(out=bbias, in_=bbias_d[:, :, :]), writes=[r_bbias], owner=r_bbias)
    P.dma("sp", lambda e: e.dma_start(out=oh, in_=oh_d[:, :, :]), writes=[r_oh], owner=r_oh)
    for i in range(NT_Q):
        P.dma("pool", lambda e, i=i: e.dma_start(out=fmask[:, i, :, :], in_=fmask_d[i]), writes=[r_fmask], owner=r_fmask)
    lf2 = lfall.rearrange("p b h -> p (b h)")
    P.op("pe", lambda e: e.matmul(pb[6][:, 0:256], lhsT=tri, rhs=lf2, start=True, stop=True),
         reads=[r_tri, r_lfall], writes=[r_pb[6]])
    P.op("pe", lambda e: e.matmul(pb[7][:, 0:256], lhsT=ones, rhs=lf2, start=True, stop=True),
         reads=[r_ones, r_lfall], writes=[r_pb[7]])
    P.op("act", lambda e: e.activation(out=tot.rearrange("p b h -> p (b h)"), in_=pb[7][:, 0:256], func=AF.Copy),
         reads=[r_pb[7]], writes=[r_tot])
    for h in range(8):
        P.op("dve", lambda e, h=h: e.tensor_tensor_scan(out=pinc[:, :, h], data0=ones[:, 0:32], data1=tot[:, :, h],
                                                        initial=0.0, op0=ALU.mult, op1=ALU.add),
             reads=[r_tot, r_ones], writes=[r_pinc])
    P.op("dve", lambda e: e.tensor_tensor(out=cP.rearrange("p b h -> p (b h)"), in0=pb[6][:, 0:256],
                                          in1=pinc.rearrange("p b h -> p (b h)"), op=ALU.add),
         reads=[r_pb[6], r_pinc], writes=[r_cP])
    P.op("dve", lambda e: e.tensor_tensor(out=cP, in0=cP, in1=tot, op=ALU.subtract), reads=[r_cP, r_tot], writes=[r_cP])
    pinc_hb = pinc.rearrange("p b h -> p h b")
    cP_hb = cP.rearrange("p b h -> p h b")
    P.op("dve", lambda e: e.tensor_tensor(out=tmp4, in0=bc(pinc_hb.unsqueeze(1), [128, NT_Q, 8, 32]),
                                          in1=bc(oh.unsqueeze(2), [128, NT_Q, 8, 32]), op=ALU.mult),
         reads=[r_pinc, r_oh], writes=[r_tmp4])
    P.op("dve", lambda e: e.tensor_reduce(out=cref.rearrange("p i h -> p (i h)"), in_=tmp4.rearrange("p i h b -> p (i h) b"),
                                          axis=AX.X, op=ALU.add),
         reads=[r_tmp4], writes=[r_cref])
    P.op("dve", lambda e: e.tensor_tensor(out=biasF, in0=bc(cref.unsqueeze(3), [128, NT_Q, 8, 32]),
                                          in1=bc(cP_hb.unsqueeze(1), [128, NT_Q, 8, 32]), op=ALU.subtract),
         reads=[r_cref, r_cP], writes=[r_biasF])
    P.op("dve", lambda e: e.tensor_tensor(out=biasF, in0=biasF, in1=bc(bbias.unsqueeze(2), [128, NT_Q, 8, 32]), op=ALU.add),
         reads=[r_biasF, r_bbias], writes=[r_biasF])

    kt_ring = A.ring("kt", [SEQ], BF16, 2)
    vt_ring = A.ring("vt", [32, 129], BF16, 2)
    qh_ring = A.ring("qh", [NT_Q, 128], BF16, 2)
    pt_ring = A.ring("ptile", [4, 128], BF16, 3)
    rd_ring = A.ring("rd", [2], F32, 4)
    pS = BankRing([2, 3])
    pO = BankRing([4, 5])
    pend = [None]

    def flush():
        if pend[0] is not None:
            pend[0]()
            pend[0] = None

    for h in range(8):
        kt, r_kt = kt_ring.next()
        vt, r_vt = vt_ring.next()
        qh, r_qh = qh_ring.next()
        P.dma("sp", lambda e, kt=kt, h=h: e.dma_start(out=kt, in_=KaT[h]), reads=[r_KaT], writes=[r_kt], owner=r_kt)
        P.dma("sp", lambda e, vt=vt, h=h: e.dma_start(out=vt, in_=VaE[h]), reads=[r_VaE], writes=[r_vt], owner=r_vt)
        P.dma("sp", lambda e, qh=qh, h=h: e.dma_start(out=qh, in_=QaT[0:NT_Q, :, h, :].rearrange("t d k -> d t k")),
              reads=[r_QaT], writes=[r_qh], owner=r_qh)
        for i in range(NT_Q):
            KBi = KB_OF(i)
            mk = MASK_KBS(i)
            bo = pO.next()
            po = pb[bo][:, 0:129]
            for k0 in range(0, KBi, 4):
                nk = min(4, KBi - k0)
                bs = pS.next()
                ps = pb[bs].rearrange("p (a b) -> p a b", b=128)
                ptile, r_ptile = pt_ring.next()
                for kk in range(nk):
                    kb = k0 + kk
                    P.op("pe", lambda e, ps=ps, kk=kk, kb=kb, kt=kt, qh=qh, i=i: e.matmul(
                        ps[:, kk, :], lhsT=kt[:, kb * 128:(kb + 1) * 128], rhs=qh[:, i, :], start=True, stop=True),
                        reads=[r_kt, r_qh], writes=[r_pb[bs]])
                for kk in range(nk):
                    kb = k0 + kk
                    P.op("act", lambda e, ps=ps, kk=kk, kb=kb, ptile=ptile, i=i, h=h: e.activation(
                        out=ptile[:, kk, :], in_=ps[:, kk, :], func=AF.Exp, scale=SCALE, bias=biasF[:, i, h, kb:kb + 1]),
                        reads=[r_pb[bs], r_biasF], writes=[r_ptile])
                    if kb in mk:
                        sl = mk.index(kb)
                        P.op("pool", lambda e, ptile=ptile, kk=kk, i=i, sl=sl: e.tensor_tensor(
                            out=ptile[:, kk, :], in0=ptile[:, kk, :], in1=fmask[:, i, sl, :], op=ALU.mult),
                            reads=[r_ptile, r_fmask], writes=[r_ptile])
                flush()

                def pv(k0=k0, nk=nk, ptile=ptile, r_ptile=r_ptile, vt=vt, r_vt=r_vt, po=po, bo=bo, KBi=KBi, i=i, h=h):
                    for kk in range(nk):
                        kb = k0 + kk
                        P.op("pe", lambda e, kk=kk, kb=kb: e.matmul(po, lhsT=ptile[:, kk, :], rhs=vt[:, kb, :],
                                                                    start=(kb == 0), stop=(kb == KBi - 1)),
                             reads=[r_ptile, r_vt], writes=[r_pb[bo]])
                    if k0 + nk == KBi:
                        rd, r_rd = rd_ring.next()
                        P.op("dve", lambda e: e.tensor_scalar(out=rd[:, 0:1], in0=po[:, 128:129], scalar1=1e-30, scalar2=None,
                                                              op0=ALU.max), reads=[r_pb[bo]], writes=[r_rd])
                        P.op("dve", lambda e: e.reciprocal(out=rd[:, 1:2], in_=rd[:, 0:1]), reads=[r_rd], writes=[r_rd])
                        P.op("act", lambda e: e.activation(out=att[:, i, h * 128:(h + 1) * 128], in_=po[:, 0:128], func=AF.Copy,
                                                           scale=rd[:, 1:2]), reads=[r_pb[bo], r_rd], writes=[r_att[i]])
                pend[0] = pv
    flush()
    barrier(P)
    A.release(att_mark)

    kbt, r_kbt = A.alloc("kbt", [2, SEQ], BF16)
    vbt, r_vbt = A.alloc("vbt", [2, 32, 129], BF16)
    kit, r_kit = A.alloc("kit", [SEQ], BF16)
    P.dma("sp", lambda e: e.dma_start(out=kbt, in_=KbT.rearrange("h d t -> d h t")), reads=[r_KbT], writes=[r_kbt], owner=r_kbt)
    P.dma("sp", lambda e: e.dma_start(out=vbt, in_=VbE.rearrange("h p b e -> p h b e")), reads=[r_VbE], writes=[r_vbt], owner=r_vbt)
    P.dma("sp", lambda e: e.dma_start(out=kit[0:64, :], in_=KiT[:, :]), reads=[r_KiT], writes=[r_kit], owner=r_kit)
    qi_ring = A.ring("qi_t", [16, 128], BF16, 2)
    qb_ring = A.ring("qb_t", [8, 128], BF16, 2)
    sc, r_sc = A.alloc("sc", [SEQ], F32)
    wk, r_wk = A.alloc("wk", [SEQ], F32)
    nm_ring = A.ring("nm", [SEQ], BF16, 1)
    sel, r_sel = A.alloc("sel", [SEQ], BF16)
    selT, r_selT = A.alloc("selT", [32, 128], BF16)
    rl_ring = A.ring("rl", [512], F32, 3)
    m8_ring = A.ring("m8", [8], F32, 2)
    thr_ring = A.ring("thr", [1], F32, 2)
    pt4_ring = A.ring("pt4", [4, 128], BF16, 3)
    rd2_ring = A.ring("rd2", [2], F32, 4)
    pI = BankRing([2, 3])
    pT2 = BankRing([0, 1])
    pS2 = BankRing([4, 5])

    for i in range(NT_Q):
        KBi = KB_OF(i)
        L = KBi * 128
        qi_t, r_qi = qi_ring.next()
        qb_t, r_qb = qb_ring.next()
        nm_t, r_nm = nm_ring.next()
        P.dma("sp", lambda e, qi_t=qi_t, i=i: e.dma_start(out=qi_t[0:64, :, :], in_=QiT[i]), reads=[r_QiT], writes=[r_qi], owner=r_qi)
        P.dma("sp", lambda e, qb_t=qb_t, i=i: e.dma_start(out=qb_t, in_=QbT[i]), reads=[r_QbT], writes=[r_qb], owner=r_qb)
        for q2 in range(0, L, 2048):
            n2 = min(2048, L - q2)
            P.dma("pool", lambda e, nm_t=nm_t, i=i, q2=q2, n2=n2: e.dma_start(out=nm_t[:, q2:q2 + n2], in_=nm_d[i, :, q2:q2 + n2]),
                  writes=[r_nm], owner=r_nm)
        for g0 in range(0, L, 512):
            ncol = min(512, L - g0)
            for hh in range(H_IDX):
                bi = pI.next()
                rl, r_rl = rl_ring.next()
                P.op("pe", lambda e, bi=bi, hh=hh, g0=g0, ncol=ncol, qi_t=qi_t: e.matmul(
                    pb[bi][:, 0:ncol], lhsT=qi_t[0:64, hh, :], rhs=kit[0:64, g0:g0 + ncol], start=True, stop=True),
                    reads=[r_qi, r_kit], writes=[r_pb[bi]])
                P.op("act", lambda e, bi=bi, rl=rl, ncol=ncol: e.activation(out=rl[:, 0:ncol], in_=pb[bi][:, 0:ncol], func=AF.Relu),
                     reads=[r_pb[bi]], writes=[r_rl])
                if hh == 0:
                    P.op("dve", lambda e, rl=rl, g0=g0, ncol=ncol, i=i, nm_t=nm_t: e.scalar_tensor_tensor(
                        out=sc[:, g0:g0 + ncol], in0=rl[:, 0:ncol], scalar=wi_all[:, i, 0:1], in1=nm_t[:, g0:g0 + ncol],
                        op0=ALU.mult, op1=ALU.add), reads=[r_rl, r_wi, r_nm], writes=[r_sc])
                else:
                    P.op("dve", lambda e, rl=rl, g0=g0, ncol=ncol, i=i, hh=hh: e.scalar_tensor_tensor(
                        out=sc[:, g0:g0 + ncol], in0=rl[:, 0:ncol], scalar=wi_all[:, i, hh:hh + 1], in1=sc[:, g0:g0 + ncol],
                        op0=ALU.mult, op1=ALU.add), reads=[r_rl, r_wi, r_sc], writes=[r_sc])
        m8, r_m8 = m8_ring.next()
        thr, r_thr = thr_ring.next()
        for r in range(32):
            src = sc if r == 0 else wk
            r_src = r_sc if r == 0 else r_wk
            P.op("dve", lambda e, src=src, m8=m8, L=L: e.max(out=m8, in_=src[:, 0:L]), reads=[r_src], writes=[r_m8])
            if r < 31:
                P.op("dve", lambda e, src=src, m8=m8, L=L: e.match_replace(out=wk[:, 0:L], in_to_replace=m8, in_values=src[:, 0:L],
                                                                          imm_value=NEG),
                     reads=[r_src, r_m8], writes=[r_wk])
        P.op("dve", lambda e, m8=m8, thr=thr: e.tensor_scalar(out=thr, in0=m8[:, 7:8], scalar1=-1.0e29, scalar2=None, op0=ALU.max),
             reads=[r_m8], writes=[r_thr])
        P.op("dve", lambda e, thr=thr, L=L: e.tensor_scalar(out=sel[:, 0:L], in0=sc[:, 0:L], scalar1=thr[:, 0:1], scalar2=None,
                                                          op0=ALU.is_ge), reads=[r_sc, r_thr], writes=[r_sel])
        for k0 in range(0, KBi, 8):
            nk = min(8, KBi - k0)
            bt = pT2.next()
            ptb = bank_bf(bt)
            for kk in range(nk):
                kb = k0 + kk
                P.op("pe", lambda e, ptb=ptb, kk=kk, kb=kb: e.transpose(out=ptb[:, kk, :], in_=sel[:, kb * 128:(kb + 1) * 128],
                                                                        identity=identb),
                     reads=[r_sel, r_identb], writes=[r_pb[bt]])
            P.op("act", lambda e, ptb=ptb, k0=k0, nk=nk: e.activation(out=selT[:, k0:k0 + nk, :], in_=ptb[:, 0:nk, :], func=AF.Copy),
                 reads=[r_pb[bt]], writes=[r_selT])
        for g2 in range(KV_B):
            obank = [6, 7]
            ov = [pb[b_][:, 0:258].rearrange("p (a b) -> p a b", b=129) for b_ in obank]
            for kb in range(KBi):
                bs = pS2.next()
                ps = pb[bs].rearrange("p (a b) -> p a b", b=128)
                pt4, r_pt4 = pt4_ring.next()
                P.op("pe", lambda e, ps=ps, kb=kb, g2=g2, qb_t=qb_t: e.matmul(
                    ps, lhsT=kbt[:, g2, kb * 128:(kb + 1) * 128], rhs=qb_t[:, 4 * g2:4 * g2 + 4, :], start=True, stop=True),
                    reads=[r_kbt, r_qb], writes=[r_pb[bs]])
                P.op("act", lambda e, ps=ps, pt4=pt4: e.activation(out=pt4, in_=ps, func=AF.Exp, scale=SCALE),
                     reads=[r_pb[bs]], writes=[r_pt4])
                P.op("dve", lambda e, pt4=pt4, kb=kb: e.tensor_tensor(out=pt4, in0=pt4, in1=bc(selT[:, kb, :].unsqueeze(1), [128, 4, 128]),
                                                                      op=ALU.mult), reads=[r_pt4, r_selT], writes=[r_pt4])
                flush()

                def pv2(kb=kb, pt4=pt4, r_pt4=r_pt4, g2=g2, KBi=KBi, i=i, ov=ov, obank=obank):
                    for hq in range(4):
                        o_ap = ov[hq // 2][:, hq % 2, :]
                        P.op("pe", lambda e, hq=hq, o_ap=o_ap: e.matmul(o_ap, lhsT=pt4[:, hq, :], rhs=vbt[:, g2, kb, :],
                                                                        start=(kb == 0 and hq % 2 == 0), stop=(kb == KBi - 1),
                                                                        skip_group_check=True),
                             reads=[r_pt4, r_vbt], writes=[r_pb[obank[hq // 2]]])
                    if kb == KBi - 1:
                        for hq in range(4):
                            o_ap = ov[hq // 2][:, hq % 2, :]
                            rb_ = r_pb[obank[hq // 2]]
                            rd, r_rd = rd2_ring.next()
                            c0 = 1024 + (4 * g2 + hq) * 128
                            P.op("dve", lambda e, o_ap=o_ap, rd=rd: e.tensor_scalar(out=rd[:, 0:1], in0=o_ap[:, 128:129], scalar1=1e-30,
                                                                                   scalar2=None, op0=ALU.max), reads=[rb_], writes=[r_rd])
                            P.op("dve", lambda e, rd=rd: e.reciprocal(out=rd[:, 1:2], in_=rd[:, 0:1]), reads=[r_rd], writes=[r_rd])
                            P.op("act", lambda e, o_ap=o_ap, rd=rd, c0=c0: e.activation(out=att[:, i, c0:c0 + 128], in_=o_ap[:, 0:128],
                                                                                       func=AF.Copy, scale=rd[:, 1:2]),
                                 reads=[rb_, r_rd], writes=[r_att[i]])
                pend[0] = pv2
            flush()
    barrier(P)
    A.release(att_mark)

    NPOOL = cfk.shape[0] // 128
    SK = PAST + 128
    ptb, r_ptb = A.alloc("ptb", [256], I32)
    idx_tok, r_idx = A.alloc("idx_tok", [256], I32)
    idx_pg, r_idxpg = A.alloc("idx_pg", [4], I32)
    iota_i, r_iota = A.alloc("iota_i", [1], I32)
    iota_f, r_iotaf = A.alloc("iota_f", [1], F32)
    lst, r_lst = A.alloc("lst", [128], F32)
    o64, r_o64 = A.alloc("o64", [128], F32)
    lblk, r_lblk = A.alloc("lblk", [128], F32)
    esel, r_esel = A.alloc("esel", [4, 128], F32)
    smask, r_smask = A.alloc("smask", [4, 32], BF16)
    P.dma("sp", lambda e: e.dma_start(out=ptb, in_=pt_d.rearrange("s l -> (s l)").partition_broadcast(128)),
          writes=[r_ptb], owner=r_ptb)
    P.op("pool", lambda e: e.memset(idx_pg, 0), writes=[r_idxpg])
    P.dma("sp", lambda e: e.dma_start(out=idx_pg[0:64, :], in_=pt_d.rearrange("s l -> l s"), allow_slow_non_contiguous=True),
          writes=[r_idxpg], owner=r_idxpg)
    P.dma("sp", lambda e: e.dma_start(out=lst, in_=lst_d[:, :]), writes=[r_lst], owner=r_lst)
    P.dma("sp", lambda e: e.dma_start(out=o64, in_=o64_d[:, :]), writes=[r_o64], owner=r_o64)
    P.dma("sp", lambda e: e.dma_start(out=lblk, in_=lblk_d[:, :]), writes=[r_lblk], owner=r_lblk)
    P.dma("sp", lambda e: e.dma_start(out=esel, in_=esel_d[:, :, :]), writes=[r_esel], owner=r_esel)
    P.dma("pool", lambda e: e.dma_start(out=smask, in_=smask_d[:, :, :]), writes=[r_smask], owner=r_smask)
    P.op("pool", lambda e: e.iota(iota_i, [[0, 1]], base=0, channel_multiplier=1), writes=[r_iota])
    P.op("dve", lambda e: e.tensor_copy(out=iota_f, in_=iota_i), reads=[r_iota], writes=[r_iotaf])
    P.op("dve", lambda e: e.tensor_scalar(out=idx_tok, in0=ptb, scalar1=128.0, scalar2=iota_f[:, 0:1], op0=ALU.mult, op1=ALU.add),
         reads=[r_ptb, r_iotaf], writes=[r_idx])

    def gather(dst_ap, r_dst, src2d, idx_col_ap, r_ix):
        P.dma("pool", lambda e: e.indirect_dma_start(out=dst_ap, out_offset=None, in_=src2d,
                                                     in_offset=bass.IndirectOffsetOnAxis(ap=idx_col_ap, axis=0)),
              reads=[r_ix], writes=[r_dst], owner=r_dst)

    kp_ring = A.ring("kp", [8, 128], BF16, 3)
    vp_ring = A.ring("vp", [8, 129], BF16, 3)
    ktp_ring = A.ring("ktp", [8, 128], BF16, 3)
    vs_ring = A.ring("vstage", [8, 128], BF16, 3)
    z_ring = A.ring("z_s", [8, 32], F32, 2)
    ps_rings = [A.ring("pt_s0", [8, 32], BF16, 2), A.ring("pt_s1", [8, 32], BF16, 2),
                A.ring("pt_s2", [8, 64], BF16, 2), A.ring("pt_s3", [8, 64], BF16, 2)]
    for t_, r_ in ps_rings[2].items:
        P.op("pool", lambda e, t_=t_: e.memset(t_[:, :, 32:64], 0.0), writes=[r_])
    for t_, r_ in ps_rings[3].items:
        P.op("pool", lambda e, t_=t_: e.memset(t_[:, :, 0:32], 0.0), writes=[r_])
    qs_a, r_qsa = A.alloc("qs_a", [8, 128], BF16)
    qs_b, r_qsb = A.alloc("qs_b", [8, 128], BF16)
    P.dma("sp", lambda e: e.dma_start(out=qs_a, in_=QaT[NT_Q]), reads=[r_QaT], writes=[r_qsa], owner=r_qsa)
    P.dma("sp", lambda e: e.dma_start(out=qs_b, in_=QbT[NT_Q]), reads=[r_QbT], writes=[r_qsb], owner=r_qsb)
    for t_, r_ in vp_ring.items:
        P.op("pool", lambda e, t_=t_: e.memset(t_[:, :, 128:129], 1.0), writes=[r_])
    pTs = BankRing([0, 1])
    pSs = BankRing([2, 3])
    OB = [4, 5, 6]

    def o_view(h, s):
        b = OB[h // 3]
        r0, r1 = (32 * s, 32 * s + 32) if s < 2 else (64, 128)
        return pb[b][r0:r1, (h % 3) * 129:(h % 3) * 129 + 129], b

    def sample_attn(kind):
        nkv = 8 if kind == "fox" else 2
        per = 1 if kind == "fox" else 4
        kcache, vcache = (cfk, cfv) if kind == "fox" else (cdk, cdv)
        qs, r_qs = (qs_a, r_qsa) if kind == "fox" else (qs_b, r_qsb)
        KTs, VEs, rKTs, rVEs = (KaTs, VaEs, r_s[0], r_s[1]) if kind == "fox" else (KbTs, VbEs, r_s[2], r_s[3])
        col0 = 0 if kind == "fox" else 1024
        for s in range(4):
            for lp in range(NPG + 1):
                new = (lp == NPG)
                ktp, r_ktp = ktp_ring.next()
                vp, r_vp = vp_ring.next()
                if not new:
                    kp, r_kp = kp_ring.next()
                    ic = idx_tok[:, s * 64 + lp:s * 64 + lp + 1]
                    vs, r_vs = vs_ring.next()
                    gather(kp.rearrange("p h d -> p (h d)")[:, 0:nkv * 128], r_kp, kcache, ic, r_idx)
                    gather(vs.rearrange("p h d -> p (h d)")[:, 0:nkv * 128], r_vs, vcache, ic, r_idx)
                    P.op("dve", lambda e, vs=vs, vp=vp: e.tensor_copy(out=vp[:, 0:nkv, 0:128], in_=vs[:, 0:nkv, :]),
                         reads=[r_vs], writes=[r_vp])
                    bt = pTs.next()
                    ptb_ = bank_bf(bt)
                    for h in range(nkv):
                        P.op("pe", lambda e, h=h, kp=kp, ptb_=ptb_: e.transpose(out=ptb_[:, h, :], in_=kp[:, h, :], identity=identb),
                             reads=[r_kp, r_identb], writes=[r_pb[bt]])
                    P.op("act", lambda e, ktp=ktp, ptb_=ptb_: e.activation(out=ktp[:, 0:nkv, :], in_=ptb_[:, 0:nkv, :], func=AF.Copy),
                         reads=[r_pb[bt]], writes=[r_ktp])
                else:
                    P.dma("sp", lambda e, ktp=ktp: e.dma_start(out=ktp[:, 0:nkv, :], in_=KTs.rearrange("h d t -> d h t")),
                          reads=[rKTs], writes=[r_ktp], owner=r_ktp)
                    P.dma("sp", lambda e, vp=vp: e.dma_start(out=vp[:, 0:nkv, :], in_=VEs[:, :, 0, :].rearrange("h p e -> p h e")),
                          reads=[rVEs], writes=[r_vp], owner=r_vp)
                bs = pSs.next()
                psv = pb[bs][:, 0:256].rearrange("p (a b) -> p a b", b=32)
                for g in range(nkv):
                    if per == 1:
                        P.op("pe", lambda e, g=g, ktp=ktp, psv=psv, s=s: e.matmul(psv[:, g, :], lhsT=ktp[:, g, :], rhs=qs[:, g, 32 * s:32 * s + 32],
                                                                            start=True, stop=True),
                             reads=[r_ktp, r_qs], writes=[r_pb[bs]])
                    else:
                        P.op("pe", lambda e, g=g, ktp=ktp, psv=psv, s=s: e.matmul(psv[:, 4 * g:4 * g + 4, :], lhsT=ktp[:, g, :],
                                                                            rhs=qs[:, 4 * g:4 * g + 4, 32 * s:32 * s + 32],
                                                                            start=True, stop=True),
                             reads=[r_ktp, r_qs], writes=[r_pb[bs]])
                pt_full, r_pts = ps_rings[s].next()
                pt_s = pt_full if s < 2 else pt_full[:, :, (s - 2) * 32:(s - 2) * 32 + 32]
                if kind == "fox":
                    z, r_z = z_ring.next()
                    bias_ap = bN[:, s, :] if new else bP[:, s, :, lp]
                    r_bias = r_bN if new else r_bP[s]
                    P.op("dve", lambda e, z=z, psv=psv, bias_ap=bias_ap: e.scalar_tensor_tensor(
                        out=z, in0=psv, scalar=SCALE, in1=bc(bias_ap.unsqueeze(2), [128, 8, 32]), op0=ALU.mult, op1=ALU.add),
                        reads=[r_pb[bs], r_bias], writes=[r_z])
                    P.op("act", lambda e, z=z, pt_s=pt_s: e.activation(out=pt_s, in_=z, func=AF.Exp), reads=[r_z], writes=[r_pts])
                    if new:
                        P.op("dve", lambda e, pt_s=pt_s, s=s: e.tensor_tensor(out=pt_s, in0=pt_s, in1=bc(smask[:, s, :].unsqueeze(1), [128, 8, 32]),
                                                                         op=ALU.mult), reads=[r_pts, r_smask], writes=[r_pts])
                else:
                    P.op("act", lambda e, psv=psv, pt_s=pt_s: e.activation(out=pt_s, in_=psv, func=AF.Exp, scale=SCALE),
                         reads=[r_pb[bs]], writes=[r_pts])
                    P.op("dve", lambda e, pt_s=pt_s, lp=lp, s=s: e.tensor_tensor(
                        out=pt_s, in0=pt_s, in1=bc(selTs[:, lp, 32 * s:32 * s + 32].unsqueeze(1), [128, 8, 32]), op=ALU.mult),
                        reads=[r_pts, r_selTs], writes=[r_pts])
                flush()

                def pv3(pt_full=pt_full, r_pts=r_pts, vp=vp, r_vp=r_vp, lp=lp, s=s, new=new):
                    for h in range(8):
                        o_ap, b = o_view(h, s)
                        P.op("pe", lambda e, h=h, o_ap=o_ap: e.matmul(o_ap, lhsT=pt_full[:, h, :], rhs=vp[:, h // per, :],
                                                                      start=(lp == 0 and h % 3 == 0 and s != 3), stop=new,
                                                                      skip_group_check=True),
                             reads=[r_pts, r_vp], writes=[r_pb[b]])
                pend[0] = pv3
            flush()
        for h in range(8):
            b = OB[h // 3]
            o_full = pb[b][:, (h % 3) * 129:(h % 3) * 129 + 129]
            rd, r_rd = rd2_ring.next()
            P.op("dve", lambda e, o_full=o_full, rd=rd: e.tensor_scalar(out=rd[:, 0:1], in0=o_full[:, 128:129], scalar1=1e-30, scalar2=None,
                                                                         op0=ALU.max), reads=[r_pb[b]], writes=[r_rd])
            P.op("dve", lambda e, rd=rd: e.reciprocal(out=rd[:, 1:2], in_=rd[:, 0:1]), reads=[r_rd], writes=[r_rd])
            P.op("act", lambda e, o_full=o_full, rd=rd, h=h: e.activation(out=att[:, NT_Q, col0 + h * 128:col0 + (h + 1) * 128],
                                                                          in_=o_full[:, 0:128], func=AF.Copy, scale=rd[:, 1:2]),
                 reads=[r_pb[b], r_rd], writes=[r_att[NT_Q]])

    rl_ring = A.ring("rl_s", [512], F32, 3)
    m8_ring = A.ring("m8_s", [8], F32, 2)
    thr_ring = A.ring("thr_s", [1], F32, 2)
    rd2_ring = A.ring("rd2_s", [2], F32, 4)
    smp_mark = A.mark()
    cs_s, r_cs = A.alloc("cs_s", [8], F32)
    csref, r_csref = A.alloc("csref", [4, 8], F32)
    bP, r_bP_base = A.alloc("bP", [4, 8, 64], F32)
    r_bP = [P.res(f"bP{s}") for s in range(4)]
    bN, r_bN = A.alloc("bN", [4, 8], F32)
    P.op("pe", lambda e: e.matmul(pb[7][:, 0:8], lhsT=lblk, rhs=lfs, start=True, stop=True), reads=[r_lblk, r_lfs], writes=[r_pb[7]])
    P.op("act", lambda e: e.activation(out=cs_s, in_=pb[7][:, 0:8], func=AF.Copy), reads=[r_pb[7]], writes=[r_cs])
    for s in range(4):
        P.op("pe", lambda e, s=s: e.matmul(pb[7][:, 8 + 8 * s:16 + 8 * s], lhsT=esel[:, s, :], rhs=cs_s, start=True, stop=True),
             reads=[r_esel, r_cs], writes=[r_pb[7]])
    P.op("act", lambda e: e.activation(out=csref.rearrange("p s h -> p (s h)"), in_=pb[7][:, 8:40], func=AF.Copy),
         reads=[r_pb[7]], writes=[r_csref])
    P.op("dve", lambda e: e.tensor_tensor(out=bN, in0=csref, in1=bc(cs_s.unsqueeze(1), [128, 4, 8]), op=ALU.subtract),
         reads=[r_csref, r_cs], writes=[r_bN])
    lp_ring = A.ring("lp_t", [128, 8], F32, 2)
    cw_ring = A.ring("cw", [8, 128], F32, 2)
    sm2_ring = A.ring("sm2", [3, 8], F32, 2)
    for s in range(4):
        lp_t, r_lp = lp_ring.next()
        cw, r_cw = cw_ring.next()
        sm2, r_sm2 = sm2_ring.next()
        gather(lp_t.rearrange("p t h -> p (t h)"), r_lp, cfl, idx_pg[:, s:s + 1], r_idxpg)
        for h in range(8):
            P.op("dve", lambda e, h=h, cw=cw, lp_t=lp_t: e.tensor_tensor_scan(out=cw[:, h, :], data0=ones[:, 0:128], data1=lp_t[:, :, h],
                                                                              initial=0.0, op0=ALU.mult, op1=ALU.add),
                 reads=[r_lp, r_ones], writes=[r_cw])
        P.op("dve", lambda e, cw=cw, sm2=sm2: e.tensor_copy(out=sm2[:, 0, :], in_=cw[:, :, 127]), reads=[r_cw], writes=[r_sm2])
        P.op("pe", lambda e, sm2=sm2: e.matmul(pb[6][:, 0:8], lhsT=lst, rhs=sm2[:, 0, :], start=True, stop=True),
             reads=[r_lst, r_sm2], writes=[r_pb[6]])
        P.op("pe", lambda e, sm2=sm2: e.matmul(pb[6][:, 8:16], lhsT=o64, rhs=sm2[:, 0, :], start=True, stop=True),
             reads=[r_o64, r_sm2], writes=[r_pb[6]])
        P.op("act", lambda e, sm2=sm2: e.activation(out=sm2[:, 1:3, :].rearrange("p a h -> p (a h)"), in_=pb[6][:, 0:16], func=AF.Copy),
             reads=[r_pb[6]], writes=[r_sm2])
        P.op("dve", lambda e, cw=cw, sm2=sm2: e.tensor_tensor(out=cw, in0=cw, in1=bc(sm2[:, 1, :].unsqueeze(2), [128, 8, 128]), op=ALU.add),
             reads=[r_cw, r_sm2], writes=[r_cw])
        P.op("dve", lambda e, sm2=sm2, s=s: e.tensor_tensor(out=sm2[:, 2, :], in0=sm2[:, 2, :], in1=csref[:, s, :], op=ALU.add),
             reads=[r_sm2, r_csref], writes=[r_sm2])
        for h0 in range(0, 8, 4):
            pcT = pb[7][:, :].rearrange("p (a b) -> p a b", b=128)
            for hh in range(4):
                P.op("pe", lambda e, hh=hh, h0=h0, cw=cw, pcT=pcT: e.transpose(out=pcT[:, hh, :], in_=cw[:, h0 + hh, :], identity=identf),
                     reads=[r_cw, r_identf], writes=[r_pb[7]])
            P.op("dve", lambda e, h0=h0, s=s, sm2=sm2, pcT=pcT: e.scalar_tensor_tensor(
                out=bP[:, s, h0:h0 + 4, :], in0=pcT[:, :, 0:64], scalar=-1.0,
                in1=bc(sm2[:, 2, h0:h0 + 4].unsqueeze(2), [128, 4, 64]), op0=ALU.mult, op1=ALU.add),
                reads=[r_pb[7], r_sm2], writes=[r_bP[s]])

    sample_attn("fox")
    barrier(P)
    A.release(smp_mark)

    kiTall = dscr("kiTall", [4, 64, SK], BF16)
    r_kiTall = P.res("kiTall")
    kip_ring = A.ring("kip", [64], BF16, 3)
    kst_ring = A.ring("kist", [8, 128], BF16, 2)
    for s in range(4):
        for l0 in range(0, NPG, 8):
            bt = pTs.next()
            ptb_ = bank_bf(bt)
            kst, r_kst = kst_ring.next()
            for ll in range(8):
                lp = l0 + ll
                kip, r_kip = kip_ring.next()
                gather(kip, r_kip, cik, idx_tok[:, s * 64 + lp:s * 64 + lp + 1], r_idx)
                P.op("pe", lambda e, ll=ll, kip=kip, ptb_=ptb_: e.transpose(out=ptb_[0:64, ll, :], in_=kip, identity=identb),
                     reads=[r_kip, r_identb], writes=[r_pb[bt]])
            P.op("act", lambda e, kst=kst, ptb_=ptb_: e.activation(out=kst[0:64, :, :], in_=ptb_[0:64, :, :], func=AF.Copy),
                 reads=[r_pb[bt]], writes=[r_kst])
            P.dma("sp", lambda e, kst=kst, s=s, l0=l0: e.dma_start(out=kiTall[s, :, l0 * 128:(l0 + 8) * 128],
                                                                   in_=kst[0:64, :, :].rearrange("p a b -> p (a b)")),
                  reads=[r_kst], writes=[r_kiTall], owner=r_kst, kind="out")
        kst, r_kst = kst_ring.next()
        P.dma("sp", lambda e, kst=kst: e.dma_start(out=kst[0:64, 0, :], in_=KiTs[:, :]), reads=[r_s[4]], writes=[r_kst], owner=r_kst)
        P.dma("sp", lambda e, kst=kst, s=s: e.dma_start(out=kiTall[s, :, PAST:SK], in_=kst[0:64, 0, :]),
              reads=[r_kst], writes=[r_kiTall], owner=r_kst, kind="out")
    qis, r_qis = A.alloc("qis", [16, 128], BF16)
    P.dma("sp", lambda e: e.dma_start(out=qis[0:64, :, :], in_=QiT[NT_Q]), reads=[r_QiT], writes=[r_qis], owner=r_qis)
    qis23, r_qis23 = A.alloc("qis23", [2, 16, 64], BF16)
    P.op("pool", lambda e: e.memset(qis23[0:64], 0.0), writes=[r_qis23])
    P.op("pool", lambda e: e.tensor_copy(out=qis23[0:64, 0, :, 0:32], in_=qis[0:64, :, 64:96]), reads=[r_qis], writes=[r_qis23])
    P.op("pool", lambda e: e.tensor_copy(out=qis23[0:64, 1, :, 32:64], in_=qis[0:64, :, 96:128]), reads=[r_qis], writes=[r_qis23])
    scs, r_scs = A.alloc("scs", [SK], F32)
    wks, r_wks = A.alloc("wks", [SK], F32)
    sels_ring = A.ring("sels", [1024], BF16, 2)
    selTs, r_selTs = A.alloc("selTs", [NPG + 1, 128], BF16)
    nms, r_nms = A.alloc("nms", [128], F32)
    P.dma("sp", lambda e: e.dma_start(out=nms, in_=nms_d[:, :]), writes=[r_nms], owner=r_nms)
    kig_ring = A.ring("kig", [4, 512], BF16, 2)
    for g0 in range(0, SK, 512):
        ncol = min(512, SK - g0)
        kig, r_kig = kig_ring.next()
        P.dma("sp", lambda e, kig=kig, g0=g0, ncol=ncol: e.dma_start(out=kig[0:64, :, 0:ncol],
                                                                     in_=kiTall[:, :, g0:g0 + ncol].rearrange("s d k -> d s k")),
              reads=[r_kiTall], writes=[r_kig], owner=r_kig)
        for hh in range(H_IDX):
            bi = pSs.next()
            rl, r_rl = rl_ring.next()
            for s in range(4):
                if s < 2:
                    P.op("pe", lambda e, bi=bi, hh=hh, s=s, ncol=ncol, kig=kig: e.matmul(
                        pb[bi][32 * s:32 * s + 32, 0:ncol], lhsT=qis[0:64, hh, 32 * s:32 * s + 32], rhs=kig[0:64, s, 0:ncol],
                        start=True, stop=True), reads=[r_qis, r_kig], writes=[r_pb[bi]])
                else:
                    P.op("pe", lambda e, bi=bi, hh=hh, s=s, ncol=ncol, kig=kig: e.matmul(
                        pb[bi][64:128, 0:ncol], lhsT=qis23[0:64, s - 2, hh, :], rhs=kig[0:64, s, 0:ncol],
                        start=(s == 2), stop=(s == 3), skip_group_check=True), reads=[r_qis23, r_kig], writes=[r_pb[bi]])
            P.op("act", lambda e, bi=bi, rl=rl, ncol=ncol: e.activation(out=rl[:, 0:ncol], in_=pb[bi][:, 0:ncol], func=AF.Relu),
                 reads=[r_pb[bi]], writes=[r_rl])
            if hh == 0:
                if g0 >= PAST:
                    P.op("dve", lambda e, rl=rl, g0=g0, ncol=ncol: e.scalar_tensor_tensor(
                        out=scs[:, g0:g0 + ncol], in0=rl[:, 0:ncol], scalar=wi_all[:, NT_Q, 0:1], in1=nms[:, 0:ncol],
                        op0=ALU.mult, op1=ALU.add), reads=[r_rl, r_wi, r_nms], writes=[r_scs])
                else:
                    P.op("dve", lambda e, rl=rl, g0=g0, ncol=ncol: e.tensor_scalar(
                        out=scs[:, g0:g0 + ncol], in0=rl[:, 0:ncol], scalar1=wi_all[:, NT_Q, 0:1], scalar2=None, op0=ALU.mult),
                        reads=[r_rl, r_wi], writes=[r_scs])
            else:
                P.op("dve", lambda e, rl=rl, g0=g0, ncol=ncol, hh=hh: e.scalar_tensor_tensor(
                    out=scs[:, g0:g0 + ncol], in0=rl[:, 0:ncol], scalar=wi_all[:, NT_Q, hh:hh + 1], in1=scs[:, g0:g0 + ncol],
                    op0=ALU.mult, op1=ALU.add), reads=[r_rl, r_wi, r_scs], writes=[r_scs])
    m8, r_m8 = m8_ring.next()
    thr, r_thr = thr_ring.next()
    for r in range(32):
        src = scs if r == 0 else wks
        r_src = r_scs if r == 0 else r_wks
        P.op("dve", lambda e, src=src: e.max(out=m8, in_=src), reads=[r_src], writes=[r_m8])
        if r < 31:
            P.op("dve", lambda e, src=src: e.match_replace(out=wks, in_to_replace=m8, in_values=src, imm_value=NEG),
                 reads=[r_src, r_m8], writes=[r_wks])
    P.op("dve", lambda e: e.tensor_scalar(out=thr, in0=m8[:, 7:8], scalar1=-1.0e29, scalar2=None, op0=ALU.max),
         reads=[r_m8], writes=[r_thr])
    for k0 in range(0, NPG + 1, 8):
        nk = min(8, NPG + 1 - k0)
        bt = pTs.next()
        ptb_ = bank_bf(bt)
        sels, r_sels = sels_ring.next()
        P.op("dve", lambda e, sels=sels, k0=k0, nk=nk: e.tensor_scalar(out=sels[:, 0:nk * 128], in0=scs[:, k0 * 128:(k0 + nk) * 128],
                                                                      scalar1=thr[:, 0:1], scalar2=None, op0=ALU.is_ge),
             reads=[r_scs, r_thr], writes=[r_sels])
        for kk in range(nk):
            kb = k0 + kk
            P.op("pe", lambda e, ptb_=ptb_, kk=kk, sels=sels: e.transpose(out=ptb_[:, kk, :], in_=sels[:, kk * 128:(kk + 1) * 128], identity=identb),
                 reads=[r_sels, r_identb], writes=[r_pb[bt]])
        P.op("act", lambda e, ptb_=ptb_, k0=k0, nk=nk: e.activation(out=selTs[:, k0:k0 + nk, :], in_=ptb_[:, 0:nk, :], func=AF.Copy),
             reads=[r_pb[bt]], writes=[r_selTs])
    sample_attn("dsa")
    barrier(P)
    A.release(att_mark)
    if dbg:
        P.dma("sp", lambda e: e.dma_start(out=dbg_att.rearrange("t p d -> p t d"), in_=att), reads=r_att, writes=[r_out],
              owner=r_att[0], kind="out", final=True)

    hnT_scr = dscr("hnT_scr", [128, NTQ, 16, 128], BF16)
    r_hnT = P.res("hnT_scr")
    aTall, _ = A.alloc("aTall", [NTQ, 16, 128], BF16)
    r_aT = [P.res(f"aT{i}") for i in range(NTQ)]
    pTe = BankRing([0, 1])
    for s in range(NTQ):
        for g in range(2):
            b = pTe.next()
            pt = bank_bf(b)
            for jj in range(8):
                kc = g * 8 + jj
                P.op("pe", lambda e, kc=kc, jj=jj, pt=pt, s=s: e.transpose(out=pt[:, jj, :], in_=att[:, s, kc * 128:(kc + 1) * 128],
                                                                          identity=identb),
                     reads=[r_att[s], r_identb], writes=[r_pb[b]])
            P.op("act", lambda e, g=g, pt=pt, s=s: e.activation(out=aTall[:, s, g * 8:(g + 1) * 8, :], in_=pt, func=AF.Copy),
                 reads=[r_pb[b]], writes=[r_aT[s]])
    wo_ring = A.ring("wo", [16, 512], BF16, 2)
    xo_ring = A.ring("xo", [512], F32, 3)
    hc1_ring = A.ring("hc1", [512], F32, 3)
    pH = BankRing([2, 3, 4, 5])
    wov = w_out.rearrange("(kc p) n -> p kc n", p=128)
    for c4 in range(4):
        wo, r_wo = wo_ring.next()
        for q4 in range(0, 16, 4):
            P.dma("pool", lambda e, c4=c4, q4=q4, wo=wo: e.dma_start(out=wo[:, q4:q4 + 4, :],
                                                                     in_=wov[:, q4:q4 + 4, c4 * 512:(c4 + 1) * 512]),
                  writes=[r_wo], owner=r_wo)
        for s in range(NTQ):
            xo, r_xo = xo_ring.next()
            hc1, r_hc1 = hc1_ring.next()
            P.dma("sp", lambda e, xo=xo, s=s, c4=c4: e.dma_start(out=xo, in_=rows(xq, s)[:, c4 * 512:(c4 + 1) * 512]),
                  writes=[r_xo], owner=r_xo)
            b = pH.next()
            for kc in range(16):
                P.op("pe", lambda e, kc=kc, s=s, b=b, wo=wo: e.matmul(pb[b][:, :], lhsT=aTall[:, s, kc, :], rhs=wo[:, kc, :],
                                                                     start=(kc == 0), stop=(kc == 15)),
                     reads=[r_aT[s], r_wo], writes=[r_pb[b]])
            P.op("dve", lambda e, b=b, hc1=hc1, xo=xo: e.tensor_tensor(out=hc1, in0=pb[b][:, :], in1=xo, op=ALU.add),
                 reads=[r_pb[b], r_xo], writes=[r_hc1])
            P.dma("sp", lambda e, hc1=hc1, s=s, c4=c4: e.dma_start(out=rows(h_scr, s)[:, c4 * 512:(c4 + 1) * 512], in_=hc1),
                  reads=[r_hc1], writes=[r_hscr], owner=r_hc1, kind="out")
    barrier(P)
    A.release(base_mark)
    mE = make_proj(NTQ, gF_d, slim=True)
    hf_ring = A.ring("hf", [D], F32, 2)
    hb_ring = A.ring("hb", [D], BF16, 2)
    for s in range(NTQ):
        hf, r_hf = hf_ring.next()
        hb, r_hb = hb_ring.next()
        P.dma("sp", lambda e, hf=hf, s=s: e.dma_start(out=hf, in_=rows(h_scr, s)), reads=[r_hscr], writes=[r_hf], owner=r_hf)
        if dbg:
            P.dma("sp", lambda e, hf=hf, s=s: e.dma_start(out=rows(dbg_h, s), in_=hf), reads=[r_hf], writes=[r_out], owner=r_hf,
                  kind="out", final=True)
        P.op("act", lambda e, hf=hf, s=s: e.activation(out=mE.junk, in_=hf, func=AF.Square, accum_out=mE.ss[:, s:s + 1]),
             reads=[r_hf], writes=[mE.r_junk, mE.r_ss[s]])
        rstd_col = mE.ss[:, s:s + 1]
        P.op("act", lambda e, rstd_col=rstd_col: e.activation(out=rstd_col, in_=rstd_col, func=AF.Ln, scale=1.0 / D, bias=EPS),
             reads=[mE.r_ss[s]], writes=[mE.r_ss[s]])
        P.op("act", lambda e, rstd_col=rstd_col: e.activation(out=rstd_col, in_=rstd_col, func=AF.Exp, scale=-0.5),
             reads=[mE.r_ss[s]], writes=[mE.r_ss[s]])
        P.op("dve", lambda e, hf=hf, hb=hb, rstd_col=rstd_col: e.scalar_tensor_tensor(out=hb, in0=hf, scalar=rstd_col, in1=mE.gvec,
                                                                                    op0=ALU.mult, op1=ALU.mult),
             reads=[r_hf, mE.r_ss[s], mE.r_gvec], writes=[r_hb])
        for g in range(2):
            b = mE.pT.next()
            pt = bank_bf(b)
            for jj in range(8):
                kc = g * 8 + jj
                P.op("pe", lambda e, kc=kc, jj=jj, pt=pt, hb=hb: e.transpose(out=pt[:, jj, :], in_=hb[:, kc * 128:(kc + 1) * 128],
                                                                            identity=identb),
                     reads=[r_hb, r_identb], writes=[r_pb[b]])
            P.op("act", lambda e, g=g, pt=pt, s=s: e.activation(out=mE.xT[:, s, g * 8:(g + 1) * 8, :], in_=pt, func=AF.Copy),
                 reads=[r_pb[b]], writes=[mE.r_xT[s]])
        P.dma("sp", lambda e, s=s: e.dma_start(out=hnT_scr[:, s, :, :], in_=mE.xT[:, s, :, :]), reads=[mE.r_xT[s]], writes=[r_hnT],
              owner=mE.r_xT[s], kind="out")
    barrier(P)
    A.release(base_mark)

    hnT, r_hn = A.alloc("hnT", [NTQ, 16, 128], BF16)
    P.dma("sp", lambda e: e.dma_start(out=hnT, in_=hnT_scr[:, :, :, :]), reads=[r_hnT], writes=[r_hn], owner=r_hn)
    cw, r_cw = A.alloc("cw", [NFF, 3], F32)
    cb, r_cb = A.alloc("cb", [NFF], F32)
    P.dma("sp", lambda e: e.dma_start(out=cw, in_=cw_d[:, :, :]), writes=[r_cw], owner=r_cw)
    P.dma("sp", lambda e: e.dma_start(out=cb, in_=cb_d[:, :]), writes=[r_cb], owner=r_cb)
    cst8, r_cst8 = A.alloc("cst8", [D_FF], F32)
    cstT, r_cstT = A.alloc("cstT", [NFF, 8], F32)
    P.dma("sp", lambda e: e.dma_start(out=cst8[0:8, :], in_=cst_d[:, :]), writes=[r_cst8], owner=r_cst8)
    pc = pb[0][:, 0:NFF * 8].rearrange("p (a b) -> p a b", b=8)
    for f in range(NFF):
        P.op("pe", lambda e, f=f: e.transpose(out=pc[:, f, :], in_=cst8[0:8, f * 128:(f + 1) * 128], identity=identf[0:8, 0:8]),
             reads=[r_cst8, r_identf], writes=[r_pb[0]])
    P.op("act", lambda e: e.activation(out=cstT, in_=pc, func=AF.Copy), reads=[r_pb[0]], writes=[r_cstT])
    wg_ring = A.ring("wg", [16, 512], BF16, 2)
    wu_ring = A.ring("wu", [16, 512], BF16, 2)
    NP = NTOK + 2
    g_ring = A.ring("g_sb", [NP], F32, 2)
    u_ring = A.ring("u_sb", [NTOK], F32, 2)
    ac_ring = A.ring("acc", [NTOK], F32, 2)
    a_ring = A.ring("a_sb", [NTQ, 128], BF16, 2)
    gs_ring = A.ring("gsel", [16], F32, 2)
    for t_, r_ in g_ring.items:
        P.op("pool", lambda e, t_=t_: e.memset(t_[:, 0:2], 0.0), writes=[r_])
    pG = BankRing([0, 1, 2, 3, 4, 5, 6, 7])
    groups = [(0, 4), (4, 4), (8, NTQ - 8)]
    wgv = w_gate.rearrange("(kc p) n -> p kc n", p=128)
    wuv = w_up.rearrange("(kc p) n -> p kc n", p=128)
    for c0 in range(0, NFF, 4):
        nf = min(4, NFF - c0)
        wg, r_wg = wg_ring.next()
        wu, r_wu = wu_ring.next()
        for q4 in range(0, 16, 4):
            P.dma("pool", lambda e, q4=q4, wg=wg, c0=c0, nf=nf: e.dma_start(out=wg[:, q4:q4 + 4, 0:nf * 128],
                                                                          in_=wgv[:, q4:q4 + 4, c0 * 128:(c0 + nf) * 128]),
                  writes=[r_wg], owner=r_wg)
            P.dma("pool", lambda e, q4=q4, wu=wu, c0=c0, nf=nf: e.dma_start(out=wu[:, q4:q4 + 4, 0:nf * 128],
                                                                          in_=wuv[:, q4:q4 + 4, c0 * 128:(c0 + nf) * 128]),
                  writes=[r_wu], owner=r_wu)
        for fi in range(nf):
            f = c0 + fi
            g_sb, r_g = g_ring.next()
            u_sb, r_u = u_ring.next()
            acc, r_acc = ac_ring.next()
            a_sb, r_a = a_ring.next()
            gsel, r_gsel = gs_ring.next()
            for (t0, nt) in groups:
                bg = pG.next()
                bu = pG.next()
                n = nt * 128
                for kc in range(16):
                    P.op("pe", lambda e, kc=kc, bg=bg, wg=wg, fi=fi, t0=t0, nt=nt, n=n: e.matmul(
                        pb[bg][:, 0:n], lhsT=wg[:, kc, fi * 128:(fi + 1) * 128], rhs=hnT[:, t0:t0 + nt, kc, :],
                        start=(kc == 0), stop=(kc == 15)), reads=[r_wg, r_hn], writes=[r_pb[bg]])
                for kc in range(16):
                    P.op("pe", lambda e, kc=kc, bu=bu, wu=wu, fi=fi, t0=t0, nt=nt, n=n: e.matmul(
                        pb[bu][:, 0:n], lhsT=wu[:, kc, fi * 128:(fi + 1) * 128], rhs=hnT[:, t0:t0 + nt, kc, :],
                        start=(kc == 0), stop=(kc == 15)), reads=[r_wu, r_hn], writes=[r_pb[bu]])
                P.op("act", lambda e, bg=bg, g_sb=g_sb, t0=t0, n=n: e.activation(out=g_sb[:, 2 + t0 * 128:2 + t0 * 128 + n],
                                                                              in_=pb[bg][:, 0:n], func=AF.Copy),
                     reads=[r_pb[bg]], writes=[r_g])
                P.op("act", lambda e, bu=bu, u_sb=u_sb, t0=t0, n=n: e.activation(out=u_sb[:, t0 * 128:t0 * 128 + n],
                                                                              in_=pb[bu][:, 0:n], func=AF.Copy),
                     reads=[r_pb[bu]], writes=[r_u])
            P.op("pool", lambda e, g_sb=g_sb, gsel=gsel: e.tensor_copy(out=gsel[:, 0:2], in_=g_sb[:, 2 + 1024:2 + 1026]),
                 reads=[r_g], writes=[r_gsel])
            sv = g_sb[:, 2 + 1152:2 + 1280].rearrange("p (s j) -> p s j", j=32)
            P.op("pool", lambda e, sv=sv, gsel=gsel: e.tensor_copy(out=gsel[:, 2:10].rearrange("p (s r) -> p s r", r=2), in_=sv[:, :, 8:10]),
                 reads=[r_g], writes=[r_gsel])
            P.dma("sp", lambda e, gsel=gsel, f=f: e.dma_start(out=gT_o[f], in_=gsel), reads=[r_gsel], writes=[r_out], owner=r_gsel,
                  kind="out", final=True)
            P.op("pool", lambda e, sv=sv, f=f: e.tensor_copy(out=sv[:, :, 0:2], in_=cstT[:, f, :].rearrange("p (s r) -> p s r", r=2)),
                 reads=[r_cstT, r_gsel], writes=[r_g])
            P.op("dve", lambda e, g_sb=g_sb, acc=acc, f=f: e.tensor_scalar(out=acc, in0=g_sb[:, 2:NP], scalar1=cw[:, f, 2:3],
                                                                         scalar2=cb[:, f:f + 1], op0=ALU.mult, op1=ALU.add),
                 reads=[r_g, r_cw, r_cb], writes=[r_acc])
            P.op("dve", lambda e, g_sb=g_sb, acc=acc, f=f: e.scalar_tensor_tensor(out=acc, in0=g_sb[:, 1:NP - 1], scalar=cw[:, f, 1:2],
                                                                                in1=acc, op0=ALU.mult, op1=ALU.add),
                 reads=[r_g, r_cw, r_acc], writes=[r_acc])
            P.op("dve", lambda e, g_sb=g_sb, acc=acc, f=f: e.scalar_tensor_tensor(out=acc, in0=g_sb[:, 0:NP - 2], scalar=cw[:, f, 0:1],
                                                                                in1=acc, op0=ALU.mult, op1=ALU.add),
                 reads=[r_g, r_cw, r_acc], writes=[r_acc])
            P.op("act", lambda e, acc=acc: e.activation(out=acc, in_=acc, func=AF.Silu), reads=[r_acc], writes=[r_acc])
            P.op("dve", lambda e, acc=acc, u_sb=u_sb, a_sb=a_sb: e.tensor_tensor(out=a_sb.rearrange("p t k -> p (t k)"), in0=acc, in1=u_sb,
                                                                               op=ALU.mult), reads=[r_acc, r_u], writes=[r_a])
            P.dma("sp", lambda e, a_sb=a_sb, f=f: e.dma_start(out=aT_scr[:, :, f, :].rearrange("t p k -> p t k"), in_=a_sb),
                  reads=[r_a], writes=[r_aTscr], owner=r_a, kind="out")
    barrier(P)
    A.release(base_mark)

    wd_ring = A.ring("wd", [NFF, 512], BF16, 2)
    at_ring = A.ring("aTt", [NFF, 128], BF16, 2)
    hc_ring = A.ring("hch", [512], F32, 3)
    yc_ring = A.ring("ych", [512], F32, 3)
    wdv = w_down.rearrange("(f p) n -> p f n", p=128)
    for c4 in range(4):
        wd, r_wd = wd_ring.next()
        for f0 in range(0, NFF, 4):
            nf = min(4, NFF - f0)
            P.dma("pool", lambda e, wd=wd, f0=f0, nf=nf, c4=c4: e.dma_start(out=wd[:, f0:f0 + nf, :],
                                                                          in_=wdv[:, f0:f0 + nf, c4 * 512:(c4 + 1) * 512]),
                  writes=[r_wd], owner=r_wd)
        for t in range(NTQ):
            aTt, r_at = at_ring.next()
            hch, r_hc = hc_ring.next()
            ych, r_yc = yc_ring.next()
            P.dma("sp", lambda e, aTt=aTt, t=t: e.dma_start(out=aTt, in_=aT_scr[t]), reads=[r_aTscr], writes=[r_at], owner=r_at)
            P.dma("sp", lambda e, hch=hch, t=t, c4=c4: e.dma_start(out=hch, in_=rows(h_scr, t)[:, c4 * 512:(c4 + 1) * 512]),
                  reads=[r_hscr], writes=[r_hc], owner=r_hc)
            b = pG.next()
            for f in range(NFF):
                P.op("pe", lambda e, f=f, b=b, aTt=aTt, wd=wd: e.matmul(pb[b][:, :], lhsT=aTt[:, f, :], rhs=wd[:, f, :],
                                                                       start=(f == 0), stop=(f == NFF - 1)),
                     reads=[r_at, r_wd], writes=[r_pb[b]])
            P.op("dve", lambda e, b=b, hch=hch, ych=ych: e.tensor_tensor(out=ych, in0=pb[b][:, :], in1=hch, op=ALU.add),
                 reads=[r_pb[b], r_hc], writes=[r_yc])
            P.dma("sp", lambda e, ych=ych, t=t, c4=c4: e.dma_start(out=rows(y_o, t)[:, c4 * 512:(c4 + 1) * 512], in_=ych),
                  reads=[r_yc], writes=[r_out], owner=r_yc, kind="out", final=True)

    P.emit()
    P.close()
    return nc


def _rope_tab(pos):
    pos = np.asarray(pos, dtype=np.float32)
    out = np.zeros((pos.shape[0], 192), np.float32)
    inv128 = (10000.0 ** (-np.arange(64, dtype=np.float32) / 64)).astype(np.float32)
    inv64 = (10000.0 ** (-np.arange(32, dtype=np.float32) / 32)).astype(np.float32)
    a = pos[:, None] * inv128[None, :]
    out[:, 0:64] = np.cos(a)
    out[:, 64:128] = np.sin(a)
    a = pos[:, None] * inv64[None, :]
    out[:, 128:160] = np.cos(a)
    out[:, 160:192] = np.sin(a)
    return out


def _core_consts(j):
    f32 = np.float32
    p0 = 1024 * j - 2
    fmask = np.zeros((NT_Q, 128, 8, 128), f32)
    bbias = np.zeros((128, NT_Q, 32), f32)
    nm = np.full((NT_Q, 128, SEQ), NEG, f32)
    oh = np.zeros((128, NT_Q, 32), f32)
    kk = np.arange(128)
    for i in range(NT_Q):
        t = p0 + 128 * i + np.arange(128)
        for sl, kb in enumerate(MASK_KBS(i)):
            s = 128 * kb + kk
            fmask[i, :, sl, :] = ((s[:, None] <= t[None, :]) & (t[None, :] >= 0)).astype(f32)
        for kb in range(32):
            if 128 * kb > t[-1]:
                bbias[:, i, kb] = NEG
        s_all = np.arange(SEQ)
        vis = (s_all[None, :] <= t[:, None]) & (t[:, None] >= 0)
        nm[i][vis] = 0.0
        oh[:, i, min(31, 8 * j + i)] = 1.0
    return fmask, bbias, nm, oh


_NC_CACHE = {}


def kernel(x_prompt, x_sample, cache_fox_k, cache_fox_v, cache_fox_logf, cache_dsa_k, cache_dsa_v,
           cache_idx_k, state_ffn_conv, page_table, w_in, b_f, g_qa, g_ka, g_qb, g_kb, g_attn,
           w_out, g_ffn, w_gate, w_up, conv_w, conv_b, w_down, _dbg=False):
    f32 = np.float32
    x_prompt = np.asarray(x_prompt, f32)
    x_sample = np.asarray(x_sample, f32)
    w_in0 = np.asarray(w_in, f32)[0]
    o = np.cumsum([0, 1024, 1024, 1024, 8, 1024, 256, 256, 1024, 64, 16])
    qa, ka, va, fa, qb, kb, vb, qi, ki, wi = [slice(o[i], o[i + 1]) for i in range(10)]
    w_kv = np.ascontiguousarray(np.concatenate([w_in0[:, ka], w_in0[:, va], w_in0[:, kb], w_in0[:, vb],
                                                w_in0[:, ki], w_in0[:, fa]], axis=1))
    w_q = np.ascontiguousarray(np.concatenate([w_in0[:, qa], w_in0[:, qb], w_in0[:, qi], w_in0[:, wi]], axis=1))
    ident = np.eye(128, dtype=f32)
    tri = np.triu(np.ones((128, 128), f32))
    gA = np.ascontiguousarray(np.broadcast_to(np.asarray(g_attn, f32)[0][None, :], (128, D)))
    gF = np.ascontiguousarray(np.broadcast_to(np.asarray(g_ffn, f32)[0][None, :], (128, D)))
    g4 = np.ascontiguousarray(np.broadcast_to(
        np.stack([np.asarray(g_qa, f32)[0], np.asarray(g_ka, f32)[0], np.asarray(g_qb, f32)[0],
                  np.asarray(g_kb, f32)[0]])[None], (128, 4, 128)))
    bfr = np.ascontiguousarray(np.broadcast_to(np.asarray(b_f, f32)[0][None, :], (128, 8)))
    cw = np.ascontiguousarray(np.asarray(conv_w, f32)[0].reshape(3, NFF, 128).transpose(2, 1, 0))
    cb = np.ascontiguousarray(np.asarray(conv_b, f32)[0].reshape(NFF, 128).T)
    tabA = _rope_tab(np.arange(SEQ))
    spos = np.zeros(128, f32)
    for s in range(4):
        spos[32 * s + 2:32 * s + 10] = PAST + np.arange(8)
    shared = dict(w_kv=w_kv, w_q=w_q, w_out=np.ascontiguousarray(np.asarray(w_out, f32)[0]),
                  w_gate=np.ascontiguousarray(np.asarray(w_gate, f32)[0]), w_up=np.ascontiguousarray(np.asarray(w_up, f32)[0]),
                  w_down=np.ascontiguousarray(np.asarray(w_down, f32)[0]), ident=ident, tri=tri, gA=gA, gF=gF, g4=g4, bfr=bfr,
                  cw=cw, cb=cb, tabA=tabA)
    consts = [_core_consts(j) for j in range(4)]
    lst = np.zeros((128, 128), f32)
    o64 = np.zeros((128, 128), f32)
    for p_ in range(64):
        lst[p_, p_ + 1:] = 1.0
        o64[p_, :] = 1.0
    lblk = np.zeros((128, 128), f32)
    esel = np.zeros((128, 4, 128), f32)
    smask = np.zeros((128, 4, 32), f32)
    nms = np.full((128, 128), NEG, f32)
    for s in range(4):
        esel[32 * s + 9, s, :] = 1.0
        for i in range(8):
            for i2 in range(i + 1):
                lblk[32 * s + 2 + i2, 32 * s + 2 + i] = 1.0
                smask[32 * s + 2 + i2, s, 2 + i] = 1.0
                nms[32 * s + 2 + i, 32 * s + 2 + i2] = 0.0
    npool = np.asarray(cache_fox_k).shape[1]
    assert npool == NPOOL_PAGES
    shared.update(
        cfk=np.asarray(cache_fox_k, f32).reshape(npool * 128, 1024), cfv=np.asarray(cache_fox_v, f32).reshape(npool * 128, 1024),
        cfl=np.asarray(cache_fox_logf, f32).reshape(npool, 1024), cdk=np.asarray(cache_dsa_k, f32).reshape(npool * 128, 256),
        cdv=np.asarray(cache_dsa_v, f32).reshape(npool * 128, 256), cik=np.asarray(cache_idx_k, f32).reshape(npool * 128, 64),
        lst=lst, o64=o64, lblk=lblk, esel=esel, smask=smask, nms=nms)
    page_table = np.asarray(page_table, np.int32)
    NTQ = NT_Q + 1
    in_maps = []
    for c in range(8):
        b, j = c // 4, c % 4
        p0 = 1024 * j - 2
        xq = np.zeros((NTQ * 128, D), f32)
        lo, hi = max(p0, 0), min(p0 + NT_Q * 128, SEQ)
        xq[lo - p0:hi - p0] = x_prompt[b, lo:hi]
        for s in range(4):
            xq[NT_Q * 128 + 32 * s + 2:NT_Q * 128 + 32 * s + 10] = x_sample[4 * c + s]
        tabQ = _rope_tab(np.concatenate([np.clip(p0 + np.arange(NT_Q * 128), 0, SEQ - 1), spos]))
        fmask, bbias, nm, oh = consts[j]
        cst = np.ascontiguousarray(np.asarray(state_ffn_conv, f32)[0, 4 * c:4 * c + 4].reshape(8, D_FF))
        m = dict(shared)
        m.update(xb=np.ascontiguousarray(x_prompt[b]), xq=xq, tabQ=tabQ, fmask=fmask, bbias=bbias, nm=nm, oh=oh, cst=cst,
                 pt=np.ascontiguousarray(page_table[4 * c:4 * c + 4]))
        in_maps.append(m)

    key = bool(_dbg)
    if key not in _NC_CACHE:
        _NC_CACHE[key] = build_program(dbg=key)
    nc = _NC_CACHE[key]
    res = run_bass_kernel_spmd(nc, in_maps, core_ids=list(range(8)))
    R = res.results

    def prow(name, shape):
        return np.stack([R[0][name], R[4][name]]).reshape((1, 2, SEQ) + shape)

    def srow(name, shape):
        out = np.zeros((32, 8) + shape, f32)
        for c in range(8):
            a = R[c][name]
            for s in range(4):
                out[4 * c + s] = a[32 * s + 2:32 * s + 10].reshape((8,) + shape)
        return out[None]

    y_p = np.zeros((2, SEQ, D), f32)
    y_s = np.zeros((32, 8, D), f32)
    conv_p = np.zeros((1, 2, 2, D_FF), f32)
    conv_s = np.zeros((1, 32, 2, D_FF), f32)
    for c in range(8):
        b, j = c // 4, c % 4
        yo = R[c]["y_o"]
        y_p[b, 1024 * j:1024 * (j + 1)] = yo[2:1026]
        gt = R[c]["gT_o"]
        for s in range(4):
            y_s[4 * c + s] = yo[NT_Q * 128 + 32 * s + 2:NT_Q * 128 + 32 * s + 10]
            for r in range(2):
                conv_s[0, 4 * c + s, r] = gt[:, :, 2 + 2 * s + r].reshape(D_FF)
        if j == 3:
            for r in range(2):
                conv_p[0, b, r] = gt[:, :, r].reshape(D_FF)
    outs = (y_p, y_s,
            prow("o_fk", (8, 128)), prow("o_fv", (8, 128)), prow("o_fl", (8,)),
            prow("o_dk", (2, 128)), prow("o_dv", (2, 128)), prow("o_ik", (64,)), conv_p,
            srow("s_fk", (8, 128)), srow("s_fv", (8, 128)), srow("s_fl", (8,)),
            srow("s_dk", (2, 128)), srow("s_dv", (2, 128)), srow("s_ik", (64,)), conv_s)
    if _dbg:
        return outs, R
    return outs
```

```python
from contextlib import ExitStack
import numpy as np
import concourse.bass as bass
import concourse.mybir as mybir
from concourse.bass_utils import run_bass_kernel_spmd

F32 = mybir.dt.float32
BF16 = mybir.dt.bfloat16
I32 = mybir.dt.int32
U32 = mybir.dt.uint32
AF = mybir.ActivationFunctionType
ALU = mybir.AluOpType
AX = mybir.AxisListType

D = 2048
HD = 128
H_A = 8
H_B = 8
KV_B = 2
H_IDX = 16
D_IDX = 64
D_FF = 5504
NFF = D_FF // 128
SEQ = 4096
PAST = 8192
NPG = 64
EPS = 1e-6
SCALE = HD ** -0.5
IDX_SCALE = (H_IDX * D_IDX) ** -0.5
NEG = -1.0e30
N_KV = 2632
N_Q = 3088
NT_Q = 9
ENGS = ("pe", "dve", "act", "pool", "sp")


class Res:
    __slots__ = ("name", "w", "r", "k_in", "k_out")

    def __init__(self, name):
        self.name = name
        self.w = None
        self.r = []
        self.k_in = None
        self.k_out = None


class Prog:
    def __init__(self, nc):
        self.nc = nc
        self.stack = ExitStack()
        self.q = {e: [] for e in ENGS}
        self.cnt = {}
        self.waited = {e: {} for e in ENGS}
        self.nres = 0
        self.final_waits = {}
        self.n_ops = 0
        self.phys_of = {}
        self.phys_cnt = []
        self.free_phys = []
        self.active_dma = []

    def sbuf(self, name, shape, dtype):
        return self.stack.enter_context(self.nc.sbuf_tensor("sb_" + name, list(shape), dtype))

    def psum(self, name, shape, dtype):
        return self.stack.enter_context(self.nc.psum_tensor("ps_" + name, list(shape), dtype))

    def res(self, name=None):
        self.nres += 1
        return Res(name or f"r{self.nres}")

    def _need(self, eng, reads, writes):
        ev = {}

        def add(e):
            if e is None:
                return
            k, v = e
            if ev.get(k, 0) < v:
                ev[k] = v
        for r in reads:
            add(r.w)
        for r in writes:
            add(r.w)
            for e in r.r:
                add(e)
        out = []
        wd = self.waited[eng]
        for k, v in ev.items():
            if eng == "pe" and k == "E:pe":
                continue
            if wd.get(k, 0) >= v:
                continue
            wd[k] = v
            out.append((k, v))
        return out

    def _commit(self, ev, reads, writes):
        for r in reads:
            r.r.append(ev)
            if len(r.r) > 64:
                best = {}
                for k, v in r.r:
                    if best.get(k, 0) < v:
                        best[k] = v
                r.r = list(best.items())
        for r in writes:
            r.w = ev
            r.r = []

    def op(self, eng, fn, reads=(), writes=()):
        waits = self._need(eng, reads, writes)
        k = "E:" + eng
        self.cnt[k] = self.cnt.get(k, 0) + 1
        ev = (k, self.cnt[k])
        self.q[eng].append((waits, fn, k, 1))
        self._commit(ev, reads, writes)
        self.n_ops += 1
        return ev

    def dma(self, queue, fn, reads=(), writes=(), owner=None, kind="in", final=False):
        waits = self._need(queue, reads, writes)
        if kind == "in":
            if owner.k_in is None:
                owner.k_in = self._new_dma_key(owner, "in")
            k = owner.k_in
        else:
            if owner.k_out is None:
                owner.k_out = self._new_dma_key(owner, "out")
            k = owner.k_out
        self.cnt[k] = self.cnt[k] + 16
        ev = (k, self.cnt[k])
        self.q[queue].append((waits, fn, k, 16))
        self._commit(ev, reads, writes)
        if final:
            self.final_waits[k] = self.cnt[k]
        self.n_ops += 1
        return ev

    def _phys(self, k):
        if k not in self.phys_of:
            self.phys_of[k] = len(self.phys_cnt)
            self.phys_cnt.append(0)
        return self.phys_of[k]

    def _new_dma_key(self, owner, kind):
        self.nres += 1
        k = f"D:{kind}:{owner.name}:{self.nres}"
        if self.free_phys:
            p = self.free_phys.pop()
        else:
            p = len(self.phys_cnt)
            self.phys_cnt.append(0)
        self.phys_of[k] = p
        self.cnt[k] = self.phys_cnt[p]
        self.active_dma.append((owner, kind, k))
        return k

    def retire_dma_keys(self):
        for owner, kind, k in self.active_dma:
            p = self.phys_of[k]
            self.phys_cnt[p] = self.cnt[k]
            self.free_phys.append(p)
            if kind == "in":
                owner.k_in = None
            else:
                owner.k_out = None
        self.active_dma = []
        self.final_waits = {}

    def emit(self):
        nc = self.nc
        for k in self.cnt:
            self._phys(k)
        psems = [self.stack.enter_context(nc.semaphore(f"s{i}")) for i in range(len(self.phys_cnt))]
        sems = {k: psems[self.phys_of[k]] for k in self.cnt}
        block = self.stack.enter_context(nc.Block())
        q = self.q
        final_waits = self.final_waits

        def run(name, e):
            for waits, fn, k, amt in q[name]:
                for (wk, wv) in waits:
                    e.wait_ge(sems[wk], wv)
                if fn is None:
                    continue
                fn(e).then_inc(sems[k], amt)
            if name == "sp":
                for k, v in final_waits.items():
                    e.wait_ge(sems[k], v)

        @block.tensor
        def _(e):
            run("pe", e)

        @block.vector
        def _(e):
            run("dve", e)

        @block.scalar
        def _(e):
            run("act", e)

        @block.gpsimd
        def _(e):
            run("pool", e)

        @block.sync
        def _(e):
            run("sp", e)

    def close(self):
        self.stack.close()


class Ring:
    def __init__(self, P, name, shape, dtype, n, psum=False):
        self.t = []
        self.r = []
        for i in range(n):
            t = P.psum(f"{name}{i}", shape, dtype) if psum else P.sbuf(f"{name}{i}", shape, dtype)
            self.t.append(t)
            self.r.append(P.res(f"{name}{i}"))
        self.i = 0
        self.n = n

    def next(self):
        i = self.i % self.n
        self.i += 1
        return self.t[i], self.r[i]


def bc(ap, shape):
    return ap.to_broadcast(list(shape))


NPOOL_PAGES = 2560
AW = 52992


def KB_OF(i):
    return min(32, 25 + i)


def MASK_KBS(i):
    return [kb for kb in range(KB_OF(i)) if (kb - i) % 8 in (0, 7)]


class Arena:
    def __init__(self, P):
        self.P = P
        self.t = P.sbuf("arena", [128, AW], F32)
        self.top = 0

    def mark(self):
        return self.top

    def release(self, m):
        self.top = m

    def alloc(self, name, free_shape, dtype=F32):
        n = 1
        for d_ in free_shape:
            n *= d_
        w = n if dtype in (F32, I32, U32) else (n + 1) // 2
        w = (w + 7) // 8 * 8
        off = self.top
        self.top += w
        assert self.top <= AW, f"arena overflow at {name}: {self.top}"
        v = self.t[:, off:off + w]
        if dtype != F32:
            v = v.bitcast(dtype)
        v = v[:, 0:n]
        if len(free_shape) == 2:
            v = v.rearrange("p (a b) -> p a b", b=free_shape[1])
        elif len(free_shape) == 3:
            v = v.rearrange("p (a b c) -> p a b c", b=free_shape[1], c=free_shape[2])
        elif len(free_shape) == 4:
            v = v.rearrange("p (a b c d) -> p a b c d", b=free_shape[1], c=free_shape[2], d=free_shape[3])
        return v, self.P.res(name)

    def ring(self, name, free_shape, dtype, n):
        return VRing([self.alloc(f"{name}{i}", free_shape, dtype) for i in range(n)])


class K:
    pass


class VRing:
    def __init__(self, items):
        self.items = items
        self.i = 0

    def next(self):
        it = self.items[self.i % len(self.items)]
        self.i += 1
        return it


def barrier(P):
    waits = []
    for k, v in P.cnt.items():
        if k == "B:bar":
            continue
        if P.waited["sp"].get(k, 0) < v:
            P.waited["sp"][k] = v
            waits.append((k, v))
    P.cnt["B:bar"] = P.cnt.get("B:bar", 0) + 1
    n = P.cnt["B:bar"]
    P.q["sp"].append((waits, lambda e: e.nop(), "B:bar", 1))
    for eng in ENGS:
        if eng == "sp":
            continue
        P.q[eng].append(([("B:bar", n)], None, None, 0))
        P.waited[eng]["B:bar"] = n
    for eng in ENGS:
        for k, v in P.cnt.items():
            if k != "B:bar":
                P.waited[eng][k] = v
    P.retire_dma_keys()


def build_program(dbg=False):
    nc = bass.Bass("TRN2", target_bir_lowering=False)
    P = Prog(nc)

    def din(name, shape, dt=F32):
        return nc.dram_tensor(name, list(shape), dt, kind="ExternalInput").ap()

    def dout(name, shape, dt=F32):
        return nc.dram_tensor(name, list(shape), dt, kind="ExternalOutput").ap()

    def dscr(name, shape, dt):
        return nc.dram_tensor(name, list(shape), dt).ap()

    NTQ = NT_Q + 1
    NTOK = NTQ * 128
    xb = din("xb", [SEQ, D])
    xq = din("xq", [NTOK, D])
    w_kv = din("w_kv", [D, N_KV])
    w_q = din("w_q", [D, N_Q])
    w_out = din("w_out", [D, D])
    w_gate = din("w_gate", [D, D_FF])
    w_up = din("w_up", [D, D_FF])
    w_down = din("w_down", [D_FF, D])
    ident_d = din("ident", [128, 128])
    tri_d = din("tri", [128, 128])
    gA_d = din("gA", [128, D])
    gF_d = din("gF", [128, D])
    g4_d = din("g4", [128, 4, 128])
    bf_d = din("bfr", [128, 8])
    cw_d = din("cw", [128, NFF, 3])
    cb_d = din("cb", [128, NFF])
    tabA = din("tabA", [SEQ, 192])
    tabQ = din("tabQ", [NTOK, 192])
    fmask_d = din("fmask", [NT_Q, 128, 8, 128])
    bbias_d = din("bbias", [128, NT_Q, 32])
    nm_d = din("nm", [NT_Q, 128, SEQ])
    oh_d = din("oh", [128, NT_Q, 32])
    cst_d = din("cst", [8, D_FF])
    NPOOLR = NPOOL_PAGES * 128
    cfk = din("cfk", [NPOOLR, 1024])
    cfv = din("cfv", [NPOOLR, 1024])
    cfl = din("cfl", [NPOOL_PAGES, 1024])
    cdk = din("cdk", [NPOOLR, 256])
    cdv = din("cdv", [NPOOLR, 256])
    cik = din("cik", [NPOOLR, 64])
    pt_d = din("pt", [4, NPG], I32)
    lst_d = din("lst", [128, 128])
    o64_d = din("o64", [128, 128])
    lblk_d = din("lblk", [128, 128])
    esel_d = din("esel", [128, 4, 128])
    smask_d = din("smask", [128, 4, 32])
    nms_d = din("nms", [128, 128])
    o_fk = dout("o_fk", [SEQ, 1024])
    o_fv = dout("o_fv", [SEQ, 1024])
    o_fl = dout("o_fl", [SEQ, 8])
    o_dk = dout("o_dk", [SEQ, 256])
    o_dv = dout("o_dv", [SEQ, 256])
    o_ik = dout("o_ik", [SEQ, 64])
    s_fk = dout("s_fk", [128, 1024])
    s_fv = dout("s_fv", [128, 1024])
    s_fl = dout("s_fl", [128, 8])
    s_dk = dout("s_dk", [128, 256])
    s_dv = dout("s_dv", [128, 256])
    s_ik = dout("s_ik", [128, 64])
    y_o = dout("y_o", [NTOK, D])
    gT_o = dout("gT_o", [NFF, 128, 16])
    if dbg:
        dbg_att = dout("dbg_att", [NTQ, 128, D], BF16)
        dbg_h = dout("dbg_h", [NTOK, D])
    KaT = dscr("KaT", [8, 128, SEQ], BF16)
    VaE = dscr("VaE", [8, 128, 32, 129], BF16)
    KbT = dscr("KbT", [2, 128, SEQ], BF16)
    VbE = dscr("VbE", [2, 128, 32, 129], BF16)
    KiT = dscr("KiT", [64, SEQ], BF16)
    KaTs = dscr("KaTs", [8, 128, 128], BF16)
    VaEs = dscr("VaEs", [8, 128, 1, 129], BF16)
    KbTs = dscr("KbTs", [2, 128, 128], BF16)
    VbEs = dscr("VbEs", [2, 128, 1, 129], BF16)
    KiTs = dscr("KiTs", [64, 128], BF16)
    QaT = dscr("QaT", [NTQ, 128, 8, 128], BF16)
    QbT = dscr("QbT", [NTQ, 128, 8, 128], BF16)
    QiT = dscr("QiT", [NTQ, 64, 16, 128], BF16)
    h_scr = dscr("h_scr", [NTOK, D], F32)
    aT_scr = dscr("aT_scr", [NTQ, 128, NFF, 128], BF16)
    r_KaT, r_VaE, r_KbT, r_VbE, r_KiT = (P.res(n) for n in ("KaT", "VaE", "KbT", "VbE", "KiT"))
    r_s = [P.res(n) for n in ("KaTs", "VaEs", "KbTs", "VbEs", "KiTs")]
    r_QaT, r_QbT, r_QiT = P.res("QaT"), P.res("QbT"), P.res("QiT")
    r_hscr, r_aTscr = P.res("h_scr"), P.res("aT_scr")
    r_out = P.res("outputs")

    A = Arena(P)
    pb = [P.psum(f"bank{i}", [128, 512], F32) for i in range(8)]
    r_pb = [P.res(f"bank{i}") for i in range(8)]

    def bank_bf(i):
        return pb[i][:, :].bitcast(BF16).rearrange("p (a b) -> p a b", b=128)

    class BankRing:
        def __init__(self, ids):
            self.ids = ids
            self.i = 0

        def next(self):
            b = self.ids[self.i % len(self.ids)]
            self.i += 1
            return b

    identf, r_identf = A.alloc("identf", [128], F32)
    identb, r_identb = A.alloc("identb", [128], BF16)
    tri, r_tri = A.alloc("tri", [128], F32)
    ones, r_ones = A.alloc("ones", [128], F32)
    g4, r_g4 = A.alloc("g4", [4, 128], F32)
    bfr, r_bfr = A.alloc("bfr", [8], F32)
    lfall, r_lfall = A.alloc("lfall", [32, 8], F32)
    wi_all, r_wi = A.alloc("wi_all", [NTQ, 16], F32)
    lfs, r_lfs = A.alloc("lfs", [8], F32)
    P.dma("sp", lambda e: e.dma_start(out=identf, in_=ident_d[:, :]), writes=[r_identf], owner=r_identf)
    P.dma("sp", lambda e: e.dma_start(out=tri, in_=tri_d[:, :]), writes=[r_tri], owner=r_tri)
    P.dma("sp", lambda e: e.dma_start(out=g4, in_=g4_d[:, :, :]), writes=[r_g4], owner=r_g4)
    P.dma("sp", lambda e: e.dma_start(out=bfr, in_=bf_d[:, :]), writes=[r_bfr], owner=r_bfr)
    P.op("dve", lambda e: e.tensor_copy(out=identb, in_=identf), reads=[r_identf], writes=[r_identb])
    P.op("pool", lambda e: e.memset(ones, 1.0), writes=[r_ones])
    base_mark = A.mark()

    def make_proj(G, gvec_d, slim=False):
        m = K()
        m.G = G
        m.gvec, m.r_gvec = A.alloc("gvec", [D], F32)
        P.dma("sp", lambda e: e.dma_start(out=m.gvec, in_=gvec_d[:, :]), writes=[m.r_gvec], owner=m.r_gvec)
        m.junk, m.r_junk = A.alloc("junk", [D], BF16)
        m.junk2, m.r_junk2 = A.alloc("junk2", [128], BF16)
        m.xT, _ = A.alloc("xT", [G, 16, 128], BF16)
        m.r_xT = [P.res(f"xT{i}") for i in range(G)]
        m.ss, _ = A.alloc("ss_t", [G], F32)
        m.rstd, _ = A.alloc("rstd_t", [G], F32)
        m.r_ss = [P.res(f"ss{i}") for i in range(G)]
        m.r_rstd = [P.res(f"rstd{i}") for i in range(G)]
        m.pT = BankRing([0, 1])
        m.pM = BankRing([2, 3])
        if slim:
            return m
        m.xf = A.ring("xf", [D], F32, 2)
        m.xbf = A.ring("xbf", [D], BF16, 2)
        m.tab, _ = A.alloc("tab", [G, 192], F32)
        m.r_tab = [P.res(f"tab{i}") for i in range(G)]
        m.w = A.ring("wch", [16, 512], BF16, 2)
        m.kf = A.ring("kf", [4, 128], F32, 4)
        m.kn = A.ring("kn", [4, 128], F32, 3)
        m.ko = A.ring("ko", [4, 128], F32, 3)
        m.kbb = A.ring("kbb", [4, 128], BF16, 3)
        m.st = A.ring("ktst", [8, 128], BF16, 2)
        m.ve = A.ring("ve", [4, 129], BF16, 2)
        m.sm = A.ring("sm", [16], F32, 6)
        m.ra = A.ring("ra", [4, 64], F32, 2)
        m.rb = A.ring("rb", [4, 64], F32, 2)
        m.lf = A.ring("lf", [8], F32, 2)
        m.lg = A.ring("lg", [8], F32, 4)
        m.pT = BankRing([0, 1])
        m.pM = BankRing([2, 3])
        for t_, r_ in m.ve.items:
            P.op("pool", lambda e, t_=t_: e.memset(t_[:, :, 128:129], 1.0), writes=[r_])
        return m

    def load_x_tile(m, src_rows_ap, slot, tab_rows_ap):
        xf, r_xf = m.xf.next()
        xbf, r_xbf = m.xbf.next()
        P.dma("sp", lambda e: e.dma_start(out=xf, in_=src_rows_ap), writes=[r_xf], owner=r_xf)
        if tab_rows_ap is not None:
            P.dma("sp", lambda e: e.dma_start(out=m.tab[:, slot, :], in_=tab_rows_ap), writes=[m.r_tab[slot]],
                  owner=m.r_tab[slot])
        norm_to_xT(m, xf, r_xf, xbf, r_xbf, slot)

    def norm_to_xT(m, xf, r_xf, xbf, r_xbf, slot):
        P.op("act", lambda e: e.activation(out=m.junk, in_=xf, func=AF.Square, accum_out=m.ss[:, slot:slot + 1]),
             reads=[r_xf], writes=[m.r_junk, m.r_ss[slot]])
        P.op("act", lambda e: e.activation(out=m.ss[:, slot:slot + 1], in_=m.ss[:, slot:slot + 1], func=AF.Ln,
                                           scale=1.0 / D, bias=EPS), reads=[m.r_ss[slot]], writes=[m.r_ss[slot]])
        P.op("act", lambda e: e.activation(out=m.rstd[:, slot:slot + 1], in_=m.ss[:, slot:slot + 1], func=AF.Exp,
                                           scale=-0.5), reads=[m.r_ss[slot]], writes=[m.r_rstd[slot]])
        P.op("dve", lambda e: e.tensor_tensor(out=xbf, in0=xf, in1=m.gvec, op=ALU.mult),
             reads=[r_xf, m.r_gvec], writes=[r_xbf])
        for g in range(2):
            b = m.pT.next()
            pt = bank_bf(b)
            for jj in range(8):
                kc = g * 8 + jj
                P.op("pe", lambda e, kc=kc, jj=jj, pt=pt: e.transpose(out=pt[:, jj, :], in_=xbf[:, kc * 128:(kc + 1) * 128],
                                                                      identity=identb),
                     reads=[r_xbf, r_identb], writes=[r_pb[b]])
            if g == 0:
                P.op("dve", lambda e, g=g, pt=pt: e.tensor_copy(out=m.xT[:, slot, g * 8:(g + 1) * 8, :], in_=pt),
                     reads=[r_pb[b]], writes=[m.r_xT[slot]])
            else:
                P.op("act", lambda e, g=g, pt=pt: e.activation(out=m.xT[:, slot, g * 8:(g + 1) * 8, :], in_=pt, func=AF.Copy),
                     reads=[r_pb[b]], writes=[m.r_xT[slot]])

    def load_w_chunk(m, wd, col0, ncols, nk=16):
        wt, r_w = m.w.next()
        wv = wd.rearrange("(kc p) n -> p kc n", p=128)
        for q4 in range(0, nk, 4):
            P.dma("pool", lambda e, q4=q4: e.dma_start(out=wt[:, q4:q4 + 4, 0:ncols],
                                                        in_=wv[:, q4:q4 + 4, col0:col0 + ncols]),
                  writes=[r_w], owner=r_w)
        return wt, r_w

    def project(m, slot, wt, r_w, ncols):
        b = m.pM.next()
        pm = pb[b]
        for kc in range(16):
            P.op("pe", lambda e, kc=kc: e.matmul(pm[:, 0:ncols], lhsT=m.xT[:, slot, kc, :], rhs=wt[:, kc, 0:ncols],
                                                 start=(kc == 0), stop=(kc == 15)),
                 reads=[m.r_xT[slot], r_w], writes=[r_pb[b]])
        return pm, r_pb[b]

    def rstd_of(ssq_ap, r_in, inv_n):
        P.op("act", lambda e: e.activation(out=ssq_ap, in_=ssq_ap, func=AF.Ln, scale=inv_n, bias=EPS),
             reads=[r_in], writes=[r_in])
        P.op("act", lambda e: e.activation(out=ssq_ap, in_=ssq_ap, func=AF.Exp, scale=-0.5),
             reads=[r_in], writes=[r_in])

    def head_norm_early(m, pm_ap, r_pm, slot, nh):
        kf, r_kf = m.kf.next()
        sm, r_sm = m.sm.next()
        P.op("act", lambda e: e.activation(out=kf[:, 0:nh, :], in_=pm_ap, func=AF.Copy, scale=m.rstd[:, slot:slot + 1]),
             reads=[r_pm, m.r_rstd[slot]], writes=[r_kf])
        for h in range(nh):
            P.op("act", lambda e, h=h: e.activation(out=m.junk2, in_=kf[:, h, :], func=AF.Square,
                                                    accum_out=sm[:, h:h + 1]),
                 reads=[r_kf], writes=[m.r_junk2, r_sm])
        rstd_of(sm[:, 0:nh], r_sm, 1.0 / HD)
        return (kf, r_kf, sm, r_sm)

    def head_norm_late(m, ctx, nh, g_idx):
        kf, r_kf, sm, r_sm = ctx
        kn, r_kn = m.kn.next()
        P.op("dve", lambda e: e.tensor_tensor(out=kn[:, 0:nh, :], in0=kf[:, 0:nh, :],
                                              in1=bc(sm[:, 0:nh].unsqueeze(2), [128, nh, 128]), op=ALU.mult),
             reads=[r_kf, r_sm], writes=[r_kn])
        P.op("pool", lambda e: e.tensor_tensor(out=kn[:, 0:nh, :], in0=kn[:, 0:nh, :],
                                               in1=bc(g4[:, g_idx, :].unsqueeze(1), [128, nh, 128]), op=ALU.mult),
             reads=[r_kn, r_g4], writes=[r_kn])
        return kn, r_kn

    class Defer:
        def __init__(self):
            self.q = []

        def late(self, fn):
            self.q.append(fn)

        def run(self):
            q, self.q = self.q, []
            for fn in q:
                fn()

    def rope(m, src, r_src, nh, hd, slot, tab_off, dst, r_dst):
        half = hd // 2
        ra, r_ra = m.ra.next()
        rb, r_rb = m.rb.next()
        ra = ra.rearrange("p a b -> p (a b)")[:, 0:nh * half].rearrange("p (a b) -> p a b", b=half)
        rb = rb.rearrange("p a b -> p (a b)")[:, 0:nh * half].rearrange("p (a b) -> p a b", b=half)
        cos = bc(m.tab[:, slot, tab_off:tab_off + half].unsqueeze(1), [128, nh, half])
        sin = bc(m.tab[:, slot, tab_off + half:tab_off + 2 * half].unsqueeze(1), [128, nh, half])
        x1 = src[:, 0:nh, 0:half]
        x2 = src[:, 0:nh, half:hd]
        rt = m.r_tab[slot]
        P.op("dve", lambda e: e.tensor_tensor(out=ra, in0=x1, in1=cos, op=ALU.mult), reads=[r_src, rt], writes=[r_ra])
        P.op("pool", lambda e: e.tensor_tensor(out=rb, in0=x2, in1=sin, op=ALU.mult), reads=[r_src, rt], writes=[r_rb])
        P.op("dve", lambda e: e.tensor_tensor(out=dst[:, 0:nh, 0:half], in0=ra, in1=rb, op=ALU.subtract),
             reads=[r_ra, r_rb], writes=[r_dst])
        P.op("dve", lambda e: e.tensor_tensor(out=ra, in0=x2, in1=cos, op=ALU.mult), reads=[r_src, rt, r_dst], writes=[r_ra])
        P.op("pool", lambda e: e.tensor_tensor(out=rb, in0=x1, in1=sin, op=ALU.mult), reads=[r_src, rt, r_dst], writes=[r_rb])
        P.op("dve", lambda e: e.tensor_tensor(out=dst[:, 0:nh, half:hd], in0=ra, in1=rb, op=ALU.add),
             reads=[r_ra, r_rb], writes=[r_dst])

    def transposes_bf(m, src_fn, r_srcb, nh, rows_d, dst, r_dst):
        b = m.pT.next()
        pt = bank_bf(b)
        for h in range(nh):
            P.op("pe", lambda e, h=h: e.transpose(out=pt[0:rows_d, h, :], in_=src_fn(h), identity=identb),
                 reads=[r_srcb, r_identb], writes=[r_pb[b]])
        P.op("act", lambda e: e.activation(out=dst, in_=pt[0:rows_d, 0:nh, :], func=AF.Copy), reads=[r_pb[b]], writes=[r_dst])

    def flat(t):
        return t.rearrange("p h d -> p (h d)")

    def kv_pass(m, n_tiles, x_rows, tab_rows, dst):
        G = m.G
        DF = Defer()
        n_groups = (n_tiles + G - 1) // G
        pending_w = load_w_chunk(m, w_kv, 0, 512)
        for g0 in range(0, n_tiles, G):
            gt = min(G, n_tiles - g0)
            for s in range(gt):
                load_x_tile(m, x_rows(g0 + s), s, tab_rows(g0 + s))
            for c in range(6):
                col0 = c * 512
                ncols = 512 if c < 5 else 72
                wt, r_w = pending_w
                c2 = (c + 1) % 6
                if c2 != 0 or g0 + G < n_tiles:
                    pending_w = load_w_chunk(m, w_kv, c2 * 512, 512 if c2 < 5 else 72)
                for s in range(gt):
                    t = g0 + s
                    pm, r_pm = project(m, s, wt, r_w, ncols)
                    if c in (0, 1):
                        ctx = head_norm_early(m, pm[:, 0:512], r_pm, s, 4)

                        def late(ctx=ctx, t=t, c=c):
                            kn, r_kn = head_norm_late(m, ctx, 4, 1)
                            P.dma("sp", lambda e: e.dma_start(out=dst["fk"](t)[:, c * 512:(c + 1) * 512], in_=flat(kn)),
                                  reads=[r_kn], writes=[r_out], owner=r_kn, kind="out", final=True)
                            kbb, r_kbb = m.kbb.next()
                            P.op("pool", lambda e: e.tensor_copy(out=kbb, in_=kn), reads=[r_kn], writes=[r_kbb])
                            st, r_st = m.st.next()
                            transposes_bf(m, lambda h: kbb[:, h, :], r_kbb, 4, 128, st[:, 0:4, :], r_st)
                            P.dma("sp", lambda e: e.dma_start(out=dst["KaT"](t, c), in_=st[:, 0:4, :]),
                                  reads=[r_st], writes=[dst["r_KaT"]], owner=r_st, kind="out")
                    elif c in (2, 3):
                        kf, r_kf = m.kf.next()
                        P.op("act", lambda e, kf=kf, pm=pm, s=s: e.activation(out=flat(kf), in_=pm[:, 0:512],
                                                                              func=AF.Copy, scale=m.rstd[:, s:s + 1]),
                             reads=[r_pm, m.r_rstd[s]], writes=[r_kf])

                        def late(kf=kf, r_kf=r_kf, t=t, c=c):
                            P.dma("sp", lambda e: e.dma_start(out=dst["fv"](t)[:, (c - 2) * 512:(c - 1) * 512], in_=flat(kf)),
                                  reads=[r_kf], writes=[r_out], owner=r_kf, kind="out", final=True)
                            ve, r_ve = m.ve.next()
                            P.op("dve", lambda e: e.tensor_copy(out=ve[:, :, 0:128], in_=kf), reads=[r_kf], writes=[r_ve])
                            P.dma("sp", lambda e: e.dma_start(out=dst["VaE"](t, c - 2), in_=ve),
                                  reads=[r_ve], writes=[dst["r_VaE"]], owner=r_ve, kind="out")
                    elif c == 4:
                        ctx = head_norm_early(m, pm[:, 0:256], r_pm, s, 2)
                        kf, r_kf = m.kf.next()
                        P.op("act", lambda e, kf=kf, pm=pm, s=s: e.activation(out=flat(kf[:, 0:2, :]), in_=pm[:, 256:512],
                                                                              func=AF.Copy, scale=m.rstd[:, s:s + 1]),
                             reads=[r_pm, m.r_rstd[s]], writes=[r_kf])

                        def late(ctx=ctx, kf=kf, r_kf=r_kf, t=t, s=s):
                            kn, r_kn = head_norm_late(m, ctx, 2, 3)
                            ko, r_ko = m.ko.next()
                            rope(m, kn, r_kn, 2, 128, s, 0, ko, r_ko)
                            P.dma("sp", lambda e: e.dma_start(out=dst["dk"](t), in_=flat(ko[:, 0:2, :])),
                                  reads=[r_ko], writes=[r_out], owner=r_ko, kind="out", final=True)
                            kbb, r_kbb = m.kbb.next()
                            P.op("pool", lambda e: e.tensor_copy(out=kbb[:, 0:2, :], in_=ko[:, 0:2, :]), reads=[r_ko], writes=[r_kbb])
                            st, r_st = m.st.next()
                            transposes_bf(m, lambda h: kbb[:, h, :], r_kbb, 2, 128, st[:, 0:2, :], r_st)
                            P.dma("sp", lambda e: e.dma_start(out=dst["KbT"](t), in_=st[:, 0:2, :]),
                                  reads=[r_st], writes=[dst["r_KbT"]], owner=r_st, kind="out")
                            P.dma("sp", lambda e: e.dma_start(out=dst["dv"](t), in_=flat(kf[:, 0:2, :])),
                                  reads=[r_kf], writes=[r_out], owner=r_kf, kind="out", final=True)
                            ve, r_ve = m.ve.next()
                            P.op("dve", lambda e: e.tensor_copy(out=ve[:, 0:2, 0:128], in_=kf[:, 0:2, :]), reads=[r_kf], writes=[r_ve])
                            P.dma("sp", lambda e: e.dma_start(out=dst["VbE"](t), in_=ve[:, 0:2, :]),
                                  reads=[r_ve], writes=[dst["r_VbE"]], owner=r_ve, kind="out")
                    else:
                        kf, r_kf = m.kf.next()
                        P.op("act", lambda e, kf=kf, pm=pm, s=s: e.activation(out=kf[:, 0, 0:72], in_=pm[:, 0:72],
                                                                              func=AF.Copy, scale=m.rstd[:, s:s + 1]),
                             reads=[r_pm, m.r_rstd[s]], writes=[r_kf])

                        def late(kf=kf, r_kf=r_kf, t=t, s=s):
                            ko, r_ko = m.ko.next()
                            rope(m, kf, r_kf, 1, 64, s, 128, ko, r_ko)
                            P.dma("sp", lambda e: e.dma_start(out=dst["ik"](t), in_=ko[:, 0, 0:64]),
                                  reads=[r_ko], writes=[r_out], owner=r_ko, kind="out", final=True)
                            kbb, r_kbb = m.kbb.next()
                            P.op("pool", lambda e: e.tensor_copy(out=kbb[:, 0, 0:64], in_=ko[:, 0, 0:64]), reads=[r_ko], writes=[r_kbb])
                            st, r_st = m.st.next()
                            transposes_bf(m, lambda h: kbb[:, 0, 0:64], r_kbb, 1, 64, st[0:64, 0:1, :], r_st)
                            P.dma("sp", lambda e: e.dma_start(out=dst["KiT"](t), in_=st[0:64, 0, :]),
                                  reads=[r_st], writes=[dst["r_KiT"]], owner=r_st, kind="out")
                            lf, r_lf = m.lf.next()
                            l1, r_l1 = m.lg.next()
                            l2, r_l2 = m.lg.next()
                            P.op("dve", lambda e: e.tensor_tensor(out=lf, in0=kf[:, 0, 64:72], in1=bfr, op=ALU.add),
                                 reads=[r_kf, r_bfr], writes=[r_lf])
                            P.op("dve", lambda e: e.scalar_tensor_tensor(out=l1, in0=lf, scalar=-1.0, in1=lf, op0=ALU.mult, op1=ALU.max),
                                 reads=[r_lf], writes=[r_l1])
                            P.op("act", lambda e: e.activation(out=l1, in_=l1, func=AF.Exp, scale=-1.0), reads=[r_l1], writes=[r_l1])
                            P.op("act", lambda e: e.activation(out=l1, in_=l1, func=AF.Ln, scale=1.0, bias=1.0), reads=[r_l1], writes=[r_l1])
                            P.op("dve", lambda e: e.tensor_single_scalar(out=l2, in_=lf, scalar=0.0, op=ALU.min),
                                 reads=[r_lf], writes=[r_l2])
                            P.op("dve", lambda e: e.tensor_tensor(out=lf, in0=l2, in1=l1, op=ALU.subtract),
                                 reads=[r_l1, r_l2], writes=[r_lf])
                            P.dma("sp", lambda e: e.dma_start(out=dst["fl"](t), in_=lf),
                                  reads=[r_lf], writes=[r_out], owner=r_lf, kind="out", final=True)
                            if dst.get("logf_keep") is not None:
                                dst["logf_keep"](t, lf, r_lf)
                    DF.run()
                    DF.late(late)
            DF.run()

    def rows(ap, t):
        return ap[t * 128:(t + 1) * 128, :]

    mA = make_proj(8, gA_d)

    def keep_logf(t, lf, r_lf):
        P.op("pool", lambda e: e.tensor_copy(out=lfall[:, t, :], in_=lf), reads=[r_lf], writes=[r_lfall])

    dstP = dict(
        fk=lambda t: rows(o_fk, t), fv=lambda t: rows(o_fv, t), fl=lambda t: rows(o_fl, t),
        dk=lambda t: rows(o_dk, t), dv=lambda t: rows(o_dv, t), ik=lambda t: rows(o_ik, t),
        KaT=lambda t, c: KaT[4 * c:4 * c + 4, :, t * 128:(t + 1) * 128].rearrange("h d t -> d h t"),
        VaE=lambda t, c: VaE[4 * c:4 * c + 4, :, t, :].rearrange("h p e -> p h e"),
        KbT=lambda t: KbT[:, :, t * 128:(t + 1) * 128].rearrange("h d t -> d h t"),
        VbE=lambda t: VbE[:, :, t, :].rearrange("h p e -> p h e"),
        KiT=lambda t: KiT[:, t * 128:(t + 1) * 128],
        r_KaT=r_KaT, r_VaE=r_VaE, r_KbT=r_KbT, r_VbE=r_VbE, r_KiT=r_KiT, logf_keep=keep_logf,
    )
    kv_pass(mA, 32, lambda t: rows(xb, t), lambda t: rows(tabA, t), dstP)
    dstS = dict(
        fk=lambda t: s_fk[:, :], fv=lambda t: s_fv[:, :], fl=lambda t: s_fl[:, :],
        dk=lambda t: s_dk[:, :], dv=lambda t: s_dv[:, :], ik=lambda t: s_ik[:, :],
        KaT=lambda t, c: KaTs[4 * c:4 * c + 4, :, :].rearrange("h d t -> d h t"),
        VaE=lambda t, c: VaEs[4 * c:4 * c + 4, :, 0, :].rearrange("h p e -> p h e"),
        KbT=lambda t: KbTs[:, :, :].rearrange("h d t -> d h t"),
        VbE=lambda t: VbEs[:, :, 0, :].rearrange("h p e -> p h e"),
        KiT=lambda t: KiTs[:, :],
        r_KaT=r_s[0], r_VaE=r_s[1], r_KbT=r_s[2], r_VbE=r_s[3], r_KiT=r_s[4],
        logf_keep=lambda t, lf, r_lf: P.op("pool", lambda e: e.tensor_copy(out=lfs, in_=lf), reads=[r_lf], writes=[r_lfs]),
    )
    kv_pass(mA, 1, lambda t: rows(xq, 9), lambda t: rows(tabQ, 9), dstS)
    barrier(P)
    A.release(base_mark)

    mB = make_proj(NTQ, gA_d)
    DFB = Defer()
    for s in range(NTQ):
        load_x_tile(mB, rows(xq, s), s, rows(tabQ, s))
    pending_wq = load_w_chunk(mB, w_q, 0, 512)
    for c in range(7):
        col0 = c * 512
        ncols = 512 if c < 6 else 16
        wt, r_w = pending_wq
        if c + 1 < 7:
            pending_wq = load_w_chunk(mB, w_q, (c + 1) * 512, 512 if c + 1 < 6 else 16)
        for s in range(NTQ):
            pm, r_pm = project(mB, s, wt, r_w, ncols)
            if c in (0, 1, 2, 3):
                ctx = head_norm_early(mB, pm[:, 0:512], r_pm, s, 4)

                def late(ctx=ctx, s=s, c=c):
                    kn, r_kn = head_norm_late(mB, ctx, 4, 0 if c < 2 else 2)
                    if c >= 2:
                        ko, r_ko = mB.ko.next()
                        rope(mB, kn, r_kn, 4, 128, s, 0, ko, r_ko)
                        kn, r_kn = ko, r_ko
                    kbb, r_kbb = mB.kbb.next()
                    P.op("pool", lambda e: e.tensor_copy(out=kbb, in_=kn), reads=[r_kn], writes=[r_kbb])
                    st, r_st = mB.st.next()
                    transposes_bf(mB, lambda h: kbb[:, h, :], r_kbb, 4, 128, st[:, 0:4, :], r_st)
                    dq, r_dq = (QaT, r_QaT) if c < 2 else (QbT, r_QbT)
                    hc = (c % 2) * 4
                    P.dma("sp", lambda e: e.dma_start(out=dq[s, :, hc:hc + 4, :], in_=st[:, 0:4, :]),
                          reads=[r_st], writes=[r_dq], owner=r_st, kind="out")
            elif c in (4, 5):
                kf, r_kf = mB.kf.next()
                P.op("act", lambda e, kf=kf, pm=pm, s=s: e.activation(out=flat(kf), in_=pm[:, 0:512],
                                                                      func=AF.Copy, scale=mB.rstd[:, s:s + 1]),
                     reads=[r_pm, mB.r_rstd[s]], writes=[r_kf])

                def late(kf=kf, r_kf=r_kf, s=s, c=c):
                    ko, r_ko = mB.ko.next()
                    kf8 = flat(kf).rearrange("p (h d) -> p h d", d=64)
                    ko8 = flat(ko).rearrange("p (h d) -> p h d", d=64)
                    rope(mB, kf8, r_kf, 8, 64, s, 128, ko8, r_ko)
                    kbb, r_kbb = mB.kbb.next()
                    kbb8 = flat(kbb).rearrange("p (h d) -> p h d", d=64)
                    P.op("pool", lambda e: e.tensor_copy(out=kbb8, in_=ko8), reads=[r_ko], writes=[r_kbb])
                    st, r_st = mB.st.next()
                    transposes_bf(mB, lambda h: kbb8[:, h, :], r_kbb, 8, 64, st[0:64, :, :], r_st)
                    hc = (c - 4) * 8
                    P.dma("sp", lambda e: e.dma_start(out=QiT[s, :, hc:hc + 8, :], in_=st[0:64, :, :]),
                          reads=[r_st], writes=[r_QiT], owner=r_st, kind="out")
            else:
                P.op("dve", lambda e, pm=pm, s=s: e.tensor_scalar(out=wi_all[:, s, :], in0=pm[:, 0:16], scalar1=mB.rstd[:, s:s + 1],
                                                                 scalar2=IDX_SCALE, op0=ALU.mult, op1=ALU.mult),
                     reads=[r_pm, mB.r_rstd[s]], writes=[r_wi])

                def late():
                    pass
            DFB.run()
            DFB.late(late)
    DFB.run()
    barrier(P)
    A.release(base_mark)

    att, _ = A.alloc("att", [NTQ, D], BF16)
    r_att = [P.res(f"att{i}") for i in range(NTQ)]
    att_mark = A.mark()

    pinc, r_pinc = A.alloc("pinc", [32, 8], F32)
    tot, r_tot = A.alloc("tot", [32, 8], F32)
    cP, r_cP = A.alloc("cP", [32, 8], F32)
    cref, r_cref = A.alloc("cref", [NT_Q, 8], F32)
    biasF, r_biasF = A.alloc("biasF", [NT_Q, 8, 32], F32)
    bbias, r_bbias = A.alloc("bbias", [NT_Q, 32], F32)
    oh, r_oh = A.alloc("oh", [NT_Q, 32], F32)
    tmp4, r_tmp4 = A.alloc("tmp4", [NT_Q, 8, 32], F32)
    fmask, r_fmask = A.alloc("fmask", [NT_Q, 8, 128], BF16)
    P.dma("sp", lambda e: e.dma_start(out=bbias, in_=bbias_d[:, :, :]), writes=[r_bbias], owner=r_bbias)
    P.dma("sp", lambda e: e.dma_start(out=oh, in_=oh_d[:, :, :]), writes=[r_oh], owner=r_oh)
    for i in range(NT_Q):
        P.dma("pool", lambda e, i=i: e.dma_start(out=fmask[:, i, :, :], in_=fmask_d[i]), writes=[r_fmask], owner=r_fmask)
    lf2 = lfall.rearrange("p b h -> p (b h)")
    P.op("pe", lambda e: e.matmul(pb[6][:, 0:256], lhsT=tri, rhs=lf2, start=True, stop=True),
         reads=[r_tri, r_lfall], writes=[r_pb[6]])
    P.op("pe", lambda e: e.matmul(pb[7][:, 0:256], lhsT=ones, rhs=lf2, start=True, stop=True),
         reads=[r_ones, r_lfall], writes=[r_pb[7]])
    P.op("act", lambda e: e.activation(out=tot.rearrange("p b h -> p (b h)"), in_=pb[7][:, 0:256], func=AF.Copy),
         reads=[r_pb[7]], writes=[r_tot])
    for h in range(8):
        P.op("dve", lambda e, h=h: e.tensor_tensor_scan(out=pinc[:, :, h], data0=ones[:, 0:32], data1=tot[:, :, h],
                                                        initial=0.0, op0=ALU.mult, op1=ALU.add),
             reads=[r_tot, r_ones], writes=[r_pinc])
    P.op("dve", lambda e: e.tensor_tensor(out=cP.rearrange("p b h -> p (b h)"), in0=pb[6][:, 0:256],
                                          in1=pinc.rearrange("p b h -> p (b h)"), op=ALU.add),
         reads=[r_pb[6], r_pinc], writes=[r_cP])
    P.op("dve", lambda e: e.tensor_tensor(out=cP, in0=cP, in1=tot, op=ALU.subtract), reads=[r_cP, r_tot], writes=[r_cP])
    pinc_hb = pinc.rearrange("p b h -> p h b")
    cP_hb = cP.rearrange("p b h -> p h b")
    P.op("dve", lambda e: e.tensor_tensor(out=tmp4, in0=bc(pinc_hb.unsqueeze(1), [128, NT_Q, 8, 32]),
                                          in1=bc(oh.unsqueeze(2), [128, NT_Q, 8, 32]), op=ALU.mult),
         reads=[r_pinc, r_oh], writes=[r_tmp4])
    P.op("dve", lambda e: e.tensor_reduce(out=cref.rearrange("p i h -> p (i h)"), in_=tmp4.rearrange("p i h b -> p (i h) b"),
                                          axis=AX.X, op=ALU.add),
         reads=[r_tmp4], writes=[r_cref])
    P.op("dve", lambda e: e.tensor_tensor(out=biasF, in0=bc(cref.unsqueeze(3), [128, NT_Q, 8, 32]),
                                          in1=bc(cP_hb.unsqueeze(1), [128, NT_Q, 8, 32]), op=ALU.subtract),
         reads=[r_cref, r_cP], writes=[r_biasF])
    P.op("dve", lambda e: e.tensor_tensor(out=biasF, in0=biasF, in1=bc(bbias.unsqueeze(2), [128, NT_Q, 8, 32]), op=ALU.add),
         reads=[r_biasF, r_bbias], writes=[r_biasF])

    kt_ring = A.ring("kt", [SEQ], BF16, 2)
    vt_ring = A.ring("vt", [32, 129], BF16, 2)
    qh_ring = A.ring("qh", [NT_Q, 128], BF16, 2)
    pt_ring = A.ring("ptile", [4, 128], BF16, 3)
    rd_ring = A.ring("rd", [2], F32, 4)
    pS = BankRing([2, 3])
    pO = BankRing([4, 5])
    pend = [None]

    def flush():
        if pend[0] is not None:
            pend[0]()
            pend[0] = None

    for h in range(8):
        kt, r_kt = kt_ring.next()
        vt, r_vt = vt_ring.next()
        qh, r_qh = qh_ring.next()
        P.dma("sp", lambda e, kt=kt, h=h: e.dma_start(out=kt, in_=KaT[h]), reads=[r_KaT], writes=[r_kt], owner=r_kt)
        P.dma("sp", lambda e, vt=vt, h=h: e.dma_start(out=vt, in_=VaE[h]), reads=[r_VaE], writes=[r_vt], owner=r_vt)
        P.dma("sp", lambda e, qh=qh, h=h: e.dma_start(out=qh, in_=QaT[0:NT_Q, :, h, :].rearrange("t d k -> d t k")),
              reads=[r_QaT], writes=[r_qh], owner=r_qh)
        for i in range(NT_Q):
            KBi = KB_OF(i)
            mk = MASK_KBS(i)
            bo = pO.next()
            po = pb[bo][:, 0:129]
            for k0 in range(0, KBi, 4):
                nk = min(4, KBi - k0)
                bs = pS.next()
                ps = pb[bs].rearrange("p (a b) -> p a b", b=128)
                ptile, r_ptile = pt_ring.next()
                for kk in range(nk):
                    kb = k0 + kk
                    P.op("pe", lambda e, ps=ps, kk=kk, kb=kb, kt=kt, qh=qh, i=i: e.matmul(
                        ps[:, kk, :], lhsT=kt[:, kb * 128:(kb + 1) * 128], rhs=qh[:, i, :], start=True, stop=True),
                        reads=[r_kt, r_qh], writes=[r_pb[bs]])
                for kk in range(nk):
                    kb = k0 + kk
                    P.op("act", lambda e, ps=ps, kk=kk, kb=kb, ptile=ptile, i=i, h=h: e.activation(
                        out=ptile[:, kk, :], in_=ps[:, kk, :], func=AF.Exp, scale=SCALE, bias=biasF[:, i, h, kb:kb + 1]),
                        reads=[r_pb[bs], r_biasF], writes=[r_ptile])
                    if kb in mk:
                        sl = mk.index(kb)
                        P.op("pool", lambda e, ptile=ptile, kk=kk, i=i, sl=sl: e.tensor_tensor(
                            out=ptile[:, kk, :], in0=ptile[:, kk, :], in1=fmask[:, i, sl, :], op=ALU.mult),
                            reads=[r_ptile, r_fmask], writes=[r_ptile])
                flush()

                def pv(k0=k0, nk=nk, ptile=ptile, r_ptile=r_ptile, vt=vt, r_vt=r_vt, po=po, bo=bo, KBi=KBi, i=i, h=h):
                    for kk in range(nk):
                        kb = k0 + kk
                        P.op("pe", lambda e, kk=kk, kb=kb: e.matmul(po, lhsT=ptile[:, kk, :], rhs=vt[:, kb, :],
                                                                    start=(kb == 0), stop=(kb == KBi - 1)),
                             reads=[r_ptile, r_vt], writes=[r_pb[bo]])
                    if k0 + nk == KBi:
                        rd, r_rd = rd_ring.next()
                        P.op("dve", lambda e: e.tensor_scalar(out=rd[:, 0:1], in0=po[:, 128:129], scalar1=1e-30, scalar2=None,
                                                              op0=ALU.max), reads=[r_pb[bo]], writes=[r_rd])
                        P.op("dve", lambda e: e.reciprocal(out=rd[:, 1:2], in_=rd[:, 0:1]), reads=[r_rd], writes=[r_rd])
                        P.op("act", lambda e: e.activation(out=att[:, i, h * 128:(h + 1) * 128], in_=po[:, 0:128], func=AF.Copy,
                                                           scale=rd[:, 1:2]), reads=[r_pb[bo], r_rd], writes=[r_att[i]])
                pend[0] = pv
    flush()
    barrier(P)
    A.release(att_mark)

    kbt, r_kbt = A.alloc("kbt", [2, SEQ], BF16)
    vbt, r_vbt = A.alloc("vbt", [2, 32, 129], BF16)
    kit, r_kit = A.alloc("kit", [SEQ], BF16)
    P.dma("sp", lambda e: e.dma_start(out=kbt, in_=KbT.rearrange("h d t -> d h t")), reads=[r_KbT], writes=[r_kbt], owner=r_kbt)
    P.dma("sp", lambda e: e.dma_start(out=vbt, in_=VbE.rearrange("h p b e -> p h b e")), reads=[r_VbE], writes=[r_vbt], owner=r_vbt)
    P.dma("sp", lambda e: e.dma_start(out=kit[0:64, :], in_=KiT[:, :]), reads=[r_KiT], writes=[r_kit], owner=r_kit)
    qi_ring = A.ring("qi_t", [16, 128], BF16, 2)
    qb_ring = A.ring("qb_t", [8, 128], BF16, 2)
    sc, r_sc = A.alloc("sc", [SEQ], F32)
    wk, r_wk = A.alloc("wk", [SEQ], F32)
    nm_ring = A.ring("nm", [SEQ], BF16, 1)
    sel, r_sel = A.alloc("sel", [SEQ], BF16)
    selT, r_selT = A.alloc("selT", [32, 128], BF16)
    rl_ring = A.ring("rl", [512], F32, 3)
    m8_ring = A.ring("m8", [8], F32, 2)
    thr_ring = A.ring("thr", [1], F32, 2)
    pt4_ring = A.ring("pt4", [4, 128], BF16, 3)
    rd2_ring = A.ring("rd2", [2], F32, 4)
    pI = BankRing([2, 3])
    pT2 = BankRing([0, 1])
    pS2 = BankRing([4, 5])

    for i in range(NT_Q):
        KBi = KB_OF(i)
        L = KBi * 128
        qi_t, r_qi = qi_ring.next()
        qb_t, r_qb = qb_ring.next()
        nm_t, r_nm = nm_ring.next()
        P.dma("sp", lambda e, qi_t=qi_t, i=i: e.dma_start(out=qi_t[0:64, :, :], in_=QiT[i]), reads=[r_QiT], writes=[r_qi], owner=r_qi)
        P.dma("sp", lambda e, qb_t=qb_t, i=i: e.dma_start(out=qb_t, in_=QbT[i]), reads=[r_QbT], writes=[r_qb], owner=r_qb)
        for q2 in range(0, L, 2048):
            n2 = min(2048, L - q2)
            P.dma("pool", lambda e, nm_t=nm_t, i=i, q2=q2, n2=n2: e.dma_start(out=nm_t[:, q2:q2 + n2], in_=nm_d[i, :, q2:q2 + n2]),
                  writes=[r_nm], owner=r_nm)
        for g0 in range(0, L, 512):
            ncol = min(512, L - g0)
            for hh in range(H_IDX):
                bi = pI.next()
                rl, r_rl = rl_ring.next()
                P.op("pe", lambda e, bi=bi, hh=hh, g0=g0, ncol=ncol, qi_t=qi_t: e.matmul(
                    pb[bi][:, 0:ncol], lhsT=qi_t[0:64, hh, :], rhs=kit[0:64, g0:g0 + ncol], start=True, stop=True),
                    reads=[r_qi, r_kit], writes=[r_pb[bi]])
                P.op("act", lambda e, bi=bi, rl=rl, ncol=ncol: e.activation(out=rl[:, 0:ncol], in_=pb[bi][:, 0:ncol], func=AF.Relu),
                     reads=[r_pb[bi]], writes=[r_rl])
                if hh == 0:
                    P.op("dve", lambda e, rl=rl, g0=g0, ncol=ncol, i=i, nm_t=nm_t: e.scalar_tensor_tensor(
                        out=sc[:, g0:g0 + ncol], in0=rl[:, 0:ncol], scalar=wi_all[:, i, 0:1], in1=nm_t[:, g0:g0 + ncol],
                        op0=ALU.mult, op1=ALU.add), reads=[r_rl, r_wi, r_nm], writes=[r_sc])
                else:
                    P.op("dve", lambda e, rl=rl, g0=g0, ncol=ncol, i=i, hh=hh: e.scalar_tensor_tensor(
                        out=sc[:, g0:g0 + ncol], in0=rl[:, 0:ncol], scalar=wi_all[:, i, hh:hh + 1], in1=sc[:, g0:g0 + ncol],
                        op0=ALU.mult, op1=ALU.add), reads=[r_rl, r_wi, r_sc], writes=[r_sc])
        m8, r_m8 = m8_ring.next()
        thr, r_thr = thr_ring.next()
        for r in range(32):
            src = sc if r == 0 else wk
            r_src = r_sc if r == 0 else r_wk
            P.op("dve", lambda e, src=src, m8=m8, L=L: e.max(out=m8, in_=src[:, 0:L]), reads=[r_src], writes=[r_m8])
            if r < 31:
                P.op("dve", lambda e, src=src, m8=m8, L=L: e.match_replace(out=wk[:, 0:L], in_to_replace=m8, in_values=src[:, 0:L],
                                                                          imm_value=NEG),
                     reads=[r_src, r_m8], writes=[r_wk])
        P.op("dve", lambda e, m8=m8, thr=thr: e.tensor_scalar(out=thr, in0=m8[:, 7:8], scalar1=-1.0e29, scalar2=None, op0=ALU.max),
             reads=[r_m8], writes=[r_thr])
        P.op("dve", lambda e, thr=thr, L=L: e.tensor_scalar(out=sel[:, 0:L], in0=sc[:, 0:L], scalar1=thr[:, 0:1], scalar2=None,
                                                          op0=ALU.is_ge), reads=[r_sc, r_thr], writes=[r_sel])
        for k0 in range(0, KBi, 8):
            nk = min(8, KBi - k0)
            bt = pT2.next()
            ptb = bank_bf(bt)
            for kk in range(nk):
                kb = k0 + kk
                P.op("pe", lambda e, ptb=ptb, kk=kk, kb=kb: e.transpose(out=ptb[:, kk, :], in_=sel[:, kb * 128:(kb + 1) * 128],
                                                                        identity=identb),
                     reads=[r_sel, r_identb], writes=[r_pb[bt]])
            P.op("act", lambda e, ptb=ptb, k0=k0, nk=nk: e.activation(out=selT[:, k0:k0 + nk, :], in_=ptb[:, 0:nk, :], func=AF.Copy),
                 reads=[r_pb[bt]], writes=[r_selT])
        for g2 in range(KV_B):
            obank = [6, 7]
            ov = [pb[b_][:, 0:258].rearrange("p (a b) -> p a b", b=129) for b_ in obank]
            for kb in range(KBi):
                bs = pS2.next()
                ps = pb[bs].rearrange("p (a b) -> p a b", b=128)
                pt4, r_pt4 = pt4_ring.next()
                P.op("pe", lambda e, ps=ps, kb=kb, g2=g2, qb_t=qb_t: e.matmul(
                    ps, lhsT=kbt[:, g2, kb * 128:(kb + 1) * 128], rhs=qb_t[:, 4 * g2:4 * g2 + 4, :], start=True, stop=True),
                    reads=[r_kbt, r_qb], writes=[r_pb[bs]])
                P.op("act", lambda e, ps=ps, pt4=pt4: e.activation(out=pt4, in_=ps, func=AF.Exp, scale=SCALE),
                     reads=[r_pb[bs]], writes=[r_pt4])
                P.op("dve", lambda e, pt4=pt4, kb=kb: e.tensor_tensor(out=pt4, in0=pt4, in1=bc(selT[:, kb, :].unsqueeze(1), [128, 4, 128]),
                                                                      op=ALU.mult), reads=[r_pt4, r_selT], writes=[r_pt4])
                flush()

                def pv2(kb=kb, pt4=pt4, r_pt4=r_pt4, g2=g2, KBi=KBi, i=i, ov=ov, obank=obank):
                    for hq in range(4):
                        o_ap = ov[hq // 2][:, hq % 2, :]
                        P.op("pe", lambda e, hq=hq, o_ap=o_ap: e.matmul(o_ap, lhsT=pt4[:, hq, :], rhs=vbt[:, g2, kb, :],
                                                                        start=(kb == 0 and hq % 2 == 0), stop=(kb == KBi - 1),
                                                                        skip_group_check=True),
                             reads=[r_pt4, r_vbt], writes=[r_pb[obank[hq // 2]]])
                    if kb == KBi - 1:
                        for hq in range(4):
                            o_ap = ov[hq // 2][:, hq % 2, :]
                            rb_ = r_pb[obank[hq // 2]]
                            rd, r_rd = rd2_ring.next()
                            c0 = 1024 + (4 * g2 + hq) * 128
                            P.op("dve", lambda e, o_ap=o_ap, rd=rd: e.tensor_scalar(out=rd[:, 0:1], in0=o_ap[:, 128:129], scalar1=1e-30,
                                                                                   scalar2=None, op0=ALU.max), reads=[rb_], writes=[r_rd])
                            P.op("dve", lambda e, rd=rd: e.reciprocal(out=rd[:, 1:2], in_=rd[:, 0:1]), reads=[r_rd], writes=[r_rd])
                            P.op("act", lambda e, o_ap=o_ap, rd=rd, c0=c0: e.activation(out=att[:, i, c0:c0 + 128], in_=o_ap[:, 0:128],
                                                                                       func=AF.Copy, scale=rd[:, 1:2]),
                                 reads=[rb_, r_rd], writes=[r_att[i]])
                pend[0] = pv2
            flush()
    barrier(P)
    A.release(att_mark)

    NPOOL = cfk.shape[0] // 128
    SK = PAST + 128
    ptb, r_ptb = A.alloc("ptb", [256], I32)
    idx_tok, r_idx = A.alloc("idx_tok", [256], I32)
    idx_pg, r_idxpg = A.alloc("idx_pg", [4], I32)
    iota_i, r_iota = A.alloc("iota_i", [1], I32)
    iota_f, r_iotaf = A.alloc("iota_f", [1], F32)
    lst, r_lst = A.alloc("lst", [128], F32)
    o64, r_o64 = A.alloc("o64", [128], F32)
    lblk, r_lblk = A.alloc("lblk", [128], F32)
    esel, r_esel = A.alloc("esel", [4, 128], F32)
    smask, r_smask = A.alloc("smask", [4, 32], BF16)
    P.dma("sp", lambda e: e.dma_start(out=ptb, in_=pt_d.rearrange("s l -> (s l)").partition_broadcast(128)),
          writes=[r_ptb], owner=r_ptb)
    P.op("pool", lambda e: e.memset(idx_pg, 0), writes=[r_idxpg])
    P.dma("sp", lambda e: e.dma_start(out=idx_pg[0:64, :], in_=pt_d.rearrange("s l -> l s"), allow_slow_non_contiguous=True),
          writes=[r_idxpg], owner=r_idxpg)
    P.dma("sp", lambda e: e.dma_start(out=lst, in_=lst_d[:, :]), writes=[r_lst], owner=r_lst)
    P.dma("sp", lambda e: e.dma_start(out=o64, in_=o64_d[:, :]), writes=[r_o64], owner=r_o64)
    P.dma("sp", lambda e: e.dma_start(out=lblk, in_=lblk_d[:, :]), writes=[r_lblk], owner=r_lblk)
    P.dma("sp", lambda e: e.dma_start(out=esel, in_=esel_d[:, :, :]), writes=[r_esel], owner=r_esel)
    P.dma("pool", lambda e: e.dma_start(out=smask, in_=smask_d[:, :, :]), writes=[r_smask], owner=r_smask)
    P.op("pool", lambda e: e.iota(iota_i, [[0, 1]], base=0, channel_multiplier=1), writes=[r_iota])
    P.op("dve", lambda e: e.tensor_copy(out=iota_f, in_=iota_i), reads=[r_iota], writes=[r_iotaf])
    P.op("dve", lambda e: e.tensor_scalar(out=idx_tok, in0=ptb, scalar1=128.0, scalar2=iota_f[:, 0:1], op0=ALU.mult, op1=ALU.add),
         reads=[r_ptb, r_iotaf], writes=[r_idx])

    def gather(dst_ap, r_dst, src2d, idx_col_ap, r_ix):
        P.dma("pool", lambda e: e.indirect_dma_start(out=dst_ap, out_offset=None, in_=src2d,
                                                     in_offset=bass.IndirectOffsetOnAxis(ap=idx_col_ap, axis=0)),
              reads=[r_ix], writes=[r_dst], owner=r_dst)

    kp_ring = A.ring("kp", [8, 128], BF16, 3)
    vp_ring = A.ring("vp", [8, 129], BF16, 3)
    ktp_ring = A.ring("ktp", [8, 128], BF16, 3)
    vs_ring = A.ring("vstage", [8, 128], BF16, 3)
    z_ring = A.ring("z_s", [8, 32], F32, 2)
    ps_rings = [A.ring("pt_s0", [8, 32], BF16, 2), A.ring("pt_s1", [8, 32], BF16, 2),
                A.ring("pt_s2", [8, 64], BF16, 2), A.ring("pt_s3", [8, 64], BF16, 2)]
    for t_, r_ in ps_rings[2].items:
        P.op("pool", lambda e, t_=t_: e.memset(t_[:, :, 32:64], 0.0), writes=[r_])
    for t_, r_ in ps_rings[3].items:
        P.op("pool", lambda e, t_=t_: e.memset(t_[:, :, 0:32], 0.0), writes=[r_])
    qs_a, r_qsa = A.alloc("qs_a", [8, 128], BF16)
    qs_b, r_qsb = A.alloc("qs_b", [8, 128], BF16)
    P.dma("sp", lambda e: e.dma_start(out=qs_a, in_=QaT[NT_Q]), reads=[r_QaT], writes=[r_qsa], owner=r_qsa)
    P.dma("sp", lambda e: e.dma_start(out=qs_b, in_=QbT[NT_Q]), reads=[r_QbT], writes=[r_qsb], owner=r_qsb)
    for t_, r_ in vp_ring.items:
        P.op("pool", lambda e, t_=t_: e.memset(t_[:, :, 128:129], 1.0), writes=[r_])
    pTs = BankRing([0, 1])
    pSs = BankRing([2, 3])
    OB = [4, 5, 6]

    def o_view(h, s):
        b = OB[h // 3]
        r0, r1 = (32 * s, 32 * s + 32) if s < 2 else (64, 128)
        return pb[b][r0:r1, (h % 3) * 129:(h % 3) * 129 + 129], b

    def sample_attn(kind):
        nkv = 8 if kind == "fox" else 2
        per = 1 if kind == "fox" else 4
        kcache, vcache = (cfk, cfv) if kind == "fox" else (cdk, cdv)
        qs, r_qs = (qs_a, r_qsa) if kind == "fox" else (qs_b, r_qsb)
        KTs, VEs, rKTs, rVEs = (KaTs, VaEs, r_s[0], r_s[1]) if kind == "fox" else (KbTs, VbEs, r_s[2], r_s[3])
        col0 = 0 if kind == "fox" else 1024
        for s in range(4):
            for lp in range(NPG + 1):
                new = (lp == NPG)
                ktp, r_ktp = ktp_ring.next()
                vp, r_vp = vp_ring.next()
                if not new:
                    kp, r_kp = kp_ring.next()
                    ic = idx_tok[:, s * 64 + lp:s * 64 + lp + 1]
                    vs, r_vs = vs_ring.next()
                    gather(kp.rearrange("p h d -> p (h d)")[:, 0:nkv * 128], r_kp, kcache, ic, r_idx)
                    gather(vs.rearrange("p h d -> p (h d)")[:, 0:nkv * 128], r_vs, vcache, ic, r_idx)
                    P.op("dve", lambda e, vs=vs, vp=vp: e.tensor_copy(out=vp[:, 0:nkv, 0:128], in_=vs[:, 0:nkv, :]),
                         reads=[r_vs], writes=[r_vp])
                    bt = pTs.next()
                    ptb_ = bank_bf(bt)
                    for h in range(nkv):
                        P.op("pe", lambda e, h=h, kp=kp, ptb_=ptb_: e.transpose(out=ptb_[:, h, :], in_=kp[:, h, :], identity=identb),
                             reads=[r_kp, r_identb], writes=[r_pb[bt]])
                    P.op("act", lambda e, ktp=ktp, ptb_=ptb_: e.activation(out=ktp[:, 0:nkv, :], in_=ptb_[:, 0:nkv, :], func=AF.Copy),
                         reads=[r_pb[bt]], writes=[r_ktp])
                else:
                    P.dma("sp", lambda e, ktp=ktp: e.dma_start(out=ktp[:, 0:nkv, :], in_=KTs.rearrange("h d t -> d h t")),
                          reads=[rKTs], writes=[r_ktp], owner=r_ktp)
                    P.dma("sp", lambda e, vp=vp: e.dma_start(out=vp[:, 0:nkv, :], in_=VEs[:, :, 0, :].rearrange("h p e -> p h e")),
                          reads=[rVEs], writes=[r_vp], owner=r_vp)
                bs = pSs.next()
                psv = pb[bs][:, 0:256].rearrange("p (a b) -> p a b", b=32)
                for g in range(nkv):
                    if per == 1:
                        P.op("pe", lambda e, g=g, ktp=ktp, psv=psv, s=s: e.matmul(psv[:, g, :], lhsT=ktp[:, g, :], rhs=qs[:, g, 32 * s:32 * s + 32],
                                                                            start=True, stop=True),
                             reads=[r_ktp, r_qs], writes=[r_pb[bs]])
                    else:
                        P.op("pe", lambda e, g=g, ktp=ktp, psv=psv, s=s: e.matmul(psv[:, 4 * g:4 * g + 4, :], lhsT=ktp[:, g, :],
                                                                            rhs=qs[:, 4 * g:4 * g + 4, 32 * s:32 * s + 32],
                                                                            start=True, stop=True),
                             reads=[r_ktp, r_qs], writes=[r_pb[bs]])
                pt_full, r_pts = ps_rings[s].next()
                pt_s = pt_full if s < 2 else pt_full[:, :, (s - 2) * 32:(s - 2) * 32 + 32]
                if kind == "fox":
                    z, r_z = z_ring.next()
                    bias_ap = bN[:, s, :] if new else bP[:, s, :, lp]
                    r_bias = r_bN if new else r_bP[s]
                    P.op("dve", lambda e, z=z, psv=psv, bias_ap=bias_ap: e.scalar_tensor_tensor(
                        out=z, in0=psv, scalar=SCALE, in1=bc(bias_ap.unsqueeze(2), [128, 8, 32]), op0=ALU.mult, op1=ALU.add),
                        reads=[r_pb[bs], r_bias], writes=[r_z])
                    P.op("act", lambda e, z=z, pt_s=pt_s: e.activation(out=pt_s, in_=z, func=AF.Exp), reads=[r_z], writes=[r_pts])
                    if new:
                        P.op("dve", lambda e, pt_s=pt_s, s=s: e.tensor_tensor(out=pt_s, in0=pt_s, in1=bc(smask[:, s, :].unsqueeze(1), [128, 8, 32]),
                                                                         op=ALU.mult), reads=[r_pts, r_smask], writes=[r_pts])
                else:
                    P.op("act", lambda e, psv=psv, pt_s=pt_s: e.activation(out=pt_s, in_=psv, func=AF.Exp, scale=SCALE),
                         reads=[r_pb[bs]], writes=[r_pts])
                    P.op("dve", lambda e, pt_s=pt_s, lp=lp, s=s: e.tensor_tensor(
                        out=pt_s, in0=pt_s, in1=bc(selTs[:, lp, 32 * s:32 * s + 32].unsqueeze(1), [128, 8, 32]), op=ALU.mult),
                        reads=[r_pts, r_selTs], writes=[r_pts])
                flush()

                def pv3(pt_full=pt_full, r_pts=r_pts, vp=vp, r_vp=r_vp, lp=lp, s=s, new=new):
                    for h in range(8):
                        o_ap, b = o_view(h, s)
                        P.op("pe", lambda e, h=h, o_ap=o_ap: e.matmul(o_ap, lhsT=pt_full[:, h, :], rhs=vp[:, h // per, :],
                                                                      start=(lp == 0 and h % 3 == 0 and s != 3), stop=new,
                                                                      skip_group_check=True),
                             reads=[r_pts, r_vp], writes=[r_pb[b]])
                pend[0] = pv3
            flush()
        for h in range(8):
            b = OB[h // 3]
            o_full = pb[b][:, (h % 3) * 129:(h % 3) * 129 + 129]
            rd, r_rd = rd2_ring.next()
            P.op("dve", lambda e, o_full=o_full, rd=rd: e.tensor_scalar(out=rd[:, 0:1], in0=o_full[:, 128:129], scalar1=1e-30, scalar2=None,
                                                                         op0=ALU.max), reads=[r_pb[b]], writes=[r_rd])
            P.op("dve", lambda e, rd=rd: e.reciprocal(out=rd[:, 1:2], in_=rd[:, 0:1]), reads=[r_rd], writes=[r_rd])
            P.op("act", lambda e, o_full=o_full, rd=rd, h=h: e.activation(out=att[:, NT_Q, col0 + h * 128:col0 + (h + 1) * 128],
                                                                          in_=o_full[:, 0:128], func=AF.Copy, scale=rd[:, 1:2]),
                 reads=[r_pb[b], r_rd], writes=[r_att[NT_Q]])

    rl_ring = A.ring("rl_s", [512], F32, 3)
    m8_ring = A.ring("m8_s", [8], F32, 2)
    thr_ring = A.ring("thr_s", [1], F32, 2)
    rd2_ring = A.ring("rd2_s", [2], F32, 4)
    smp_mark = A.mark()
    cs_s, r_cs = A.alloc("cs_s", [8], F32)
    csref, r_csref = A.alloc("csref", [4, 8], F32)
    bP, r_bP_base = A.alloc("bP", [4, 8, 64], F32)
    r_bP = [P.res(f"bP{s}") for s in range(4)]
    bN, r_bN = A.alloc("bN", [4, 8], F32)
    P.op("pe", lambda e: e.matmul(pb[7][:, 0:8], lhsT=lblk, rhs=lfs, start=True, stop=True), reads=[r_lblk, r_lfs], writes=[r_pb[7]])
    P.op("act", lambda e: e.activation(out=cs_s, in_=pb[7][:, 0:8], func=AF.Copy), reads=[r_pb[7]], writes=[r_cs])
    for s in range(4):
        P.op("pe", lambda e, s=s: e.matmul(pb[7][:, 8 + 8 * s:16 + 8 * s], lhsT=esel[:, s, :], rhs=cs_s, start=True, stop=True),
             reads=[r_esel, r_cs], writes=[r_pb[7]])
    P.op("act", lambda e: e.activation(out=csref.rearrange("p s h -> p (s h)"), in_=pb[7][:, 8:40], func=AF.Copy),
         reads=[r_pb[7]], writes=[r_csref])
    P.op("dve", lambda e: e.tensor_tensor(out=bN, in0=csref, in1=bc(cs_s.unsqueeze(1), [128, 4, 8]), op=ALU.subtract),
         reads=[r_csref, r_cs], writes=[r_bN])
    lp_ring = A.ring("lp_t", [128, 8], F32, 2)
    cw_ring = A.ring("cw", [8, 128], F32, 2)
    sm2_ring = A.ring("sm2", [3, 8], F32, 2)
    for s in range(4):
        lp_t, r_lp = lp_ring.next()
        cw, r_cw = cw_ring.next()
        sm2, r_sm2 = sm2_ring.next()
        gather(lp_t.rearrange("p t h -> p (t h)"), r_lp, cfl, idx_pg[:, s:s + 1], r_idxpg)
        for h in range(8):
            P.op("dve", lambda e, h=h, cw=cw, lp_t=lp_t: e.tensor_tensor_scan(out=cw[:, h, :], data0=ones[:, 0:128], data1=lp_t[:, :, h],
                                                                              initial=0.0, op0=ALU.mult, op1=ALU.add),
                 reads=[r_lp, r_ones], writes=[r_cw])
        P.op("dve", lambda e, cw=cw, sm2=sm2: e.tensor_copy(out=sm2[:, 0, :], in_=cw[:, :, 127]), reads=[r_cw], writes=[r_sm2])
        P.op("pe", lambda e, sm2=sm2: e.matmul(pb[6][:, 0:8], lhsT=lst, rhs=sm2[:, 0, :], start=True, stop=True),
             reads=[r_lst, r_sm2], writes=[r_pb[6]])
        P.op("pe", lambda e, sm2=sm2: e.matmul(pb[6][:, 8:16], lhsT=o64, rhs=sm2[:, 0, :], start=True, stop=True),
             reads=[r_o64, r_sm2], writes=[r_pb[6]])
        P.op("act", lambda e, sm2=sm2: e.activation(out=sm2[:, 1:3, :].rearrange("p a h -> p (a h)"), in_=pb[6][:, 0:16], func=AF.Copy),
             reads=[r_pb[6]], writes=[r_sm2])
        P.op("dve", lambda e, cw=cw, sm2=sm2: e.tensor_tensor(out=cw, in0=cw, in1=bc(sm2[:, 1, :].unsqueeze(2), [128, 8, 128]), op=ALU.add),
             reads=[r_cw, r_sm2], writes=[r_cw])
        P.op("dve", lambda e, sm2=sm2, s=s: e.tensor_tensor(out=sm2[:, 2, :], in0=sm2[:, 2, :], in1=csref[:, s, :], op=ALU.add),
             reads=[r_sm2, r_csref], writes=[r_sm2])
        for h0 in range(0, 8, 4):
            pcT = pb[7][:, :].rearrange("p (a b) -> p a b", b=128)
            for hh in range(4):
                P.op("pe", lambda e, hh=hh, h0=h0, cw=cw, pcT=pcT: e.transpose(out=pcT[:, hh, :], in_=cw[:, h0 + hh, :], identity=identf),
                     reads=[r_cw, r_identf], writes=[r_pb[7]])
            P.op("dve", lambda e, h0=h0, s=s, sm2=sm2, pcT=pcT: e.scalar_tensor_tensor(
                out=bP[:, s, h0:h0 + 4, :], in0=pcT[:, :, 0:64], scalar=-1.0,
                in1=bc(sm2[:, 2, h0:h0 + 4].unsqueeze(2), [128, 4, 64]), op0=ALU.mult, op1=ALU.add),
                reads=[r_pb[7], r_sm2], writes=[r_bP[s]])

    sample_attn("fox")
    barrier(P)
    A.release(smp_mark)

    kiTall = dscr("kiTall", [4, 64, SK], BF16)
    r_kiTall = P.res("kiTall")
    kip_ring = A.ring("kip", [64], BF16, 3)
    kst_ring = A.ring("kist", [8, 128], BF16, 2)
    for s in range(4):
        for l0 in range(0, NPG, 8):
            bt = pTs.next()
            ptb_ = bank_bf(bt)
            kst, r_kst = kst_ring.next()
            for ll in range(8):
                lp = l0 + ll
                kip, r_kip = kip_ring.next()
                gather(kip, r_kip, cik, idx_tok[:, s * 64 + lp:s * 64 + lp + 1], r_idx)
                P.op("pe", lambda e, ll=ll, kip=kip, ptb_=ptb_: e.transpose(out=ptb_[0:64, ll, :], in_=kip, identity=identb),
                     reads=[r_kip, r_identb], writes=[r_pb[bt]])
            P.op("act", lambda e, kst=kst, ptb_=ptb_: e.activation(out=kst[0:64, :, :], in_=ptb_[0:64, :, :], func=AF.Copy),
                 reads=[r_pb[bt]], writes=[r_kst])
            P.dma("sp", lambda e, kst=kst, s=s, l0=l0: e.dma_start(out=kiTall[s, :, l0 * 128:(l0 + 8) * 128],
                                                                   in_=kst[0:64, :, :].rearrange("p a b -> p (a b)")),
                  reads=[r_kst], writes=[r_kiTall], owner=r_kst, kind="out")
        kst, r_kst = kst_ring.next()
        P.dma("sp", lambda e, kst=kst: e.dma_start(out=kst[0:64, 0, :], in_=KiTs[:, :]), reads=[r_s[4]], writes=[r_kst], owner=r_kst)
        P.dma("sp", lambda e, kst=kst, s=s: e.dma_start(out=kiTall[s, :, PAST:SK], in_=kst[0:64, 0, :]),
              reads=[r_kst], writes=[r_kiTall], owner=r_kst, kind="out")
    qis, r_qis = A.alloc("qis", [16, 128], BF16)
    P.dma("sp", lambda e: e.dma_start(out=qis[0:64, :, :], in_=QiT[NT_Q]), reads=[r_QiT], writes=[r_qis], owner=r_qis)
    qis23, r_qis23 = A.alloc("qis23", [2, 16, 64], BF16)
    P.op("pool", lambda e: e.memset(qis23[0:64], 0.0), writes=[r_qis23])
    P.op("pool", lambda e: e.tensor_copy(out=qis23[0:64, 0, :, 0:32], in_=qis[0:64, :, 64:96]), reads=[r_qis], writes=[r_qis23])
    P.op("pool", lambda e: e.tensor_copy(out=qis23[0:64, 1, :, 32:64], in_=qis[0:64, :, 96:128]), reads=[r_qis], writes=[r_qis23])
    scs, r_scs = A.alloc("scs", [SK], F32)
    wks, r_wks = A.alloc("wks", [SK], F32)
    sels_ring = A.ring("sels", [1024], BF16, 2)
    selTs, r_selTs = A.alloc("selTs", [NPG + 1, 128], BF16)
    nms, r_nms = A.alloc("nms", [128], F32)
    P.dma("sp", lambda e: e.dma_start(out=nms, in_=nms_d[:, :]), writes=[r_nms], owner=r_nms)
    kig_ring = A.ring("kig", [4, 512], BF16, 2)
    for g0 in range(0, SK, 512):
        ncol = min(512, SK - g0)
        kig, r_kig = kig_ring.next()
        P.dma("sp", lambda e, kig=kig, g0=g0, ncol=ncol: e.dma_start(out=kig[0:64, :, 0:ncol],
                                                                     in_=kiTall[:, :, g0:g0 + ncol].rearrange("s d k -> d s k")),
              reads=[r_kiTall], writes=[r_kig], owner=r_kig)
        for hh in range(H_IDX):
            bi = pSs.next()
            rl, r_rl = rl_ring.next()
            for s in range(4):
                if s < 2:
                    P.op("pe", lambda e, bi=bi, hh=hh, s=s, ncol=ncol, kig=kig: e.matmul(
                        pb[bi][32 * s:32 * s + 32, 0:ncol], lhsT=qis[0:64, hh, 32 * s:32 * s + 32], rhs=kig[0:64, s, 0:ncol],
                        start=True, stop=True), reads=[r_qis, r_kig], writes=[r_pb[bi]])
                else:
                    P.op("pe", lambda e, bi=bi, hh=hh, s=s, ncol=ncol, kig=kig: e.matmul(
                        pb[bi][64:128, 0:ncol], lhsT=qis23[0:64, s - 2, hh, :], rhs=kig[0:64, s, 0:ncol],
                        start=(s == 2), stop=(s == 3), skip_group_check=True), reads=[r_qis23, r_kig], writes=[r_pb[bi]])
            P.op("act", lambda e, bi=bi, rl=rl, ncol=ncol: e.activation(out=rl[:, 0:ncol], in_=pb[bi][:, 0:ncol], func=AF.Relu),
                 reads=[r_pb[bi]], writes=[r_rl])
            if hh == 0:
                if g0 >= PAST:
                    P.op("dve", lambda e, rl=rl, g0=g0, ncol=ncol: e.scalar_tensor_tensor(
                        out=scs[:, g0:g0 + ncol], in0=rl[:, 0:ncol], scalar=wi_all[:, NT_Q, 0:1], in1=nms[:, 0:ncol],
                        op0=ALU.mult, op1=ALU.add), reads=[r_rl, r_wi, r_nms], writes=[r_scs])
                else:
                    P.op("dve", lambda e, rl=rl, g0=g0, ncol=ncol: e.tensor_scalar(
                        out=scs[:, g0:g0 + ncol], in0=rl[:, 0:ncol], scalar1=wi_all[:, NT_Q, 0:1], scalar2=None, op0=ALU.mult),
                        reads=[r_rl, r_wi], writes=[r_scs])
            else:
                P.op("dve", lambda e, rl=rl, g0=g0, ncol=ncol, hh=hh: e.scalar_tensor_tensor(
                    out=scs[:, g0:g0 + ncol], in0=rl[:, 0:ncol], scalar=wi_all[:, NT_Q, hh:hh + 1], in1=scs[:, g0:g0 + ncol],
                    op0=ALU.mult, op1=ALU.add), reads=[r_rl, r_wi, r_scs], writes=[r_scs])
    m8, r_m8 = m8_ring.next()
    thr, r_thr = thr_ring.next()
    for r in range(32):
        src = scs if r == 0 else wks
        r_src = r_scs if r == 0 else r_wks
        P.op("dve", lambda e, src=src: e.max(out=m8, in_=src), reads=[r_src], writes=[r_m8])
        if r < 31:
            P.op("dve", lambda e, src=src: e.match_replace(out=wks, in_to_replace=m8, in_values=src, imm_value=NEG),
                 reads=[r_src, r_m8], writes=[r_wks])
    P.op("dve", lambda e: e.tensor_scalar(out=thr, in0=m8[:, 7:8], scalar1=-1.0e29, scalar2=None, op0=ALU.max),
         reads=[r_m8], writes=[r_thr])
    for k0 in range(0, NPG + 1, 8):
        nk = min(8, NPG + 1 - k0)
        bt = pTs.next()
        ptb_ = bank_bf(bt)
        sels, r_sels = sels_ring.next()
        P.op("dve", lambda e, sels=sels, k0=k0, nk=nk: e.tensor_scalar(out=sels[:, 0:nk * 128], in0=scs[:, k0 * 128:(k0 + nk) * 128],
                                                                      scalar1=thr[:, 0:1], scalar2=None, op0=ALU.is_ge),
             reads=[r_scs, r_thr], writes=[r_sels])
        for kk in range(nk):
            kb = k0 + kk
            P.op("pe", lambda e, ptb_=ptb_, kk=kk, sels=sels: e.transpose(out=ptb_[:, kk, :], in_=sels[:, kk * 128:(kk + 1) * 128], identity=identb),
                 reads=[r_sels, r_identb], writes=[r_pb[bt]])
        P.op("act", lambda e, ptb_=ptb_, k0=k0, nk=nk: e.activation(out=selTs[:, k0:k0 + nk, :], in_=ptb_[:, 0:nk, :], func=AF.Copy),
             reads=[r_pb[bt]], writes=[r_selTs])
    sample_attn("dsa")
    barrier(P)
    A.release(att_mark)
    if dbg:
        P.dma("sp", lambda e: e.dma_start(out=dbg_att.rearrange("t p d -> p t d"), in_=att), reads=r_att, writes=[r_out],
              owner=r_att[0], kind="out", final=True)

    hnT_scr = dscr("hnT_scr", [128, NTQ, 16, 128], BF16)
    r_hnT = P.res("hnT_scr")
    aTall, _ = A.alloc("aTall", [NTQ, 16, 128], BF16)
    r_aT = [P.res(f"aT{i}") for i in range(NTQ)]
    pTe = BankRing([0, 1])
    for s in range(NTQ):
        for g in range(2):
            b = pTe.next()
            pt = bank_bf(b)
            for jj in range(8):
                kc = g * 8 + jj
                P.op("pe", lambda e, kc=kc, jj=jj, pt=pt, s=s: e.transpose(out=pt[:, jj, :], in_=att[:, s, kc * 128:(kc + 1) * 128],
                                                                          identity=identb),
                     reads=[r_att[s], r_identb], writes=[r_pb[b]])
            P.op("act", lambda e, g=g, pt=pt, s=s: e.activation(out=aTall[:, s, g * 8:(g + 1) * 8, :], in_=pt, func=AF.Copy),
                 reads=[r_pb[b]], writes=[r_aT[s]])
    wo_ring = A.ring("wo", [16, 512], BF16, 2)
    xo_ring = A.ring("xo", [512], F32, 3)
    hc1_ring = A.ring("hc1", [512], F32, 3)
    pH = BankRing([2, 3, 4, 5])
    wov = w_out.rearrange("(kc p) n -> p kc n", p=128)
    def load_wo(c4):
        wo, r_wo = wo_ring.next()
        for q4 in range(0, 16, 4):
            P.dma("pool", lambda e, c4=c4, q4=q4, wo=wo: e.dma_start(out=wo[:, q4:q4 + 4, :],
                                                                     in_=wov[:, q4:q4 + 4, c4 * 512:(c4 + 1) * 512]),
                  writes=[r_wo], owner=r_wo)
        return wo, r_wo

    pending_wo = load_wo(0)
    for c4 in range(4):
        wo, r_wo = pending_wo
        if c4 + 1 < 4:
            pending_wo = load_wo(c4 + 1)
        for s in range(NTQ):
            xo, r_xo = xo_ring.next()
            hc1, r_hc1 = hc1_ring.next()
            P.dma("sp", lambda e, xo=xo, s=s, c4=c4: e.dma_start(out=xo, in_=rows(xq, s)[:, c4 * 512:(c4 + 1) * 512]),
                  writes=[r_xo], owner=r_xo)
            b = pH.next()
            for kc in range(16):
                P.op("pe", lambda e, kc=kc, s=s, b=b, wo=wo: e.matmul(pb[b][:, :], lhsT=aTall[:, s, kc, :], rhs=wo[:, kc, :],
                                                                     start=(kc == 0), stop=(kc == 15)),
                     reads=[r_aT[s], r_wo], writes=[r_pb[b]])
            P.op("dve", lambda e, b=b, hc1=hc1, xo=xo: e.tensor_tensor(out=hc1, in0=pb[b][:, :], in1=xo, op=ALU.add),
                 reads=[r_pb[b], r_xo], writes=[r_hc1])
            P.dma("sp", lambda e, hc1=hc1, s=s, c4=c4: e.dma_start(out=rows(h_scr, s)[:, c4 * 512:(c4 + 1) * 512], in_=hc1),
                  reads=[r_hc1], writes=[r_hscr], owner=r_hc1, kind="out")
    barrier(P)
    A.release(base_mark)
    mE = make_proj(NTQ, gF_d, slim=True)
    hf_ring = A.ring("hf", [D], F32, 2)
    hb_ring = A.ring("hb", [D], BF16, 2)
    for s in range(NTQ):
        hf, r_hf = hf_ring.next()
        hb, r_hb = hb_ring.next()
        P.dma("sp", lambda e, hf=hf, s=s: e.dma_start(out=hf, in_=rows(h_scr, s)), reads=[r_hscr], writes=[r_hf], owner=r_hf)
        if dbg:
            P.dma("sp", lambda e, hf=hf, s=s: e.dma_start(out=rows(dbg_h, s), in_=hf), reads=[r_hf], writes=[r_out], owner=r_hf,
                  kind="out", final=True)
        P.op("act", lambda e, hf=hf, s=s: e.activation(out=mE.junk, in_=hf, func=AF.Square, accum_out=mE.ss[:, s:s + 1]),
             reads=[r_hf], writes=[mE.r_junk, mE.r_ss[s]])
        rstd_col = mE.ss[:, s:s + 1]
        P.op("act", lambda e, rstd_col=rstd_col: e.activation(out=rstd_col, in_=rstd_col, func=AF.Ln, scale=1.0 / D, bias=EPS),
             reads=[mE.r_ss[s]], writes=[mE.r_ss[s]])
        P.op("act", lambda e, rstd_col=rstd_col: e.activation(out=rstd_col, in_=rstd_col, func=AF.Exp, scale=-0.5),
             reads=[mE.r_ss[s]], writes=[mE.r_ss[s]])
        P.op("dve", lambda e, hf=hf, hb=hb, rstd_col=rstd_col: e.scalar_tensor_tensor(out=hb, in0=hf, scalar=rstd_col, in1=mE.gvec,
                                                                                    op0=ALU.mult, op1=ALU.mult),
             reads=[r_hf, mE.r_ss[s], mE.r_gvec], writes=[r_hb])
        for g in range(2):
            b = mE.pT.next()
            pt = bank_bf(b)
            for jj in range(8):
                kc = g * 8 + jj
                P.op("pe", lambda e, kc=kc, jj=jj, pt=pt, hb=hb: e.transpose(out=pt[:, jj, :], in_=hb[:, kc * 128:(kc + 1) * 128],
                                                                            identity=identb),
                     reads=[r_hb, r_identb], writes=[r_pb[b]])
            P.op("act", lambda e, g=g, pt=pt, s=s: e.activation(out=mE.xT[:, s, g * 8:(g + 1) * 8, :], in_=pt, func=AF.Copy),
                 reads=[r_pb[b]], writes=[mE.r_xT[s]])
        P.dma("sp", lambda e, s=s: e.dma_start(out=hnT_scr[:, s, :, :], in_=mE.xT[:, s, :, :]), reads=[mE.r_xT[s]], writes=[r_hnT],
              owner=mE.r_xT[s], kind="out")
    barrier(P)
    A.release(base_mark)

    hnT, r_hn = A.alloc("hnT", [NTQ, 16, 128], BF16)
    P.dma("sp", lambda e: e.dma_start(out=hnT, in_=hnT_scr[:, :, :, :]), reads=[r_hnT], writes=[r_hn], owner=r_hn)
    cw, r_cw = A.alloc("cw", [NFF, 3], F32)
    cb, r_cb = A.alloc("cb", [NFF], F32)
    P.dma("sp", lambda e: e.dma_start(out=cw, in_=cw_d[:, :, :]), writes=[r_cw], owner=r_cw)
    P.dma("sp", lambda e: e.dma_start(out=cb, in_=cb_d[:, :]), writes=[r_cb], owner=r_cb)
    cst8, r_cst8 = A.alloc("cst8", [D_FF], F32)
    cstT, r_cstT = A.alloc("cstT", [NFF, 8], F32)
    P.dma("sp", lambda e: e.dma_start(out=cst8[0:8, :], in_=cst_d[:, :]), writes=[r_cst8], owner=r_cst8)
    pc = pb[0][:, 0:NFF * 8].rearrange("p (a b) -> p a b", b=8)
    for f in range(NFF):
        P.op("pe", lambda e, f=f: e.transpose(out=pc[:, f, :], in_=cst8[0:8, f * 128:(f + 1) * 128], identity=identf[0:8, 0:8]),
             reads=[r_cst8, r_identf], writes=[r_pb[0]])
    P.op("act", lambda e: e.activation(out=cstT, in_=pc, func=AF.Copy), reads=[r_pb[0]], writes=[r_cstT])
    wg_ring = A.ring("wg", [16, 512], BF16, 2)
    wu_ring = A.ring("wu", [16, 512], BF16, 2)
    NP = NTOK + 2
    g_ring = A.ring("g_sb", [NP], F32, 2)
    u_ring = A.ring("u_sb", [NTOK], F32, 2)
    ac_ring = A.ring("acc", [NTOK], F32, 2)
    a_ring = A.ring("a_sb", [NTQ, 128], BF16, 2)
    gs_ring = A.ring("gsel", [16], F32, 2)
    for t_, r_ in g_ring.items:
        P.op("pool", lambda e, t_=t_: e.memset(t_[:, 0:2], 0.0), writes=[r_])
    pG = BankRing([0, 1, 2, 3, 4, 5, 6, 7])
    groups = [(0, 4), (4, 4), (8, NTQ - 8)]
    wgv = w_gate.rearrange("(kc p) n -> p kc n", p=128)
    wuv = w_up.rearrange("(kc p) n -> p kc n", p=128)
    def load_gu(c0):
        nf = min(4, NFF - c0)
        wg, r_wg = wg_ring.next()
        wu, r_wu = wu_ring.next()
        for q4 in range(0, 16, 4):
            P.dma("pool", lambda e, q4=q4, wg=wg, c0=c0, nf=nf: e.dma_start(out=wg[:, q4:q4 + 4, 0:nf * 128],
                                                                          in_=wgv[:, q4:q4 + 4, c0 * 128:(c0 + nf) * 128]),
                  writes=[r_wg], owner=r_wg)
            P.dma("pool", lambda e, q4=q4, wu=wu, c0=c0, nf=nf: e.dma_start(out=wu[:, q4:q4 + 4, 0:nf * 128],
                                                                          in_=wuv[:, q4:q4 + 4, c0 * 128:(c0 + nf) * 128]),
                  writes=[r_wu], owner=r_wu)
        return wg, r_wg, wu, r_wu

    pending_gu = load_gu(0)
    for c0 in range(0, NFF, 4):
        nf = min(4, NFF - c0)
        wg, r_wg, wu, r_wu = pending_gu
        if c0 + 4 < NFF:
            pending_gu = load_gu(c0 + 4)
        for fi in range(nf):
            f = c0 + fi
            g_sb, r_g = g_ring.next()
            u_sb, r_u = u_ring.next()
            acc, r_acc = ac_ring.next()
            a_sb, r_a = a_ring.next()
            gsel, r_gsel = gs_ring.next()
            for (t0, nt) in groups:
                bg = pG.next()
                bu = pG.next()
                n = nt * 128
                for kc in range(16):
                    P.op("pe", lambda e, kc=kc, bg=bg, wg=wg, fi=fi, t0=t0, nt=nt, n=n: e.matmul(
                        pb[bg][:, 0:n], lhsT=wg[:, kc, fi * 128:(fi + 1) * 128], rhs=hnT[:, t0:t0 + nt, kc, :],
                        start=(kc == 0), stop=(kc == 15)), reads=[r_wg, r_hn], writes=[r_pb[bg]])
                for kc in range(16):
                    P.op("pe", lambda e, kc=kc, bu=bu, wu=wu, fi=fi, t0=t0, nt=nt, n=n: e.matmul(
                        pb[bu][:, 0:n], lhsT=wu[:, kc, fi * 128:(fi + 1) * 128], rhs=hnT[:, t0:t0 + nt, kc, :],
                        start=(kc == 0), stop=(kc == 15)), reads=[r_wu, r_hn], writes=[r_pb[bu]])
                P.op("act", lambda e, bg=bg, g_sb=g_sb, t0=t0, n=n: e.activation(out=g_sb[:, 2 + t0 * 128:2 + t0 * 128 + n],
                                                                              in_=pb[bg][:, 0:n], func=AF.Copy),
                     reads=[r_pb[bg]], writes=[r_g])
                P.op("act", lambda e, bu=bu, u_sb=u_sb, t0=t0, n=n: e.activation(out=u_sb[:, t0 * 128:t0 * 128 + n],
                                                                              in_=pb[bu][:, 0:n], func=AF.Copy),
                     reads=[r_pb[bu]], writes=[r_u])
            P.op("pool", lambda e, g_sb=g_sb, gsel=gsel: e.tensor_copy(out=gsel[:, 0:2], in_=g_sb[:, 2 + 1024:2 + 1026]),
                 reads=[r_g], writes=[r_gsel])
            sv = g_sb[:, 2 + 1152:2 + 1280].rearrange("p (s j) -> p s j", j=32)
            P.op("pool", lambda e, sv=sv, gsel=gsel: e.tensor_copy(out=gsel[:, 2:10].rearrange("p (s r) -> p s r", r=2), in_=sv[:, :, 8:10]),
                 reads=[r_g], writes=[r_gsel])
            P.dma("sp", lambda e, gsel=gsel, f=f: e.dma_start(out=gT_o[f], in_=gsel), reads=[r_gsel], writes=[r_out], owner=r_gsel,
                  kind="out", final=True)
            P.op("pool", lambda e, sv=sv, f=f: e.tensor_copy(out=sv[:, :, 0:2], in_=cstT[:, f, :].rearrange("p (s r) -> p s r", r=2)),
                 reads=[r_cstT, r_gsel], writes=[r_g])
            P.op("dve", lambda e, g_sb=g_sb, acc=acc, f=f: e.tensor_scalar(out=acc, in0=g_sb[:, 2:NP], scalar1=cw[:, f, 2:3],
                                                                         scalar2=cb[:, f:f + 1], op0=ALU.mult, op1=ALU.add),
                 reads=[r_g, r_cw, r_cb], writes=[r_acc])
            P.op("dve", lambda e, g_sb=g_sb, acc=acc, f=f: e.scalar_tensor_tensor(out=acc, in0=g_sb[:, 1:NP - 1], scalar=cw[:, f, 1:2],
                                                                                in1=acc, op0=ALU.mult, op1=ALU.add),
                 reads=[r_g, r_cw, r_acc], writes=[r_acc])
            P.op("dve", lambda e, g_sb=g_sb, acc=acc, f=f: e.scalar_tensor_tensor(out=acc, in0=g_sb[:, 0:NP - 2], scalar=cw[:, f, 0:1],
                                                                                in1=acc, op0=ALU.mult, op1=ALU.add),
                 reads=[r_g, r_cw, r_acc], writes=[r_acc])
            P.op("act", lambda e, acc=acc: e.activation(out=acc, in_=acc, func=AF.Silu), reads=[r_acc], writes=[r_acc])
            P.op("dve", lambda e, acc=acc, u_sb=u_sb, a_sb=a_sb: e.tensor_tensor(out=a_sb.rearrange("p t k -> p (t k)"), in0=acc, in1=u_sb,
                                                                               op=ALU.mult), reads=[r_acc, r_u], writes=[r_a])
            P.dma("sp", lambda e, a_sb=a_sb, f=f: e.dma_start(out=aT_scr[:, :, f, :].rearrange("t p k -> p t k"), in_=a_sb),
                  reads=[r_a], writes=[r_aTscr], owner=r_a, kind="out")
    barrier(P)
    A.release(base_mark)

    wd_ring = A.ring("wd", [NFF, 512], BF16, 2)
    at_ring = A.ring("aTt", [NFF, 128], BF16, 2)
    hc_ring = A.ring("hch", [512], F32, 3)
    yc_ring = A.ring("ych", [512], F32, 3)
    wdv = w_down.rearrange("(f p) n -> p f n", p=128)
    def load_wd(c4):
        wd, r_wd = wd_ring.next()
        for f0 in range(0, NFF, 4):
            nf = min(4, NFF - f0)
            P.dma("pool", lambda e, wd=wd, f0=f0, nf=nf, c4=c4: e.dma_start(out=wd[:, f0:f0 + nf, :],
                                                                          in_=wdv[:, f0:f0 + nf, c4 * 512:(c4 + 1) * 512]),
                  writes=[r_wd], owner=r_wd)
        return wd, r_wd

    for c4 in range(4):
        wd, r_wd = load_wd(c4)
        for t in range(NTQ):
            aTt, r_at = at_ring.next()
            hch, r_hc = hc_ring.next()
            ych, r_yc = yc_ring.next()
            P.dma("sp", lambda e, aTt=aTt, t=t: e.dma_start(out=aTt, in_=aT_scr[t]), reads=[r_aTscr], writes=[r_at], owner=r_at)
            P.dma("sp", lambda e, hch=hch, t=t, c4=c4: e.dma_start(out=hch, in_=rows(h_scr, t)[:, c4 * 512:(c4 + 1) * 512]),
                  reads=[r_hscr], writes=[r_hc], owner=r_hc)
            b = pG.next()
            for f in range(NFF):
                P.op("pe", lambda e, f=f, b=b, aTt=aTt, wd=wd: e.matmul(pb[b][:, :], lhsT=aTt[:, f, :], rhs=wd[:, f, :],
                                                                       start=(f == 0), stop=(f == NFF - 1)),
                     reads=[r_at, r_wd], writes=[r_pb[b]])
            P.op("dve", lambda e, b=b, hch=hch, ych=ych: e.tensor_tensor(out=ych, in0=pb[b][:, :], in1=hch, op=ALU.add),
                 reads=[r_pb[b], r_hc], writes=[r_yc])
            P.dma("sp", lambda e, ych=ych, t=t, c4=c4: e.dma_start(out=rows(y_o, t)[:, c4 * 512:(c4 + 1) * 512], in_=ych),
                  reads=[r_yc], writes=[r_out], owner=r_yc, kind="out", final=True)

    P.emit()
    P.close()
    return nc


def _rope_tab(pos):
    pos = np.asarray(pos, dtype=np.float32)
    out = np.zeros((pos.shape[0], 192), np.float32)
    inv128 = (10000.0 ** (-np.arange(64, dtype=np.float32) / 64)).astype(np.float32)
    inv64 = (10000.0 ** (-np.arange(32, dtype=np.float32) / 32)).astype(np.float32)
    a = pos[:, None] * inv128[None, :]
    out[:, 0:64] = np.cos(a)
    out[:, 64:128] = np.sin(a)
    a = pos[:, None] * inv64[None, :]
    out[:, 128:160] = np.cos(a)
    out[:, 160:192] = np.sin(a)
    return out


def _core_consts(j):
    f32 = np.float32
    p0 = 1024 * j - 2
    fmask = np.zeros((NT_Q, 128, 8, 128), f32)
    bbias = np.zeros((128, NT_Q, 32), f32)
    nm = np.full((NT_Q, 128, SEQ), NEG, f32)
    oh = np.zeros((128, NT_Q, 32), f32)
    kk = np.arange(128)
    for i in range(NT_Q):
        t = p0 + 128 * i + np.arange(128)
        for sl, kb in enumerate(MASK_KBS(i)):
            s = 128 * kb + kk
            fmask[i, :, sl, :] = ((s[:, None] <= t[None, :]) & (t[None, :] >= 0)).astype(f32)
        for kb in range(32):
            if 128 * kb > t[-1]:
                bbias[:, i, kb] = NEG
        s_all = np.arange(SEQ)
        vis = (s_all[None, :] <= t[:, None]) & (t[:, None] >= 0)
        nm[i][vis] = 0.0
        oh[:, i, min(31, 8 * j + i)] = 1.0
    return fmask, bbias, nm, oh


_NC_CACHE = {}


def kernel(x_prompt, x_sample, cache_fox_k, cache_fox_v, cache_fox_logf, cache_dsa_k, cache_dsa_v,
           cache_idx_k, state_ffn_conv, page_table, w_in, b_f, g_qa, g_ka, g_qb, g_kb, g_attn,
           w_out, g_ffn, w_gate, w_up, conv_w, conv_b, w_down, _dbg=False):
    f32 = np.float32
    x_prompt = np.asarray(x_prompt, f32)
    x_sample = np.asarray(x_sample, f32)
    w_in0 = np.asarray(w_in, f32)[0]
    o = np.cumsum([0, 1024, 1024, 1024, 8, 1024, 256, 256, 1024, 64, 16])
    qa, ka, va, fa, qb, kb, vb, qi, ki, wi = [slice(o[i], o[i + 1]) for i in range(10)]
    w_kv = np.ascontiguousarray(np.concatenate([w_in0[:, ka], w_in0[:, va], w_in0[:, kb], w_in0[:, vb],
                                                w_in0[:, ki], w_in0[:, fa]], axis=1))
    w_q = np.ascontiguousarray(np.concatenate([w_in0[:, qa], w_in0[:, qb], w_in0[:, qi], w_in0[:, wi]], axis=1))
    ident = np.eye(128, dtype=f32)
    tri = np.triu(np.ones((128, 128), f32))
    gA = np.ascontiguousarray(np.broadcast_to(np.asarray(g_attn, f32)[0][None, :], (128, D)))
    gF = np.ascontiguousarray(np.broadcast_to(np.asarray(g_ffn, f32)[0][None, :], (128, D)))
    g4 = np.ascontiguousarray(np.broadcast_to(
        np.stack([np.asarray(g_qa, f32)[0], np.asarray(g_ka, f32)[0], np.asarray(g_qb, f32)[0],
                  np.asarray(g_kb, f32)[0]])[None], (128, 4, 128)))
    bfr = np.ascontiguousarray(np.broadcast_to(np.asarray(b_f, f32)[0][None, :], (128, 8)))
    cw = np.ascontiguousarray(np.asarray(conv_w, f32)[0].reshape(3, NFF, 128).transpose(2, 1, 0))
    cb = np.ascontiguousarray(np.asarray(conv_b, f32)[0].reshape(NFF, 128).T)
    tabA = _rope_tab(np.arange(SEQ))
    spos = np.zeros(128, f32)
    for s in range(4):
        spos[32 * s + 2:32 * s + 10] = PAST + np.arange(8)
    shared = dict(w_kv=w_kv, w_q=w_q, w_out=np.ascontiguousarray(np.asarray(w_out, f32)[0]),
                  w_gate=np.ascontiguousarray(np.asarray(w_gate, f32)[0]), w_up=np.ascontiguousarray(np.asarray(w_up, f32)[0]),
                  w_down=np.ascontiguousarray(np.asarray(w_down, f32)[0]), ident=ident, tri=tri, gA=gA, gF=gF, g4=g4, bfr=bfr,
                  cw=cw, cb=cb, tabA=tabA)
    consts = [_core_consts(j) for j in range(4)]
    lst = np.zeros((128, 128), f32)
    o64 = np.zeros((128, 128), f32)
    for p_ in range(64):
        lst[p_, p_ + 1:] = 1.0
        o64[p_, :] = 1.0
    lblk = np.zeros((128, 128), f32)
    esel = np.zeros((128, 4, 128), f32)
    smask = np.zeros((128, 4, 32), f32)
    nms = np.full((128, 128), NEG, f32)
    for s in range(4):
        esel[32 * s + 9, s, :] = 1.0
        for i in range(8):
            for i2 in range(i + 1):
                lblk[32 * s + 2 + i2, 32 * s + 2 + i] = 1.0
                smask[32 * s + 2 + i2, s, 2 + i] = 1.0
                nms[32 * s + 2 + i, 32 * s + 2 + i2] = 0.0
    npool = np.asarray(cache_fox_k).shape[1]
    assert npool == NPOOL_PAGES
    shared.update(
        cfk=np.asarray(cache_fox_k, f32).reshape(npool * 128, 1024), cfv=np.asarray(cache_fox_v, f32).reshape(npool * 128, 1024),
        cfl=np.asarray(cache_fox_logf, f32).reshape(npool, 1024), cdk=np.asarray(cache_dsa_k, f32).reshape(npool * 128, 256),
        cdv=np.asarray(cache_dsa_v, f32).reshape(npool * 128, 256), cik=np.asarray(cache_idx_k, f32).reshape(npool * 128, 64),
        lst=lst, o64=o64, lblk=lblk, esel=esel, smask=smask, nms=nms)
    page_table = np.asarray(page_table, np.int32)
    NTQ = NT_Q + 1
    in_maps = []
    for c in range(8):
        b, j = c // 4, c % 4
        p0 = 1024 * j - 2
        xq = np.zeros((NTQ * 128, D), f32)
        lo, hi = max(p0, 0), min(p0 + NT_Q * 128, SEQ)
        xq[lo - p0:hi - p0] = x_prompt[b, lo:hi]
        for s in range(4):
            xq[NT_Q * 128 + 32 * s + 2:NT_Q * 128 + 32 * s + 10] = x_sample[4 * c + s]
        tabQ = _rope_tab(np.concatenate([np.clip(p0 + np.arange(NT_Q * 128), 0, SEQ - 1), spos]))
        fmask, bbias, nm, oh = consts[j]
        cst = np.ascontiguousarray(np.asarray(state_ffn_conv, f32)[0, 4 * c:4 * c + 4].reshape(8, D_FF))
        m = dict(shared)
        m.update(xb=np.ascontiguousarray(x_prompt[b]), xq=xq, tabQ=tabQ, fmask=fmask, bbias=bbias, nm=nm, oh=oh, cst=cst,
                 pt=np.ascontiguousarray(page_table[4 * c:4 * c + 4]))
        in_maps.append(m)

    key = bool(_dbg)
    if key not in _NC_CACHE:
        _NC_CACHE[key] = build_program(dbg=key)
    nc = _NC_CACHE[key]
    res = run_bass_kernel_spmd(nc, in_maps, core_ids=list(range(8)))
    R = res.results

    def prow(name, shape):
        return np.stack([R[0][name], R[4][name]]).reshape((1, 2, SEQ) + shape)

    def srow(name, shape):
        out = np.zeros((32, 8) + shape, f32)
        for c in range(8):
            a = R[c][name]
            for s in range(4):
                out[4 * c + s] = a[32 * s + 2:32 * s + 10].reshape((8,) + shape)
        return out[None]

    y_p = np.zeros((2, SEQ, D), f32)
    y_s = np.zeros((32, 8, D), f32)
    conv_p = np.zeros((1, 2, 2, D_FF), f32)
    conv_s = np.zeros((1, 32, 2, D_FF), f32)
    for c in range(8):
        b, j = c // 4, c % 4
        yo = R[c]["y_o"]
        y_p[b, 1024 * j:1024 * (j + 1)] = yo[2:1026]
        gt = R[c]["gT_o"]
        for s in range(4):
            y_s[4 * c + s] = yo[NT_Q * 128 + 32 * s + 2:NT_Q * 128 + 32 * s + 10]
            for r in range(2):
                conv_s[0, 4 * c + s, r] = gt[:, :, 2 + 2 * s + r].reshape(D_FF)
        if j == 3:
            for r in range(2):
                conv_p[0, b, r] = gt[:, :, r].reshape(D_FF)
    outs = (y_p, y_s,
            prow("o_fk", (8, 128)), prow("o_fv", (8, 128)), prow("o_fl", (8,)),
            prow("o_dk", (2, 128)), prow("o_dv", (2, 128)), prow("o_ik", (64,)), conv_p,
            srow("s_fk", (8, 128)), srow("s_fv", (8, 128)), srow("s_fl", (8,)),
            srow("s_dk", (2, 128)), srow("s_dv", (2, 128)), srow("s_ik", (64,)), conv_s)
    if _dbg:
        return outs, R
    return outs
```

```python
from contextlib import ExitStack
import numpy as np
import concourse.bass as bass
import concourse.mybir as mybir
from concourse.bass_utils import run_bass_kernel_spmd

F32 = mybir.dt.float32
BF16 = mybir.dt.bfloat16
I32 = mybir.dt.int32
U32 = mybir.dt.uint32
AF = mybir.ActivationFunctionType
ALU = mybir.AluOpType
AX = mybir.AxisListType

D = 2048
HD = 128
H_A = 8
H_B = 8
KV_B = 2
H_IDX = 16
D_IDX = 64
D_FF = 5504
NFF = D_FF // 128
SEQ = 4096
PAST = 8192
NPG = 64
EPS = 1e-6
SCALE = HD ** -0.5
IDX_SCALE = (H_IDX * D_IDX) ** -0.5
NEG = -1.0e30
N_KV = 2632
N_Q = 3088
NT_Q = 9
ENGS = ("pe", "dve", "act", "pool", "sp")


class Res:
    __slots__ = ("name", "w", "r", "k_in", "k_out")

    def __init__(self, name):
        self.name = name
        self.w = None
        self.r = []
        self.k_in = None
        self.k_out = None


class Prog:
    def __init__(self, nc):
        self.nc = nc
        self.stack = ExitStack()
        self.q = {e: [] for e in ENGS}
        self.cnt = {}
        self.waited = {e: {} for e in ENGS}
        self.nres = 0
        self.final_waits = {}
        self.n_ops = 0
        self.phys_of = {}
        self.phys_cnt = []
        self.free_phys = []
        self.active_dma = []

    def sbuf(self, name, shape, dtype):
        return self.stack.enter_context(self.nc.sbuf_tensor("sb_" + name, list(shape), dtype))

    def psum(self, name, shape, dtype):
        return self.stack.enter_context(self.nc.psum_tensor("ps_" + name, list(shape), dtype))

    def res(self, name=None):
        self.nres += 1
        return Res(name or f"r{self.nres}")

    def _need(self, eng, reads, writes):
        ev = {}

        def add(e):
            if e is None:
                return
            k, v = e
            if ev.get(k, 0) < v:
                ev[k] = v
        for r in reads:
            add(r.w)
        for r in writes:
            add(r.w)
            for e in r.r:
                add(e)
        out = []
        wd = self.waited[eng]
        for k, v in ev.items():
            if eng == "pe" and k == "E:pe":
                continue
            if wd.get(k, 0) >= v:
                continue
            wd[k] = v
            out.append((k, v))
        return out

    def _commit(self, ev, reads, writes):
        for r in reads:
            r.r.append(ev)
            if len(r.r) > 64:
                best = {}
                for k, v in r.r:
                    if best.get(k, 0) < v:
                        best[k] = v
                r.r = list(best.items())
        for r in writes:
            r.w = ev
            r.r = []

    def op(self, eng, fn, reads=(), writes=()):
        waits = self._need(eng, reads, writes)
        k = "E:" + eng
        self.cnt[k] = self.cnt.get(k, 0) + 1
        ev = (k, self.cnt[k])
        self.q[eng].append((waits, fn, k, 1))
        self._commit(ev, reads, writes)
        self.n_ops += 1
        return ev

    def dma(self, queue, fn, reads=(), writes=(), owner=None, kind="in", final=False):
        waits = self._need(queue, reads, writes)
        if kind == "in":
            if owner.k_in is None:
                owner.k_in = self._new_dma_key(owner, "in")
            k = owner.k_in
        else:
            if owner.k_out is None:
                owner.k_out = self._new_dma_key(owner, "out")
            k = owner.k_out
        self.cnt[k] = self.cnt[k] + 16
        ev = (k, self.cnt[k])
        self.q[queue].append((waits, fn, k, 16))
        self._commit(ev, reads, writes)
        if final:
            self.final_waits[k] = self.cnt[k]
        self.n_ops += 1
        return ev

    def _phys(self, k):
        if k not in self.phys_of:
            self.phys_of[k] = len(self.phys_cnt)
            self.phys_cnt.append(0)
        return self.phys_of[k]

    def _new_dma_key(self, owner, kind):
        self.nres += 1
        k = f"D:{kind}:{owner.name}:{self.nres}"
        if self.free_phys:
            p = self.free_phys.pop()
        else:
            p = len(self.phys_cnt)
            self.phys_cnt.append(0)
        self.phys_of[k] = p
        self.cnt[k] = self.phys_cnt[p]
        self.active_dma.append((owner, kind, k))
        return k

    def retire_dma_keys(self):
        for owner, kind, k in self.active_dma:
            p = self.phys_of[k]
            self.phys_cnt[p] = self.cnt[k]
            self.free_phys.append(p)
            if kind == "in":
                owner.k_in = None
            else:
                owner.k_out = None
        self.active_dma = []
        self.final_waits = {}

    def emit(self):
        nc = self.nc
        for k in self.cnt:
            self._phys(k)
        psems = [self.stack.enter_context(nc.semaphore(f"s{i}")) for i in range(len(self.phys_cnt))]
        sems = {k: psems[self.phys_of[k]] for k in self.cnt}
        block = self.stack.enter_context(nc.Block())
        q = self.q
        final_waits = self.final_waits

        def run(name, e):
            for waits, fn, k, amt in q[name]:
                for (wk, wv) in waits:
                    e.wait_ge(sems[wk], wv)
                if fn is None:
                    continue
                fn(e).then_inc(sems[k], amt)
            if name == "sp":
                for k, v in final_waits.items():
                    e.wait_ge(sems[k], v)

        @block.tensor
        def _(e):
            run("pe", e)

        @block.vector
        def _(e):
            run("dve", e)

        @block.scalar
        def _(e):
            run("act", e)

        @block.gpsimd
        def _(e):
            run("pool", e)

        @block.sync
        def _(e):
            run("sp", e)

    def close(self):
        self.stack.close()


class Ring:
    def __init__(self, P, name, shape, dtype, n, psum=False):
        self.t = []
        self.r = []
        for i in range(n):
            t = P.psum(f"{name}{i}", shape, dtype) if psum else P.sbuf(f"{name}{i}", shape, dtype)
            self.t.append(t)
            self.r.append(P.res(f"{name}{i}"))
        self.i = 0
        self.n = n

    def next(self):
        i = self.i % self.n
        self.i += 1
        return self.t[i], self.r[i]


def bc(ap, shape):
    return ap.to_broadcast(list(shape))


NPOOL_PAGES = 2560
AW = 52992


def KB_OF(i):
    return min(32, 25 + i)


def MASK_KBS(i):
    return [kb for kb in range(KB_OF(i)) if (kb - i) % 8 in (0, 7)]


class Arena:
    def __init__(self, P):
        self.P = P
        self.t = P.sbuf("arena", [128, AW], F32)
        self.top = 0

    def mark(self):
        return self.top

    def release(self, m):
        self.top = m

    def alloc(self, name, free_shape, dtype=F32):
        n = 1
        for d_ in free_shape:
            n *= d_
        w = n if dtype in (F32, I32, U32) else (n + 1) // 2
        w = (w + 7) // 8 * 8
        off = self.top
        self.top += w
        assert self.top <= AW, f"arena overflow at {name}: {self.top}"
        v = self.t[:, off:off + w]
        if dtype != F32:
            v = v.bitcast(dtype)
        v = v[:, 0:n]
        if len(free_shape) == 2:
            v = v.rearrange("p (a b) -> p a b", b=free_shape[1])
        elif len(free_shape) == 3:
            v = v.rearrange("p (a b c) -> p a b c", b=free_shape[1], c=free_shape[2])
        elif len(free_shape) == 4:
            v = v.rearrange("p (a b c d) -> p a b c d", b=free_shape[1], c=free_shape[2], d=free_shape[3])
        return v, self.P.res(name)

    def ring(self, name, free_shape, dtype, n):
        return VRing([self.alloc(f"{name}{i}", free_shape, dtype) for i in range(n)])


class K:
    pass


class VRing:
    def __init__(self, items):
        self.items = items
        self.i = 0

    def next(self):
        it = self.items[self.i % len(self.items)]
        self.i += 1
        return it


def barrier(P):
    waits = []
    for k, v in P.cnt.items():
        if k == "B:bar":
            continue
        if P.waited["sp"].get(k, 0) < v:
            P.waited["sp"][k] = v
            waits.append((k, v))
    P.cnt["B:bar"] = P.cnt.get("B:bar", 0) + 1
    n = P.cnt["B:bar"]
    P.q["sp"].append((waits, lambda e: e.nop(), "B:bar", 1))
    for eng in ENGS:
        if eng == "sp":
            continue
        P.q[eng].append(([("B:bar", n)], None, None, 0))
        P.waited[eng]["B:bar"] = n
    for eng in ENGS:
        for k, v in P.cnt.items():
            if k != "B:bar":
                P.waited[eng][k] = v
    P.retire_dma_keys()


def build_program(dbg=False):
    nc = bass.Bass("TRN2", target_bir_lowering=False)
    P = Prog(nc)

    def din(name, shape, dt=F32):
        return nc.dram_tensor(name, list(shape), dt, kind="ExternalInput").ap()

    def dout(name, shape, dt=F32):
        return nc.dram_tensor(name, list(shape), dt, kind="ExternalOutput").ap()

    def dscr(name, shape, dt):
        return nc.dram_tensor(name, list(shape), dt).ap()

    NTQ = NT_Q + 1
    NTOK = NTQ * 128
    xb = din("xb", [SEQ, D])
    xq = din("xq", [NTOK, D])
    w_kv = din("w_kv", [D, N_KV])
    w_q = din("w_q", [D, N_Q])
    w_out = din("w_out", [D, D])
    w_gate = din("w_gate", [D, D_FF])
    w_up = din("w_up", [D, D_FF])
    w_down = din("w_down", [D_FF, D])
    ident_d = din("ident", [128, 128])
    tri_d = din("tri", [128, 128])
    gA_d = din("gA", [128, D])
    gF_d = din("gF", [128, D])
    g4_d = din("g4", [128, 4, 128])
    bf_d = din("bfr", [128, 8])
    cw_d = din("cw", [128, NFF, 3])
    cb_d = din("cb", [128, NFF])
    tabA = din("tabA", [SEQ, 192])
    tabQ = din("tabQ", [NTOK, 192])
    fmask_d = din("fmask", [NT_Q, 128, 8, 128])
    bbias_d = din("bbias", [128, NT_Q, 32])
    nm_d = din("nm", [NT_Q, 128, SEQ])
    oh_d = din("oh", [128, NT_Q, 32])
    cst_d = din("cst", [8, D_FF])
    NPOOLR = NPOOL_PAGES * 128
    cfk = din("cfk", [NPOOLR, 1024])
    cfv = din("cfv", [NPOOLR, 1024])
    cfl = din("cfl", [NPOOL_PAGES, 1024])
    cdk = din("cdk", [NPOOLR, 256])
    cdv = din("cdv", [NPOOLR, 256])
    cik = din("cik", [NPOOLR, 64])
    pt_d = din("pt", [4, NPG], I32)
    lst_d = din("lst", [128, 128])
    o64_d = din("o64", [128, 128])
    lblk_d = din("lblk", [128, 128])
    esel_d = din("esel", [128, 4, 128])
    smask_d = din("smask", [128, 4, 32])
    nms_d = din("nms", [128, 128])
    o_fk = dout("o_fk", [SEQ, 1024])
    o_fv = dout("o_fv", [SEQ, 1024])
    o_fl = dout("o_fl", [SEQ, 8])
    o_dk = dout("o_dk", [SEQ, 256])
    o_dv = dout("o_dv", [SEQ, 256])
    o_ik = dout("o_ik", [SEQ, 64])
    s_fk = dout("s_fk", [128, 1024])
    s_fv = dout("s_fv", [128, 1024])
    s_fl = dout("s_fl", [128, 8])
    s_dk = dout("s_dk", [128, 256])
    s_dv = dout("s_dv", [128, 256])
    s_ik = dout("s_ik", [128, 64])
    y_o = dout("y_o", [NTOK, D])
    gT_o = dout("gT_o", [NFF, 128, 16])
    if dbg:
        dbg_att = dout("dbg_att", [NTQ, 128, D], BF16)
        dbg_h = dout("dbg_h", [NTOK, D])
    KaT = dscr("KaT", [8, 128, SEQ], BF16)
    VaE = dscr("VaE", [8, 128, 32, 129], BF16)
    KbT = dscr("KbT", [2, 128, SEQ], BF16)
    VbE = dscr("VbE", [2, 128, 32, 129], BF16)
    KiT = dscr("KiT", [64, SEQ], BF16)
    KaTs = dscr("KaTs", [8, 128, 128], BF16)
    VaEs = dscr("VaEs", [8, 128, 1, 129], BF16)
    KbTs = dscr("KbTs", [2, 128, 128], BF16)
    VbEs = dscr("VbEs", [2, 128, 1, 129], BF16)
    KiTs = dscr("KiTs", [64, 128], BF16)
    QaT = dscr("QaT", [NTQ, 128, 8, 128], BF16)
    QbT = dscr("QbT", [NTQ, 128, 8, 128], BF16)
    QiT = dscr("QiT", [NTQ, 64, 16, 128], BF16)
    h_scr = dscr("h_scr", [NTOK, D], F32)
    aT_scr = dscr("aT_scr", [NTQ, 128, NFF, 128], BF16)
    r_KaT, r_VaE, r_KbT, r_VbE, r_KiT = (P.res(n) for n in ("KaT", "VaE", "KbT", "VbE", "KiT"))
    r_s = [P.res(n) for n in ("KaTs", "VaEs", "KbTs", "VbEs", "KiTs")]
    r_QaT, r_QbT, r_QiT = P.res("QaT"), P.res("QbT"), P.res("QiT")
    r_hscr, r_aTscr = P.res("h_scr"), P.res("aT_scr")
    r_out = P.res("outputs")

    A = Arena(P)
    pb = [P.psum(f"bank{i}", [128, 512], F32) for i in range(8)]
    r_pb = [P.res(f"bank{i}") for i in range(8)]

    def bank_bf(i):
        return pb[i][:, :].bitcast(BF16).rearrange("p (a b) -> p a b", b=128)

    class BankRing:
        def __init__(self, ids):
            self.ids = ids
            self.i = 0

        def next(self):
            b = self.ids[self.i % len(self.ids)]
            self.i += 1
            return b

    identf, r_identf = A.alloc("identf", [128], F32)
    identb, r_identb = A.alloc("identb", [128], BF16)
    tri, r_tri = A.alloc("tri", [128], F32)
    ones, r_ones = A.alloc("ones", [128], F32)
    g4, r_g4 = A.alloc("g4", [4, 128], F32)
    bfr, r_bfr = A.alloc("bfr", [8], F32)
    lfall, r_lfall = A.alloc("lfall", [32, 8], F32)
    wi_all, r_wi = A.alloc("wi_all", [NTQ, 16], F32)
    lfs, r_lfs = A.alloc("lfs", [8], F32)
    P.dma("sp", lambda e: e.dma_start(out=identf, in_=ident_d[:, :]), writes=[r_identf], owner=r_identf)
    P.dma("sp", lambda e: e.dma_start(out=tri, in_=tri_d[:, :]), writes=[r_tri], owner=r_tri)
    P.dma("sp", lambda e: e.dma_start(out=g4, in_=g4_d[:, :, :]), writes=[r_g4], owner=r_g4)
    P.dma("sp", lambda e: e.dma_start(out=bfr, in_=bf_d[:, :]), writes=[r_bfr], owner=r_bfr)
    P.op("dve", lambda e: e.tensor_copy(out=identb, in_=identf), reads=[r_identf], writes=[r_identb])
    P.op("pool", lambda e: e.memset(ones, 1.0), writes=[r_ones])
    base_mark = A.mark()

    def make_proj(G, gvec_d, slim=False):
        m = K()
        m.G = G
        m.gvec, m.r_gvec = A.alloc("gvec", [D], F32)
        P.dma("sp", lambda e: e.dma_start(out=m.gvec, in_=gvec_d[:, :]), writes=[m.r_gvec], owner=m.r_gvec)
        m.junk, m.r_junk = A.alloc("junk", [D], BF16)
        m.junk2, m.r_junk2 = A.alloc("junk2", [128], BF16)
        m.xT, _ = A.alloc("xT", [G, 16, 128], BF16)
        m.r_xT = [P.res(f"xT{i}") for i in range(G)]
        m.ss, _ = A.alloc("ss_t", [G], F32)
        m.rstd, _ = A.alloc("rstd_t", [G], F32)
        m.r_ss = [P.res(f"ss{i}") for i in range(G)]
        m.r_rstd = [P.res(f"rstd{i}") for i in range(G)]
        m.pT = BankRing([0, 1])
        m.pM = BankRing([2, 3])
        if slim:
            return m
        m.xf = A.ring("xf", [D], F32, 2)
        m.xbf = A.ring("xbf", [D], BF16, 2)
        m.tab, _ = A.alloc("tab", [G, 192], F32)
        m.r_tab = [P.res(f"tab{i}") for i in range(G)]
        m.w = A.ring("wch", [16, 512], BF16, 2)
        m.kf = A.ring("kf", [4, 128], F32, 4)
        m.kn = A.ring("kn", [4, 128], F32, 3)
        m.ko = A.ring("ko", [4, 128], F32, 3)
        m.kbb = A.ring("kbb", [4, 128], BF16, 3)
        m.st = A.ring("ktst", [8, 128], BF16, 2)
        m.ve = A.ring("ve", [4, 129], BF16, 2)
        m.sm = A.ring("sm", [16], F32, 6)
        m.ra = A.ring("ra", [4, 64], F32, 2)
        m.rb = A.ring("rb", [4, 64], F32, 2)
        m.lf = A.ring("lf", [8], F32, 2)
        m.lg = A.ring("lg", [8], F32, 4)
        m.pT = BankRing([0, 1])
        m.pM = BankRing([2, 3])
        for t_, r_ in m.ve.items:
            P.op("pool", lambda e, t_=t_: e.memset(t_[:, :, 128:129], 1.0), writes=[r_])
        return m

    def load_x_tile(m, src_rows_ap, slot, tab_rows_ap):
        xf, r_xf = m.xf.next()
        xbf, r_xbf = m.xbf.next()
        P.dma("sp", lambda e: e.dma_start(out=xf, in_=src_rows_ap), writes=[r_xf], owner=r_xf)
        if tab_rows_ap is not None:
            P.dma("sp", lambda e: e.dma_start(out=m.tab[:, slot, :], in_=tab_rows_ap), writes=[m.r_tab[slot]],
                  owner=m.r_tab[slot])
        norm_to_xT(m, xf, r_xf, xbf, r_xbf, slot)

    def norm_to_xT(m, xf, r_xf, xbf, r_xbf, slot):
        P.op("act", lambda e: e.activation(out=m.junk, in_=xf, func=AF.Square, accum_out=m.ss[:, slot:slot + 1]),
             reads=[r_xf], writes=[m.r_junk, m.r_ss[slot]])
        P.op("act", lambda e: e.activation(out=m.ss[:, slot:slot + 1], in_=m.ss[:, slot:slot + 1], func=AF.Ln,
                                           scale=1.0 / D, bias=EPS), reads=[m.r_ss[slot]], writes=[m.r_ss[slot]])
        P.op("act", lambda e: e.activation(out=m.rstd[:, slot:slot + 1], in_=m.ss[:, slot:slot + 1], func=AF.Exp,
                                           scale=-0.5), reads=[m.r_ss[slot]], writes=[m.r_rstd[slot]])
        P.op("dve", lambda e: e.tensor_tensor(out=xbf, in0=xf, in1=m.gvec, op=ALU.mult),
             reads=[r_xf, m.r_gvec], writes=[r_xbf])
        for g in range(2):
            b = m.pT.next()
            pt = bank_bf(b)
            for jj in range(8):
                kc = g * 8 + jj
                P.op("pe", lambda e, kc=kc, jj=jj, pt=pt: e.transpose(out=pt[:, jj, :], in_=xbf[:, kc * 128:(kc + 1) * 128],
                                                                      identity=identb),
                     reads=[r_xbf, r_identb], writes=[r_pb[b]])
            if g == 0:
                P.op("dve", lambda e, g=g, pt=pt: e.tensor_copy(out=m.xT[:, slot, g * 8:(g + 1) * 8, :], in_=pt),
                     reads=[r_pb[b]], writes=[m.r_xT[slot]])
            else:
                P.op("act", lambda e, g=g, pt=pt: e.activation(out=m.xT[:, slot, g * 8:(g + 1) * 8, :], in_=pt, func=AF.Copy),
                     reads=[r_pb[b]], writes=[m.r_xT[slot]])

    def load_w_chunk(m, wd, col0, ncols, nk=16):
        wt, r_w = m.w.next()
        wv = wd.rearrange("(kc p) n -> p kc n", p=128)
        for q4 in range(0, nk, 4):
            P.dma("pool", lambda e, q4=q4: e.dma_start(out=wt[:, q4:q4 + 4, 0:ncols],
                                                        in_=wv[:, q4:q4 + 4, col0:col0 + ncols]),
                  writes=[r_w], owner=r_w)
        return wt, r_w

    def project(m, slot, wt, r_w, ncols):
        b = m.pM.next()
        pm = pb[b]
        for kc in range(16):
            P.op("pe", lambda e, kc=kc: e.matmul(pm[:, 0:ncols], lhsT=m.xT[:, slot, kc, :], rhs=wt[:, kc, 0:ncols],
                                                 start=(kc == 0), stop=(kc == 15)),
                 reads=[m.r_xT[slot], r_w], writes=[r_pb[b]])
        return pm, r_pb[b]

    def rstd_of(ssq_ap, r_in, inv_n):
        P.op("act", lambda e: e.activation(out=ssq_ap, in_=ssq_ap, func=AF.Ln, scale=inv_n, bias=EPS),
             reads=[r_in], writes=[r_in])
        P.op("act", lambda e: e.activation(out=ssq_ap, in_=ssq_ap, func=AF.Exp, scale=-0.5),
             reads=[r_in], writes=[r_in])

    def head_norm_early(m, pm_ap, r_pm, slot, nh):
        kf, r_kf = m.kf.next()
        sm, r_sm = m.sm.next()
        P.op("act", lambda e: e.activation(out=kf[:, 0:nh, :], in_=pm_ap, func=AF.Copy, scale=m.rstd[:, slot:slot + 1]),
             reads=[r_pm, m.r_rstd[slot]], writes=[r_kf])
        for h in range(nh):
            P.op("act", lambda e, h=h: e.activation(out=m.junk2, in_=kf[:, h, :], func=AF.Square,
                                                    accum_out=sm[:, h:h + 1]),
                 reads=[r_kf], writes=([r_sm] if h in (0, nh - 1) else []))
        rstd_of(sm[:, 0:nh], r_sm, 1.0 / HD)
        return (kf, r_kf, sm, r_sm)

    def head_norm_late(m, ctx, nh, g_idx):
        kf, r_kf, sm, r_sm = ctx
        kn, r_kn = m.kn.next()
        P.op("dve", lambda e: e.tensor_tensor(out=kn[:, 0:nh, :], in0=kf[:, 0:nh, :],
                                              in1=bc(sm[:, 0:nh].unsqueeze(2), [128, nh, 128]), op=ALU.mult),
             reads=[r_kf, r_sm], writes=[r_kn])
        P.op("pool", lambda e: e.tensor_tensor(out=kn[:, 0:nh, :], in0=kn[:, 0:nh, :],
                                               in1=bc(g4[:, g_idx, :].unsqueeze(1), [128, nh, 128]), op=ALU.mult),
             reads=[r_kn, r_g4], writes=[r_kn])
        return kn, r_kn

    class Defer:
        def __init__(self):
            self.q = []

        def late(self, fn):
            self.q.append(fn)

        def run(self):
            q, self.q = self.q, []
            for fn in q:
                fn()

    def rope(m, src, r_src, nh, hd, slot, tab_off, dst, r_dst):
        half = hd // 2
        ra, r_ra = m.ra.next()
        rb, r_rb = m.rb.next()
        ra = ra.rearrange("p a b -> p (a b)")[:, 0:nh * half].rearrange("p (a b) -> p a b", b=half)
        rb = rb.rearrange("p a b -> p (a b)")[:, 0:nh * half].rearrange("p (a b) -> p a b", b=half)
        cos = bc(m.tab[:, slot, tab_off:tab_off + half].unsqueeze(1), [128, nh, half])
        sin = bc(m.tab[:, slot, tab_off + half:tab_off + 2 * half].unsqueeze(1), [128, nh, half])
        x1 = src[:, 0:nh, 0:half]
        x2 = src[:, 0:nh, half:hd]
        rt = m.r_tab[slot]
        P.op("dve", lambda e: e.tensor_tensor(out=ra, in0=x1, in1=cos, op=ALU.mult), reads=[r_src, rt], writes=[r_ra])
        P.op("pool", lambda e: e.tensor_tensor(out=rb, in0=x2, in1=sin, op=ALU.mult), reads=[r_src, rt], writes=[r_rb])
        P.op("dve", lambda e: e.tensor_tensor(out=dst[:, 0:nh, 0:half], in0=ra, in1=rb, op=ALU.subtract),
             reads=[r_ra, r_rb], writes=[r_dst])
        P.op("dve", lambda e: e.tensor_tensor(out=ra, in0=x2, in1=cos, op=ALU.mult), reads=[r_src, rt, r_dst], writes=[r_ra])
        P.op("pool", lambda e: e.tensor_tensor(out=rb, in0=x1, in1=sin, op=ALU.mult), reads=[r_src, rt, r_dst], writes=[r_rb])
        P.op("dve", lambda e: e.tensor_tensor(out=dst[:, 0:nh, half:hd], in0=ra, in1=rb, op=ALU.add),
             reads=[r_ra, r_rb], writes=[r_dst])

    def transposes_bf(m, src_fn, r_srcb, nh, rows_d, dst, r_dst):
        b = m.pT.next()
        pt = bank_bf(b)
        for h in range(nh):
            P.op("pe", lambda e, h=h: e.transpose(out=pt[0:rows_d, h, :], in_=src_fn(h), identity=identb),
                 reads=[r_srcb, r_identb], writes=[r_pb[b]])
        P.op("act", lambda e: e.activation(out=dst, in_=pt[0:rows_d, 0:nh, :], func=AF.Copy), reads=[r_pb[b]], writes=[r_dst])

    def flat(t):
        return t.rearrange("p h d -> p (h d)")

    def kv_pass(m, n_tiles, x_rows, tab_rows, dst):
        G = m.G
        DF = Defer()
        n_groups = (n_tiles + G - 1) // G
        pending_w = load_w_chunk(m, w_kv, 0, 512)
        for g0 in range(0, n_tiles, G):
            gt = min(G, n_tiles - g0)
            for s in range(gt):
                load_x_tile(m, x_rows(g0 + s), s, tab_rows(g0 + s))
            for c in range(6):
                col0 = c * 512
                ncols = 512 if c < 5 else 72
                wt, r_w = pending_w
                c2 = (c + 1) % 6
                if c2 != 0 or g0 + G < n_tiles:
                    pending_w = load_w_chunk(m, w_kv, c2 * 512, 512 if c2 < 5 else 72)
                for s in range(gt):
                    t = g0 + s
                    pm, r_pm = project(m, s, wt, r_w, ncols)
                    if c in (0, 1):
                        ctx = head_norm_early(m, pm[:, 0:512], r_pm, s, 4)

                        def late(ctx=ctx, t=t, c=c):
                            kn, r_kn = head_norm_late(m, ctx, 4, 1)
                            P.dma("sp", lambda e: e.dma_start(out=dst["fk"](t)[:, c * 512:(c + 1) * 512], in_=flat(kn)),
                                  reads=[r_kn], writes=[r_out], owner=r_kn, kind="out", final=True)
                            kbb, r_kbb = m.kbb.next()
                            P.op("pool", lambda e: e.tensor_copy(out=kbb, in_=kn), reads=[r_kn], writes=[r_kbb])
                            st, r_st = m.st.next()
                            transposes_bf(m, lambda h: kbb[:, h, :], r_kbb, 4, 128, st[:, 0:4, :], r_st)
                            P.dma("sp", lambda e: e.dma_start(out=dst["KaT"](t, c), in_=st[:, 0:4, :]),
                                  reads=[r_st], writes=[dst["r_KaT"]], owner=r_st, kind="out")
                    elif c in (2, 3):
                        kf, r_kf = m.kf.next()
                        P.op("act", lambda e, kf=kf, pm=pm, s=s: e.activation(out=flat(kf), in_=pm[:, 0:512],
                                                                              func=AF.Copy, scale=m.rstd[:, s:s + 1]),
                             reads=[r_pm, m.r_rstd[s]], writes=[r_kf])

                        def late(kf=kf, r_kf=r_kf, t=t, c=c):
                            P.dma("sp", lambda e: e.dma_start(out=dst["fv"](t)[:, (c - 2) * 512:(c - 1) * 512], in_=flat(kf)),
                                  reads=[r_kf], writes=[r_out], owner=r_kf, kind="out", final=True)
                            ve, r_ve = m.ve.next()
                            P.op("dve", lambda e: e.tensor_copy(out=ve[:, :, 0:128], in_=kf), reads=[r_kf], writes=[r_ve])
                            P.dma("sp", lambda e: e.dma_start(out=dst["VaE"](t, c - 2), in_=ve),
                                  reads=[r_ve], writes=[dst["r_VaE"]], owner=r_ve, kind="out")
                    elif c == 4:
                        ctx = head_norm_early(m, pm[:, 0:256], r_pm, s, 2)
                        kf, r_kf = m.kf.next()
                        P.op("act", lambda e, kf=kf, pm=pm, s=s: e.activation(out=flat(kf[:, 0:2, :]), in_=pm[:, 256:512],
                                                                              func=AF.Copy, scale=m.rstd[:, s:s + 1]),
                             reads=[r_pm, m.r_rstd[s]], writes=[r_kf])

                        def late(ctx=ctx, kf=kf, r_kf=r_kf, t=t, s=s):
                            kn, r_kn = head_norm_late(m, ctx, 2, 3)
                            ko, r_ko = m.ko.next()
                            rope(m, kn, r_kn, 2, 128, s, 0, ko, r_ko)
                            P.dma("sp", lambda e: e.dma_start(out=dst["dk"](t), in_=flat(ko[:, 0:2, :])),
                                  reads=[r_ko], writes=[r_out], owner=r_ko, kind="out", final=True)
                            kbb, r_kbb = m.kbb.next()
                            P.op("pool", lambda e: e.tensor_copy(out=kbb[:, 0:2, :], in_=ko[:, 0:2, :]), reads=[r_ko], writes=[r_kbb])
                            st, r_st = m.st.next()
                            transposes_bf(m, lambda h: kbb[:, h, :], r_kbb, 2, 128, st[:, 0:2, :], r_st)
                            P.dma("sp", lambda e: e.dma_start(out=dst["KbT"](t), in_=st[:, 0:2, :]),
                                  reads=[r_st], writes=[dst["r_KbT"]], owner=r_st, kind="out")
                            P.dma("sp", lambda e: e.dma_start(out=dst["dv"](t), in_=flat(kf[:, 0:2, :])),
                                  reads=[r_kf], writes=[r_out], owner=r_kf, kind="out", final=True)
                            ve, r_ve = m.ve.next()
                            P.op("dve", lambda e: e.tensor_copy(out=ve[:, 0:2, 0:128], in_=kf[:, 0:2, :]), reads=[r_kf], writes=[r_ve])
                            P.dma("sp", lambda e: e.dma_start(out=dst["VbE"](t), in_=ve[:, 0:2, :]),
                                  reads=[r_ve], writes=[dst["r_VbE"]], owner=r_ve, kind="out")
                    else:
                        kf, r_kf = m.kf.next()
                        P.op("act", lambda e, kf=kf, pm=pm, s=s: e.activation(out=kf[:, 0, 0:72], in_=pm[:, 0:72],
                                                                              func=AF.Copy, scale=m.rstd[:, s:s + 1]),
                             reads=[r_pm, m.r_rstd[s]], writes=[r_kf])

                        def late(kf=kf, r_kf=r_kf, t=t, s=s):
                            ko, r_ko = m.ko.next()
                            rope(m, kf, r_kf, 1, 64, s, 128, ko, r_ko)
                            P.dma("sp", lambda e: e.dma_start(out=dst["ik"](t), in_=ko[:, 0, 0:64]),
                                  reads=[r_ko], writes=[r_out], owner=r_ko, kind="out", final=True)
                            kbb, r_kbb = m.kbb.next()
                            P.op("pool", lambda e: e.tensor_copy(out=kbb[:, 0, 0:64], in_=ko[:, 0, 0:64]), reads=[r_ko], writes=[r_kbb])
                            st, r_st = m.st.next()
                            transposes_bf(m, lambda h: kbb[:, 0, 0:64], r_kbb, 1, 64, st[0:64, 0:1, :], r_st)
                            P.dma("sp", lambda e: e.dma_start(out=dst["KiT"](t), in_=st[0:64, 0, :]),
                                  reads=[r_st], writes=[dst["r_KiT"]], owner=r_st, kind="out")
                            lf, r_lf = m.lf.next()
                            l1, r_l1 = m.lg.next()
                            l2, r_l2 = m.lg.next()
                            P.op("dve", lambda e: e.tensor_tensor(out=lf, in0=kf[:, 0, 64:72], in1=bfr, op=ALU.add),
                                 reads=[r_kf, r_bfr], writes=[r_lf])
                            P.op("dve", lambda e: e.scalar_tensor_tensor(out=l1, in0=lf, scalar=-1.0, in1=lf, op0=ALU.mult, op1=ALU.max),
                                 reads=[r_lf], writes=[r_l1])
                            P.op("act", lambda e: e.activation(out=l1, in_=l1, func=AF.Exp, scale=-1.0), reads=[r_l1], writes=[r_l1])
                            P.op("act", lambda e: e.activation(out=l1, in_=l1, func=AF.Ln, scale=1.0, bias=1.0), reads=[r_l1], writes=[r_l1])
                            P.op("dve", lambda e: e.tensor_single_scalar(out=l2, in_=lf, scalar=0.0, op=ALU.min),
                                 reads=[r_lf], writes=[r_l2])
                            P.op("dve", lambda e: e.tensor_tensor(out=lf, in0=l2, in1=l1, op=ALU.subtract),
                                 reads=[r_l1, r_l2], writes=[r_lf])
                            P.dma("sp", lambda e: e.dma_start(out=dst["fl"](t), in_=lf),
                                  reads=[r_lf], writes=[r_out], owner=r_lf, kind="out", final=True)
                            if dst.get("logf_keep") is not None:
                                dst["logf_keep"](t, lf, r_lf)
                    DF.run()
                    DF.late(late)
            DF.run()

    def rows(ap, t):
        return ap[t * 128:(t + 1) * 128, :]

    mA = make_proj(8, gA_d)

    def keep_logf(t, lf, r_lf):
        P.op("pool", lambda e: e.tensor_copy(out=lfall[:, t, :], in_=lf), reads=[r_lf], writes=[r_lfall])

    dstP = dict(
        fk=lambda t: rows(o_fk, t), fv=lambda t: rows(o_fv, t), fl=lambda t: rows(o_fl, t),
        dk=lambda t: rows(o_dk, t), dv=lambda t: rows(o_dv, t), ik=lambda t: rows(o_ik, t),
        KaT=lambda t, c: KaT[4 * c:4 * c + 4, :, t * 128:(t + 1) * 128].rearrange("h d t -> d h t"),
        VaE=lambda t, c: VaE[4 * c:4 * c + 4, :, t, :].rearrange("h p e -> p h e"),
        KbT=lambda t: KbT[:, :, t * 128:(t + 1) * 128].rearrange("h d t -> d h t"),
        VbE=lambda t: VbE[:, :, t, :].rearrange("h p e -> p h e"),
        KiT=lambda t: KiT[:, t * 128:(t + 1) * 128],
        r_KaT=r_KaT, r_VaE=r_VaE, r_KbT=r_KbT, r_VbE=r_VbE, r_KiT=r_KiT, logf_keep=keep_logf,
    )
    kv_pass(mA, 32, lambda t: rows(xb, t), lambda t: rows(tabA, t), dstP)
    dstS = dict(
        fk=lambda t: s_fk[:, :], fv=lambda t: s_fv[:, :], fl=lambda t: s_fl[:, :],
        dk=lambda t: s_dk[:, :], dv=lambda t: s_dv[:, :], ik=lambda t: s_ik[:, :],
        KaT=lambda t, c: KaTs[4 * c:4 * c + 4, :, :].rearrange("h d t -> d h t"),
        VaE=lambda t, c: VaEs[4 * c:4 * c + 4, :, 0, :].rearrange("h p e -> p h e"),
        KbT=lambda t: KbTs[:, :, :].rearrange("h d t -> d h t"),
        VbE=lambda t: VbEs[:, :, 0, :].rearrange("h p e -> p h e"),
        KiT=lambda t: KiTs[:, :],
        r_KaT=r_s[0], r_VaE=r_s[1], r_KbT=r_s[2], r_VbE=r_s[3], r_KiT=r_s[4],
        logf_keep=lambda t, lf, r_lf: P.op("pool", lambda e: e.tensor_copy(out=lfs, in_=lf), reads=[r_lf], writes=[r_lfs]),
    )
    kv_pass(mA, 1, lambda t: rows(xq, 9), lambda t: rows(tabQ, 9), dstS)
    barrier(P)
    A.release(base_mark)

    mB = make_proj(NTQ, gA_d)
    DFB = Defer()
    for s in range(NTQ):
        load_x_tile(mB, rows(xq, s), s, rows(tabQ, s))
    pending_wq = load_w_chunk(mB, w_q, 0, 512)
    for c in range(7):
        col0 = c * 512
        ncols = 512 if c < 6 else 16
        wt, r_w = pending_wq
        if c + 1 < 7:
            pending_wq = load_w_chunk(mB, w_q, (c + 1) * 512, 512 if c + 1 < 6 else 16)
        for s in range(NTQ):
            pm, r_pm = project(mB, s, wt, r_w, ncols)
            if c in (0, 1, 2, 3):
                ctx = head_norm_early(mB, pm[:, 0:512], r_pm, s, 4)

                def late(ctx=ctx, s=s, c=c):
                    kn, r_kn = head_norm_late(mB, ctx, 4, 0 if c < 2 else 2)
                    if c >= 2:
                        ko, r_ko = mB.ko.next()
                        rope(mB, kn, r_kn, 4, 128, s, 0, ko, r_ko)
                        kn, r_kn = ko, r_ko
                    kbb, r_kbb = mB.kbb.next()
                    P.op("pool", lambda e: e.tensor_copy(out=kbb, in_=kn), reads=[r_kn], writes=[r_kbb])
                    st, r_st = mB.st.next()
                    transposes_bf(mB, lambda h: kbb[:, h, :], r_kbb, 4, 128, st[:, 0:4, :], r_st)
                    dq, r_dq = (QaT, r_QaT) if c < 2 else (QbT, r_QbT)
                    hc = (c % 2) * 4
                    P.dma("sp", lambda e: e.dma_start(out=dq[s, :, hc:hc + 4, :], in_=st[:, 0:4, :]),
                          reads=[r_st], writes=[r_dq], owner=r_st, kind="out")
            elif c in (4, 5):
                kf, r_kf = mB.kf.next()
                P.op("act", lambda e, kf=kf, pm=pm, s=s: e.activation(out=flat(kf), in_=pm[:, 0:512],
                                                                      func=AF.Copy, scale=mB.rstd[:, s:s + 1]),
                     reads=[r_pm, mB.r_rstd[s]], writes=[r_kf])

                def late(kf=kf, r_kf=r_kf, s=s, c=c):
                    ko, r_ko = mB.ko.next()
                    kf8 = flat(kf).rearrange("p (h d) -> p h d", d=64)
                    ko8 = flat(ko).rearrange("p (h d) -> p h d", d=64)
                    rope(mB, kf8, r_kf, 8, 64, s, 128, ko8, r_ko)
                    kbb, r_kbb = mB.kbb.next()
                    kbb8 = flat(kbb).rearrange("p (h d) -> p h d", d=64)
                    P.op("pool", lambda e: e.tensor_copy(out=kbb8, in_=ko8), reads=[r_ko], writes=[r_kbb])
                    st, r_st = mB.st.next()
                    transposes_bf(mB, lambda h: kbb8[:, h, :], r_kbb, 8, 64, st[0:64, :, :], r_st)
                    hc = (c - 4) * 8
                    P.dma("sp", lambda e: e.dma_start(out=QiT[s, :, hc:hc + 8, :], in_=st[0:64, :, :]),
                          reads=[r_st], writes=[r_QiT], owner=r_st, kind="out")
            else:
                P.op("dve", lambda e, pm=pm, s=s: e.tensor_scalar(out=wi_all[:, s, :], in0=pm[:, 0:16], scalar1=mB.rstd[:, s:s + 1],
                                                                 scalar2=IDX_SCALE, op0=ALU.mult, op1=ALU.mult),
                     reads=[r_pm, mB.r_rstd[s]], writes=[r_wi])

                def late():
                    pass
            DFB.run()
            DFB.late(late)
    DFB.run()
    barrier(P)
    A.release(base_mark)

    att, _ = A.alloc("att", [NTQ, D], BF16)
    r_att = [P.res(f"att{i}") for i in range(NTQ)]
    att_mark = A.mark()

    pinc, r_pinc = A.alloc("pinc", [32, 8], F32)
    tot, r_tot = A.alloc("tot", [32, 8], F32)
    cP, r_cP = A.alloc("cP", [32, 8], F32)
    cref, r_cref = A.alloc("cref", [NT_Q, 8], F32)
    biasF, r_biasF = A.alloc("biasF", [NT_Q, 8, 32], F32)
    bbias, r_bbias = A.alloc("bbias", [NT_Q, 32], F32)
    oh, r_oh = A.alloc("oh", [NT_Q, 32], F32)
    tmp4, r_tmp4 = A.alloc("tmp4", [NT_Q, 8, 32], F32)
    fmask, r_fmask = A.alloc("fmask", [NT_Q, 8, 128], BF16)
    P.dma("sp", lambda e: e.dma_start(out=bbias, in_=bbias_d[:, :, :]), writes=[r_bbias], owner=r_bbias)
    P.dma("sp", lambda e: e.dma_start(out=oh, in_=oh_d[:, :, :]), writes=[r_oh], owner=r_oh)
    for i in range(NT_Q):
        P.dma("pool", lambda e, i=i: e.dma_start(out=fmask[:, i, :, :], in_=fmask_d[i]), writes=[r_fmask], owner=r_fmask)
    lf2 = lfall.rearrange("p b h -> p (b h)")
    P.op("pe", lambda e: e.matmul(pb[6][:, 0:256], lhsT=tri, rhs=lf2, start=True, stop=True),
         reads=[r_tri, r_lfall], writes=[r_pb[6]])
    P.op("pe", lambda e: e.matmul(pb[7][:, 0:256], lhsT=ones, rhs=lf2, start=True, stop=True),
         reads=[r_ones, r_lfall], writes=[r_pb[7]])
    P.op("act", lambda e: e.activation(out=tot.rearrange("p b h -> p (b h)"), in_=pb[7][:, 0:256], func=AF.Copy),
         reads=[r_pb[7]], writes=[r_tot])
    for h in range(8):
        P.op("dve", lambda e, h=h: e.tensor_tensor_scan(out=pinc[:, :, h], data0=ones[:, 0:32], data1=tot[:, :, h],
                                                        initial=0.0, op0=ALU.mult, op1=ALU.add),
             reads=[r_tot, r_ones], writes=[r_pinc])
    P.op("dve", lambda e: e.tensor_tensor(out=cP.rearrange("p b h -> p (b h)"), in0=pb[6][:, 0:256],
                                          in1=pinc.rearrange("p b h -> p (b h)"), op=ALU.add),
         reads=[r_pb[6], r_pinc], writes=[r_cP])
    P.op("dve", lambda e: e.tensor_tensor(out=cP, in0=cP, in1=tot, op=ALU.subtract), reads=[r_cP, r_tot], writes=[r_cP])
    pinc_hb = pinc.rearrange("p b h -> p h b")
    cP_hb = cP.rearrange("p b h -> p h b")
    P.op("dve", lambda e: e.tensor_tensor(out=tmp4, in0=bc(pinc_hb.unsqueeze(1), [128, NT_Q, 8, 32]),
                                          in1=bc(oh.unsqueeze(2), [128, NT_Q, 8, 32]), op=ALU.mult),
         reads=[r_pinc, r_oh], writes=[r_tmp4])
    P.op("dve", lambda e: e.tensor_reduce(out=cref.rearrange("p i h -> p (i h)"), in_=tmp4.rearrange("p i h b -> p (i h) b"),
                                          axis=AX.X, op=ALU.add),
         reads=[r_tmp4], writes=[r_cref])
    P.op("dve", lambda e: e.tensor_tensor(out=biasF, in0=bc(cref.unsqueeze(3), [128, NT_Q, 8, 32]),
                                          in1=bc(cP_hb.unsqueeze(1), [128, NT_Q, 8, 32]), op=ALU.subtract),
         reads=[r_cref, r_cP], writes=[r_biasF])
    P.op("dve", lambda e: e.tensor_tensor(out=biasF, in0=biasF, in1=bc(bbias.unsqueeze(2), [128, NT_Q, 8, 32]), op=ALU.add),
         reads=[r_biasF, r_bbias], writes=[r_biasF])

    kt_ring = A.ring("kt", [SEQ], BF16, 2)
    vt_ring = A.ring("vt", [32, 129], BF16, 2)
    qh_ring = A.ring("qh", [NT_Q, 128], BF16, 2)
    pt_ring = A.ring("ptile", [4, 128], BF16, 3)
    rd_ring = A.ring("rd", [2], F32, 4)
    pS = BankRing([2, 3])
    pO = BankRing([4, 5])
    pend = [None]

    def flush():
        if pend[0] is not None:
            pend[0]()
            pend[0] = None

    for h in range(8):
        kt, r_kt = kt_ring.next()
        vt, r_vt = vt_ring.next()
        qh, r_qh = qh_ring.next()
        P.dma("sp", lambda e, kt=kt, h=h: e.dma_start(out=kt, in_=KaT[h]), reads=[r_KaT], writes=[r_kt], owner=r_kt)
        P.dma("sp", lambda e, vt=vt, h=h: e.dma_start(out=vt, in_=VaE[h]), reads=[r_VaE], writes=[r_vt], owner=r_vt)
        P.dma("sp", lambda e, qh=qh, h=h: e.dma_start(out=qh, in_=QaT[0:NT_Q, :, h, :].rearrange("t d k -> d t k")),
              reads=[r_QaT], writes=[r_qh], owner=r_qh)
        for i in range(NT_Q):
            KBi = KB_OF(i)
            mk = MASK_KBS(i)
            bo = pO.next()
            po = pb[bo][:, 0:129]
            for k0 in range(0, KBi, 4):
                nk = min(4, KBi - k0)
                bs = pS.next()
                ps = pb[bs].rearrange("p (a b) -> p a b", b=128)
                ptile, r_ptile = pt_ring.next()
                for kk in range(nk):
                    kb = k0 + kk
                    P.op("pe", lambda e, ps=ps, kk=kk, kb=kb, kt=kt, qh=qh, i=i: e.matmul(
                        ps[:, kk, :], lhsT=kt[:, kb * 128:(kb + 1) * 128], rhs=qh[:, i, :], start=True, stop=True),
                        reads=[r_kt, r_qh], writes=[r_pb[bs]])
                for kk in range(nk):
                    kb = k0 + kk
                    P.op("act", lambda e, ps=ps, kk=kk, kb=kb, ptile=ptile, i=i, h=h: e.activation(
                        out=ptile[:, kk, :], in_=ps[:, kk, :], func=AF.Exp, scale=SCALE, bias=biasF[:, i, h, kb:kb + 1]),
                        reads=[r_pb[bs], r_biasF], writes=[r_ptile])
                    if kb in mk:
                        sl = mk.index(kb)
                        P.op("pool", lambda e, ptile=ptile, kk=kk, i=i, sl=sl: e.tensor_tensor(
                            out=ptile[:, kk, :], in0=ptile[:, kk, :], in1=fmask[:, i, sl, :], op=ALU.mult),
                            reads=[r_ptile, r_fmask], writes=[r_ptile])
                flush()

                def pv(k0=k0, nk=nk, ptile=ptile, r_ptile=r_ptile, vt=vt, r_vt=r_vt, po=po, bo=bo, KBi=KBi, i=i, h=h):
                    for kk in range(nk):
                        kb = k0 + kk
                        P.op("pe", lambda e, kk=kk, kb=kb: e.matmul(po, lhsT=ptile[:, kk, :], rhs=vt[:, kb, :],
                                                                    start=(kb == 0), stop=(kb == KBi - 1)),
                             reads=[r_ptile, r_vt], writes=[r_pb[bo]])
                    if k0 + nk == KBi:
                        rd, r_rd = rd_ring.next()
                        P.op("dve", lambda e: e.tensor_scalar(out=rd[:, 0:1], in0=po[:, 128:129], scalar1=1e-30, scalar2=None,
                                                              op0=ALU.max), reads=[r_pb[bo]], writes=[r_rd])
                        P.op("dve", lambda e: e.reciprocal(out=rd[:, 1:2], in_=rd[:, 0:1]), reads=[r_rd], writes=[r_rd])
                        P.op("act", lambda e: e.activation(out=att[:, i, h * 128:(h + 1) * 128], in_=po[:, 0:128], func=AF.Copy,
                                                           scale=rd[:, 1:2]), reads=[r_pb[bo], r_rd], writes=[r_att[i]])
                pend[0] = pv
    flush()
    barrier(P)
    A.release(att_mark)

    kbt, r_kbt = A.alloc("kbt", [2, SEQ], BF16)
    vbt, r_vbt = A.alloc("vbt", [2, 32, 129], BF16)
    kit, r_kit = A.alloc("kit", [SEQ], BF16)
    P.dma("sp", lambda e: e.dma_start(out=kbt, in_=KbT.rearrange("h d t -> d h t")), reads=[r_KbT], writes=[r_kbt], owner=r_kbt)
    P.dma("sp", lambda e: e.dma_start(out=vbt, in_=VbE.rearrange("h p b e -> p h b e")), reads=[r_VbE], writes=[r_vbt], owner=r_vbt)
    P.dma("sp", lambda e: e.dma_start(out=kit[0:64, :], in_=KiT[:, :]), reads=[r_KiT], writes=[r_kit], owner=r_kit)
    qi_ring = A.ring("qi_t", [16, 128], BF16, 2)
    qb_ring = A.ring("qb_t", [8, 128], BF16, 2)
    sc, r_sc = A.alloc("sc", [SEQ], F32)
    wk, r_wk = A.alloc("wk", [SEQ], F32)
    nm_ring = A.ring("nm", [SEQ], BF16, 1)
    sel, r_sel = A.alloc("sel", [SEQ], BF16)
    selT, r_selT = A.alloc("selT", [32, 128], BF16)
    rl_ring = A.ring("rl", [512], F32, 3)
    m8_ring = A.ring("m8", [8], F32, 2)
    thr_ring = A.ring("thr", [1], F32, 2)
    pt4_ring = A.ring("pt4", [4, 128], BF16, 3)
    rd2_ring = A.ring("rd2", [2], F32, 4)
    pI = BankRing([2, 3])
    pT2 = BankRing([0, 1])
    pS2 = BankRing([4, 5])

    for i in range(NT_Q):
        KBi = KB_OF(i)
        L = KBi * 128
        qi_t, r_qi = qi_ring.next()
        qb_t, r_qb = qb_ring.next()
        nm_t, r_nm = nm_ring.next()
        P.dma("sp", lambda e, qi_t=qi_t, i=i: e.dma_start(out=qi_t[0:64, :, :], in_=QiT[i]), reads=[r_QiT], writes=[r_qi], owner=r_qi)
        P.dma("sp", lambda e, qb_t=qb_t, i=i: e.dma_start(out=qb_t, in_=QbT[i]), reads=[r_QbT], writes=[r_qb], owner=r_qb)
        for q2 in range(0, L, 2048):
            n2 = min(2048, L - q2)
            P.dma("pool", lambda e, nm_t=nm_t, i=i, q2=q2, n2=n2: e.dma_start(out=nm_t[:, q2:q2 + n2], in_=nm_d[i, :, q2:q2 + n2]),
                  writes=[r_nm], owner=r_nm)
        for g0 in range(0, L, 512):
            ncol = min(512, L - g0)
            for hh in range(H_IDX):
                bi = pI.next()
                rl, r_rl = rl_ring.next()
                P.op("pe", lambda e, bi=bi, hh=hh, g0=g0, ncol=ncol, qi_t=qi_t: e.matmul(
                    pb[bi][:, 0:ncol], lhsT=qi_t[0:64, hh, :], rhs=kit[0:64, g0:g0 + ncol], start=True, stop=True),
                    reads=[r_qi, r_kit], writes=[r_pb[bi]])
                P.op("act", lambda e, bi=bi, rl=rl, ncol=ncol: e.activation(out=rl[:, 0:ncol], in_=pb[bi][:, 0:ncol], func=AF.Relu),
                     reads=[r_pb[bi]], writes=[r_rl])
                if hh == 0:
                    P.op("dve", lambda e, rl=rl, g0=g0, ncol=ncol, i=i, nm_t=nm_t: e.scalar_tensor_tensor(
                        out=sc[:, g0:g0 + ncol], in0=rl[:, 0:ncol], scalar=wi_all[:, i, 0:1], in1=nm_t[:, g0:g0 + ncol],
                        op0=ALU.mult, op1=ALU.add), reads=[r_rl, r_wi, r_nm], writes=[r_sc])
                else:
                    P.op("dve", lambda e, rl=rl, g0=g0, ncol=ncol, i=i, hh=hh: e.scalar_tensor_tensor(
                        out=sc[:, g0:g0 + ncol], in0=rl[:, 0:ncol], scalar=wi_all[:, i, hh:hh + 1], in1=sc[:, g0:g0 + ncol],
                        op0=ALU.mult, op1=ALU.add), reads=[r_rl, r_wi, r_sc], writes=[r_sc])
        m8, r_m8 = m8_ring.next()
        thr, r_thr = thr_ring.next()
        for r in range(32):
            src = sc if r == 0 else wk
            r_src = r_sc if r == 0 else r_wk
            P.op("dve", lambda e, src=src, m8=m8, L=L: e.max(out=m8, in_=src[:, 0:L]), reads=[r_src], writes=[r_m8])
            if r < 31:
                P.op("dve", lambda e, src=src, m8=m8, L=L: e.match_replace(out=wk[:, 0:L], in_to_replace=m8, in_values=src[:, 0:L],
                                                                          imm_value=NEG),
                     reads=[r_src, r_m8], writes=[r_wk])
        P.op("dve", lambda e, m8=m8, thr=thr: e.tensor_scalar(out=thr, in0=m8[:, 7:8], scalar1=-1.0e29, scalar2=None, op0=ALU.max),
             reads=[r_m8], writes=[r_thr])
        P.op("dve", lambda e, thr=thr, L=L: e.tensor_scalar(out=sel[:, 0:L], in0=sc[:, 0:L], scalar1=thr[:, 0:1], scalar2=None,
                                                          op0=ALU.is_ge), reads=[r_sc, r_thr], writes=[r_sel])
        for k0 in range(0, KBi, 8):
            nk = min(8, KBi - k0)
            bt = pT2.next()
            ptb = bank_bf(bt)
            for kk in range(nk):
                kb = k0 + kk
                P.op("pe", lambda e, ptb=ptb, kk=kk, kb=kb: e.transpose(out=ptb[:, kk, :], in_=sel[:, kb * 128:(kb + 1) * 128],
                                                                        identity=identb),
                     reads=[r_sel, r_identb], writes=[r_pb[bt]])
            P.op("act", lambda e, ptb=ptb, k0=k0, nk=nk: e.activation(out=selT[:, k0:k0 + nk, :], in_=ptb[:, 0:nk, :], func=AF.Copy),
                 reads=[r_pb[bt]], writes=[r_selT])
        for g2 in range(KV_B):
            obank = [6, 7]
            ov = [pb[b_][:, 0:258].rearrange("p (a b) -> p a b", b=129) for b_ in obank]
            for kb in range(KBi):
                bs = pS2.next()
                ps = pb[bs].rearrange("p (a b) -> p a b", b=128)
                pt4, r_pt4 = pt4_ring.next()
                P.op("pe", lambda e, ps=ps, kb=kb, g2=g2, qb_t=qb_t: e.matmul(
                    ps, lhsT=kbt[:, g2, kb * 128:(kb + 1) * 128], rhs=qb_t[:, 4 * g2:4 * g2 + 4, :], start=True, stop=True),
                    reads=[r_kbt, r_qb], writes=[r_pb[bs]])
                P.op("act", lambda e, ps=ps, pt4=pt4: e.activation(out=pt4, in_=ps, func=AF.Exp, scale=SCALE),
                     reads=[r_pb[bs]], writes=[r_pt4])
                P.op("dve", lambda e, pt4=pt4, kb=kb: e.tensor_tensor(out=pt4, in0=pt4, in1=bc(selT[:, kb, :].unsqueeze(1), [128, 4, 128]),
                                                                      op=ALU.mult), reads=[r_pt4, r_selT], writes=[r_pt4])
                flush()

                def pv2(kb=kb, pt4=pt4, r_pt4=r_pt4, g2=g2, KBi=KBi, i=i, ov=ov, obank=obank):
                    for hq in range(4):
                        o_ap = ov[hq // 2][:, hq % 2, :]
                        P.op("pe", lambda e, hq=hq, o_ap=o_ap: e.matmul(o_ap, lhsT=pt4[:, hq, :], rhs=vbt[:, g2, kb, :],
                                                                        start=(kb == 0 and hq % 2 == 0), stop=(kb == KBi - 1),
                                                                        skip_group_check=True),
                             reads=[r_pt4, r_vbt], writes=[r_pb[obank[hq // 2]]])
                    if kb == KBi - 1:
                        for hq in range(4):
                            o_ap = ov[hq // 2][:, hq % 2, :]
                            rb_ = r_pb[obank[hq // 2]]
                            rd, r_rd = rd2_ring.next()
                            c0 = 1024 + (4 * g2 + hq) * 128
                            P.op("dve", lambda e, o_ap=o_ap, rd=rd: e.tensor_scalar(out=rd[:, 0:1], in0=o_ap[:, 128:129], scalar1=1e-30,
                                                                                   scalar2=None, op0=ALU.max), reads=[rb_], writes=[r_rd])
                            P.op("dve", lambda e, rd=rd: e.reciprocal(out=rd[:, 1:2], in_=rd[:, 0:1]), reads=[r_rd], writes=[r_rd])
                            P.op("act", lambda e, o_ap=o_ap, rd=rd, c0=c0: e.activation(out=att[:, i, c0:c0 + 128], in_=o_ap[:, 0:128],
                                                                                       func=AF.Copy, scale=rd[:, 1:2]),
                                 reads=[rb_, r_rd], writes=[r_att[i]])
                pend[0] = pv2
            flush()
    barrier(P)
    A.release(att_mark)

    NPOOL = cfk.shape[0] // 128
    SK = PAST + 128
    ptb, r_ptb = A.alloc("ptb", [256], I32)
    idx_tok, r_idx = A.alloc("idx_tok", [256], I32)
    idx_pg, r_idxpg = A.alloc("idx_pg", [4], I32)
    iota_i, r_iota = A.alloc("iota_i", [1], I32)
    iota_f, r_iotaf = A.alloc("iota_f", [1], F32)
    lst, r_lst = A.alloc("lst", [128], F32)
    o64, r_o64 = A.alloc("o64", [128], F32)
    lblk, r_lblk = A.alloc("lblk", [128], F32)
    esel, r_esel = A.alloc("esel", [4, 128], F32)
    smask, r_smask = A.alloc("smask", [4, 32], BF16)
    P.dma("sp", lambda e: e.dma_start(out=ptb, in_=pt_d.rearrange("s l -> (s l)").partition_broadcast(128)),
          writes=[r_ptb], owner=r_ptb)
    P.op("pool", lambda e: e.memset(idx_pg, 0), writes=[r_idxpg])
    P.dma("sp", lambda e: e.dma_start(out=idx_pg[0:64, :], in_=pt_d.rearrange("s l -> l s"), allow_slow_non_contiguous=True),
          writes=[r_idxpg], owner=r_idxpg)
    P.dma("sp", lambda e: e.dma_start(out=lst, in_=lst_d[:, :]), writes=[r_lst], owner=r_lst)
    P.dma("sp", lambda e: e.dma_start(out=o64, in_=o64_d[:, :]), writes=[r_o64], owner=r_o64)
    P.dma("sp", lambda e: e.dma_start(out=lblk, in_=lblk_d[:, :]), writes=[r_lblk], owner=r_lblk)
    P.dma("sp", lambda e: e.dma_start(out=esel, in_=esel_d[:, :, :]), writes=[r_esel], owner=r_esel)
    P.dma("pool", lambda e: e.dma_start(out=smask, in_=smask_d[:, :, :]), writes=[r_smask], owner=r_smask)
    P.op("pool", lambda e: e.iota(iota_i, [[0, 1]], base=0, channel_multiplier=1), writes=[r_iota])
    P.op("dve", lambda e: e.tensor_copy(out=iota_f, in_=iota_i), reads=[r_iota], writes=[r_iotaf])
    P.op("dve", lambda e: e.tensor_scalar(out=idx_tok, in0=ptb, scalar1=128.0, scalar2=iota_f[:, 0:1], op0=ALU.mult, op1=ALU.add),
         reads=[r_ptb, r_iotaf], writes=[r_idx])

    def gather(dst_ap, r_dst, src2d, idx_col_ap, r_ix):
        P.dma("pool", lambda e: e.indirect_dma_start(out=dst_ap, out_offset=None, in_=src2d,
                                                     in_offset=bass.IndirectOffsetOnAxis(ap=idx_col_ap, axis=0)),
              reads=[r_ix], writes=[r_dst], owner=r_dst)

    kp_ring = A.ring("kp", [8, 128], BF16, 3)
    vp_ring = A.ring("vp", [8, 129], BF16, 3)
    ktp_ring = A.ring("ktp", [8, 128], BF16, 3)
    vs_ring = A.ring("vstage", [8, 128], BF16, 3)
    z_ring = A.ring("z_s", [8, 32], F32, 2)
    ps_rings = [A.ring("pt_s0", [8, 32], BF16, 2), A.ring("pt_s1", [8, 32], BF16, 2),
                A.ring("pt_s2", [8, 64], BF16, 2), A.ring("pt_s3", [8, 64], BF16, 2)]
    for t_, r_ in ps_rings[2].items:
        P.op("pool", lambda e, t_=t_: e.memset(t_[:, :, 32:64], 0.0), writes=[r_])
    for t_, r_ in ps_rings[3].items:
        P.op("pool", lambda e, t_=t_: e.memset(t_[:, :, 0:32], 0.0), writes=[r_])
    qs_a, r_qsa = A.alloc("qs_a", [8, 128], BF16)
    qs_b, r_qsb = A.alloc("qs_b", [8, 128], BF16)
    P.dma("sp", lambda e: e.dma_start(out=qs_a, in_=QaT[NT_Q]), reads=[r_QaT], writes=[r_qsa], owner=r_qsa)
    P.dma("sp", lambda e: e.dma_start(out=qs_b, in_=QbT[NT_Q]), reads=[r_QbT], writes=[r_qsb], owner=r_qsb)
    for t_, r_ in vp_ring.items:
        P.op("pool", lambda e, t_=t_: e.memset(t_[:, :, 128:129], 1.0), writes=[r_])
    pTs = BankRing([0, 1])
    pSs = BankRing([2, 3])
    OB = [4, 5, 6]

    def o_view(h, s):
        b = OB[h // 3]
        r0, r1 = (32 * s, 32 * s + 32) if s < 2 else (64, 128)
        return pb[b][r0:r1, (h % 3) * 129:(h % 3) * 129 + 129], b

    def sample_attn(kind):
        nkv = 8 if kind == "fox" else 2
        per = 1 if kind == "fox" else 4
        kcache, vcache = (cfk, cfv) if kind == "fox" else (cdk, cdv)
        qs, r_qs = (qs_a, r_qsa) if kind == "fox" else (qs_b, r_qsb)
        KTs, VEs, rKTs, rVEs = (KaTs, VaEs, r_s[0], r_s[1]) if kind == "fox" else (KbTs, VbEs, r_s[2], r_s[3])
        col0 = 0 if kind == "fox" else 1024
        for s in range(4):
            for lp in range(NPG + 1):
                new = (lp == NPG)
                ktp, r_ktp = ktp_ring.next()
                vp, r_vp = vp_ring.next()
                if not new:
                    kp, r_kp = kp_ring.next()
                    ic = idx_tok[:, s * 64 + lp:s * 64 + lp + 1]
                    vs, r_vs = vs_ring.next()
                    gather(kp.rearrange("p h d -> p (h d)")[:, 0:nkv * 128], r_kp, kcache, ic, r_idx)
                    gather(vs.rearrange("p h d -> p (h d)")[:, 0:nkv * 128], r_vs, vcache, ic, r_idx)
                    P.op("dve", lambda e, vs=vs, vp=vp: e.tensor_copy(out=vp[:, 0:nkv, 0:128], in_=vs[:, 0:nkv, :]),
                         reads=[r_vs], writes=[r_vp])
                    bt = pTs.next()
                    ptb_ = bank_bf(bt)
                    for h in range(nkv):
                        P.op("pe", lambda e, h=h, kp=kp, ptb_=ptb_: e.transpose(out=ptb_[:, h, :], in_=kp[:, h, :], identity=identb),
                             reads=[r_kp, r_identb], writes=[r_pb[bt]])
                    P.op("act", lambda e, ktp=ktp, ptb_=ptb_: e.activation(out=ktp[:, 0:nkv, :], in_=ptb_[:, 0:nkv, :], func=AF.Copy),
                         reads=[r_pb[bt]], writes=[r_ktp])
                else:
                    P.dma("sp", lambda e, ktp=ktp: e.dma_start(out=ktp[:, 0:nkv, :], in_=KTs.rearrange("h d t -> d h t")),
                          reads=[rKTs], writes=[r_ktp], owner=r_ktp)
                    P.dma("sp", lambda e, vp=vp: e.dma_start(out=vp[:, 0:nkv, :], in_=VEs[:, :, 0, :].rearrange("h p e -> p h e")),
                          reads=[rVEs], writes=[r_vp], owner=r_vp)
                bs = pSs.next()
                psv = pb[bs][:, 0:256].rearrange("p (a b) -> p a b", b=32)
                for g in range(nkv):
                    if per == 1:
                        P.op("pe", lambda e, g=g, ktp=ktp, psv=psv, s=s: e.matmul(psv[:, g, :], lhsT=ktp[:, g, :], rhs=qs[:, g, 32 * s:32 * s + 32],
                                                                            start=True, stop=True),
                             reads=[r_ktp, r_qs], writes=[r_pb[bs]])
                    else:
                        P.op("pe", lambda e, g=g, ktp=ktp, psv=psv, s=s: e.matmul(psv[:, 4 * g:4 * g + 4, :], lhsT=ktp[:, g, :],
                                                                            rhs=qs[:, 4 * g:4 * g + 4, 32 * s:32 * s + 32],
                                                                            start=True, stop=True),
                             reads=[r_ktp, r_qs], writes=[r_pb[bs]])
                pt_full, r_pts = ps_rings[s].next()
                pt_s = pt_full if s < 2 else pt_full[:, :, (s - 2) * 32:(s - 2) * 32 + 32]
                if kind == "fox":
                    z, r_z = z_ring.next()
                    bias_ap = bN[:, s, :] if new else bP[:, s, :, lp]
                    r_bias = r_bN if new else r_bP[s]
                    P.op("dve", lambda e, z=z, psv=psv, bias_ap=bias_ap: e.scalar_tensor_tensor(
                        out=z, in0=psv, scalar=SCALE, in1=bc(bias_ap.unsqueeze(2), [128, 8, 32]), op0=ALU.mult, op1=ALU.add),
                        reads=[r_pb[bs], r_bias], writes=[r_z])
                    P.op("act", lambda e, z=z, pt_s=pt_s: e.activation(out=pt_s, in_=z, func=AF.Exp), reads=[r_z], writes=[r_pts])
                    if new:
                        P.op("dve", lambda e, pt_s=pt_s, s=s: e.tensor_tensor(out=pt_s, in0=pt_s, in1=bc(smask[:, s, :].unsqueeze(1), [128, 8, 32]),
                                                                         op=ALU.mult), reads=[r_pts, r_smask], writes=[r_pts])
                else:
                    P.op("act", lambda e, psv=psv, pt_s=pt_s: e.activation(out=pt_s, in_=psv, func=AF.Exp, scale=SCALE),
                         reads=[r_pb[bs]], writes=[r_pts])
                    P.op("dve", lambda e, pt_s=pt_s, lp=lp, s=s: e.tensor_tensor(
                        out=pt_s, in0=pt_s, in1=bc(selTs[:, lp, 32 * s:32 * s + 32].unsqueeze(1), [128, 8, 32]), op=ALU.mult),
                        reads=[r_pts, r_selTs], writes=[r_pts])
                flush()

                def pv3(pt_full=pt_full, r_pts=r_pts, vp=vp, r_vp=r_vp, lp=lp, s=s, new=new):
                    for h in range(8):
                        o_ap, b = o_view(h, s)
                        P.op("pe", lambda e, h=h, o_ap=o_ap: e.matmul(o_ap, lhsT=pt_full[:, h, :], rhs=vp[:, h // per, :],
                                                                      start=(lp == 0 and h % 3 == 0 and s != 3), stop=new,
                                                                      skip_group_check=True),
                             reads=[r_pts, r_vp], writes=[r_pb[b]])
                pend[0] = pv3
            flush()
        for h in range(8):
            b = OB[h // 3]
            o_full = pb[b][:, (h % 3) * 129:(h % 3) * 129 + 129]
            rd, r_rd = rd2_ring.next()
            P.op("dve", lambda e, o_full=o_full, rd=rd: e.tensor_scalar(out=rd[:, 0:1], in0=o_full[:, 128:129], scalar1=1e-30, scalar2=None,
                                                                         op0=ALU.max), reads=[r_pb[b]], writes=[r_rd])
            P.op("dve", lambda e, rd=rd: e.reciprocal(out=rd[:, 1:2], in_=rd[:, 0:1]), reads=[r_rd], writes=[r_rd])
            P.op("act", lambda e, o_full=o_full, rd=rd, h=h: e.activation(out=att[:, NT_Q, col0 + h * 128:col0 + (h + 1) * 128],
                                                                          in_=o_full[:, 0:128], func=AF.Copy, scale=rd[:, 1:2]),
                 reads=[r_pb[b], r_rd], writes=[r_att[NT_Q]])

    rl_ring = A.ring("rl_s", [512], F32, 3)
    m8_ring = A.ring("m8_s", [8], F32, 2)
    thr_ring = A.ring("thr_s", [1], F32, 2)
    rd2_ring = A.ring("rd2_s", [2], F32, 4)
    smp_mark = A.mark()
    cs_s, r_cs = A.alloc("cs_s", [8], F32)
    csref, r_csref = A.alloc("csref", [4, 8], F32)
    bP, r_bP_base = A.alloc("bP", [4, 8, 64], F32)
    r_bP = [P.res(f"bP{s}") for s in range(4)]
    bN, r_bN = A.alloc("bN", [4, 8], F32)
    P.op("pe", lambda e: e.matmul(pb[7][:, 0:8], lhsT=lblk, rhs=lfs, start=True, stop=True), reads=[r_lblk, r_lfs], writes=[r_pb[7]])
    P.op("act", lambda e: e.activation(out=cs_s, in_=pb[7][:, 0:8], func=AF.Copy), reads=[r_pb[7]], writes=[r_cs])
    for s in range(4):
        P.op("pe", lambda e, s=s: e.matmul(pb[7][:, 8 + 8 * s:16 + 8 * s], lhsT=esel[:, s, :], rhs=cs_s, start=True, stop=True),
             reads=[r_esel, r_cs], writes=[r_pb[7]])
    P.op("act", lambda e: e.activation(out=csref.rearrange("p s h -> p (s h)"), in_=pb[7][:, 8:40], func=AF.Copy),
         reads=[r_pb[7]], writes=[r_csref])
    P.op("dve", lambda e: e.tensor_tensor(out=bN, in0=csref, in1=bc(cs_s.unsqueeze(1), [128, 4, 8]), op=ALU.subtract),
         reads=[r_csref, r_cs], writes=[r_bN])
    lp_ring = A.ring("lp_t", [128, 8], F32, 2)
    cw_ring = A.ring("cw", [8, 128], F32, 2)
    sm2_ring = A.ring("sm2", [3, 8], F32, 2)
    for s in range(4):
        lp_t, r_lp = lp_ring.next()
        cw, r_cw = cw_ring.next()
        sm2, r_sm2 = sm2_ring.next()
        gather(lp_t.rearrange("p t h -> p (t h)"), r_lp, cfl, idx_pg[:, s:s + 1], r_idxpg)
        for h in range(8):
            P.op("dve", lambda e, h=h, cw=cw, lp_t=lp_t: e.tensor_tensor_scan(out=cw[:, h, :], data0=ones[:, 0:128], data1=lp_t[:, :, h],
                                                                              initial=0.0, op0=ALU.mult, op1=ALU.add),
                 reads=[r_lp, r_ones], writes=[r_cw])
        P.op("dve", lambda e, cw=cw, sm2=sm2: e.tensor_copy(out=sm2[:, 0, :], in_=cw[:, :, 127]), reads=[r_cw], writes=[r_sm2])
        P.op("pe", lambda e, sm2=sm2: e.matmul(pb[6][:, 0:8], lhsT=lst, rhs=sm2[:, 0, :], start=True, stop=True),
             reads=[r_lst, r_sm2], writes=[r_pb[6]])
        P.op("pe", lambda e, sm2=sm2: e.matmul(pb[6][:, 8:16], lhsT=o64, rhs=sm2[:, 0, :], start=True, stop=True),
             reads=[r_o64, r_sm2], writes=[r_pb[6]])
        P.op("act", lambda e, sm2=sm2: e.activation(out=sm2[:, 1:3, :].rearrange("p a h -> p (a h)"), in_=pb[6][:, 0:16], func=AF.Copy),
             reads=[r_pb[6]], writes=[r_sm2])
        P.op("dve", lambda e, cw=cw, sm2=sm2: e.tensor_tensor(out=cw, in0=cw, in1=bc(sm2[:, 1, :].unsqueeze(2), [128, 8, 128]), op=ALU.add),
             reads=[r_cw, r_sm2], writes=[r_cw])
        P.op("dve", lambda e, sm2=sm2, s=s: e.tensor_tensor(out=sm2[:, 2, :], in0=sm2[:, 2, :], in1=csref[:, s, :], op=ALU.add),
             reads=[r_sm2, r_csref], writes=[r_sm2])
        for h0 in range(0, 8, 4):
            pcT = pb[7][:, :].rearrange("p (a b) -> p a b", b=128)
            for hh in range(4):
                P.op("pe", lambda e, hh=hh, h0=h0, cw=cw, pcT=pcT: e.transpose(out=pcT[:, hh, :], in_=cw[:, h0 + hh, :], identity=identf),
                     reads=[r_cw, r_identf], writes=[r_pb[7]])
            P.op("dve", lambda e, h0=h0, s=s, sm2=sm2, pcT=pcT: e.scalar_tensor_tensor(
                out=bP[:, s, h0:h0 + 4, :], in0=pcT[:, :, 0:64], scalar=-1.0,
                in1=bc(sm2[:, 2, h0:h0 + 4].unsqueeze(2), [128, 4, 64]), op0=ALU.mult, op1=ALU.add),
                reads=[r_pb[7], r_sm2], writes=[r_bP[s]])

    sample_attn("fox")
    barrier(P)
    A.release(smp_mark)

    kiTall = dscr("kiTall", [4, 64, SK], BF16)
    r_kiTall = P.res("kiTall")
    kip_ring = A.ring("kip", [64], BF16, 3)
    kst_ring = A.ring("kist", [8, 128], BF16, 2)
    for s in range(4):
        for l0 in range(0, NPG, 8):
            bt = pTs.next()
            ptb_ = bank_bf(bt)
            kst, r_kst = kst_ring.next()
            for ll in range(8):
                lp = l0 + ll
                kip, r_kip = kip_ring.next()
                gather(kip, r_kip, cik, idx_tok[:, s * 64 + lp:s * 64 + lp + 1], r_idx)
                P.op("pe", lambda e, ll=ll, kip=kip, ptb_=ptb_: e.transpose(out=ptb_[0:64, ll, :], in_=kip, identity=identb),
                     reads=[r_kip, r_identb], writes=[r_pb[bt]])
            P.op("act", lambda e, kst=kst, ptb_=ptb_: e.activation(out=kst[0:64, :, :], in_=ptb_[0:64, :, :], func=AF.Copy),
                 reads=[r_pb[bt]], writes=[r_kst])
            P.dma("sp", lambda e, kst=kst, s=s, l0=l0: e.dma_start(out=kiTall[s, :, l0 * 128:(l0 + 8) * 128],
                                                                   in_=kst[0:64, :, :].rearrange("p a b -> p (a b)")),
                  reads=[r_kst], writes=[r_kiTall], owner=r_kst, kind="out")
        kst, r_kst = kst_ring.next()
        P.dma("sp", lambda e, kst=kst: e.dma_start(out=kst[0:64, 0, :], in_=KiTs[:, :]), reads=[r_s[4]], writes=[r_kst], owner=r_kst)
        P.dma("sp", lambda e, kst=kst, s=s: e.dma_start(out=kiTall[s, :, PAST:SK], in_=kst[0:64, 0, :]),
              reads=[r_kst], writes=[r_kiTall], owner=r_kst, kind="out")
    qis, r_qis = A.alloc("qis", [16, 128], BF16)
    P.dma("sp", lambda e: e.dma_start(out=qis[0:64, :, :], in_=QiT[NT_Q]), reads=[r_QiT], writes=[r_qis], owner=r_qis)
    qis23, r_qis23 = A.alloc("qis23", [2, 16, 64], BF16)
    P.op("pool", lambda e: e.memset(qis23[0:64], 0.0), writes=[r_qis23])
    P.op("pool", lambda e: e.tensor_copy(out=qis23[0:64, 0, :, 0:32], in_=qis[0:64, :, 64:96]), reads=[r_qis], writes=[r_qis23])
    P.op("pool", lambda e: e.tensor_copy(out=qis23[0:64, 1, :, 32:64], in_=qis[0:64, :, 96:128]), reads=[r_qis], writes=[r_qis23])
    scs, r_scs = A.alloc("scs", [SK], F32)
    wks, r_wks = A.alloc("wks", [SK], F32)
    sels_ring = A.ring("sels", [1024], BF16, 2)
    selTs, r_selTs = A.alloc("selTs", [NPG + 1, 128], BF16)
    nms, r_nms = A.alloc("nms", [128], F32)
    P.dma("sp", lambda e: e.dma_start(out=nms, in_=nms_d[:, :]), writes=[r_nms], owner=r_nms)
    kig_ring = A.ring("kig", [4, 512], BF16, 2)
    for g0 in range(0, SK, 512):
        ncol = min(512, SK - g0)
        kig, r_kig = kig_ring.next()
        P.dma("sp", lambda e, kig=kig, g0=g0, ncol=ncol: e.dma_start(out=kig[0:64, :, 0:ncol],
                                                                     in_=kiTall[:, :, g0:g0 + ncol].rearrange("s d k -> d s k")),
              reads=[r_kiTall], writes=[r_kig], owner=r_kig)
        for hh in range(H_IDX):
            bi = pSs.next()
            rl, r_rl = rl_ring.next()
            for s in range(4):
                if s < 2:
                    P.op("pe", lambda e, bi=bi, hh=hh, s=s, ncol=ncol, kig=kig: e.matmul(
                        pb[bi][32 * s:32 * s + 32, 0:ncol], lhsT=qis[0:64, hh, 32 * s:32 * s + 32], rhs=kig[0:64, s, 0:ncol],
                        start=True, stop=True), reads=[r_qis, r_kig], writes=[r_pb[bi]])
                else:
                    P.op("pe", lambda e, bi=bi, hh=hh, s=s, ncol=ncol, kig=kig: e.matmul(
                        pb[bi][64:128, 0:ncol], lhsT=qis23[0:64, s - 2, hh, :], rhs=kig[0:64, s, 0:ncol],
                        start=(s == 2), stop=(s == 3), skip_group_check=True), reads=[r_qis23, r_kig], writes=[r_pb[bi]])
            P.op("act", lambda e, bi=bi, rl=rl, ncol=ncol: e.activation(out=rl[:, 0:ncol], in_=pb[bi][:, 0:ncol], func=AF.Relu),
                 reads=[r_pb[bi]], writes=[r_rl])
            if hh == 0:
                if g0 >= PAST:
                    P.op("dve", lambda e, rl=rl, g0=g0, ncol=ncol: e.scalar_tensor_tensor(
                        out=scs[:, g0:g0 + ncol], in0=rl[:, 0:ncol], scalar=wi_all[:, NT_Q, 0:1], in1=nms[:, 0:ncol],
                        op0=ALU.mult, op1=ALU.add), reads=[r_rl, r_wi, r_nms], writes=[r_scs])
                else:
                    P.op("dve", lambda e, rl=rl, g0=g0, ncol=ncol: e.tensor_scalar(
                        out=scs[:, g0:g0 + ncol], in0=rl[:, 0:ncol], scalar1=wi_all[:, NT_Q, 0:1], scalar2=None, op0=ALU.mult),
                        reads=[r_rl, r_wi], writes=[r_scs])
            else:
                P.op("dve", lambda e, rl=rl, g0=g0, ncol=ncol, hh=hh: e.scalar_tensor_tensor(
                    out=scs[:, g0:g0 + ncol], in0=rl[:, 0:ncol], scalar=wi_all[:, NT_Q, hh:hh + 1], in1=scs[:, g0:g0 + ncol],
                    op0=ALU.mult, op1=ALU.add), reads=[r_rl, r_wi, r_scs], writes=[r_scs])
    m8, r_m8 = m8_ring.next()
    thr, r_thr = thr_ring.next()
    for r in range(32):
        src = scs if r == 0 else wks
        r_src = r_scs if r == 0 else r_wks
        P.op("dve", lambda e, src=src: e.max(out=m8, in_=src), reads=[r_src], writes=[r_m8])
        if r < 31:
            P.op("dve", lambda e, src=src: e.match_replace(out=wks, in_to_replace=m8, in_values=src, imm_value=NEG),
                 reads=[r_src, r_m8], writes=[r_wks])
    P.op("dve", lambda e: e.tensor_scalar(out=thr, in0=m8[:, 7:8], scalar1=-1.0e29, scalar2=None, op0=ALU.max),
         reads=[r_m8], writes=[r_thr])
    for k0 in range(0, NPG + 1, 8):
        nk = min(8, NPG + 1 - k0)
        bt = pTs.next()
        ptb_ = bank_bf(bt)
        sels, r_sels = sels_ring.next()
        P.op("dve", lambda e, sels=sels, k0=k0, nk=nk: e.tensor_scalar(out=sels[:, 0:nk * 128], in0=scs[:, k0 * 128:(k0 + nk) * 128],
                                                                      scalar1=thr[:, 0:1], scalar2=None, op0=ALU.is_ge),
             reads=[r_scs, r_thr], writes=[r_sels])
        for kk in range(nk):
            kb = k0 + kk
            P.op("pe", lambda e, ptb_=ptb_, kk=kk, sels=sels: e.transpose(out=ptb_[:, kk, :], in_=sels[:, kk * 128:(kk + 1) * 128], identity=identb),
                 reads=[r_sels, r_identb], writes=[r_pb[bt]])
        P.op("act", lambda e, ptb_=ptb_, k0=k0, nk=nk: e.activation(out=selTs[:, k0:k0 + nk, :], in_=ptb_[:, 0:nk, :], func=AF.Copy),
             reads=[r_pb[bt]], writes=[r_selTs])
    sample_attn("dsa")
    barrier(P)
    A.release(att_mark)
    if dbg:
        P.dma("sp", lambda e: e.dma_start(out=dbg_att.rearrange("t p d -> p t d"), in_=att), reads=r_att, writes=[r_out],
              owner=r_att[0], kind="out", final=True)

    hnT_scr = dscr("hnT_scr", [128, NTQ, 16, 128], BF16)
    r_hnT = P.res("hnT_scr")
    aTall, _ = A.alloc("aTall", [NTQ, 16, 128], BF16)
    r_aT = [P.res(f"aT{i}") for i in range(NTQ)]
    pTe = BankRing([0, 1])
    for s in range(NTQ):
        for g in range(2):
            b = pTe.next()
            pt = bank_bf(b)
            for jj in range(8):
                kc = g * 8 + jj
                P.op("pe", lambda e, kc=kc, jj=jj, pt=pt, s=s: e.transpose(out=pt[:, jj, :], in_=att[:, s, kc * 128:(kc + 1) * 128],
                                                                          identity=identb),
                     reads=[r_att[s], r_identb], writes=[r_pb[b]])
            P.op("act", lambda e, g=g, pt=pt, s=s: e.activation(out=aTall[:, s, g * 8:(g + 1) * 8, :], in_=pt, func=AF.Copy),
                 reads=[r_pb[b]], writes=[r_aT[s]])
    wo_ring = A.ring("wo", [16, 512], BF16, 2)
    xo_ring = A.ring("xo", [512], F32, 3)
    hc1_ring = A.ring("hc1", [512], F32, 3)
    pH = BankRing([2, 3, 4, 5])
    wov = w_out.rearrange("(kc p) n -> p kc n", p=128)
    def load_wo(c4):
        wo, r_wo = wo_ring.next()
        for q4 in range(0, 16, 4):
            P.dma("pool", lambda e, c4=c4, q4=q4, wo=wo: e.dma_start(out=wo[:, q4:q4 + 4, :],
                                                                     in_=wov[:, q4:q4 + 4, c4 * 512:(c4 + 1) * 512]),
                  writes=[r_wo], owner=r_wo)
        return wo, r_wo

    pending_wo = load_wo(0)
    for c4 in range(4):
        wo, r_wo = pending_wo
        if c4 + 1 < 4:
            pending_wo = load_wo(c4 + 1)
        for s in range(NTQ):
            xo, r_xo = xo_ring.next()
            hc1, r_hc1 = hc1_ring.next()
            P.dma("sp", lambda e, xo=xo, s=s, c4=c4: e.dma_start(out=xo, in_=rows(xq, s)[:, c4 * 512:(c4 + 1) * 512]),
                  writes=[r_xo], owner=r_xo)
            b = pH.next()
            for kc in range(16):
                P.op("pe", lambda e, kc=kc, s=s, b=b, wo=wo: e.matmul(pb[b][:, :], lhsT=aTall[:, s, kc, :], rhs=wo[:, kc, :],
                                                                     start=(kc == 0), stop=(kc == 15)),
                     reads=[r_aT[s], r_wo], writes=[r_pb[b]])
            P.op("dve", lambda e, b=b, hc1=hc1, xo=xo: e.tensor_tensor(out=hc1, in0=pb[b][:, :], in1=xo, op=ALU.add),
                 reads=[r_pb[b], r_xo], writes=[r_hc1])
            P.dma("sp", lambda e, hc1=hc1, s=s, c4=c4: e.dma_start(out=rows(h_scr, s)[:, c4 * 512:(c4 + 1) * 512], in_=hc1),
                  reads=[r_hc1], writes=[r_hscr], owner=r_hc1, kind="out")
    barrier(P)
    A.release(base_mark)
    mE = make_proj(NTQ, gF_d, slim=True)
    hf_ring = A.ring("hf", [D], F32, 2)
    hb_ring = A.ring("hb", [D], BF16, 2)
    for s in range(NTQ):
        hf, r_hf = hf_ring.next()
        hb, r_hb = hb_ring.next()
        P.dma("sp", lambda e, hf=hf, s=s: e.dma_start(out=hf, in_=rows(h_scr, s)), reads=[r_hscr], writes=[r_hf], owner=r_hf)
        if dbg:
            P.dma("sp", lambda e, hf=hf, s=s: e.dma_start(out=rows(dbg_h, s), in_=hf), reads=[r_hf], writes=[r_out], owner=r_hf,
                  kind="out", final=True)
        P.op("act", lambda e, hf=hf, s=s: e.activation(out=mE.junk, in_=hf, func=AF.Square, accum_out=mE.ss[:, s:s + 1]),
             reads=[r_hf], writes=[mE.r_junk, mE.r_ss[s]])
        rstd_col = mE.ss[:, s:s + 1]
        P.op("act", lambda e, rstd_col=rstd_col: e.activation(out=rstd_col, in_=rstd_col, func=AF.Ln, scale=1.0 / D, bias=EPS),
             reads=[mE.r_ss[s]], writes=[mE.r_ss[s]])
        P.op("act", lambda e, rstd_col=rstd_col: e.activation(out=rstd_col, in_=rstd_col, func=AF.Exp, scale=-0.5),
             reads=[mE.r_ss[s]], writes=[mE.r_ss[s]])
        P.op("dve", lambda e, hf=hf, hb=hb, rstd_col=rstd_col: e.scalar_tensor_tensor(out=hb, in0=hf, scalar=rstd_col, in1=mE.gvec,
                                                                                    op0=ALU.mult, op1=ALU.mult),
             reads=[r_hf, mE.r_ss[s], mE.r_gvec], writes=[r_hb])
        for g in range(2):
            b = mE.pT.next()
            pt = bank_bf(b)
            for jj in range(8):
                kc = g * 8 + jj
                P.op("pe", lambda e, kc=kc, jj=jj, pt=pt, hb=hb: e.transpose(out=pt[:, jj, :], in_=hb[:, kc * 128:(kc + 1) * 128],
                                                                            identity=identb),
                     reads=[r_hb, r_identb], writes=[r_pb[b]])
            P.op("act", lambda e, g=g, pt=pt, s=s: e.activation(out=mE.xT[:, s, g * 8:(g + 1) * 8, :], in_=pt, func=AF.Copy),
                 reads=[r_pb[b]], writes=[mE.r_xT[s]])
        P.dma("sp", lambda e, s=s: e.dma_start(out=hnT_scr[:, s, :, :], in_=mE.xT[:, s, :, :]), reads=[mE.r_xT[s]], writes=[r_hnT],
              owner=mE.r_xT[s], kind="out")
    barrier(P)
    A.release(base_mark)

    hnT, r_hn = A.alloc("hnT", [NTQ, 16, 128], BF16)
    P.dma("sp", lambda e: e.dma_start(out=hnT, in_=hnT_scr[:, :, :, :]), reads=[r_hnT], writes=[r_hn], owner=r_hn)
    cw, r_cw = A.alloc("cw", [NFF, 3], F32)
    cb, r_cb = A.alloc("cb", [NFF], F32)
    P.dma("sp", lambda e: e.dma_start(out=cw, in_=cw_d[:, :, :]), writes=[r_cw], owner=r_cw)
    P.dma("sp", lambda e: e.dma_start(out=cb, in_=cb_d[:, :]), writes=[r_cb], owner=r_cb)
    cst8, r_cst8 = A.alloc("cst8", [D_FF], F32)
    cstT, r_cstT = A.alloc("cstT", [NFF, 8], F32)
    P.dma("sp", lambda e: e.dma_start(out=cst8[0:8, :], in_=cst_d[:, :]), writes=[r_cst8], owner=r_cst8)
    pc = pb[0][:, 0:NFF * 8].rearrange("p (a b) -> p a b", b=8)
    for f in range(NFF):
        P.op("pe", lambda e, f=f: e.transpose(out=pc[:, f, :], in_=cst8[0:8, f * 128:(f + 1) * 128], identity=identf[0:8, 0:8]),
             reads=[r_cst8, r_identf], writes=[r_pb[0]])
    P.op("act", lambda e: e.activation(out=cstT, in_=pc, func=AF.Copy), reads=[r_pb[0]], writes=[r_cstT])
    wg_ring = A.ring("wg", [16, 512], BF16, 2)
    wu_ring = A.ring("wu", [16, 512], BF16, 2)
    NP = NTOK + 2
    g_ring = A.ring("g_sb", [NP], F32, 2)
    u_ring = A.ring("u_sb", [NTOK], F32, 2)
    ac_ring = A.ring("acc", [NTOK], F32, 2)
    a_ring = A.ring("a_sb", [NTQ, 128], BF16, 2)
    gs_ring = A.ring("gsel", [16], F32, 2)
    for t_, r_ in g_ring.items:
        P.op("pool", lambda e, t_=t_: e.memset(t_[:, 0:2], 0.0), writes=[r_])
    pG = BankRing([0, 1, 2, 3, 4, 5, 6, 7])
    groups = [(0, 4), (4, 4), (8, NTQ - 8)]
    wgv = w_gate.rearrange("(kc p) n -> p kc n", p=128)
    wuv = w_up.rearrange("(kc p) n -> p kc n", p=128)
    def load_gu(c0):
        nf = min(4, NFF - c0)
        wg, r_wg = wg_ring.next()
        wu, r_wu = wu_ring.next()
        for q4 in range(0, 16, 4):
            P.dma("pool", lambda e, q4=q4, wg=wg, c0=c0, nf=nf: e.dma_start(out=wg[:, q4:q4 + 4, 0:nf * 128],
                                                                          in_=wgv[:, q4:q4 + 4, c0 * 128:(c0 + nf) * 128]),
                  writes=[r_wg], owner=r_wg)
            P.dma("pool", lambda e, q4=q4, wu=wu, c0=c0, nf=nf: e.dma_start(out=wu[:, q4:q4 + 4, 0:nf * 128],
                                                                          in_=wuv[:, q4:q4 + 4, c0 * 128:(c0 + nf) * 128]),
                  writes=[r_wu], owner=r_wu)
        return wg, r_wg, wu, r_wu

    pending_gu = load_gu(0)
    for c0 in range(0, NFF, 4):
        nf = min(4, NFF - c0)
        wg, r_wg, wu, r_wu = pending_gu
        if c0 + 4 < NFF:
            pending_gu = load_gu(c0 + 4)
        for fi in range(nf):
            f = c0 + fi
            g_sb, r_g = g_ring.next()
            u_sb, r_u = u_ring.next()
            acc, r_acc = ac_ring.next()
            a_sb, r_a = a_ring.next()
            gsel, r_gsel = gs_ring.next()
            for (t0, nt) in groups:
                bg = pG.next()
                bu = pG.next()
                n = nt * 128
                for kc in range(16):
                    P.op("pe", lambda e, kc=kc, bg=bg, wg=wg, fi=fi, t0=t0, nt=nt, n=n: e.matmul(
                        pb[bg][:, 0:n], lhsT=wg[:, kc, fi * 128:(fi + 1) * 128], rhs=hnT[:, t0:t0 + nt, kc, :],
                        start=(kc == 0), stop=(kc == 15)), reads=[r_wg, r_hn], writes=[r_pb[bg]])
                for kc in range(16):
                    P.op("pe", lambda e, kc=kc, bu=bu, wu=wu, fi=fi, t0=t0, nt=nt, n=n: e.matmul(
                        pb[bu][:, 0:n], lhsT=wu[:, kc, fi * 128:(fi + 1) * 128], rhs=hnT[:, t0:t0 + nt, kc, :],
                        start=(kc == 0), stop=(kc == 15)), reads=[r_wu, r_hn], writes=[r_pb[bu]])
                P.op("act", lambda e, bg=bg, g_sb=g_sb, t0=t0, n=n: e.activation(out=g_sb[:, 2 + t0 * 128:2 + t0 * 128 + n],
                                                                              in_=pb[bg][:, 0:n], func=AF.Copy),
                     reads=[r_pb[bg]], writes=[r_g])
                P.op("act", lambda e, bu=bu, u_sb=u_sb, t0=t0, n=n: e.activation(out=u_sb[:, t0 * 128:t0 * 128 + n],
                                                                              in_=pb[bu][:, 0:n], func=AF.Copy),
                     reads=[r_pb[bu]], writes=[r_u])
            P.op("pool", lambda e, g_sb=g_sb, gsel=gsel: e.tensor_copy(out=gsel[:, 0:2], in_=g_sb[:, 2 + 1024:2 + 1026]),
                 reads=[r_g], writes=[r_gsel])
            sv = g_sb[:, 2 + 1152:2 + 1280].rearrange("p (s j) -> p s j", j=32)
            P.op("pool", lambda e, sv=sv, gsel=gsel: e.tensor_copy(out=gsel[:, 2:10].rearrange("p (s r) -> p s r", r=2), in_=sv[:, :, 8:10]),
                 reads=[r_g], writes=[r_gsel])
            P.dma("sp", lambda e, gsel=gsel, f=f: e.dma_start(out=gT_o[f], in_=gsel), reads=[r_gsel], writes=[r_out], owner=r_gsel,
                  kind="out", final=True)
            P.op("pool", lambda e, sv=sv, f=f: e.tensor_copy(out=sv[:, :, 0:2], in_=cstT[:, f, :].rearrange("p (s r) -> p s r", r=2)),
                 reads=[r_cstT, r_gsel], writes=[r_g])
            P.op("dve", lambda e, g_sb=g_sb, acc=acc, f=f: e.tensor_scalar(out=acc, in0=g_sb[:, 2:NP], scalar1=cw[:, f, 2:3],
                                                                         scalar2=cb[:, f:f + 1], op0=ALU.mult, op1=ALU.add),
                 reads=[r_g, r_cw, r_cb], writes=[r_acc])
            P.op("dve", lambda e, g_sb=g_sb, acc=acc, f=f: e.scalar_tensor_tensor(out=acc, in0=g_sb[:, 1:NP - 1], scalar=cw[:, f, 1:2],
                                                                                in1=acc, op0=ALU.mult, op1=ALU.add),
                 reads=[r_g, r_cw, r_acc], writes=[r_acc])
            P.op("dve", lambda e, g_sb=g_sb, acc=acc, f=f: e.scalar_tensor_tensor(out=acc, in0=g_sb[:, 0:NP - 2], scalar=cw[:, f, 0:1],
                                                                                in1=acc, op0=ALU.mult, op1=ALU.add),
                 reads=[r_g, r_cw, r_acc], writes=[r_acc])
            P.op("act", lambda e, acc=acc: e.activation(out=acc, in_=acc, func=AF.Silu), reads=[r_acc], writes=[r_acc])
            P.op("dve", lambda e, acc=acc, u_sb=u_sb, a_sb=a_sb: e.tensor_tensor(out=a_sb.rearrange("p t k -> p (t k)"), in0=acc, in1=u_sb,
                                                                               op=ALU.mult), reads=[r_acc, r_u], writes=[r_a])
            P.dma("sp", lambda e, a_sb=a_sb, f=f: e.dma_start(out=aT_scr[:, :, f, :].rearrange("t p k -> p t k"), in_=a_sb),
                  reads=[r_a], writes=[r_aTscr], owner=r_a, kind="out")
    barrier(P)
    A.release(base_mark)

    wd_ring = A.ring("wd", [NFF, 512], BF16, 2)
    at_ring = A.ring("aTt", [NFF, 128], BF16, 2)
    hc_ring = A.ring("hch", [512], F32, 3)
    yc_ring = A.ring("ych", [512], F32, 3)
    wdv = w_down.rearrange("(f p) n -> p f n", p=128)
    def load_wd(c4):
        wd, r_wd = wd_ring.next()
        for f0 in range(0, NFF, 4):
            nf = min(4, NFF - f0)
            P.dma("pool", lambda e, wd=wd, f0=f0, nf=nf, c4=c4: e.dma_start(out=wd[:, f0:f0 + nf, :],
                                                                          in_=wdv[:, f0:f0 + nf, c4 * 512:(c4 + 1) * 512]),
                  writes=[r_wd], owner=r_wd)
        return wd, r_wd

    pending_wd = load_wd(0)
    for c4 in range(4):
        wd, r_wd = pending_wd
        if c4 + 1 < 4:
            pending_wd = load_wd(c4 + 1)
        for t in range(NTQ):
            aTt, r_at = at_ring.next()
            hch, r_hc = hc_ring.next()
            ych, r_yc = yc_ring.next()
            P.dma("sp", lambda e, aTt=aTt, t=t: e.dma_start(out=aTt, in_=aT_scr[t]), reads=[r_aTscr], writes=[r_at], owner=r_at)
            P.dma("sp", lambda e, hch=hch, t=t, c4=c4: e.dma_start(out=hch, in_=rows(h_scr, t)[:, c4 * 512:(c4 + 1) * 512]),
                  reads=[r_hscr], writes=[r_hc], owner=r_hc)
            b = pG.next()
            for f in range(NFF):
                P.op("pe", lambda e, f=f, b=b, aTt=aTt, wd=wd: e.matmul(pb[b][:, :], lhsT=aTt[:, f, :], rhs=wd[:, f, :],
                                                                       start=(f == 0), stop=(f == NFF - 1)),
                     reads=[r_at, r_wd], writes=[r_pb[b]])
            P.op("dve", lambda e, b=b, hch=hch, ych=ych: e.tensor_tensor(out=ych, in0=pb[b][:, :], in1=hch, op=ALU.add),
                 reads=[r_pb[b], r_hc], writes=[r_yc])
            P.dma("sp", lambda e, ych=ych, t=t, c4=c4: e.dma_start(out=rows(y_o, t)[:, c4 * 512:(c4 + 1) * 512], in_=ych),
                  reads=[r_yc], writes=[r_out], owner=r_yc, kind="out", final=True)

    P.emit()
    P.close()
    return nc


def _rope_tab(pos):
    pos = np.asarray(pos, dtype=np.float32)
    out = np.zeros((pos.shape[0], 192), np.float32)
    inv128 = (10000.0 ** (-np.arange(64, dtype=np.float32) / 64)).astype(np.float32)
    inv64 = (10000.0 ** (-np.arange(32, dtype=np.float32) / 32)).astype(np.float32)
    a = pos[:, None] * inv128[None, :]
    out[:, 0:64] = np.cos(a)
    out[:, 64:128] = np.sin(a)
    a = pos[:, None] * inv64[None, :]
    out[:, 128:160] = np.cos(a)
    out[:, 160:192] = np.sin(a)
    return out


def _core_consts(j):
    f32 = np.float32
    p0 = 1024 * j - 2
    fmask = np.zeros((NT_Q, 128, 8, 128), f32)
    bbias = np.zeros((128, NT_Q, 32), f32)
    nm = np.full((NT_Q, 128, SEQ), NEG, f32)
    oh = np.zeros((128, NT_Q, 32), f32)
    kk = np.arange(128)
    for i in range(NT_Q):
        t = p0 + 128 * i + np.arange(128)
        for sl, kb in enumerate(MASK_KBS(i)):
            s = 128 * kb + kk
            fmask[i, :, sl, :] = ((s[:, None] <= t[None, :]) & (t[None, :] >= 0)).astype(f32)
        for kb in range(32):
            if 128 * kb > t[-1]:
                bbias[:, i, kb] = NEG
        s_all = np.arange(SEQ)
        vis = (s_all[None, :] <= t[:, None]) & (t[:, None] >= 0)
        nm[i][vis] = 0.0
        oh[:, i, min(31, 8 * j + i)] = 1.0
    return fmask, bbias, nm, oh


_NC_CACHE = {}


def kernel(x_prompt, x_sample, cache_fox_k, cache_fox_v, cache_fox_logf, cache_dsa_k, cache_dsa_v,
           cache_idx_k, state_ffn_conv, page_table, w_in, b_f, g_qa, g_ka, g_qb, g_kb, g_attn,
           w_out, g_ffn, w_gate, w_up, conv_w, conv_b, w_down, _dbg=False):
    f32 = np.float32
    x_prompt = np.asarray(x_prompt, f32)
    x_sample = np.asarray(x_sample, f32)
    w_in0 = np.asarray(w_in, f32)[0]
    o = np.cumsum([0, 1024, 1024, 1024, 8, 1024, 256, 256, 1024, 64, 16])
    qa, ka, va, fa, qb, kb, vb, qi, ki, wi = [slice(o[i], o[i + 1]) for i in range(10)]
    w_kv = np.ascontiguousarray(np.concatenate([w_in0[:, ka], w_in0[:, va], w_in0[:, kb], w_in0[:, vb],
                                                w_in0[:, ki], w_in0[:, fa]], axis=1))
    w_q = np.ascontiguousarray(np.concatenate([w_in0[:, qa], w_in0[:, qb], w_in0[:, qi], w_in0[:, wi]], axis=1))
    ident = np.eye(128, dtype=f32)
    tri = np.triu(np.ones((128, 128), f32))
    gA = np.ascontiguousarray(np.broadcast_to(np.asarray(g_attn, f32)[0][None, :], (128, D)))
    gF = np.ascontiguousarray(np.broadcast_to(np.asarray(g_ffn, f32)[0][None, :], (128, D)))
    g4 = np.ascontiguousarray(np.broadcast_to(
        np.stack([np.asarray(g_qa, f32)[0], np.asarray(g_ka, f32)[0], np.asarray(g_qb, f32)[0],
                  np.asarray(g_kb, f32)[0]])[None], (128, 4, 128)))
    bfr = np.ascontiguousarray(np.broadcast_to(np.asarray(b_f, f32)[0][None, :], (128, 8)))
    cw = np.ascontiguousarray(np.asarray(conv_w, f32)[0].reshape(3, NFF, 128).transpose(2, 1, 0))
    cb = np.ascontiguousarray(np.asarray(conv_b, f32)[0].reshape(NFF, 128).T)
    tabA = _rope_tab(np.arange(SEQ))
    spos = np.zeros(128, f32)
    for s in range(4):
        spos[32 * s + 2:32 * s + 10] = PAST + np.arange(8)
    shared = dict(w_kv=w_kv, w_q=w_q, w_out=np.ascontiguousarray(np.asarray(w_out, f32)[0]),
                  w_gate=np.ascontiguousarray(np.asarray(w_gate, f32)[0]), w_up=np.ascontiguousarray(np.asarray(w_up, f32)[0]),
                  w_down=np.ascontiguousarray(np.asarray(w_down, f32)[0]), ident=ident, tri=tri, gA=gA, gF=gF, g4=g4, bfr=bfr,
                  cw=cw, cb=cb, tabA=tabA)
    consts = [_core_consts(j) for j in range(4)]
    lst = np.zeros((128, 128), f32)
    o64 = np.zeros((128, 128), f32)
    for p_ in range(64):
        lst[p_, p_ + 1:] = 1.0
        o64[p_, :] = 1.0
    lblk = np.zeros((128, 128), f32)
    esel = np.zeros((128, 4, 128), f32)
    smask = np.zeros((128, 4, 32), f32)
    nms = np.full((128, 128), NEG, f32)
    for s in range(4):
        esel[32 * s + 9, s, :] = 1.0
        for i in range(8):
            for i2 in range(i + 1):
                lblk[32 * s + 2 + i2, 32 * s + 2 + i] = 1.0
                smask[32 * s + 2 + i2, s, 2 + i] = 1.0
                nms[32 * s + 2 + i, 32 * s + 2 + i2] = 0.0
    npool = np.asarray(cache_fox_k).shape[1]
    assert npool == NPOOL_PAGES
    shared.update(
        cfk=np.asarray(cache_fox_k, f32).reshape(npool * 128, 1024), cfv=np.asarray(cache_fox_v, f32).reshape(npool * 128, 1024),
        cfl=np.asarray(cache_fox_logf, f32).reshape(npool, 1024), cdk=np.asarray(cache_dsa_k, f32).reshape(npool * 128, 256),
        cdv=np.asarray(cache_dsa_v, f32).reshape(npool * 128, 256), cik=np.asarray(cache_idx_k, f32).reshape(npool * 128, 64),
        lst=lst, o64=o64, lblk=lblk, esel=esel, smask=smask, nms=nms)
    page_table = np.asarray(page_table, np.int32)
    NTQ = NT_Q + 1
    in_maps = []
    for c in range(8):
        b, j = c // 4, c % 4
        p0 = 1024 * j - 2
        xq = np.zeros((NTQ * 128, D), f32)
        lo, hi = max(p0, 0), min(p0 + NT_Q * 128, SEQ)
        xq[lo - p0:hi - p0] = x_prompt[b, lo:hi]
        for s in range(4):
            xq[NT_Q * 128 + 32 * s + 2:NT_Q * 128 + 32 * s + 10] = x_sample[4 * c + s]
        tabQ = _rope_tab(np.concatenate([np.clip(p0 + np.arange(NT_Q * 128), 0, SEQ - 1), spos]))
        fmask, bbias, nm, oh = consts[j]
        cst = np.ascontiguousarray(np.asarray(state_ffn_conv, f32)[0, 4 * c:4 * c + 4].reshape(8, D_FF))
        m = dict(shared)
        m.update(xb=np.ascontiguousarray(x_prompt[b]), xq=xq, tabQ=tabQ, fmask=fmask, bbias=bbias, nm=nm, oh=oh, cst=cst,
                 pt=np.ascontiguousarray(page_table[4 * c:4 * c + 4]))
        in_maps.append(m)

    key = bool(_dbg)
    if key not in _NC_CACHE:
        _NC_CACHE[key] = build_program(dbg=key)
    nc = _NC_CACHE[key]
    res = run_bass_kernel_spmd(nc, in_maps, core_ids=list(range(8)))
    R = res.results

    def prow(name, shape):
        return np.stack([R[0][name], R[4][name]]).reshape((1, 2, SEQ) + shape)

    def srow(name, shape):
        out = np.zeros((32, 8) + shape, f32)
        for c in range(8):
            a = R[c][name]
            for s in range(4):
                out[4 * c + s] = a[32 * s + 2:32 * s + 10].reshape((8,) + shape)
        return out[None]

    y_p = np.zeros((2, SEQ, D), f32)
    y_s = np.zeros((32, 8, D), f32)
    conv_p = np.zeros((1, 2, 2, D_FF), f32)
    conv_s = np.zeros((1, 32, 2, D_FF), f32)
    for c in range(8):
        b, j = c // 4, c % 4
        yo = R[c]["y_o"]
        y_p[b, 1024 * j:1024 * (j + 1)] = yo[2:1026]
        gt = R[c]["gT_o"]
        for s in range(4):
            y_s[4 * c + s] = yo[NT_Q * 128 + 32 * s + 2:NT_Q * 128 + 32 * s + 10]
            for r in range(2):
                conv_s[0, 4 * c + s, r] = gt[:, :, 2 + 2 * s + r].reshape(D_FF)
        if j == 3:
            for r in range(2):
                conv_p[0, b, r] = gt[:, :, r].reshape(D_FF)
    outs = (y_p, y_s,
            prow("o_fk", (8, 128)), prow("o_fv", (8, 128)), prow("o_fl", (8,)),
            prow("o_dk", (2, 128)), prow("o_dv", (2, 128)), prow("o_ik", (64,)), conv_p,
            srow("s_fk", (8, 128)), srow("s_fv", (8, 128)), srow("s_fl", (8,)),
            srow("s_dk", (2, 128)), srow("s_dv", (2, 128)), srow("s_ik", (64,)), conv_s)
    if _dbg:
        return outs, R
    return outs
```
